# Optimizing a Trainium2 kernel written in Bass

```python
import math
import jax
import jax.numpy as jnp
from jax import lax
import numpy as np

D_MODEL = 4096
BATCH = 2
SEQ = 4096
DEPTH = 1

D_RWKV = D_MODEL // 2
RWKV_HEAD = 64
N_RWKV_HEADS = D_RWKV // RWKV_HEAD
RANK_DECAY = max(32, int(round(1.8 * D_RWKV ** 0.5 / 32)) * 32)
RANK_ICLR = max(32, int(round(1.8 * D_RWKV ** 0.5 / 32)) * 32)
RANK_GATE = max(32, int(round(0.6 * D_RWKV ** 0.8 / 32)) * 32)
N_RWKV_COLS = 3 * D_RWKV + RANK_DECAY + RANK_ICLR + RANK_GATE
EPS_GN = 64e-5

D_DIFF = D_MODEL // 2
DIFF_HEAD = 64
N_DIFF_HEADS = D_DIFF // (2 * DIFF_HEAD)
DIFF_VDIM = 2 * DIFF_HEAD
Q_BLOCK = 128
EPS_SUBLN = 1e-5

N_BRANCHES = 2
N_IN_COLS = N_RWKV_COLS + 3 * D_DIFF + N_BRANCHES * D_MODEL
D_FF = 4 * D_MODEL
EPS_RMS = 1e-6

kernel_name = "hybrid_rwkv7_diffattn_gated_encoder_block"


def _rmsnorm(t, g, eps=EPS_RMS):
    tf = t.astype(jnp.float32)
    y = tf * lax.rsqrt(jnp.mean(tf * tf, axis=-1, keepdims=True) + eps)
    return (y * g).astype(t.dtype)


def _wkv7_scan(r, w, k, v, kk, a, reverse):
    B, S, H, N = r.shape
    xs = tuple(jnp.moveaxis(t, 1, 0) for t in (r, w, k, v, kk, a))

    def step(state, inp):
        r_t, w_t, k_t, v_t, kk_t, a_t = inp
        sa = jnp.einsum('bhij,bhj->bhi', state, -kk_t)
        state = (state * w_t[:, :, None, :]
                 + sa[..., None] * (kk_t * a_t)[:, :, None, :]
                 + v_t[..., None] * k_t[:, :, None, :])
        o_t = jnp.einsum('bhij,bhj->bhi', state, r_t)
        return state, o_t

    init = jnp.zeros((B, H, N, N), jnp.float32)
    _, out = lax.scan(step, init, xs, reverse=reverse)
    return jnp.moveaxis(out, 0, 1)


def _rwkv7_bidir(p, mu_prev, mu_next, w0f, w2f, w0b, w2b, a0f, a2f, a0b, a2b,
                 g2, k_k, k_a, r_k, ln_g, ln_b):
    B, S, _ = p.shape
    p = p.astype(jnp.float32)
    prev = jnp.pad(p[:, :-1], ((0, 0), (1, 0), (0, 0)))
    nxt = jnp.pad(p[:, 1:], ((0, 0), (0, 1), (0, 0)))
    p = p + mu_prev * (prev - p) + mu_next * (nxt - p)
    splits = [D_RWKV, 2 * D_RWKV, 3 * D_RWKV, 3 * D_RWKV + RANK_DECAY,
              3 * D_RWKV + RANK_DECAY + RANK_ICLR]
    r, k, v, dw, da, dg = jnp.split(p, splits, axis=-1)

    def heads(t):
        return t.reshape(B, S, N_RWKV_HEADS, RWKV_HEAD)

    g = jax.nn.sigmoid(dg) @ g2
    kk = heads(k * k_k)
    kk = kk * lax.rsqrt(jnp.maximum(jnp.sum(kk * kk, axis=-1, keepdims=True), 1e-24))
    tw = jnp.tanh(dw)
    rh, vh = heads(r), heads(v)

    def direction(w0, w2, a0, a2, reverse):
        log_w = -jax.nn.softplus(-(w0 + tw @ w2)) - 0.5
        w = jnp.exp(-jnp.exp(log_w))
        a = jax.nn.sigmoid(a0 + da @ a2)
        kd = k * (1.0 + (a - 1.0) * k_a)
        o = _wkv7_scan(rh, heads(w), heads(kd), vh, kk, heads(a), reverse)
        return o, kd

    o_f, k_f = direction(w0f, w2f, a0f, a2f, False)
    o_b, k_b = direction(w0b, w2b, a0b, a2b, True)
    o = o_f + o_b
    mu = jnp.mean(o, axis=-1, keepdims=True)
    var = jnp.mean(jnp.square(o - mu), axis=-1, keepdims=True)
    on = ((o - mu) * lax.rsqrt(var + EPS_GN)).reshape(B, S, D_RWKV) * ln_g + ln_b
    bonus = jnp.sum(rh * heads(0.5 * (k_f + k_b)) * r_k, axis=-1, keepdims=True) * vh
    return (on + bonus.reshape(B, S, D_RWKV)) * g


def _diff_attention(q, k, v, lq1, lk1, lq2, lk2, subln_g, lambda_init):
    B, S, _ = q.shape
    H = N_DIFF_HEADS
    q = q.astype(jnp.float32).reshape(B, S, H, 2, DIFF_HEAD)
    k = k.astype(jnp.float32).reshape(B, S, H, 2, DIFF_HEAD)
    v = v.astype(jnp.float32).reshape(B, S, H, DIFF_VDIM)
    lam = (jnp.exp(jnp.sum(lq1 * lk1).astype(jnp.float32))
           - jnp.exp(jnp.sum(lq2 * lk2).astype(jnp.float32)) + lambda_init)
    scale = DIFF_HEAD ** -0.5
    slopes = jnp.exp2(-8.0 * jnp.arange(1, H + 1, dtype=jnp.float32) / H)
    kpos = jnp.arange(S, dtype=jnp.int32)
    n_blk = S // Q_BLOCK
    qb = jnp.moveaxis(q.reshape(B, n_blk, Q_BLOCK, H, 2, DIFF_HEAD), 1, 0)
    starts = jnp.arange(n_blk, dtype=jnp.int32) * Q_BLOCK

    def block(args):
        q_blk, start = args
        qpos = start + jnp.arange(Q_BLOCK, dtype=jnp.int32)
        dist = jnp.abs(qpos[:, None] - kpos[None, :]).astype(jnp.float32)
        bias = -slopes[:, None, None] * dist[None]
        s = jnp.einsum('bqhcd,bkhcd->bchqk', q_blk, k) * scale + bias[None, None]
        pr = jax.nn.softmax(s, axis=-1)
        attn = pr[:, 0] - lam * pr[:, 1]
        return jnp.einsum('bhqk,bkhe->bqhe', attn, v)

    out = lax.map(block, (qb, starts))
    out = jnp.moveaxis(out, 0, 1).reshape(B, S, H, DIFF_VDIM)
    out = out * lax.rsqrt(jnp.mean(out * out, axis=-1, keepdims=True) + EPS_SUBLN) * subln_g
    return (out * (1.0 - lambda_init)).reshape(B, S, D_DIFF)


def setup_inputs(seed: int = 0) -> dict:
    key = jax.random.key(seed)
    ks = iter(jax.random.split(key, 48))
    L = DEPTH

    def normal(shape, scale):
        return jax.random.normal(next(ks), shape, jnp.float32) * scale

    def gain(shape):
        return 1.0 + normal(shape, 0.02)

    def unif(shape, lo, hi):
        return jax.random.uniform(next(ks), shape, jnp.float32, minval=lo, maxval=hi)

    return {
        "x": normal((BATCH, SEQ, D_MODEL), 1.0),
        "attn_pre_norm": gain((L, D_MODEL)),
        "attn_post_norm": gain((L, D_MODEL)),
        "w_in": normal((L, D_MODEL, N_IN_COLS), D_MODEL ** -0.5),
        "shift_prev": unif((L, N_RWKV_COLS), 0.0, 0.5),
        "shift_next": unif((L, N_RWKV_COLS), 0.0, 0.5),
        "decay_bias_fwd": unif((L, D_RWKV), -6.0, -1.0),
        "decay_up_fwd": normal((L, RANK_DECAY, D_RWKV), 0.1 * RANK_DECAY ** -0.5),
        "decay_bias_bwd": unif((L, D_RWKV), -6.0, -1.0),
        "decay_up_bwd": normal((L, RANK_DECAY, D_RWKV), 0.1 * RANK_DECAY ** -0.5),
        "iclr_bias_fwd": normal((L, D_RWKV), 0.1),
        "iclr_up_fwd": normal((L, RANK_ICLR, D_RWKV), 0.1 * RANK_ICLR ** -0.5),
        "iclr_bias_bwd": normal((L, D_RWKV), 0.1),
        "iclr_up_bwd": normal((L, RANK_ICLR, D_RWKV), 0.1 * RANK_ICLR ** -0.5),
        "gate_up": normal((L, RANK_GATE, D_RWKV), RANK_GATE ** -0.5),
        "k_k": 0.85 + normal((L, D_RWKV), 0.02),
        "k_a": 1.0 + normal((L, D_RWKV), 0.02),
        "r_k": normal((L, N_RWKV_HEADS, RWKV_HEAD), 0.1),
        "ln_x_gain": gain((L, D_RWKV)),
        "ln_x_bias": normal((L, D_RWKV), 0.01),
        "lambda_q1": normal((L, DIFF_HEAD), 0.1),
        "lambda_k1": normal((L, DIFF_HEAD), 0.1),
        "lambda_q2": normal((L, DIFF_HEAD), 0.1),
        "lambda_k2": normal((L, DIFF_HEAD), 0.1),
        "subln_gain": gain((L, DIFF_VDIM)),
        "w_up_rwkv": normal((L, D_RWKV, D_MODEL), D_RWKV ** -0.5),
        "w_up_diff": normal((L, D_DIFF, D_MODEL), D_DIFF ** -0.5),
        "w_out": normal((L, D_MODEL, D_MODEL), D_MODEL ** -0.5),
        "mlp_pre_norm": gain((L, D_MODEL)),
        "mlp_post_norm": gain((L, D_MODEL)),
        "w_mlp_in": normal((L, D_MODEL, D_FF), D_MODEL ** -0.5),
        "w_mlp_out": normal((L, D_FF, D_MODEL), D_FF ** -0.5),
    }


def reference(x, attn_pre_norm, attn_post_norm, w_in, shift_prev, shift_next,
              decay_bias_fwd, decay_up_fwd, decay_bias_bwd, decay_up_bwd,
              iclr_bias_fwd, iclr_up_fwd, iclr_bias_bwd, iclr_up_bwd, gate_up,
              k_k, k_a, r_k, ln_x_gain, ln_x_bias,
              lambda_q1, lambda_k1, lambda_q2, lambda_k2, subln_gain,
              w_up_rwkv, w_up_diff, w_out, mlp_pre_norm, mlp_post_norm,
              w_mlp_in, w_mlp_out):
    col_splits = [N_RWKV_COLS, N_RWKV_COLS + D_DIFF, N_RWKV_COLS + 2 * D_DIFF,
                  N_RWKV_COLS + 3 * D_DIFF, N_RWKV_COLS + 3 * D_DIFF + D_MODEL]
    for l in range(DEPTH):
        lambda_init = 0.8 - 0.6 * math.exp(-0.3 * l)
        h = _rmsnorm(x, attn_pre_norm[l])
        proj = h @ w_in[l]
        p_rwkv, q, k, v, gate_a, gate_b = jnp.split(proj, col_splits, axis=-1)
        y_a = _rwkv7_bidir(p_rwkv, shift_prev[l], shift_next[l],
                           decay_bias_fwd[l], decay_up_fwd[l], decay_bias_bwd[l], decay_up_bwd[l],
                           iclr_bias_fwd[l], iclr_up_fwd[l], iclr_bias_bwd[l], iclr_up_bwd[l],
                           gate_up[l], k_k[l], k_a[l], r_k[l], ln_x_gain[l], ln_x_bias[l]).astype(x.dtype)
        y_b = _diff_attention(q, k, v, lambda_q1[l], lambda_k1[l], lambda_q2[l], lambda_k2[l],
                              subln_gain[l], lambda_init).astype(x.dtype)
        mixed = (jax.nn.sigmoid(gate_a) * (y_a @ w_up_rwkv[l])
                 + jax.nn.sigmoid(gate_b) * (y_b @ w_up_diff[l]))
        x = x + _rmsnorm(mixed @ w_out[l], attn_post_norm[l])
        h = _rmsnorm(x, mlp_pre_norm[l])
        u = jnp.square(jax.nn.relu(h @ w_mlp_in[l]))
        x = x + _rmsnorm(u @ w_mlp_out[l], mlp_post_norm[l])
    return x
```

```python
import math
import bisect
import numpy as np
import ml_dtypes
from contextlib import ExitStack
import concourse.bass as bass
import concourse.mybir as mybir
from concourse.bass_utils import run_bass_kernel_spmd


ENGS = ("tensor", "vector", "scalar", "gpsimd", "sync")
ROT = 12000


class Prog:
    def __init__(self, nc):
        self.nc = nc
        self.ops = []

    def op(self, eng, fn, reads=(), writes=(), dma_key=None, inc=None):
        self.ops.append(dict(eng=eng, fn=fn, reads=tuple(reads), writes=tuple(writes),
                             dma=dma_key is not None, key=dma_key, inc=inc))
        return len(self.ops) - 1

    def fence(self, eng="vector"):
        self.ops.append(dict(eng=eng, fn=self.fence_fn, reads=(), writes="ALL", dma=False, key=None, inc=None))

    def emit(self, stack):
        nc = self.nc
        ops = self.ops
        n = len(ops)
        allkeys = set()
        for o in ops:
            if o["writes"] != "ALL":
                allkeys.update(o["reads"]); allkeys.update(o["writes"])
        allkeys = tuple(sorted(allkeys, key=str))
        for o in ops:
            if o["writes"] == "ALL":
                o["writes"] = allkeys
        last_w = {}
        readers = {}
        deps = [None] * n
        for i, o in enumerate(ops):
            d = set()
            for r in o["reads"]:
                if r in last_w:
                    d.add(last_w[r])
            for w in o["writes"]:
                if w in last_w:
                    d.add(last_w[w])
                for j in readers.get(w, ()):
                    d.add(j)
            d.discard(i)
            dd = []
            for j in d:
                oj = ops[j]
                if (not oj["dma"]) and (not o["dma"]) and oj["eng"] == o["eng"] == "tensor":
                    continue
                dd.append(j)
            deps[i] = dd
            for r in o["reads"]:
                readers.setdefault(r, []).append(i)
            for w in o["writes"]:
                last_w[w] = i
                readers[w] = []
        needed = [False] * n
        for i in range(n):
            for j in deps[i]:
                needed[j] = True
        sem_handles = {}

        def get_sem(name):
            if name not in sem_handles:
                sem_handles[name] = stack.enter_context(nc.semaphore(name))
            return sem_handles[name]

        cnt = {}
        sig = [None] * n
        dma_cum_at = {}
        for i, o in enumerate(ops):
            if o["dma"]:
                base = "d_" + str(o["key"])
                inc = o["inc"] or 16
                lim = ROT
            else:
                if not needed[i]:
                    continue
                base = "e_" + o["eng"]
                inc = 1
                lim = ROT
            g, c = cnt.get(base, (0, 0))
            if c + inc > lim * (16 if o["dma"] else 1):
                g, c = g + 1, 0
            c += inc
            cnt[base] = (g, c)
            sig[i] = (base + "_" + str(g), c)
            if o["dma"]:
                dma_cum_at.setdefault(base, []).append((i, sig[i][0], c))
        import bisect
        dma_idx = {k: [t[0] for t in v] for k, v in dma_cum_at.items()}
        waits = [None] * n
        waited = {e: {} for e in ENGS}
        for i, o in enumerate(ops):
            need = {}
            for j in deps[i]:
                oj = ops[j]
                if oj["dma"]:
                    base = "d_" + str(oj["key"])
                    lst = dma_cum_at[base]
                    pos = bisect.bisect_left(dma_idx[base], i) - 1
                    sname_j, vj = sig[j]
                    k = pos
                    while lst[k][1] != sname_j:
                        k -= 1
                    sname, val = lst[k][1], lst[k][2]
                else:
                    sname, val = sig[j]
                if need.get(sname, 0) < val:
                    need[sname] = val
            wl = []
            wd = waited[o["eng"]]
            for sname, val in need.items():
                if wd.get(sname, 0) >= val:
                    continue
                wd[sname] = val
                wl.append((sname, val))
            waits[i] = wl
        self.n_waits = sum(len(w) for w in waits)
        per_eng = {e: [i for i, o in enumerate(ops) if o["eng"] == e] for e in ENGS}
        block = stack.enter_context(nc.Block())

        def body(engname):
            def f(eng):
                for i in per_eng[engname]:
                    for sname, val in waits[i]:
                        eng.wait_ge(get_sem(sname), val)
                    inst = ops[i]["fn"](eng)
                    if sig[i] is not None:
                        inst.then_inc(get_sem(sig[i][0]), (ops[i]["inc"] or 16) if ops[i]["dma"] else 1)
            return f

        for i in range(n):
            if sig[i] is not None:
                get_sem(sig[i][0])
        block.tensor(body("tensor"))
        block.vector(body("vector"))
        block.scalar(body("scalar"))
        block.gpsimd(body("gpsimd"))
        block.sync(body("sync"))
        return {e: len(v) for e, v in per_eng.items()}


F32 = mybir.dt.float32
BF16 = mybir.dt.bfloat16
AF = mybir.ActivationFunctionType
ALU = mybir.AluOpType
D = 4096
DFF = 16384
EPS = 1e-6
NTOK = 1024
TP = 512
FB = 256
NFB = DFF // FB


def make_shared(nc, P, st):
    ident_d = nc.dram_tensor("ident", [128, 128], BF16, kind="ExternalInput").ap()
    ones_d = nc.dram_tensor("ones", [128, 128], BF16, kind="ExternalInput").ap()
    gains = nc.dram_tensor("gains", [4, 128, D], F32, kind="ExternalInput").ap()
    ident = st.enter_context(nc.sbuf_tensor("ident_s", [128, 128], BF16))
    ones = st.enter_context(nc.sbuf_tensor("ones_s", [128, 128], BF16))
    small = st.enter_context(nc.sbuf_tensor("small", [128, 16], F32))
    dummy = st.enter_context(nc.sbuf_tensor("fdummy", [128, 8], F32))
    epsb = st.enter_context(nc.sbuf_tensor("epsb", [128, 2], F32))
    P.fence_fn = lambda e: e.memset(dummy[:], 0.0)
    P.op("vector", lambda e: e.memset(epsb[:, 0:1], EPS), writes=["epsb"])
    P.op("vector", lambda e: e.memset(epsb[:, 1:2], 1e-5), writes=["epsb"])
    P.op("sync", lambda e: e.dma_start(out=ident[:], in_=ident_d), writes=["ident"], dma_key="ident")
    P.op("sync", lambda e: e.dma_start(out=ones[:], in_=ones_d), writes=["ones"], dma_key="ones")
    psum = [st.enter_context(nc.psum_tensor(f"ps{i}", [128, 512], F32)) for i in range(8)]
    pst = [psum[6 + i][:, :].bitcast(BF16) for i in range(2)]
    state = dict(ps=0, pt=0)

    def next_ps():
        i = state["ps"]; state["ps"] = (i + 1) % 6
        return i
    return dict(ident=ident, ones=ones, small=small, epsb=epsb, psum=psum, pst=pst, state=state, next_ps=next_ps, gains=gains)


def body_B(nc, P, st, sh, ysrc, npass=2, stages=(0, 1, 2, 3, 4, 5), nfb=NFB):
    x = nc.dram_tensor("xo", [NTOK, D], F32, kind="ExternalInput").ap()
    wg = nc.dram_tensor("wg", [64, 128, 32, 128], F32, kind="ExternalInput").ap()
    wu = nc.dram_tensor("wu", [64, 128, 16, 128], F32, kind="ExternalInput").ap()
    wo = nc.dram_tensor("wo", [16, 128, 32, 256], F32, kind="ExternalInput").ap()
    w1 = nc.dram_tensor("w1", [NFB, 128, 32, FB], F32, kind="ExternalInput").ap()
    w2 = nc.dram_tensor("w2", [NFB, 128, FB // 128, D], F32, kind="ExternalInput").ap()
    out = nc.dram_tensor("out", [NTOK, D], F32, kind="ExternalOutput").ap()
    x1_d = nc.dram_tensor("x1_d", [NTOK, D], F32).ap()
    gains, ident, small, epsb = sh["gains"], sh["ident"], sh["small"], sh["epsb"]
    psum, pst, state, next_ps = sh["psum"], sh["pst"], sh["state"], sh["next_ps"]
    if True:
        arena = st.enter_context(nc.sbuf_tensor("arena", [128, 172 * 256], F32))
        if ysrc[0] == "gather":
            selt = st.enter_context(nc.sbuf_tensor("selt", [128, 4], F32))
            P.op("sync", lambda e: e.dma_start(out=selt[:], in_=ysrc[2]), writes=["selt"], dma_key="selt")

        def AV(off_kib, size_kib, dt):
            v = arena[:, off_kib * 256:(off_kib + size_kib) * 256]
            return v.bitcast(BF16) if dt == BF16 else v

        bufA = AV(0, 32, BF16).rearrange("p (k t) -> p k t", k=32)
        bufB = AV(32, 32, BF16).rearrange("p (k t) -> p k t", k=32)
        bufM = AV(64, 32, BF16).rearrange("p (k t) -> p k t", k=32)
        bufZ = AV(96, 64, F32).rearrange("p (a d) -> p a d", a=4)
        wbuf = [AV(96 + 24 * s, 24, BF16) for s in range(2)]
        wo_v = [AV(32 + 16 * s, 16, BF16).rearrange("p (k j) -> p k j", k=32) for s in range(2)]
        w1_v = [AV(64 + 16 * s, 16, BF16).rearrange("p (k j) -> p k j", k=32) for s in range(2)]
        w2_v = [AV(32 + 16 * s, 16, BF16).rearrange("p (k j) -> p k j", k=FB // 128) for s in range(2)]
        xt_R3, gt_R3 = AV(64, 16, F32), AV(80, 16, F32)
        xt_R2, gt_R2 = AV(32, 16, F32), AV(48, 16, F32)
        hb_R4 = AV(144, 8, BF16)
        hb_R3 = AV(64, 8, BF16)
        t1 = [AV(160 + 2 * i, 2, F32) for i in range(2)]
        t2 = [AV(164 + 2 * i, 2, F32) for i in range(2)]
        ub = [AV(168 + 2 * i, 2, BF16).rearrange("p (k t) -> p k t", k=FB // 128) for i in range(2)]
        def load_gain(idx, gt):
            P.op("sync", lambda e: e.dma_start(out=gt, in_=gains[idx]), reads=["gains"], writes=["gt"], dma_key="gt")


        def rstd_from(src_ap, src_key, col, hb):
            P.op("vector", lambda e: e.memset(small[:, col:col + 1], 0.0), writes=[f"sm{col}"])
            P.op("scalar", lambda e: e.activation(out=hb, in_=src_ap, func=AF.Square, accum_out=small[:, col:col + 1]),
                 reads=[src_key, f"sm{col}"], writes=["hb", f"sm{col}"])
            P.op("scalar", lambda e: e.activation(out=small[:, col:col + 1], in_=small[:, col:col + 1], func=AF.Sqrt, scale=1.0 / D, bias=epsb[:, 0:1]),
                 reads=[f"sm{col}", "epsb"], writes=[f"sm{col}"])
            P.op("vector", lambda e: e.reciprocal(out=small[:, col:col + 1], in_=small[:, col:col + 1]), reads=[f"sm{col}"], writes=[f"sm{col}"])

        def norm_transpose(src_ap, src_key, col, gt, hb, tt):
            P.op("vector", lambda e: e.scalar_tensor_tensor(out=hb, in0=src_ap, scalar=small[:, col:col + 1], in1=gt,
                                                            op0=ALU.mult, op1=ALU.mult), reads=[src_key, f"sm{col}", "gt"], writes=["hb"])
            for k8 in range(4):
                pi = state["pt"]; state["pt"] ^= 1
                for j in range(8):
                    kc = k8 * 8 + j
                    P.op("tensor", lambda e, kc=kc, j=j, pi=pi: e.transpose(out=pst[pi][:, j * 128:(j + 1) * 128], in_=hb[:, kc * 128:(kc + 1) * 128], identity=ident[:]),
                         reads=["hb", "ident"], writes=[f"ps{6 + pi}"])
                dst = bufA[:, k8 * 8:(k8 + 1) * 8, tt * 128:(tt + 1) * 128]
                srcp = pst[pi][:, :].rearrange("p (k t) -> p k t", k=8)
                if k8 % 2 == 0:
                    P.op("scalar", lambda e, dst=dst, srcp=srcp: e.activation(out=dst, in_=srcp, func=AF.Copy), reads=[f"ps{6 + pi}"], writes=["bufA"])
                else:
                    P.op("vector", lambda e, dst=dst, srcp=srcp: e.tensor_copy(out=dst, in_=srcp), reads=[f"ps{6 + pi}"], writes=["bufA"])

        for ps_i in range(npass):
            tok0 = ps_i * TP
            P.fence()
            if 0 in stages:
                xt, gt, hb = xt_R3, gt_R3, hb_R4
                load_gain(0, gt)
                for tt in range(4):
                    r0 = tok0 + tt * 128
                    P.op("sync", lambda e, r0=r0, xt=xt: e.dma_start(out=xt, in_=x[r0:r0 + 128, :]), writes=["xt"], dma_key="xt")
                    rstd_from(xt, "xt", 0, hb)
                    norm_transpose(xt, "xt", 0, gt, hb, tt)
                if ysrc[0] == "input":
                    P.op("sync", lambda e, tok0=tok0: e.dma_start(out=bufB, in_=ysrc[1][:, :, tok0:tok0 + TP].rearrange("k p t -> p k t")), writes=["bufB"], dma_key="bufB")
                else:
                    G = ysrc[1]
                    cand = AV(96, 32, BF16).rearrange("p (k t) -> p k t", k=32)
                    for q in range(4):
                        t0 = 1024 * q + tok0
                        for part in range(2):
                            dstv = cand[:, 16 * part:16 * (part + 1), :].rearrange("p (g k) t -> p g k t", g=4)
                            for gq in range(4):
                                P.op("sync", lambda e, dstv=dstv, part=part, t0=t0, gq=gq: e.dma_start(out=dstv[:, gq, :, :], in_=G[4 * part:4 * part + 4, gq, :, t0:t0 + TP].rearrange("k p t -> p k t")),
                                     reads=["G"], writes=["cand"], dma_key="cand")
                        if q == 0:
                            P.op("vector", lambda e: e.tensor_scalar(out=bufB, in0=cand, scalar1=selt[:, 0:1], scalar2=None, op0=ALU.mult), reads=["cand", "selt"], writes=["bufB"])
                        else:
                            P.op("vector", lambda e, q=q: e.scalar_tensor_tensor(out=bufB, in0=cand, scalar=selt[:, q:q + 1], in1=bufB, op0=ALU.mult, op1=ALU.add), reads=["cand", "selt", "bufB"], writes=["bufB"])
            P.fence()
            if 1 in stages:
                for cc in range(32):
                    s = cc % 2
                    wb = wbuf[s]
                    vgA = wb[:, 0:4096].rearrange("p (k j) -> p k j", k=32)
                    vgB = wb[:, 4096:8192].rearrange("p (k j) -> p k j", k=32)
                    vuA = wb[:, 8192:10240].rearrange("p (k j) -> p k j", k=16)
                    vuB = wb[:, 10240:12288].rearrange("p (k j) -> p k j", k=16)
                    for (dst, src) in ((vgA, wg[cc]), (vgB, wg[32 + cc]), (vuA, wu[cc]), (vuB, wu[32 + cc])):
                        P.op("gpsimd", lambda e, dst=dst, src=src: e.dma_start(out=dst, in_=src, max_dma_last_dim=4096), writes=[f"wbuf{s}"], dma_key=f"wbuf{s}")
                    pgA, pgB, puA, puB = next_ps(), next_ps(), next_ps(), next_ps()
                    for (pi, wv, nk, src, koff) in ((pgA, vgA, 32, bufA, 0), (puA, vuA, 16, bufB, 0), (pgB, vgB, 32, bufA, 0), (puB, vuB, 16, bufB, 16)):
                        for kc in range(nk):
                            P.op("tensor", lambda e, kc=kc, pi=pi, wv=wv, nk=nk, src=src, koff=koff: e.matmul(psum[pi][:], lhsT=wv[:, kc, :], rhs=src[:, koff + kc, :], start=(kc == 0), stop=(kc == nk - 1)),
                                 reads=[f"wbuf{s}", "bufA", "bufB"], writes=[f"ps{pi}"])
                    P.op("scalar", lambda e, pgA=pgA, s=s: e.activation(out=t1[s], in_=psum[pgA][:], func=AF.Sigmoid), reads=[f"ps{pgA}"], writes=[f"t1_{s}"])
                    P.op("vector", lambda e, puA=puA, s=s: e.tensor_tensor(out=t1[s], in0=t1[s], in1=psum[puA][:], op=ALU.mult), reads=[f"ps{puA}", f"t1_{s}"], writes=[f"t1_{s}"])
                    P.op("scalar", lambda e, pgB=pgB, s=s: e.activation(out=t2[s], in_=psum[pgB][:], func=AF.Sigmoid), reads=[f"ps{pgB}"], writes=[f"t2_{s}"])
                    P.op("vector", lambda e, puB=puB, s=s: e.tensor_tensor(out=t2[s], in0=t2[s], in1=psum[puB][:], op=ALU.mult), reads=[f"ps{puB}", f"t2_{s}"], writes=[f"t2_{s}"])
                    P.op("vector", lambda e, cc=cc, s=s: e.tensor_tensor(out=bufM[:, cc, :], in0=t1[s], in1=t2[s], op=ALU.add), reads=[f"t1_{s}", f"t2_{s}"], writes=["bufM"])
            P.fence()
            if 2 in stages:
                for nb in range(16):
                    s = nb % 2
                    wv = wo_v[s]
                    for half in range(2):
                        P.op("gpsimd", lambda e, wv=wv, nb=nb, half=half: e.dma_start(out=wv[:, half * 16:(half + 1) * 16, :], in_=wo[nb][:, half * 16:(half + 1) * 16, :], max_dma_last_dim=4096),
                             writes=[f"wo{s}"], dma_key=f"wo{s}")
                    for tt in range(4):
                        pi = next_ps()
                        for kc in range(32):
                            P.op("tensor", lambda e, kc=kc, pi=pi, wv=wv, tt=tt: e.matmul(psum[pi][:, 0:256], lhsT=bufM[:, kc, tt * 128:(tt + 1) * 128], rhs=wv[:, kc, :], start=(kc == 0), stop=(kc == 31)),
                                 reads=[f"wo{s}", "bufM"], writes=[f"ps{pi}"])
                        dst = bufZ[:, tt, nb * 256:(nb + 1) * 256]
                        if (tt + nb) % 2 == 0:
                            P.op("scalar", lambda e, dst=dst, pi=pi: e.activation(out=dst, in_=psum[pi][:, 0:256], func=AF.Copy), reads=[f"ps{pi}"], writes=[f"bufZ{tt}_{nb % 8}"])
                        else:
                            P.op("vector", lambda e, dst=dst, pi=pi: e.tensor_copy(out=dst, in_=psum[pi][:, 0:256]), reads=[f"ps{pi}"], writes=[f"bufZ{tt}_{nb % 8}"])
            P.fence()
            if 3 in stages:
                xt, gt, hb = xt_R2, gt_R2, hb_R3
                load_gain(1, gt)
                for tt in range(4):
                    r0 = tok0 + tt * 128
                    zt = bufZ[:, tt, :]
                    rstd_from(zt, f"bufZ{tt}", 1, hb)
                    P.op("sync", lambda e, r0=r0, xt=xt: e.dma_start(out=xt, in_=x[r0:r0 + 128, :]), writes=["xt"], dma_key="xt")
                    P.op("vector", lambda e, zt=zt, gt=gt: e.scalar_tensor_tensor(out=zt, in0=zt, scalar=small[:, 1:2], in1=gt, op0=ALU.mult, op1=ALU.mult),
                         reads=[f"bufZ{tt}", "sm1", "gt"], writes=[f"bufZ{tt}"])
                    P.op("vector", lambda e, zt=zt, xt=xt: e.tensor_tensor(out=zt, in0=zt, in1=xt, op=ALU.add), reads=[f"bufZ{tt}", "xt"], writes=[f"bufZ{tt}"])
                    P.op("sync", lambda e, r0=r0, zt=zt: e.dma_start(out=x1_d[r0:r0 + 128, :], in_=zt), reads=[f"bufZ{tt}"], writes=[f"x1d{ps_i}_{tt}"], dma_key=f"x1st{tt}")
                load_gain(2, gt)
                for tt in range(4):
                    zt = bufZ[:, tt, :]
                    rstd_from(zt, f"bufZ{tt}", 2, hb)
                    norm_transpose(zt, f"bufZ{tt}", 2, gt, hb, tt)
            P.fence()
            if 4 in stages:
                for fb in range(nfb):
                    s = fb % 2
                    w1v = w1_v[s]
                    w2v = w2_v[s]
                    for half in range(2):
                        P.op("gpsimd", lambda e, w1v=w1v, fb=fb, half=half: e.dma_start(out=w1v[:, half * 16:(half + 1) * 16, :], in_=w1[fb][:, half * 16:(half + 1) * 16, :], max_dma_last_dim=4096),
                             writes=[f"w1_{s}"], dma_key=f"w1_{s}")
                    for kc2 in range(FB // 128):
                        P.op("gpsimd", lambda e, w2v=w2v, fb=fb, kc2=kc2: e.dma_start(out=w2v[:, kc2, :], in_=w2[fb][:, kc2, :], max_dma_last_dim=4096),
                             writes=[f"w2_{s}"], dma_key=f"w2_{s}")
                    for fc in range(FB // 128):
                        pi = next_ps()
                        for kc in range(32):
                            P.op("tensor", lambda e, kc=kc, pi=pi, w1v=w1v, fc=fc: e.matmul(psum[pi][:], lhsT=w1v[:, kc, fc * 128:(fc + 1) * 128], rhs=bufA[:, kc, :], start=(kc == 0), stop=(kc == 31)),
                                 reads=[f"w1_{s}", "bufA"], writes=[f"ps{pi}"])
                        P.op("scalar", lambda e, pi=pi, s=s: e.activation(out=t1[s], in_=psum[pi][:], func=AF.Relu), reads=[f"ps{pi}"], writes=[f"t1_{s}"])
                        P.op("vector", lambda e, s=s, fc=fc: e.tensor_tensor(out=ub[s][:, fc, :], in0=t1[s], in1=t1[s], op=ALU.mult), reads=[f"t1_{s}"], writes=[f"ub{s}"])
                    for tt in range(4):
                        for nb in range(8):
                            pi = next_ps()
                            for kc2 in range(FB // 128):
                                P.op("tensor", lambda e, kc2=kc2, pi=pi, w2v=w2v, tt=tt, nb=nb, s=s: e.matmul(psum[pi][:], lhsT=ub[s][:, kc2, tt * 128:(tt + 1) * 128], rhs=w2v[:, kc2, nb * 512:(nb + 1) * 512],
                                                                                                          start=(kc2 == 0), stop=(kc2 == FB // 128 - 1)),
                                     reads=[f"w2_{s}", f"ub{s}"], writes=[f"ps{pi}"])
                            dst = bufZ[:, tt, nb * 512:(nb + 1) * 512]
                            key = f"bufZ{tt}_{nb}"
                            if fb == 0:
                                P.op("vector", lambda e, dst=dst, pi=pi: e.tensor_copy(out=dst, in_=psum[pi][:]), reads=[f"ps{pi}"], writes=[key])
                            else:
                                P.op("vector", lambda e, dst=dst, pi=pi: e.tensor_tensor(out=dst, in0=dst, in1=psum[pi][:], op=ALU.add), reads=[f"ps{pi}", key], writes=[key])
            P.fence()
            if 5 in stages:
                xt, gt, hb = xt_R3, gt_R3, AV(32, 8, BF16)
                load_gain(3, gt)
                for tt in range(4):
                    r0 = tok0 + tt * 128
                    zt = bufZ[:, tt, :]
                    rstd_from(zt, f"bufZ{tt}", 3, hb)
                    P.op("sync", lambda e, r0=r0, xt=xt: e.dma_start(out=xt, in_=x1_d[r0:r0 + 128, :]), reads=[f"x1d{ps_i}_{tt}"], writes=["xt"], dma_key="xt")
                    P.op("vector", lambda e, zt=zt, gt=gt: e.scalar_tensor_tensor(out=zt, in0=zt, scalar=small[:, 3:4], in1=gt, op0=ALU.mult, op1=ALU.mult),
                         reads=[f"bufZ{tt}", "sm3", "gt"], writes=[f"bufZ{tt}"])
                    P.op("vector", lambda e, zt=zt, xt=xt: e.tensor_tensor(out=zt, in0=zt, in1=xt, op=ALU.add), reads=[f"bufZ{tt}", "xt"], writes=[f"bufZ{tt}"])
                    P.op("sync", lambda e, r0=r0, zt=zt: e.dma_start(out=out[r0:r0 + 128, :], in_=zt), reads=[f"bufZ{tt}"], writes=["out"], dma_key=f"x1st{tt}")
        P.fence()


def build_B(npass=2, stages=(0, 1, 2, 3, 4, 5), nfb=NFB):
    nc = bass.Bass("TRN2", target_bir_lowering=False)
    yT = nc.dram_tensor("yT", [32, 128, NTOK], BF16, kind="ExternalInput").ap()
    P = Prog(nc)
    with ExitStack() as st:
        sh = make_shared(nc, P, st)
        body_B(nc, P, st, sh, ("input", yT), npass=npass, stages=stages, nfb=nfb)
        counts = P.emit(st)
        print("B ops", counts, "waits", P.n_waits)
    return nc


def relayout_B(inp, l=0):
    w_in = inp["w_in"][l]
    NR = 6592
    gcol0 = NR + 3 * 2048
    Wg = w_in[:, gcol0:gcol0 + 8192]
    wg = np.ascontiguousarray(Wg.reshape(32, 128, 64, 128).transpose(2, 1, 0, 3))
    Wu = np.concatenate([inp["w_up_rwkv"][l], inp["w_up_diff"][l]], axis=1)
    wu = np.ascontiguousarray(Wu.reshape(16, 128, 64, 128).transpose(2, 1, 0, 3))
    wo = np.ascontiguousarray(inp["w_out"][l].reshape(32, 128, 16, 256).transpose(2, 1, 0, 3))
    w1 = np.ascontiguousarray(inp["w_mlp_in"][l].reshape(32, 128, NFB, FB).transpose(2, 1, 0, 3))
    w2 = np.ascontiguousarray(inp["w_mlp_out"][l].reshape(NFB, FB // 128, 128, D).transpose(0, 2, 1, 3))
    gains = np.stack([np.broadcast_to(inp[k][l][None, :], (128, D)) for k in ("attn_pre_norm", "attn_post_norm", "mlp_pre_norm", "mlp_post_norm")]).astype(np.float32)
    ident = np.eye(128, dtype=np.float32).astype(ml_dtypes.bfloat16)
    ones = np.ones((128, 128), np.float32).astype(ml_dtypes.bfloat16)
    return dict(wg=wg, wu=wu, wo=wo, w1=w1, w2=w2, gains=np.ascontiguousarray(gains), ident=ident, ones=ones)


F32 = mybir.dt.float32
BF16 = mybir.dt.bfloat16
AF = mybir.ActivationFunctionType
ALU = mybir.AluOpType
S = 4096
C = 128
BLK = 512
NB = S // BLK
EPS_GN = 64e-5
DEC = -0.6065306597126334
RWMODE = 3


def emit_rwkv(nc, P, st, PT_d, yT, psum, pst, state, next_ps, ident, ones, epsb, rw_heads):
    par_d = nc.dram_tensor("par", [64, 8, 16], F32, kind="ExternalInput").ap()
    parl_d = nc.dram_tensor("parl", [128, 4, 2], F32, kind="ExternalInput").ap()
    lw2_d = nc.dram_tensor("lw2", [96, 4, 512], F32, kind="ExternalInput").ap()
    g2_d = nc.dram_tensor("g2", [128, 2, 512], F32, kind="ExternalInput").ap()
    masks_d = nc.dram_tensor("masks", [128, 2, 640], F32, kind="ExternalInput").ap()
    mreset_d = nc.dram_tensor("mreset", [64, BLK], F32, kind="ExternalInput").ap()

    def sb(name, shape, dt):
        return st.enter_context(nc.sbuf_tensor(name, shape, dt))
    par = sb("par_s", [64, 8, 20], F32)
    parl = sb("parl_s", [128, 4, 3], F32)
    lw2b = sb("lw2b", [96, 4, 512], BF16)
    g2b = sb("g2b", [128, 2, 512], BF16)
    masks = sb("masks_s", [128, 2, 640], F32)
    mreset = sb("mreset_s", [64, BLK], F32)
    twT = sb("twT", [128, S], BF16)
    daT = sb("daT", [128, S], BF16)
    sgT = sb("sgT", [128, 2, S], BF16)
    RAW = sb("RAW", [128, S // 2 + 2], F32)
    SH = sb("SH", [128, S // 2], F32)
    Rb16 = sb("R16", [64, S], BF16)
    Kb16 = sb("K16", [64, S], BF16)
    Vb16 = sb("V16", [64, S], BF16)
    Vt = sb("rVt", [128, 32, 64], BF16)
    KKN = sb("KKN", [64, S], BF16)
    OT = sb("OT", [64, S], F32)
    BONV = sb("BONV", [64, S], BF16)
    gnb = sb("gnb", [64, 1], F32)
    tiny = sb("tinyb", [64, 1], F32)
    LW = sb("LW", [64, BLK], F32)
    CI = sb("CI", [64, BLK], F32)
    CE = sb("CE", [64, BLK], F32)
    Ece = sb("Ece", [64, BLK], F32)
    Enci = sb("Enci", [64, BLK], F32)
    Eci = [sb(f"Eci{i}", [64, BLK], F32) for i in range(2)]
    At = sb("At", [64, BLK], F32)
    T1 = sb("T1", [64, BLK], F32)
    KD = sb("KD", [64, BLK], F32)
    T2 = sb("T2b", [64, BLK], BF16)
    ops4 = [sb(f"ops4_{i}", [64, 4, BLK], BF16) for i in range(2)]
    tok3 = [sb(f"tok3_{i}", [128, 12, 64], BF16) for i in range(2)]
    G1s = [sb(f"G1s{c}", [128, 384], BF16) for c in range(8)]
    G2s = [sb(f"G2s{c}", [128, 256], BF16) for c in range(8)]
    MM = [[sb(f"MM{c}_{i}", [128, 256], BF16) for i in range(2)] for c in range(8)]
    Qs = [[sb(f"Qs{c}_{i}", [128, 128], BF16) for i in range(2)] for c in range(8)]
    IMs = [[sb(f"IM{c}_{i}", [128, 128], BF16) for i in range(2)] for c in range(8)]
    nW1T = [sb(f"nW1T{c}", [64, 128], BF16) for c in range(8)]
    nXs = [sb(f"nXs{c}", [128, 64], BF16) for c in range(8)]
    Us = sb("Us", [128, 64], BF16)
    Hf = sb("Hf", [64, 64], F32)
    Hb = sb("Hb", [64, 64], BF16)
    yo = sb("ryo", [64, BLK], BF16)
    ob = sb("ob", [64, BLK], BF16)
    dd = sb("dd", [64, BLK], F32)
    ones64 = ones[0:64, 0:64]
    id64 = ident[0:64, 0:64]

    def V(fn, reads, writes):
        P.op("vector", fn, reads=reads, writes=writes)

    def A(fn, reads, writes):
        P.op("scalar", fn, reads=reads, writes=writes)

    def T(fn, reads, writes):
        P.op("tensor", fn, reads=reads, writes=writes)

    for (dst, src, k, q) in ((par[:, :, 0:16], par_d, "par", "sync"), (parl[:, :, 0:2], parl_d, "parl", "sync"), (masks[:], masks_d, "masks", "sync"),
                             (mreset[:], mreset_d, "mreset", "sync"), (lw2b[:], lw2_d, "lw2b", "gpsimd"), (g2b[:], g2_d, "g2b", "gpsimd")):
        P.op(q, lambda e, dst=dst, src=src: e.dma_start(out=dst, in_=src), writes=[k], dma_key=k)
    V(lambda e: e.memset(gnb[:], EPS_GN), [], ["gnb"])
    V(lambda e: e.memset(tiny[:], 1e-24), [], ["gnb"])
    for i in range(3):
        V(lambda e, i=i: e.tensor_tensor(out=par[:, :, 16 + i], in0=par[:, :, 2 * i], in1=par[:, :, 2 * i + 1], op=ALU.add), ["par"], ["par"])
        V(lambda e, i=i: e.tensor_scalar(out=par[:, :, 16 + i], in0=par[:, :, 16 + i], scalar1=-1.0, scalar2=1.0, op0=ALU.mult, op1=ALU.add), ["par"], ["par"])
    V(lambda e: e.tensor_tensor(out=parl[:, :, 2], in0=parl[:, :, 0], in1=parl[:, :, 1], op=ALU.add), ["parl"], ["parl"])
    V(lambda e: e.tensor_scalar(out=parl[:, :, 2], in0=parl[:, :, 2], scalar1=-1.0, scalar2=1.0, op0=ALU.mult, op1=ALU.add), ["parl"], ["parl"])

    HS = S // 2

    def load_shift(ch, r0, nrow, c0, mup, mun, consume):
        for half in range(2):
            t0 = half * HS
            lo = max(t0 - 1, 0)
            hi = min(t0 + HS + 1, S)
            off = lo - (t0 - 1)
            if half == 0:
                V(lambda e: e.memset(RAW[0:nrow, 0:1], 0.0), [], ["RAW"])
            else:
                V(lambda e: e.memset(RAW[0:nrow, HS + 1:HS + 2], 0.0), [], ["RAW"])
            P.op("sync", lambda e, lo=lo, hi=hi, off=off: e.dma_start(out=RAW[0:nrow, off:off + (hi - lo)], in_=PT_d[ch][r0:r0 + nrow, lo:hi]), reads=[f"PT{ch}"], writes=["RAW"], dma_key="RAW")
            V(lambda e: e.tensor_scalar(out=SH[0:nrow, :], in0=RAW[0:nrow, 1:HS + 1], scalar1=c0, scalar2=None, op0=ALU.mult), ["RAW", "par", "parl"], ["SH"])
            V(lambda e: e.scalar_tensor_tensor(out=SH[0:nrow, :], in0=RAW[0:nrow, 0:HS], scalar=mup, in1=SH[0:nrow, :], op0=ALU.mult, op1=ALU.add), ["RAW", "SH", "par", "parl"], ["SH"])
            V(lambda e: e.scalar_tensor_tensor(out=SH[0:nrow, :], in0=RAW[0:nrow, 2:HS + 2], scalar=mun, in1=SH[0:nrow, :], op0=ALU.mult, op1=ALU.add), ["RAW", "SH", "par", "parl"], ["SH"])
            consume(half)

    for i, (ch, dstT, fn) in enumerate(((12, twT, AF.Tanh), (13, daT, AF.Copy), (14, sgT[:, 0, :], AF.Sigmoid), (15, sgT[:, 1, :], AF.Sigmoid))):
        def cons(half, dstT=dstT, fn=fn):
            A(lambda e: e.activation(out=dstT[:, half * HS:(half + 1) * HS], in_=SH[:, :], func=fn), ["SH"], ["lora_act"])
        load_shift(ch, 0, 128, parl[:, i, 2:3], parl[:, i, 0:1], parl[:, i, 1:2], cons)

    def head(hh):
        ch_r, ch_k, ch_v = hh // 2, 4 + hh // 2, 8 + hh // 2
        r0 = (hh % 2) * 64
        pc = lambda j: par[:, hh, j:j + 1]
        for (ch, i, dst) in ((ch_r, 0, Rb16), (ch_k, 1, Kb16), (ch_v, 2, Vb16)):
            def cons(half, dst=dst):
                A(lambda e: e.activation(out=dst[:, half * HS:(half + 1) * HS], in_=SH[0:64, :], func=AF.Copy), ["SH"], ["rkv"])
            load_shift(ch, r0, 64, pc(16 + i), pc(2 * i), pc(2 * i + 1), cons)
        for half in range(2):
            pi = state["pt"]; state["pt"] ^= 1
            for j in range(16):
                blk = half * 16 + j
                T(lambda e, blk=blk, j=j, pi=pi: e.transpose(out=pst[pi][:, j * 64:(j + 1) * 64], in_=Vb16[:, blk * 128:(blk + 1) * 128], identity=id64), ["rkv", "ident"], [f"ps{6 + pi}"])
            V(lambda e, half=half, pi=pi: e.tensor_copy(out=Vt[:, half * 16:(half + 1) * 16, :], in_=pst[pi][:, :].rearrange("p (k t) -> p k t", k=16)), [f"ps{6 + pi}"], ["rVt"])
        for b in range(NB):
            bs = slice(b * BLK, (b + 1) * BLK)
            V(lambda e, bs=bs: e.tensor_scalar(out=T1[:], in0=Kb16[:, bs], scalar1=pc(10), scalar2=None, op0=ALU.mult), ["rkv", "par"], ["T1"])
            A(lambda e: e.activation(out=T2[:], in_=T1[:], func=AF.Square), ["T1"], ["T2"])
            pi = next_ps()
            T(lambda e, pi=pi: e.matmul(psum[pi][0:64, :], lhsT=ones64, rhs=T2[:], start=True, stop=True), ["T2", "ones"], [f"ps{pi}"])
            A(lambda e, pi=pi: e.activation(out=KD[:], in_=psum[pi][0:64, :], func=AF.Sqrt, bias=tiny[:, 0:1]), [f"ps{pi}", "gnb"], ["KD"])
            V(lambda e: e.reciprocal(out=KD[:], in_=KD[:]), ["KD"], ["KD"])
            V(lambda e, bs=bs: e.tensor_tensor(out=KKN[:, bs], in0=T1[:], in1=KD[:], op=ALU.mult), ["T1", "KD"], ["KKN"])
        def direction(d):
            V(lambda e: e.memset(Hf[:], 0.0), [], ["Hf"])
            V(lambda e: e.memset(Hb[:], 0.0), [], ["Hb"])
            blocks = range(NB) if d == 0 else range(NB - 1, -1, -1)
            def block(bi, b):
                bs = slice(b * BLK, (b + 1) * BLK)
                sl = bi % 2
                O4, K3, EC = ops4[sl], tok3[sl], Eci[sl]
                hs = slice(hh * 64, (hh + 1) * 64)
                pi = 6
                T(lambda e, pi=pi, bs=bs, hs=hs: e.matmul(psum[pi][0:64, :], lhsT=lw2b[:, d, hs], rhs=twT[0:96, bs], start=True, stop=True), ["lw2b", "lora_act"], [f"ps{pi}"])
                A(lambda e, pi=pi: e.activation(out=LW[:], in_=psum[pi][0:64, :], func=AF.Sigmoid, bias=pc(6 + d)), [f"ps{pi}", "par"], ["LW"])
                V(lambda e: e.tensor_scalar(out=LW[:], in0=LW[:], scalar1=DEC, scalar2=None, op0=ALU.mult), ["LW"], ["LW"])
                V(lambda e: e.tensor_tensor_scan(out=CI[:], data0=mreset[:], data1=LW[:], initial=0.0, op0=ALU.mult, op1=ALU.add), ["LW", "mreset"], ["CI"])
                if d == 0:
                    V(lambda e: e.tensor_tensor(out=CE[:], in0=CI[:], in1=LW[:], op=ALU.subtract), ["CI", "LW"], ["CE"])
                else:
                    for c in range(4):
                        cs = slice(c * C, (c + 1) * C)
                        V(lambda e, cs=cs, c=c: e.tensor_scalar(out=CE[:, cs], in0=CI[:, cs], scalar1=-1.0, scalar2=CI[:, c * C + C - 1:c * C + C], op0=ALU.mult, op1=ALU.add), ["CI"], ["CE"])
                    V(lambda e: e.tensor_tensor(out=CI[:], in0=CE[:], in1=LW[:], op=ALU.add), ["CE", "LW"], ["CI"])
                A(lambda e: e.activation(out=Ece[:], in_=CE[:], func=AF.Exp), ["CE"], ["Ece"])
                A(lambda e: e.activation(out=Enci[:], in_=CI[:], func=AF.Exp, scale=-1.0), ["CI"], ["Enci"])
                A(lambda e, EC=EC: e.activation(out=EC[:], in_=CI[:], func=AF.Exp), ["CI"], [f"Eci{sl}"])
                pi = 7
                T(lambda e, pi=pi, bs=bs, hs=hs: e.matmul(psum[pi][0:64, :], lhsT=lw2b[:, 2 + d, hs], rhs=daT[0:96, bs], start=True, stop=True), ["lw2b", "lora_act"], [f"ps{pi}"])
                A(lambda e, pi=pi: e.activation(out=At[:], in_=psum[pi][0:64, :], func=AF.Sigmoid, bias=pc(8 + d)), [f"ps{pi}", "par"], ["At"])
                V(lambda e: e.tensor_scalar(out=T1[:], in0=At[:], scalar1=-1.0, scalar2=pc(11), op0=ALU.add, op1=ALU.mult), ["At", "par"], ["T1"])
                V(lambda e, bs=bs: e.scalar_tensor_tensor(out=KD[:], in0=T1[:], scalar=1.0, in1=Kb16[:, bs], op0=ALU.add, op1=ALU.mult), ["T1", "rkv"], ["KD"])
                V(lambda e, bs=bs: e.tensor_tensor(out=At[:], in0=At[:], in1=KKN[:, bs], op=ALU.mult), ["At", "KKN"], ["At"])
                V(lambda e, bs=bs, O4=O4: e.tensor_tensor(out=O4[:, 0, :], in0=KKN[:, bs], in1=Ece[:], op=ALU.mult), ["KKN", "Ece"], [f"ops4_{sl}"])
                V(lambda e, O4=O4: e.tensor_tensor(out=O4[:, 1, :], in0=At[:], in1=Enci[:], op=ALU.mult), ["At", "Enci"], [f"ops4_{sl}"])
                V(lambda e, O4=O4: e.tensor_tensor(out=O4[:, 2, :], in0=KD[:], in1=Enci[:], op=ALU.mult), ["KD", "Enci"], [f"ops4_{sl}"])
                V(lambda e, bs=bs, O4=O4, EC=EC: e.tensor_tensor(out=O4[:, 3, :], in0=Rb16[:, bs], in1=EC[:], op=ALU.mult), ["rkv", f"Eci{sl}"], [f"ops4_{sl}"])
                V(lambda e, bs=bs: e.scalar_tensor_tensor(out=T2[:], in0=Rb16[:, bs], scalar=pc(12), in1=KD[:], op0=ALU.mult, op1=ALU.mult), ["rkv", "KD", "par"], ["T2"])
                pi = 6
                T(lambda e, pi=pi: e.matmul(psum[pi][0:64, :], lhsT=ones64, rhs=T2[:], start=True, stop=True), ["T2", "ones"], [f"ps{pi}"])
                V(lambda e, pi=pi, bs=bs: e.scalar_tensor_tensor(out=T1[:], in0=psum[pi][0:64, :], scalar=0.5, in1=Vb16[:, bs], op0=ALU.mult, op1=ALU.mult), [f"ps{pi}", "rkv"], ["T1"])
                if d == 0:
                    V(lambda e, bs=bs: e.tensor_copy(out=BONV[:, bs], in_=T1[:]), ["T1"], ["BONV"])
                else:
                    V(lambda e, bs=bs: e.tensor_tensor(out=BONV[:, bs], in0=BONV[:, bs], in1=T1[:], op=ALU.add), ["T1", "BONV"], ["BONV"])
                pi = state["pt"]; state["pt"] ^= 1
                for o in range(3):
                    for c in range(4):
                        T(lambda e, o=o, c=c, pi=pi, O4=O4: e.transpose(out=pst[pi][:, (o * 4 + c) * 64:(o * 4 + c + 1) * 64], in_=O4[:, o, c * C:(c + 1) * C], identity=id64),
                          [f"ops4_{sl}", "ident"], [f"ps{6 + pi}"])
                V(lambda e, pi=pi, K3=K3: e.tensor_copy(out=K3[:, :, :], in_=pst[pi][:, 0:768].rearrange("p (k t) -> p k t", k=12)), [f"ps{6 + pi}"], [f"tok3_{sl}"])
                chunks = list(range(4)) if d == 0 else list(range(3, -1, -1))
                ok = [f"ops4_{sl}"]
                cst_ = {}

                def opsof(c):
                    cs = slice(c * C, (c + 1) * C)
                    return O4[:, 0, cs], O4[:, 1, cs], O4[:, 2, cs], O4[:, 3, cs]

                kx = lambda c: sl * 4 + c

                def gram1(c):
                    Ab_c, Bb_c, Kb_c, Rb_c = opsof(c)
                    p1, p2 = next_ps(), next_ps()
                    cst_[c] = dict(p1=p1, p2=p2)
                    T(lambda e: e.matmul(psum[p1][:, 0:128], lhsT=Bb_c, rhs=Ab_c, start=True, stop=True), ok, [f"ps{p1}"])
                    T(lambda e: e.matmul(psum[p1][:, 128:256], lhsT=Kb_c, rhs=Ab_c, start=True, stop=True), ok, [f"ps{p1}"])
                    T(lambda e: e.matmul(psum[p1][:, 256:384], lhsT=Ab_c, rhs=Bb_c, start=True, stop=True), ok, [f"ps{p1}"])
                    T(lambda e: e.matmul(psum[p2][:, 0:128], lhsT=Bb_c, rhs=Rb_c, start=True, stop=True), ok, [f"ps{p2}"])
                    T(lambda e: e.matmul(psum[p2][:, 128:256], lhsT=Kb_c, rhs=Rb_c, start=True, stop=True), ok, [f"ps{p2}"])

                def evac1(c):
                    p1, p2 = cst_[c]["p1"], cst_[c]["p2"]
                    V(lambda e: e.tensor_tensor(out=G1s[kx(c)][:], in0=psum[p1][:, 0:384], in1=masks[:, d, 0:384], op=ALU.mult), [f"ps{p1}", "masks"], [f"G1s{kx(c)}"])
                    V(lambda e: e.tensor_tensor(out=G2s[kx(c)][:], in0=psum[p2][:, 0:256], in1=masks[:, d, 384:640], op=ALU.mult), [f"ps{p2}", "masks"], [f"G2s{kx(c)}"])
                    V(lambda e: e.tensor_tensor(out=Qs[kx(c)][0][:], in0=G1s[kx(c)][:, 0:128], in1=ident[:], op=ALU.add), [f"G1s{kx(c)}", "ident"], [f"Qs{kx(c)}_0"])
                    cst_[c].update(M=G1s[kx(c)][:, 256:384], MT=G1s[kx(c)][:, 0:128], mk=f"G1s{kx(c)}", qi=0)

                def levelA(c, lev):
                    stc = cst_[c]
                    Mprev, MTprev, mk = stc["M"], stc["MT"], stc["mk"]
                    mi = lev % 2
                    pm = next_ps()
                    T(lambda e: e.matmul(psum[pm][:, 0:128], lhsT=MTprev, rhs=Mprev, start=True, stop=True), [mk], [f"ps{pm}"])
                    if lev < 6:
                        T(lambda e: e.matmul(psum[pm][:, 128:256], lhsT=Mprev, rhs=MTprev, start=True, stop=True), [mk], [f"ps{pm}"])
                    A(lambda e: e.activation(out=MM[kx(c)][mi][:], in_=psum[pm][:, 0:256], func=AF.Copy), [f"ps{pm}"], [f"MM{kx(c)}_{mi}"])
                    V(lambda e: e.tensor_tensor(out=IMs[kx(c)][mi][:], in0=MM[kx(c)][mi][:, 0:128], in1=ident[:], op=ALU.add), [f"MM{kx(c)}_{mi}", "ident"], [f"IM{kx(c)}_{mi}"])
                    stc["M"], stc["MT"], stc["mk"] = MM[kx(c)][mi][:, 0:128], MM[kx(c)][mi][:, 128:256], f"MM{kx(c)}_{mi}"

                def levelB(c, lev):
                    stc = cst_[c]
                    qi = stc["qi"]
                    mi = lev % 2
                    pq = next_ps()
                    T(lambda e: e.matmul(psum[pq][:, 0:128], lhsT=IMs[kx(c)][mi][:], rhs=Qs[kx(c)][qi][:], start=True, stop=True), [f"Qs{kx(c)}_{qi}", f"IM{kx(c)}_{mi}"], [f"ps{pq}"])
                    V(lambda e: e.tensor_copy(out=Qs[kx(c)][1 - qi][:], in_=psum[pq][:, 0:128]), [f"ps{pq}"], [f"Qs{kx(c)}_{1 - qi}"])
                    stc["qi"] = 1 - qi

                def w1x(c):
                    gc = b * 4 + c
                    qi = cst_[c]["qi"]
                    Q, qk = Qs[kx(c)][qi], f"Qs{kx(c)}_{qi}"
                    Abt = K3[:, 0 + c, :]
                    pw = next_ps()
                    T(lambda e: e.matmul(psum[pw][0:64, 0:128], lhsT=Abt, rhs=Q[:], start=True, stop=True), [f"tok3_{sl}", qk], [f"ps{pw}"])
                    A(lambda e: e.activation(out=nW1T[kx(c)][:], in_=psum[pw][0:64, 0:128], func=AF.Copy, scale=-1.0), [f"ps{pw}"], [f"nW1T{kx(c)}"])
                    px = next_ps()
                    T(lambda e: e.matmul(psum[px][:, 0:64], lhsT=G1s[kx(c)][:, 128:256], rhs=Vt[:, gc, :], start=True, stop=True), [f"G1s{kx(c)}", "rVt"], [f"ps{px}"])
                    A(lambda e: e.activation(out=nXs[kx(c)][:], in_=psum[px][:, 0:64], func=AF.Copy, scale=-1.0), [f"ps{px}"], [f"nXs{kx(c)}"])

                def seq(c):
                    gc = b * 4 + c
                    qi = cst_[c]["qi"]
                    Q, qk = Qs[kx(c)][qi], f"Qs{kx(c)}_{qi}"
                    Ab_c, Bb_c, Kb_c, Rb_c = opsof(c)
                    Bbt, Kbt = K3[:, 4 + c, :], K3[:, 8 + c, :]
                    Vtc = Vt[:, gc, :]
                    pu = next_ps()
                    T(lambda e: e.matmul(psum[pu][:, 0:64], lhsT=Q[:], rhs=nXs[kx(c)][:], start=True, stop=False), [qk, f"nXs{kx(c)}"], [f"ps{pu}"])
                    T(lambda e: e.matmul(psum[pu][:, 0:64], lhsT=nW1T[kx(c)][:], rhs=Hb[:], start=False, stop=True), [f"nW1T{kx(c)}", "Hb"], [f"ps{pu}"])
                    V(lambda e: e.tensor_copy(out=Us[:], in_=psum[pu][:, 0:64]), [f"ps{pu}"], ["Us"])
                    po = next_ps()
                    T(lambda e: e.matmul(psum[po][0:64, 0:128], lhsT=Hb[:], rhs=Rb_c, start=True, stop=False), ["Hb"] + ok, [f"ps{po}"])
                    T(lambda e: e.matmul(psum[po][0:64, 0:128], lhsT=Us[:], rhs=G2s[kx(c)][:, 0:128], start=False, stop=False), ["Us", f"G2s{kx(c)}"], [f"ps{po}"])
                    T(lambda e: e.matmul(psum[po][0:64, 0:128], lhsT=Vtc, rhs=G2s[kx(c)][:, 128:256], start=False, stop=True), ["rVt", f"G2s{kx(c)}"], [f"ps{po}"])
                    gs = slice(gc * C, (gc + 1) * C)
                    if d == 0:
                        A(lambda e: e.activation(out=OT[:, gs], in_=psum[po][0:64, 0:128], func=AF.Copy), [f"ps{po}"], ["OT"])
                    else:
                        V(lambda e: e.tensor_tensor(out=OT[:, gs], in0=OT[:, gs], in1=psum[po][0:64, 0:128], op=ALU.add), [f"ps{po}", "OT"], ["OT"])
                    ph = next_ps()
                    T(lambda e: e.matmul(psum[ph][0:64, 0:64], lhsT=Bbt, rhs=Us[:], start=True, stop=False), [f"tok3_{sl}", "Us"], [f"ps{ph}"])
                    T(lambda e: e.matmul(psum[ph][0:64, 0:64], lhsT=Kbt, rhs=Vtc, start=False, stop=True), [f"tok3_{sl}", "rVt"], [f"ps{ph}"])
                    gidx = c * C + C - 1 if d == 0 else c * C
                    gam = EC[:, gidx:gidx + 1]
                    V(lambda e: e.tensor_scalar(out=Hf[:], in0=Hf[:], scalar1=gam, scalar2=None, op0=ALU.mult), ["Hf", f"Eci{sl}"], ["Hf"])
                    V(lambda e: e.scalar_tensor_tensor(out=Hf[:], in0=psum[ph][0:64, 0:64], scalar=gam, in1=Hf[:], op0=ALU.mult, op1=ALU.add), ["Hf", f"Eci{sl}", f"ps{ph}"], ["Hf"])
                    A(lambda e: e.activation(out=Hb[:], in_=Hf[:], func=AF.Copy), ["Hf"], ["Hb"])

                stages = []
                for cg in (chunks[0:2], chunks[2:4]):
                    for c in cg:
                        stages.append(lambda c=c: gram1(c))
                    for c in cg:
                        stages.append(lambda c=c: evac1(c))
                for lev in range(1, 7):
                    for c in chunks:
                        stages.append(lambda c=c, lev=lev: levelA(c, lev))
                    for c in chunks:
                        stages.append(lambda c=c, lev=lev: levelB(c, lev))
                for c in chunks:
                    stages.append(lambda c=c: w1x(c))
                seqs = [(lambda c=c: seq(c)) for c in chunks]
                return stages, seqs
            prev = None
            for bi, b in enumerate(blocks):
                stages, seqs = block(bi, b)
                if prev is None or RWMODE != 3:
                    if prev is not None:
                        for f in prev:
                            f()
                    for f in stages:
                        f()
                else:
                    n = len(stages)
                    marks = {int((k + 1) * n / 5): k for k in range(4)}
                    for i, f in enumerate(stages):
                        f()
                        if (i + 1) in marks:
                            prev[marks[i + 1]]()
                prev = seqs
            for f in prev:
                f()
        P.fence()
        direction(0)
        direction(1)
        P.fence()
        for b in range(NB):
            bs = slice(b * BLK, (b + 1) * BLK)
            A(lambda e, bs=bs: e.activation(out=ob[:], in_=OT[:, bs], func=AF.Copy), ["OT"], ["ob"])
            pi = next_ps()
            T(lambda e, pi=pi: e.matmul(psum[pi][0:64, :], lhsT=ones64, rhs=ob[:], start=True, stop=True), ["ob", "ones"], [f"ps{pi}"])
            V(lambda e, pi=pi, bs=bs: e.scalar_tensor_tensor(out=dd[:], in0=psum[pi][0:64, :], scalar=-1.0 / 64, in1=OT[:, bs], op0=ALU.mult, op1=ALU.add), [f"ps{pi}", "OT"], ["dd"])
            A(lambda e: e.activation(out=ob[:], in_=dd[:], func=AF.Square), ["dd"], ["ob"])
            pi = next_ps()
            T(lambda e, pi=pi: e.matmul(psum[pi][0:64, :], lhsT=ones64, rhs=ob[:], start=True, stop=True), ["ob", "ones"], [f"ps{pi}"])
            A(lambda e, pi=pi: e.activation(out=T1[:], in_=psum[pi][0:64, :], func=AF.Sqrt, scale=1.0 / 64, bias=gnb[:, 0:1]), [f"ps{pi}", "gnb"], ["T1"])
            V(lambda e: e.reciprocal(out=T1[:], in_=T1[:]), ["T1"], ["T1"])
            V(lambda e: e.tensor_tensor(out=dd[:], in0=dd[:], in1=T1[:], op=ALU.mult), ["dd", "T1"], ["dd"])
            V(lambda e: e.tensor_scalar(out=dd[:], in0=dd[:], scalar1=pc(13), scalar2=pc(14), op0=ALU.mult, op1=ALU.add), ["dd", "par"], ["dd"])
            V(lambda e, bs=bs: e.tensor_tensor(out=dd[:], in0=dd[:], in1=BONV[:, bs], op=ALU.add), ["dd", "BONV"], ["dd"])
            pi = next_ps()
            for kc in range(2):
                T(lambda e, pi=pi, kc=kc, bs=bs: e.matmul(psum[pi][0:64, :], lhsT=g2b[:, kc, hh * 64:(hh + 1) * 64], rhs=sgT[:, kc, bs], start=(kc == 0), stop=(kc == 1)), ["g2b", "lora_act"], [f"ps{pi}"])
            V(lambda e, pi=pi: e.tensor_tensor(out=yo[:], in0=dd[:], in1=psum[pi][0:64, :], op=ALU.mult), ["dd", f"ps{pi}"], ["ryo"])
            P.op("sync", lambda e, bs=bs: e.dma_start(out=yT[hh // 2][(hh % 2) * 64:(hh % 2) * 64 + 64, bs], in_=yo[:]), reads=["ryo"], writes=["yT"], dma_key="ryo")

    for hh in range(rw_heads):
        head(hh)


def relayout_rwkv(inp, c, l=0):
    b, g = c // 4, c % 4
    chs = slice(512 * g, 512 * (g + 1))
    par = np.zeros((64, 8, 16), np.float32)
    sp, sn = inp["shift_prev"][l], inp["shift_next"][l]
    for hh in range(8):
        cg = slice(512 * g + hh * 64, 512 * g + (hh + 1) * 64)
        for i in range(3):
            par[:, hh, 2 * i] = sp[i * 2048:(i + 1) * 2048][cg]
            par[:, hh, 2 * i + 1] = sn[i * 2048:(i + 1) * 2048][cg]
        par[:, hh, 6] = inp["decay_bias_fwd"][l][cg]; par[:, hh, 7] = inp["decay_bias_bwd"][l][cg]
        par[:, hh, 8] = inp["iclr_bias_fwd"][l][cg]; par[:, hh, 9] = inp["iclr_bias_bwd"][l][cg]
        par[:, hh, 10] = inp["k_k"][l][cg]; par[:, hh, 11] = inp["k_a"][l][cg]
        par[:, hh, 12] = inp["r_k"][l].reshape(-1)[cg]
        par[:, hh, 13] = inp["ln_x_gain"][l][cg]; par[:, hh, 14] = inp["ln_x_bias"][l][cg]
    parl = np.zeros((128, 4, 2), np.float32)
    lo = 3 * 2048
    for i, (a0, n) in enumerate(((lo, 96), (lo + 96, 96), (lo + 192, 128), (lo + 320, 128))):
        parl[:n, i, 0] = sp[a0:a0 + n]; parl[:n, i, 1] = sn[a0:a0 + n]
    lw2 = np.stack([inp[k][l][:, chs] for k in ("decay_up_fwd", "decay_up_bwd", "iclr_up_fwd", "iclr_up_bwd")], axis=1)
    g2 = np.ascontiguousarray(inp["gate_up"][l][:, chs].reshape(2, 128, 512).transpose(1, 0, 2))
    ii = np.arange(128)
    row, col = ii[:, None], ii[None, :]
    masks = np.zeros((128, 2, 640), np.float32)
    for d in range(2):
        lt = (row < col) if d == 0 else (row > col)
        le = (row <= col) if d == 0 else (row >= col)
        masks[:, d, 0:128] = -lt.astype(np.float32)
        masks[:, d, 128:256] = lt
        masks[:, d, 256:384] = -lt.T.astype(np.float32)
        masks[:, d, 384:512] = le
        masks[:, d, 512:640] = le
    mreset = np.ones((64, BLK), np.float32)
    mreset[:, ::C] = 0.0
    return dict(par=par, parl=parl, lw2=np.ascontiguousarray(lw2).astype(np.float32), g2=g2.astype(np.float32), masks=masks, mreset=mreset)


F32 = mybir.dt.float32
BF16 = mybir.dt.bfloat16
AF = mybir.ActivationFunctionType
ALU = mybir.AluOpType
D = 4096
S = 4096
EPS = 1e-6
NCH = 28
NR = 6592
LAMBDA_INIT = 0.8 - 0.6
WARMN = 16
NFILL = 1


def body_A(nc, P, st, sh, yT, do_proj=True, do_attn=True, do_rwkv=True, n_tb=8, attn_heads=4, attn_qb=8, rw_heads=8):
    xb = nc.dram_tensor("xb", [S, D], F32, kind="ExternalInput").ap()
    wA = nc.dram_tensor("wA", [NCH, 128, 32, 128], F32, kind="ExternalInput").ap()
    abias = nc.dram_tensor("abias", [4, 5, 128, 512], F32, kind="ExternalInput").ap()
    acst = nc.dram_tensor("acst", [128, 4 * 64], F32, kind="ExternalInput").ap()
    lam_d = nc.dram_tensor("lam", [128, 4, 64], F32, kind="ExternalInput").ap()
    subg = nc.dram_tensor("subg", [128, 1], F32, kind="ExternalInput").ap()
    PT_d = nc.dram_tensor("PT_d", [NCH, 128, S], F32).ap()
    gain = sh["gains"][0]
    ident, ones, small, epsb = sh["ident"], sh["ones"], sh["small"], sh["epsb"]
    psum, pst, state, next_ps = sh["psum"], sh["pst"], sh["state"], sh["next_ps"]
    if True:
        if do_proj:
            with ExitStack() as st2:
                bufAs = [st2.enter_context(nc.sbuf_tensor(f"bufA{i}", [128, 32, 512], BF16)) for i in range(2)]
                xt = st2.enter_context(nc.sbuf_tensor("xt", [128, D], F32))
                gt = st2.enter_context(nc.sbuf_tensor("gt", [128, D], F32))
                hb = st2.enter_context(nc.sbuf_tensor("hb", [128, D], BF16))
                wbuf = [st2.enter_context(nc.sbuf_tensor(f"wb{i}", [128, 32, 128], BF16)) for i in range(3)]
                ot = [st2.enter_context(nc.sbuf_tensor(f"ot{i}", [128, 512], F32)) for i in range(3)]
                P.op("sync", lambda e: e.dma_start(out=gt[:], in_=gain), writes=["gt"], dma_key="gt")
                for tb in range(n_tb):
                    bufA = bufAs[tb % 2]
                    bk = f"bufA{tb % 2}"
                    for tt in range(4):
                        r0 = tb * 512 + tt * 128
                        P.op("sync", lambda e, r0=r0: e.dma_start(out=xt[:], in_=xb[r0:r0 + 128, :]), writes=["xt"], dma_key="xt")
                        P.op("vector", lambda e: e.memset(small[:, 0:1], 0.0), writes=["sm0"])
                        P.op("scalar", lambda e: e.activation(out=hb[:], in_=xt[:], func=AF.Square, accum_out=small[:, 0:1]), reads=["xt", "sm0"], writes=["hb", "sm0"])
                        P.op("scalar", lambda e: e.activation(out=small[:, 0:1], in_=small[:, 0:1], func=AF.Sqrt, scale=1.0 / D, bias=epsb[:, 0:1]), reads=["sm0", "epsb"], writes=["sm0"])
                        P.op("vector", lambda e: e.reciprocal(out=small[:, 0:1], in_=small[:, 0:1]), reads=["sm0"], writes=["sm0"])
                        P.op("vector", lambda e: e.scalar_tensor_tensor(out=hb[:], in0=xt[:], scalar=small[:, 0:1], in1=gt[:], op0=ALU.mult, op1=ALU.mult),
                             reads=["xt", "sm0", "gt"], writes=["hb"])
                        for k8 in range(4):
                            pi = state["pt"]; state["pt"] ^= 1
                            for j in range(8):
                                kc = k8 * 8 + j
                                P.op("tensor", lambda e, kc=kc, j=j, pi=pi: e.transpose(out=pst[pi][:, j * 128:(j + 1) * 128], in_=hb[:, kc * 128:(kc + 1) * 128], identity=ident[:]),
                                     reads=["hb", "ident"], writes=[f"ps{6 + pi}"])
                            dst = bufA[:, k8 * 8:(k8 + 1) * 8, tt * 128:(tt + 1) * 128]
                            srcp = pst[pi][:, :].rearrange("p (k t) -> p k t", k=8)
                            if k8 % 2 == 0:
                                P.op("scalar", lambda e, dst=dst, srcp=srcp: e.activation(out=dst, in_=srcp, func=AF.Copy), reads=[f"ps{6 + pi}"], writes=[bk])
                            else:
                                P.op("vector", lambda e, dst=dst, srcp=srcp: e.tensor_copy(out=dst, in_=srcp), reads=[f"ps{6 + pi}"], writes=[bk])
                    for ch in range(NCH):
                        s = ch % 3
                        for half in range(2):
                            P.op("gpsimd", lambda e, s=s, ch=ch, half=half: e.dma_start(out=wbuf[s][:, half * 16:(half + 1) * 16, :], in_=wA[ch][:, half * 16:(half + 1) * 16, :], max_dma_last_dim=4096),
                                 writes=[f"wb{s}"], dma_key=f"wb{s}")
                        pi = next_ps()
                        for kc in range(32):
                            P.op("tensor", lambda e, kc=kc, pi=pi, s=s, bufA=bufA: e.matmul(psum[pi][:], lhsT=wbuf[s][:, kc, :], rhs=bufA[:, kc, :], start=(kc == 0), stop=(kc == 31)),
                                 reads=[f"wb{s}", bk], writes=[f"ps{pi}"])
                        if ch % 2 == 0:
                            P.op("scalar", lambda e, pi=pi, s=s: e.activation(out=ot[s][:], in_=psum[pi][:], func=AF.Copy), reads=[f"ps{pi}"], writes=[f"ot{s}"])
                        else:
                            P.op("vector", lambda e, pi=pi, s=s: e.tensor_copy(out=ot[s][:], in_=psum[pi][:]), reads=[f"ps{pi}"], writes=[f"ot{s}"])
                        P.op("sync", lambda e, s=s, ch=ch, tb=tb: e.dma_start(out=PT_d[ch][:, tb * 512:(tb + 1) * 512], in_=ot[s][:]), reads=[f"ot{s}"], writes=[f"PT{ch}"], dma_key=f"ot{s}")
                P.fence()
        if do_attn:
            with ExitStack() as st2:
                def sb2(name, shape, dt):
                    return st2.enter_context(nc.sbuf_tensor(name, shape, dt))
                LA = 2
                NSL = LA + 1
                qk32 = sb2("qk32", [128, S], F32)
                QTs = [sb2(f"QT{i}", [64, 2, S], BF16) for i in range(2)]
                KTs = [sb2(f"KT{i}", [64, 2, S], BF16) for i in range(2)]
                Vts = [sb2(f"Vt{i}", [128, 32, 128], BF16) for i in range(2)]
                vb = sb2("vb", [128, S], BF16)
                bts = [sb2(f"bt{i}", [128, 5, 512], F32) for i in range(2)]
                cst = sb2("cst", [128, 256], F32)
                lamt = sb2("lamt", [128, 4, 64], F32)
                lsm = sb2("lsm", [128, 8], F32)
                sgt = sb2("sgt", [128, 1], F32)
                tmp = [sb2(f"atmp{i}", [128, 512], F32) for i in range(NSL)]
                Eb = [sb2(f"Eb{i}", [128, 512], BF16) for i in range(NSL)]
                o0 = sb2("o0", [128, 512], F32)
                o1 = sb2("o1", [128, 512], F32)
                rr = sb2("rr", [128, 512], F32)
                rr2 = sb2("rr2", [128, 512], F32)
                sq = sb2("sq", [128, 512], BF16)
                yo = sb2("yo", [128, 512], BF16)
                wz = sb2("wz", [128, 512], BF16)
                P.op("vector", lambda e: e.memset(wz[:], 0.0), writes=["wz"])

                def warm(n):
                    for _ in range(n):
                        P.op("tensor", lambda e: e.matmul(psum[7][:], lhsT=ones[:], rhs=wz[:], start=True, stop=True), reads=["wz", "ones"], writes=["ps7"])
                P.op("sync", lambda e: e.dma_start(out=cst[:], in_=acst), writes=["cst"], dma_key="cst")
                P.op("sync", lambda e: e.dma_start(out=lamt[:], in_=lam_d), writes=["lamt"], dma_key="lamt")
                P.op("sync", lambda e: e.dma_start(out=sgt[:], in_=subg), writes=["sgt"], dma_key="sgt")
                for i in range(2):
                    P.op("vector", lambda e, i=i: e.tensor_tensor(out=lamt[:, 2 * i, :], in0=lamt[:, 2 * i, :], in1=lamt[:, 2 * i + 1, :], op=ALU.mult), reads=["lamt"], writes=["lamt"])
                    P.op("vector", lambda e, i=i: e.tensor_reduce(out=lsm[:, i:i + 1], in_=lamt[:, 2 * i, :], axis=mybir.AxisListType.X, op=ALU.add), reads=["lamt"], writes=["lsm"])
                    P.op("scalar", lambda e, i=i: e.activation(out=lsm[:, i:i + 1], in_=lsm[:, i:i + 1], func=AF.Exp), reads=["lsm"], writes=["lsm"])
                P.op("vector", lambda e: e.tensor_tensor(out=lsm[:, 2:3], in0=lsm[:, 1:2], in1=lsm[:, 0:1], op=ALU.subtract), reads=["lsm"], writes=["lsm"])
                P.op("vector", lambda e: e.tensor_scalar(out=lsm[:, 2:3], in0=lsm[:, 2:3], scalar1=-LAMBDA_INIT, scalar2=None, op0=ALU.add), reads=["lsm"], writes=["lsm"])
                P.op("vector", lambda e: e.tensor_scalar(out=sgt[:], in0=sgt[:], scalar1=1.0 - LAMBDA_INIT, scalar2=None, op0=ALU.mult), reads=["sgt"], writes=["sgt"])

                def load_head(hd):
                    hs = hd % 2
                    QT, KT, Vt, bt = QTs[hs], KTs[hs], Vts[hs], bts[hs]
                    P.op("sync", lambda e: e.dma_start(out=bt[:], in_=abias[hd].rearrange("f p q -> p f q")), writes=[f"bt{hs}"], dma_key=f"bt{hs}")
                    for (dstT, ch, nm) in ((QT, 16 + hd, f"QT{hs}"), (KT, 20 + hd, f"KT{hs}")):
                        for c in range(2):
                            P.op("sync", lambda e, ch=ch, c=c: e.dma_start(out=qk32[0:64, :], in_=PT_d[ch][c * 64:(c + 1) * 64, :]), reads=[f"PT{ch}"], writes=["qk32"], dma_key="qk32")
                            P.op("scalar", lambda e, dstT=dstT, c=c: e.activation(out=dstT[:, c, :], in_=qk32[0:64, :], func=AF.Copy), reads=["qk32"], writes=[nm])
                    P.op("sync", lambda e: e.dma_start(out=qk32[:], in_=PT_d[24 + hd]), reads=[f"PT{24 + hd}"], writes=["qk32"], dma_key="qk32")
                    P.op("vector", lambda e: e.tensor_copy(out=vb[:], in_=qk32[:]), reads=["qk32"], writes=["vb"])
                    for k8 in range(4):
                        pi = state["pt"]; state["pt"] ^= 1
                        for j in range(8):
                            blk = k8 * 8 + j
                            P.op("tensor", lambda e, blk=blk, j=j, pi=pi: e.transpose(out=pst[pi][:, j * 128:(j + 1) * 128], in_=vb[:, blk * 128:(blk + 1) * 128], identity=ident[:]),
                                 reads=["vb", "ident"], writes=[f"ps{6 + pi}"])
                        P.op("vector", lambda e, k8=k8, pi=pi: e.tensor_copy(out=Vt[:, k8 * 8:(k8 + 1) * 8, :], in_=pst[pi][:, :].rearrange("p (k t) -> p k t", k=8)), reads=[f"ps{6 + pi}"], writes=[f"Vt{hs}"])

                pacc = [0, 1, 2, 3]
                pending = [None]

                def unit_front(hd, qb, i):
                    hs = hd % 2
                    kb, c = i // 2, i % 2
                    delta = kb - 4 * qb
                    pi = 4 + (i % NSL)
                    ti = i % NSL
                    P.op("tensor", lambda e: e.matmul(psum[pi][:], lhsT=KTs[hs][:, c, kb * 128:(kb + 1) * 128], rhs=QTs[hs][:, c, qb * 512:(qb + 1) * 512], start=True, stop=True),
                         reads=[f"QT{hs}", f"KT{hs}"], writes=[f"ps{pi}"])
                    if delta >= 4:
                        bti, op1 = 0, ALU.add
                    elif delta < 0:
                        bti, op1 = 0, ALU.subtract
                    else:
                        bti, op1 = 1 + delta, ALU.add
                    P.op("vector", lambda e: e.scalar_tensor_tensor(out=tmp[ti][:], in0=psum[pi][:], scalar=0.125, in1=bts[hs][:, bti, :], op0=ALU.mult, op1=op1),
                         reads=[f"ps{pi}", f"bt{hs}"], writes=[f"atmp{ti}"])
                    ci = hd * 64 + (delta + 32)
                    P.op("scalar", lambda e: e.activation(out=Eb[ti][:], in_=tmp[ti][:], func=AF.Exp, bias=cst[:, ci:ci + 1]), reads=[f"atmp{ti}", "cst"], writes=[f"Eb{ti}"])

                def unit_back(hd, qb, i):
                    hs = hd % 2
                    kb, c = i // 2, i % 2
                    ti = i % NSL
                    P.op("tensor", lambda e: e.matmul(psum[pacc[2 * c]][:], lhsT=Vts[hs][:, kb, :], rhs=Eb[ti][:], start=(kb == 0), stop=(kb == 31)),
                         reads=[f"Eb{ti}", f"Vt{hs}"], writes=[f"ps{pacc[2 * c]}"])
                    P.op("tensor", lambda e: e.matmul(psum[pacc[2 * c + 1]][:], lhsT=ones[:], rhs=Eb[ti][:], start=(kb == 0), stop=(kb == 31)),
                         reads=[f"Eb{ti}", "ones"], writes=[f"ps{pacc[2 * c + 1]}"])

                def fin1():
                    P.op("vector", lambda e: e.reciprocal(out=rr[:], in_=psum[pacc[1]][:]), reads=[f"ps{pacc[1]}"], writes=["rr"])
                    P.op("vector", lambda e: e.tensor_tensor(out=o0[:], in0=psum[pacc[0]][:], in1=rr[:], op=ALU.mult), reads=[f"ps{pacc[0]}", "rr"], writes=["o0"])
                    P.op("vector", lambda e: e.reciprocal(out=rr[:], in_=psum[pacc[3]][:]), reads=[f"ps{pacc[3]}"], writes=["rr"])
                    P.op("vector", lambda e: e.tensor_tensor(out=o1[:], in0=psum[pacc[2]][:], in1=rr[:], op=ALU.mult), reads=[f"ps{pacc[2]}", "rr"], writes=["o1"])
                    P.op("vector", lambda e: e.scalar_tensor_tensor(out=o0[:], in0=o1[:], scalar=lsm[:, 2:3], in1=o0[:], op0=ALU.mult, op1=ALU.add), reads=["o0", "o1", "lsm"], writes=["o0"])
                    P.op("scalar", lambda e: e.activation(out=sq[:], in_=o0[:], func=AF.Square), reads=["o0"], writes=["sq"])

                def fin2(hd, qb, pi):
                    P.op("tensor", lambda e: e.matmul(psum[pi][:], lhsT=ones[:], rhs=sq[:], start=True, stop=True), reads=["sq", "ones"], writes=[f"ps{pi}"])
                    P.op("scalar", lambda e: e.activation(out=rr2[:], in_=psum[pi][:], func=AF.Sqrt, scale=1.0 / 128, bias=epsb[:, 1:2]), reads=[f"ps{pi}", "epsb"], writes=["rr2"])
                    P.op("vector", lambda e: e.reciprocal(out=rr2[:], in_=rr2[:]), reads=["rr2"], writes=["rr2"])
                    P.op("vector", lambda e: e.scalar_tensor_tensor(out=yo[:], in0=o0[:], scalar=sgt[:, 0:1], in1=rr2[:], op0=ALU.mult, op1=ALU.mult), reads=["o0", "sgt", "rr2"], writes=["yo"])
                    P.op("sync", lambda e: e.dma_start(out=yT[4 + hd][:, qb * 512:(qb + 1) * 512], in_=yo[:]), reads=["yo"], writes=["yT"], dma_key="yo")

                load_head(0)
                NU = 64
                for hd in range(attn_heads):
                    for qb in range(attn_qb):
                        warm(WARMN)
                        for i in range(NU + LA):
                            if i < NU:
                                unit_front(hd, qb, i)
                            if i >= LA:
                                unit_back(hd, qb, i - LA)
                                warm(NFILL)
                            if i == LA + 1 and pending[0] is not None:
                                ph, pq = pending[0]
                                pending[0] = None
                                fin2(ph, pq, 7)
                        fin1()
                        pending[0] = (hd, qb)
                        if qb == 1 and hd + 1 < attn_heads:
                            load_head(hd + 1)
                        if hd == attn_heads - 1 and qb == attn_qb - 1:
                            fin2(hd, qb, 7)
                            pending[0] = None
                P.fence()
        if do_rwkv:
            with ExitStack() as st3:
                emit_rwkv(nc, P, st3, PT_d, yT, psum, pst, state, next_ps, ident, ones, epsb, rw_heads)
            P.fence()


def build_A(**kw):
    nc = bass.Bass("TRN2", target_bir_lowering=False)
    yT = nc.dram_tensor("yT", [8, 128, S], BF16, kind="ExternalOutput").ap()
    P = Prog(nc)
    with ExitStack() as st:
        sh = make_shared(nc, P, st)
        body_A(nc, P, st, sh, yT, **kw)
        counts = P.emit(st)
        print("A ops", counts, "waits", P.n_waits)
    return nc


def build_fused():
    nc = bass.Bass("TRN2", target_bir_lowering=False)
    yTi = nc.dram_tensor("yTi", [8, 128, S], BF16).ap()
    G = nc.dram_tensor("Gy", [8, 4, 128, S], BF16).ap()
    sel = nc.dram_tensor("sel", [128, 4], F32, kind="ExternalInput").ap()
    P = Prog(nc)
    with ExitStack() as st:
        sh = make_shared(nc, P, st)
        body_A(nc, P, st, sh, yTi)
        for k in range(8):
            P.op("gpsimd", lambda e, k=k: e.collective_compute("AllGather", ALU.bypass, replica_groups=[[0, 1, 2, 3], [4, 5, 6, 7]],
                                                               ins=[yTi[k].opt()], outs=[G[k].rearrange("g p t -> (g p) t").opt()]),
                 reads=["yT"], writes=["G"], dma_key="cc", inc=1)
        P.fence()
        with ExitStack() as st4:
            body_B(nc, P, st4, sh, ("gather", G, sel))
        counts = P.emit(st)
        print("fused ops", counts, "waits", P.n_waits)
    return nc


def slopes():
    H = 16
    return np.exp2(-8.0 * np.arange(1, H + 1, dtype=np.float32) / H).astype(np.float32)


def relayout_A(inp, c, l=0):
    b, g = c // 4, c % 4
    w_in = inp["w_in"][l]
    cols = []
    for part in range(3):
        cols.append(np.arange(part * 2048 + 512 * g, part * 2048 + 512 * (g + 1)))
    lo = 3 * 2048
    cols.append(np.arange(lo, lo + 96)); pad1 = 32
    cols.append(np.arange(lo + 96, lo + 192)); pad2 = 32
    cols.append(np.arange(lo + 192, lo + 448))
    for part in range(3):
        cols.append(np.arange(NR + part * 2048 + 512 * g, NR + part * 2048 + 512 * (g + 1)))
    W = np.zeros((D, NCH * 128), np.float32)
    W[:, 0:1536] = w_in[:, np.concatenate(cols[0:3])]
    W[:, 1536:1536 + 96] = w_in[:, cols[3]]
    W[:, 1664:1664 + 96] = w_in[:, cols[4]]
    W[:, 1792:2048] = w_in[:, cols[5]]
    W[:, 2048:3584] = w_in[:, np.concatenate(cols[6:9])]
    wA = np.ascontiguousarray(W.reshape(32, 128, NCH, 128).transpose(2, 1, 0, 3))
    gain = np.ascontiguousarray(np.broadcast_to(inp["attn_pre_norm"][l][None, :], (128, D))).astype(np.float32)
    ident = np.eye(128, dtype=np.float32).astype(ml_dtypes.bfloat16)
    ones = np.ones((128, 128), np.float32).astype(ml_dtypes.bfloat16)
    sl = slopes()
    kk = np.arange(128, dtype=np.float32)[:, None]
    qq = np.arange(512, dtype=np.float32)[None, :]
    abias = np.zeros((4, 5, 128, 512), np.float32)
    acst = np.zeros((128, 256), np.float32)
    for hd in range(4):
        s_ = sl[4 * g + hd]
        abias[hd, 0] = -s_ * (kk - qq)
        for dl in range(4):
            abias[hd, 1 + dl] = -s_ * np.abs(128.0 * dl + kk - qq)
        for delta in range(-32, 32):
            if delta >= 4:
                v = -s_ * 128.0 * delta
            elif delta < 0:
                v = s_ * 128.0 * delta
            else:
                v = 0.0
            acst[:, hd * 64 + delta + 32] = v
    lam = np.stack([inp[k][l] for k in ("lambda_q1", "lambda_k1", "lambda_q2", "lambda_k2")])
    lam = np.ascontiguousarray(np.broadcast_to(lam[None], (128, 4, 64))).astype(np.float32)
    subg = np.ascontiguousarray(inp["subln_gain"][l].reshape(128, 1)).astype(np.float32)
    return dict(xb=np.ascontiguousarray(inp["x"][b]), wA=wA, abias=abias, acst=acst, lam=lam, subg=subg)


def kernel(**inp):
    inp = {k: np.asarray(v) for k, v in inp.items()}
    n = 8
    nc = build_fused()
    W = relayout_B(inp)
    in_maps = []
    for c in range(n):
        b, g = c // 4, c % 4
        im = dict(W)
        im.update(relayout_A(inp, c))
        im.update(relayout_rwkv(inp, c))
        im["xo"] = np.ascontiguousarray(inp["x"][b, 1024 * g:1024 * (g + 1)])
        sel = np.zeros((128, 4), np.float32)
        sel[:, g] = 1.0
        im["sel"] = sel
        in_maps.append(im)
    res = run_bass_kernel_spmd(nc, in_maps, core_ids=list(range(n)))
    out = np.zeros((2, S, D), np.float32)
    for c in range(n):
        b, g = c // 4, c % 4
        out[b, 1024 * g:1024 * (g + 1)] = res.results[c]["out"]
    return out
```

```python
import math
import bisect
import numpy as np
import ml_dtypes
from contextlib import ExitStack
import concourse.bass as bass
import concourse.mybir as mybir
from concourse.bass_utils import run_bass_kernel_spmd


ENGS = ("tensor", "vector", "scalar", "gpsimd", "sync")
ROT = 12000


class Prog:
    def __init__(self, nc):
        self.nc = nc
        self.ops = []

    def op(self, eng, fn, reads=(), writes=(), dma_key=None, inc=None):
        self.ops.append(dict(eng=eng, fn=fn, reads=tuple(reads), writes=tuple(writes),
                             dma=dma_key is not None, key=dma_key, inc=inc))
        return len(self.ops) - 1

    def fence(self, eng="vector"):
        self.ops.append(dict(eng=eng, fn=self.fence_fn, reads=(), writes="ALL", dma=False, key=None, inc=None))

    def emit(self, stack):
        nc = self.nc
        ops = self.ops
        n = len(ops)
        allkeys = set()
        for o in ops:
            if o["writes"] != "ALL":
                allkeys.update(o["reads"]); allkeys.update(o["writes"])
        allkeys = tuple(sorted(allkeys, key=str))
        for o in ops:
            if o["writes"] == "ALL":
                o["writes"] = allkeys
        last_w = {}
        readers = {}
        deps = [None] * n
        for i, o in enumerate(ops):
            d = set()
            for r in o["reads"]:
                if r in last_w:
                    d.add(last_w[r])
            for w in o["writes"]:
                if w in last_w:
                    d.add(last_w[w])
                for j in readers.get(w, ()):
                    d.add(j)
            d.discard(i)
            dd = []
            for j in d:
                oj = ops[j]
                if (not oj["dma"]) and (not o["dma"]) and oj["eng"] == o["eng"] == "tensor":
                    continue
                dd.append(j)
            deps[i] = dd
            for r in o["reads"]:
                readers.setdefault(r, []).append(i)
            for w in o["writes"]:
                last_w[w] = i
                readers[w] = []
        needed = [False] * n
        for i in range(n):
            for j in deps[i]:
                needed[j] = True
        sem_handles = {}

        def get_sem(name):
            if name not in sem_handles:
                sem_handles[name] = stack.enter_context(nc.semaphore(name))
            return sem_handles[name]

        cnt = {}
        sig = [None] * n
        dma_cum_at = {}
        for i, o in enumerate(ops):
            if o["dma"]:
                base = "d_" + str(o["key"])
                inc = o["inc"] or 16
                lim = ROT
            else:
                if not needed[i]:
                    continue
                base = "e_" + o["eng"]
                inc = 1
                lim = ROT
            g, c = cnt.get(base, (0, 0))
            if c + inc > lim * (16 if o["dma"] else 1):
                g, c = g + 1, 0
            c += inc
            cnt[base] = (g, c)
            sig[i] = (base + "_" + str(g), c)
            if o["dma"]:
                dma_cum_at.setdefault(base, []).append((i, sig[i][0], c))
        import bisect
        dma_idx = {k: [t[0] for t in v] for k, v in dma_cum_at.items()}
        waits = [None] * n
        waited = {e: {} for e in ENGS}
        for i, o in enumerate(ops):
            need = {}
            for j in deps[i]:
                oj = ops[j]
                if oj["dma"]:
                    base = "d_" + str(oj["key"])
                    lst = dma_cum_at[base]
                    pos = bisect.bisect_left(dma_idx[base], i) - 1
                    sname_j, vj = sig[j]
                    k = pos
                    while lst[k][1] != sname_j:
                        k -= 1
                    sname, val = lst[k][1], lst[k][2]
                else:
                    sname, val = sig[j]
                if need.get(sname, 0) < val:
                    need[sname] = val
            wl = []
            wd = waited[o["eng"]]
            for sname, val in need.items():
                if wd.get(sname, 0) >= val:
                    continue
                wd[sname] = val
                wl.append((sname, val))
            waits[i] = wl
        self.n_waits = sum(len(w) for w in waits)
        per_eng = {e: [i for i, o in enumerate(ops) if o["eng"] == e] for e in ENGS}
        block = stack.enter_context(nc.Block())

        def body(engname):
            def f(eng):
                for i in per_eng[engname]:
                    for sname, val in waits[i]:
                        eng.wait_ge(get_sem(sname), val)
                    inst = ops[i]["fn"](eng)
                    if sig[i] is not None:
                        inst.then_inc(get_sem(sig[i][0]), (ops[i]["inc"] or 16) if ops[i]["dma"] else 1)
            return f

        for i in range(n):
            if sig[i] is not None:
                get_sem(sig[i][0])
        block.tensor(body("tensor"))
        block.vector(body("vector"))
        block.scalar(body("scalar"))
        block.gpsimd(body("gpsimd"))
        block.sync(body("sync"))
        return {e: len(v) for e, v in per_eng.items()}


F32 = mybir.dt.float32
BF16 = mybir.dt.bfloat16
AF = mybir.ActivationFunctionType
ALU = mybir.AluOpType
D = 4096
DFF = 16384
EPS = 1e-6
NTOK = 1024
TP = 512
FB = 256
NFB = DFF // FB


def make_shared(nc, P, st):
    ident_d = nc.dram_tensor("ident", [128, 128], BF16, kind="ExternalInput").ap()
    ones_d = nc.dram_tensor("ones", [128, 128], BF16, kind="ExternalInput").ap()
    gains = nc.dram_tensor("gains", [4, 128, D], F32, kind="ExternalInput").ap()
    ident = st.enter_context(nc.sbuf_tensor("ident_s", [128, 128], BF16))
    ones = st.enter_context(nc.sbuf_tensor("ones_s", [128, 128], BF16))
    small = st.enter_context(nc.sbuf_tensor("small", [128, 16], F32))
    dummy = st.enter_context(nc.sbuf_tensor("fdummy", [128, 8], F32))
    epsb = st.enter_context(nc.sbuf_tensor("epsb", [128, 2], F32))
    P.fence_fn = lambda e: e.memset(dummy[:], 0.0)
    P.op("vector", lambda e: e.memset(epsb[:, 0:1], EPS), writes=["epsb"])
    P.op("vector", lambda e: e.memset(epsb[:, 1:2], 1e-5), writes=["epsb"])
    P.op("sync", lambda e: e.dma_start(out=ident[:], in_=ident_d), writes=["ident"], dma_key="ident")
    P.op("sync", lambda e: e.dma_start(out=ones[:], in_=ones_d), writes=["ones"], dma_key="ones")
    psum = [st.enter_context(nc.psum_tensor(f"ps{i}", [128, 512], F32)) for i in range(8)]
    pst = [psum[6 + i][:, :].bitcast(BF16) for i in range(2)]
    state = dict(ps=0, pt=0)

    def next_ps():
        i = state["ps"]; state["ps"] = (i + 1) % 6
        return i
    return dict(ident=ident, ones=ones, small=small, epsb=epsb, psum=psum, pst=pst, state=state, next_ps=next_ps, gains=gains)


def body_B(nc, P, st, sh, ysrc, npass=2, stages=(0, 1, 2, 3, 4, 5), nfb=NFB):
    x = nc.dram_tensor("xo", [NTOK, D], F32, kind="ExternalInput").ap()
    wg = nc.dram_tensor("wg", [64, 128, 32, 128], F32, kind="ExternalInput").ap()
    wu = nc.dram_tensor("wu", [64, 128, 16, 128], F32, kind="ExternalInput").ap()
    wo = nc.dram_tensor("wo", [16, 128, 32, 256], F32, kind="ExternalInput").ap()
    w1 = nc.dram_tensor("w1", [NFB, 128, 32, FB], F32, kind="ExternalInput").ap()
    w2 = nc.dram_tensor("w2", [NFB, 128, FB // 128, D], F32, kind="ExternalInput").ap()
    out = nc.dram_tensor("out", [NTOK, D], F32, kind="ExternalOutput").ap()
    x1_d = nc.dram_tensor("x1_d", [NTOK, D], F32).ap()
    gains, ident, small, epsb = sh["gains"], sh["ident"], sh["small"], sh["epsb"]
    psum, pst, state, next_ps = sh["psum"], sh["pst"], sh["state"], sh["next_ps"]
    if True:
        arena = st.enter_context(nc.sbuf_tensor("arena", [128, 172 * 256], F32))
        if ysrc[0] == "gather":
            selt = st.enter_context(nc.sbuf_tensor("selt", [128, 4], F32))
            P.op("sync", lambda e: e.dma_start(out=selt[:], in_=ysrc[2]), writes=["selt"], dma_key="selt")

        def AV(off_kib, size_kib, dt):
            v = arena[:, off_kib * 256:(off_kib + size_kib) * 256]
            return v.bitcast(BF16) if dt == BF16 else v

        bufA = AV(0, 32, BF16).rearrange("p (k t) -> p k t", k=32)
        bufB = AV(32, 32, BF16).rearrange("p (k t) -> p k t", k=32)
        bufM = AV(64, 32, BF16).rearrange("p (k t) -> p k t", k=32)
        bufZ = AV(96, 64, F32).rearrange("p (a d) -> p a d", a=4)
        wbuf = [AV(96 + 24 * s, 24, BF16) for s in range(2)]
        wo_v = [AV(32 + 16 * s, 16, BF16).rearrange("p (k j) -> p k j", k=32) for s in range(2)]
        w1_v = [AV(64 + 16 * s, 16, BF16).rearrange("p (k j) -> p k j", k=32) for s in range(2)]
        w2_v = [AV(32 + 16 * s, 16, BF16).rearrange("p (k j) -> p k j", k=FB // 128) for s in range(2)]
        xt_R3, gt_R3 = AV(64, 16, F32), AV(80, 16, F32)
        xt_R2, gt_R2 = AV(32, 16, F32), AV(48, 16, F32)
        hb_R4 = AV(144, 8, BF16)
        hb_R3 = AV(64, 8, BF16)
        t1 = [AV(160 + 2 * i, 2, F32) for i in range(2)]
        t2 = [AV(164 + 2 * i, 2, F32) for i in range(2)]
        ub = [AV(168 + 2 * i, 2, BF16).rearrange("p (k t) -> p k t", k=FB // 128) for i in range(2)]
        def load_gain(idx, gt):
            P.op("sync", lambda e: e.dma_start(out=gt, in_=gains[idx]), reads=["gains"], writes=["gt"], dma_key="gt")


        def rstd_from(src_ap, src_key, col, hb):
            P.op("vector", lambda e: e.memset(small[:, col:col + 1], 0.0), writes=[f"sm{col}"])
            P.op("scalar", lambda e: e.activation(out=hb, in_=src_ap, func=AF.Square, accum_out=small[:, col:col + 1]),
                 reads=[src_key, f"sm{col}"], writes=["hb", f"sm{col}"])
            P.op("scalar", lambda e: e.activation(out=small[:, col:col + 1], in_=small[:, col:col + 1], func=AF.Sqrt, scale=1.0 / D, bias=epsb[:, 0:1]),
                 reads=[f"sm{col}", "epsb"], writes=[f"sm{col}"])
            P.op("vector", lambda e: e.reciprocal(out=small[:, col:col + 1], in_=small[:, col:col + 1]), reads=[f"sm{col}"], writes=[f"sm{col}"])

        def norm_transpose(src_ap, src_key, col, gt, hb, tt):
            P.op("vector", lambda e: e.scalar_tensor_tensor(out=hb, in0=src_ap, scalar=small[:, col:col + 1], in1=gt,
                                                            op0=ALU.mult, op1=ALU.mult), reads=[src_key, f"sm{col}", "gt"], writes=["hb"])
            for k8 in range(4):
                pi = state["pt"]; state["pt"] ^= 1
                for j in range(8):
                    kc = k8 * 8 + j
                    P.op("tensor", lambda e, kc=kc, j=j, pi=pi: e.transpose(out=pst[pi][:, j * 128:(j + 1) * 128], in_=hb[:, kc * 128:(kc + 1) * 128], identity=ident[:]),
                         reads=["hb", "ident"], writes=[f"ps{6 + pi}"])
                dst = bufA[:, k8 * 8:(k8 + 1) * 8, tt * 128:(tt + 1) * 128]
                srcp = pst[pi][:, :].rearrange("p (k t) -> p k t", k=8)
                if k8 % 2 == 0:
                    P.op("scalar", lambda e, dst=dst, srcp=srcp: e.activation(out=dst, in_=srcp, func=AF.Copy), reads=[f"ps{6 + pi}"], writes=["bufA"])
                else:
                    P.op("vector", lambda e, dst=dst, srcp=srcp: e.tensor_copy(out=dst, in_=srcp), reads=[f"ps{6 + pi}"], writes=["bufA"])

        for ps_i in range(npass):
            tok0 = ps_i * TP
            if ps_i > 0 or ysrc[0] != "gather":
                P.fence()
            if 0 in stages:
                xt, gt, hb = xt_R3, gt_R3, hb_R4
                load_gain(0, gt)
                for tt in range(4):
                    r0 = tok0 + tt * 128
                    P.op("sync", lambda e, r0=r0, xt=xt: e.dma_start(out=xt, in_=x[r0:r0 + 128, :]), writes=["xt"], dma_key="xt")
                    rstd_from(xt, "xt", 0, hb)
                    norm_transpose(xt, "xt", 0, gt, hb, tt)
                if ysrc[0] == "input":
                    P.op("sync", lambda e, tok0=tok0: e.dma_start(out=bufB, in_=ysrc[1][:, :, tok0:tok0 + TP].rearrange("k p t -> p k t")), writes=["bufB"], dma_key="bufB")
                else:
                    G = ysrc[1]
                    cand = AV(96, 32, BF16).rearrange("p (k t) -> p k t", k=32)
                    for q in range(4):
                        t0 = 1024 * q + tok0
                        for part in range(2):
                            dstv = cand[:, 16 * part:16 * (part + 1), :].rearrange("p (g k) t -> p g k t", g=4)
                            for gq in range(4):
                                P.op("sync", lambda e, dstv=dstv, part=part, t0=t0, gq=gq: e.dma_start(out=dstv[:, gq, :, :], in_=G[4 * part:4 * part + 4, gq, :, t0:t0 + TP].rearrange("k p t -> p k t")),
                                     reads=["G"], writes=["cand"], dma_key="cand")
                        if q == 0:
                            P.op("vector", lambda e: e.tensor_scalar(out=bufB, in0=cand, scalar1=selt[:, 0:1], scalar2=None, op0=ALU.mult), reads=["cand", "selt"], writes=["bufB"])
                        else:
                            P.op("vector", lambda e, q=q: e.scalar_tensor_tensor(out=bufB, in0=cand, scalar=selt[:, q:q + 1], in1=bufB, op0=ALU.mult, op1=ALU.add), reads=["cand", "selt", "bufB"], writes=["bufB"])
            P.fence()
            if 1 in stages:
                for cc in range(32):
                    s = cc % 2
                    wb = wbuf[s]
                    vgA = wb[:, 0:4096].rearrange("p (k j) -> p k j", k=32)
                    vgB = wb[:, 4096:8192].rearrange("p (k j) -> p k j", k=32)
                    vuA = wb[:, 8192:10240].rearrange("p (k j) -> p k j", k=16)
                    vuB = wb[:, 10240:12288].rearrange("p (k j) -> p k j", k=16)
                    for (dst, src) in ((vgA, wg[cc]), (vgB, wg[32 + cc]), (vuA, wu[cc]), (vuB, wu[32 + cc])):
                        P.op("gpsimd", lambda e, dst=dst, src=src: e.dma_start(out=dst, in_=src, max_dma_last_dim=4096), writes=[f"wbuf{s}"], dma_key=f"wbuf{s}")
                    pgA, pgB, puA, puB = next_ps(), next_ps(), next_ps(), next_ps()
                    for (pi, wv, nk, src, koff) in ((pgA, vgA, 32, bufA, 0), (puA, vuA, 16, bufB, 0), (pgB, vgB, 32, bufA, 0), (puB, vuB, 16, bufB, 16)):
                        for kc in range(nk):
                            P.op("tensor", lambda e, kc=kc, pi=pi, wv=wv, nk=nk, src=src, koff=koff: e.matmul(psum[pi][:], lhsT=wv[:, kc, :], rhs=src[:, koff + kc, :], start=(kc == 0), stop=(kc == nk - 1)),
                                 reads=[f"wbuf{s}", "bufA", "bufB"], writes=[f"ps{pi}"])
                    P.op("scalar", lambda e, pgA=pgA, s=s: e.activation(out=t1[s], in_=psum[pgA][:], func=AF.Sigmoid), reads=[f"ps{pgA}"], writes=[f"t1_{s}"])
                    P.op("vector", lambda e, puA=puA, s=s: e.tensor_tensor(out=t1[s], in0=t1[s], in1=psum[puA][:], op=ALU.mult), reads=[f"ps{puA}", f"t1_{s}"], writes=[f"t1_{s}"])
                    P.op("scalar", lambda e, pgB=pgB, s=s: e.activation(out=t2[s], in_=psum[pgB][:], func=AF.Sigmoid), reads=[f"ps{pgB}"], writes=[f"t2_{s}"])
                    P.op("vector", lambda e, puB=puB, s=s: e.tensor_tensor(out=t2[s], in0=t2[s], in1=psum[puB][:], op=ALU.mult), reads=[f"ps{puB}", f"t2_{s}"], writes=[f"t2_{s}"])
                    P.op("vector", lambda e, cc=cc, s=s: e.tensor_tensor(out=bufM[:, cc, :], in0=t1[s], in1=t2[s], op=ALU.add), reads=[f"t1_{s}", f"t2_{s}"], writes=["bufM"])
            P.fence()
            if 2 in stages:
                for nb in range(16):
                    s = nb % 2
                    wv = wo_v[s]
                    for half in range(2):
                        P.op("gpsimd", lambda e, wv=wv, nb=nb, half=half: e.dma_start(out=wv[:, half * 16:(half + 1) * 16, :], in_=wo[nb][:, half * 16:(half + 1) * 16, :], max_dma_last_dim=4096),
                             writes=[f"wo{s}"], dma_key=f"wo{s}")
                    for tt in range(4):
                        pi = next_ps()
                        for kc in range(32):
                            P.op("tensor", lambda e, kc=kc, pi=pi, wv=wv, tt=tt: e.matmul(psum[pi][:, 0:256], lhsT=bufM[:, kc, tt * 128:(tt + 1) * 128], rhs=wv[:, kc, :], start=(kc == 0), stop=(kc == 31)),
                                 reads=[f"wo{s}", "bufM"], writes=[f"ps{pi}"])
                        dst = bufZ[:, tt, nb * 256:(nb + 1) * 256]
                        if (tt + nb) % 2 == 0:
                            P.op("scalar", lambda e, dst=dst, pi=pi: e.activation(out=dst, in_=psum[pi][:, 0:256], func=AF.Copy), reads=[f"ps{pi}"], writes=[f"bufZ{tt}_{nb % 8}"])
                        else:
                            P.op("vector", lambda e, dst=dst, pi=pi: e.tensor_copy(out=dst, in_=psum[pi][:, 0:256]), reads=[f"ps{pi}"], writes=[f"bufZ{tt}_{nb % 8}"])
            P.fence()
            if 3 in stages:
                xt, gt, hb = xt_R2, gt_R2, hb_R3
                load_gain(1, gt)
                for tt in range(4):
                    r0 = tok0 + tt * 128
                    zt = bufZ[:, tt, :]
                    rstd_from(zt, f"bufZ{tt}", 1, hb)
                    P.op("sync", lambda e, r0=r0, xt=xt: e.dma_start(out=xt, in_=x[r0:r0 + 128, :]), writes=["xt"], dma_key="xt")
                    P.op("vector", lambda e, zt=zt, gt=gt: e.scalar_tensor_tensor(out=zt, in0=zt, scalar=small[:, 1:2], in1=gt, op0=ALU.mult, op1=ALU.mult),
                         reads=[f"bufZ{tt}", "sm1", "gt"], writes=[f"bufZ{tt}"])
                    P.op("vector", lambda e, zt=zt, xt=xt: e.tensor_tensor(out=zt, in0=zt, in1=xt, op=ALU.add), reads=[f"bufZ{tt}", "xt"], writes=[f"bufZ{tt}"])
                    P.op("sync", lambda e, r0=r0, zt=zt: e.dma_start(out=x1_d[r0:r0 + 128, :], in_=zt), reads=[f"bufZ{tt}"], writes=[f"x1d{ps_i}_{tt}"], dma_key=f"x1st{tt}")
                load_gain(2, gt)
                for tt in range(4):
                    zt = bufZ[:, tt, :]
                    rstd_from(zt, f"bufZ{tt}", 2, hb)
                    norm_transpose(zt, f"bufZ{tt}", 2, gt, hb, tt)
            P.fence()
            if 4 in stages:
                for fb in range(nfb):
                    s = fb % 2
                    w1v = w1_v[s]
                    w2v = w2_v[s]
                    for half in range(2):
                        P.op("gpsimd", lambda e, w1v=w1v, fb=fb, half=half: e.dma_start(out=w1v[:, half * 16:(half + 1) * 16, :], in_=w1[fb][:, half * 16:(half + 1) * 16, :], max_dma_last_dim=4096),
                             writes=[f"w1_{s}"], dma_key=f"w1_{s}")
                    for kc2 in range(FB // 128):
                        P.op("gpsimd", lambda e, w2v=w2v, fb=fb, kc2=kc2: e.dma_start(out=w2v[:, kc2, :], in_=w2[fb][:, kc2, :], max_dma_last_dim=4096),
                             writes=[f"w2_{s}"], dma_key=f"w2_{s}")
                    for fc in range(FB // 128):
                        pi = next_ps()
                        for kc in range(32):
                            P.op("tensor", lambda e, kc=kc, pi=pi, w1v=w1v, fc=fc: e.matmul(psum[pi][:], lhsT=w1v[:, kc, fc * 128:(fc + 1) * 128], rhs=bufA[:, kc, :], start=(kc == 0), stop=(kc == 31)),
                                 reads=[f"w1_{s}", "bufA"], writes=[f"ps{pi}"])
                        P.op("scalar", lambda e, pi=pi, s=s: e.activation(out=t1[s], in_=psum[pi][:], func=AF.Relu), reads=[f"ps{pi}"], writes=[f"t1_{s}"])
                        P.op("vector", lambda e, s=s, fc=fc: e.tensor_tensor(out=ub[s][:, fc, :], in0=t1[s], in1=t1[s], op=ALU.mult), reads=[f"t1_{s}"], writes=[f"ub{s}"])
                    for tt in range(4):
                        for nb in range(8):
                            pi = next_ps()
                            for kc2 in range(FB // 128):
                                P.op("tensor", lambda e, kc2=kc2, pi=pi, w2v=w2v, tt=tt, nb=nb, s=s: e.matmul(psum[pi][:], lhsT=ub[s][:, kc2, tt * 128:(tt + 1) * 128], rhs=w2v[:, kc2, nb * 512:(nb + 1) * 512],
                                                                                                          start=(kc2 == 0), stop=(kc2 == FB // 128 - 1)),
                                     reads=[f"w2_{s}", f"ub{s}"], writes=[f"ps{pi}"])
                            dst = bufZ[:, tt, nb * 512:(nb + 1) * 512]
                            key = f"bufZ{tt}_{nb}"
                            if fb == 0:
                                P.op("vector", lambda e, dst=dst, pi=pi: e.tensor_copy(out=dst, in_=psum[pi][:]), reads=[f"ps{pi}"], writes=[key])
                            else:
                                P.op("vector", lambda e, dst=dst, pi=pi: e.tensor_tensor(out=dst, in0=dst, in1=psum[pi][:], op=ALU.add), reads=[f"ps{pi}", key], writes=[key])
            P.fence()
            if 5 in stages:
                xt, gt, hb = xt_R3, gt_R3, AV(32, 8, BF16)
                load_gain(3, gt)
                for tt in range(4):
                    r0 = tok0 + tt * 128
                    zt = bufZ[:, tt, :]
                    rstd_from(zt, f"bufZ{tt}", 3, hb)
                    P.op("sync", lambda e, r0=r0, xt=xt: e.dma_start(out=xt, in_=x1_d[r0:r0 + 128, :]), reads=[f"x1d{ps_i}_{tt}"], writes=["xt"], dma_key="xt")
                    P.op("vector", lambda e, zt=zt, gt=gt: e.scalar_tensor_tensor(out=zt, in0=zt, scalar=small[:, 3:4], in1=gt, op0=ALU.mult, op1=ALU.mult),
                         reads=[f"bufZ{tt}", "sm3", "gt"], writes=[f"bufZ{tt}"])
                    P.op("vector", lambda e, zt=zt, xt=xt: e.tensor_tensor(out=zt, in0=zt, in1=xt, op=ALU.add), reads=[f"bufZ{tt}", "xt"], writes=[f"bufZ{tt}"])
                    P.op("sync", lambda e, r0=r0, zt=zt: e.dma_start(out=out[r0:r0 + 128, :], in_=zt), reads=[f"bufZ{tt}"], writes=["out"], dma_key=f"x1st{tt}")
        P.fence()


def build_B(npass=2, stages=(0, 1, 2, 3, 4, 5), nfb=NFB):
    nc = bass.Bass("TRN2", target_bir_lowering=False)
    yT = nc.dram_tensor("yT", [32, 128, NTOK], BF16, kind="ExternalInput").ap()
    P = Prog(nc)
    with ExitStack() as st:
        sh = make_shared(nc, P, st)
        body_B(nc, P, st, sh, ("input", yT), npass=npass, stages=stages, nfb=nfb)
        counts = P.emit(st)
        print("B ops", counts, "waits", P.n_waits)
    return nc


def relayout_B(inp, l=0):
    w_in = inp["w_in"][l]
    NR = 6592
    gcol0 = NR + 3 * 2048
    Wg = w_in[:, gcol0:gcol0 + 8192]
    wg = np.ascontiguousarray(Wg.reshape(32, 128, 64, 128).transpose(2, 1, 0, 3))
    Wu = np.concatenate([inp["w_up_rwkv"][l], inp["w_up_diff"][l]], axis=1)
    wu = np.ascontiguousarray(Wu.reshape(16, 128, 64, 128).transpose(2, 1, 0, 3))
    wo = np.ascontiguousarray(inp["w_out"][l].reshape(32, 128, 16, 256).transpose(2, 1, 0, 3))
    w1 = np.ascontiguousarray(inp["w_mlp_in"][l].reshape(32, 128, NFB, FB).transpose(2, 1, 0, 3))
    w2 = np.ascontiguousarray(inp["w_mlp_out"][l].reshape(NFB, FB // 128, 128, D).transpose(0, 2, 1, 3))
    gains = np.stack([np.broadcast_to(inp[k][l][None, :], (128, D)) for k in ("attn_pre_norm", "attn_post_norm", "mlp_pre_norm", "mlp_post_norm")]).astype(np.float32)
    ident = np.eye(128, dtype=np.float32).astype(ml_dtypes.bfloat16)
    ones = np.ones((128, 128), np.float32).astype(ml_dtypes.bfloat16)
    return dict(wg=wg, wu=wu, wo=wo, w1=w1, w2=w2, gains=np.ascontiguousarray(gains), ident=ident, ones=ones)


F32 = mybir.dt.float32
BF16 = mybir.dt.bfloat16
AF = mybir.ActivationFunctionType
ALU = mybir.AluOpType
S = 4096
C = 128
BLK = 512
NB = S // BLK
EPS_GN = 64e-5
DEC = -0.6065306597126334
RWMODE = 3


def emit_rwkv(nc, P, st, PT_d, yT, psum, pst, state, next_ps, ident, ones, epsb, rw_heads):
    par_d = nc.dram_tensor("par", [64, 8, 16], F32, kind="ExternalInput").ap()
    parl_d = nc.dram_tensor("parl", [128, 4, 2], F32, kind="ExternalInput").ap()
    lw2_d = nc.dram_tensor("lw2", [96, 4, 512], F32, kind="ExternalInput").ap()
    g2_d = nc.dram_tensor("g2", [128, 2, 512], F32, kind="ExternalInput").ap()
    masks_d = nc.dram_tensor("masks", [128, 2, 640], F32, kind="ExternalInput").ap()
    mreset_d = nc.dram_tensor("mreset", [64, BLK], F32, kind="ExternalInput").ap()

    def sb(name, shape, dt):
        return st.enter_context(nc.sbuf_tensor(name, shape, dt))
    par = sb("par_s", [64, 8, 20], F32)
    parl = sb("parl_s", [128, 4, 3], F32)
    lw2b = sb("lw2b", [96, 4, 512], BF16)
    g2b = sb("g2b", [128, 2, 512], BF16)
    masks = sb("masks_s", [128, 2, 640], F32)
    mreset = sb("mreset_s", [64, BLK], F32)
    twT = sb("twT", [128, S], BF16)
    daT = sb("daT", [128, S], BF16)
    sgT = sb("sgT", [128, 2, S], BF16)
    RAW = sb("RAW", [128, S // 2 + 2], F32)
    SH = sb("SH", [128, S // 2], F32)
    Rb16 = sb("R16", [64, S], BF16)
    Kb16 = sb("K16", [64, S], BF16)
    Vb16 = sb("V16", [64, S], BF16)
    Vt = sb("rVt", [128, 32, 64], BF16)
    KKN = sb("KKN", [64, S], BF16)
    OT = sb("OT", [64, S], F32)
    BONV = sb("BONV", [64, S], BF16)
    gnb = sb("gnb", [64, 1], F32)
    tiny = sb("tinyb", [64, 1], F32)
    LW = sb("LW", [64, BLK], F32)
    CI = sb("CI", [64, BLK], F32)
    CE = sb("CE", [64, BLK], F32)
    Ece = sb("Ece", [64, BLK], F32)
    Enci = sb("Enci", [64, BLK], F32)
    Eci = [sb(f"Eci{i}", [64, BLK], F32) for i in range(2)]
    At = sb("At", [64, BLK], F32)
    T1 = sb("T1", [64, BLK], F32)
    KD = sb("KD", [64, BLK], F32)
    T2 = sb("T2b", [64, BLK], BF16)
    ops4 = [sb(f"ops4_{i}", [64, 4, BLK], BF16) for i in range(2)]
    tok3 = [sb(f"tok3_{i}", [128, 12, 64], BF16) for i in range(2)]
    G1s = [sb(f"G1s{c}", [128, 384], BF16) for c in range(8)]
    G2s = [sb(f"G2s{c}", [128, 256], BF16) for c in range(8)]
    MM = [[sb(f"MM{c}_{i}", [128, 256], BF16) for i in range(2)] for c in range(8)]
    Qs = [[sb(f"Qs{c}_{i}", [128, 128], BF16) for i in range(2)] for c in range(8)]
    IMs = [[sb(f"IM{c}_{i}", [128, 128], BF16) for i in range(2)] for c in range(8)]
    nW1T = [sb(f"nW1T{c}", [64, 128], BF16) for c in range(8)]
    nXs = [sb(f"nXs{c}", [128, 64], BF16) for c in range(8)]
    Us = sb("Us", [128, 64], BF16)
    Hf = sb("Hf", [64, 64], F32)
    Hb = sb("Hb", [64, 64], BF16)
    yo = sb("ryo", [64, BLK], BF16)
    ob = sb("ob", [64, BLK], BF16)
    dd = sb("dd", [64, BLK], F32)
    ones64 = ones[0:64, 0:64]
    id64 = ident[0:64, 0:64]

    def V(fn, reads, writes):
        P.op("vector", fn, reads=reads, writes=writes)

    def A(fn, reads, writes):
        P.op("scalar", fn, reads=reads, writes=writes)

    def T(fn, reads, writes):
        P.op("tensor", fn, reads=reads, writes=writes)

    for (dst, src, k, q) in ((par[:, :, 0:16], par_d, "par", "sync"), (parl[:, :, 0:2], parl_d, "parl", "sync"), (masks[:], masks_d, "masks", "sync"),
                             (mreset[:], mreset_d, "mreset", "sync"), (lw2b[:], lw2_d, "lw2b", "gpsimd"), (g2b[:], g2_d, "g2b", "gpsimd")):
        P.op(q, lambda e, dst=dst, src=src: e.dma_start(out=dst, in_=src), writes=[k], dma_key=k)
    V(lambda e: e.memset(gnb[:], EPS_GN), [], ["gnb"])
    V(lambda e: e.memset(tiny[:], 1e-24), [], ["gnb"])
    for i in range(3):
        V(lambda e, i=i: e.tensor_tensor(out=par[:, :, 16 + i], in0=par[:, :, 2 * i], in1=par[:, :, 2 * i + 1], op=ALU.add), ["par"], ["par"])
        V(lambda e, i=i: e.tensor_scalar(out=par[:, :, 16 + i], in0=par[:, :, 16 + i], scalar1=-1.0, scalar2=1.0, op0=ALU.mult, op1=ALU.add), ["par"], ["par"])
    V(lambda e: e.tensor_tensor(out=parl[:, :, 2], in0=parl[:, :, 0], in1=parl[:, :, 1], op=ALU.add), ["parl"], ["parl"])
    V(lambda e: e.tensor_scalar(out=parl[:, :, 2], in0=parl[:, :, 2], scalar1=-1.0, scalar2=1.0, op0=ALU.mult, op1=ALU.add), ["parl"], ["parl"])

    HS = S // 2

    def load_shift(ch, r0, nrow, c0, mup, mun, consume):
        for half in range(2):
            t0 = half * HS
            lo = max(t0 - 1, 0)
            hi = min(t0 + HS + 1, S)
            off = lo - (t0 - 1)
            if half == 0:
                V(lambda e: e.memset(RAW[0:nrow, 0:1], 0.0), [], ["RAW"])
            else:
                V(lambda e: e.memset(RAW[0:nrow, HS + 1:HS + 2], 0.0), [], ["RAW"])
            P.op("sync", lambda e, lo=lo, hi=hi, off=off: e.dma_start(out=RAW[0:nrow, off:off + (hi - lo)], in_=PT_d[ch][r0:r0 + nrow, lo:hi]), reads=[f"PT{ch}"], writes=["RAW"], dma_key="RAW")
            V(lambda e: e.tensor_scalar(out=SH[0:nrow, :], in0=RAW[0:nrow, 1:HS + 1], scalar1=c0, scalar2=None, op0=ALU.mult), ["RAW", "par", "parl"], ["SH"])
            V(lambda e: e.scalar_tensor_tensor(out=SH[0:nrow, :], in0=RAW[0:nrow, 0:HS], scalar=mup, in1=SH[0:nrow, :], op0=ALU.mult, op1=ALU.add), ["RAW", "SH", "par", "parl"], ["SH"])
            V(lambda e: e.scalar_tensor_tensor(out=SH[0:nrow, :], in0=RAW[0:nrow, 2:HS + 2], scalar=mun, in1=SH[0:nrow, :], op0=ALU.mult, op1=ALU.add), ["RAW", "SH", "par", "parl"], ["SH"])
            consume(half)

    for i, (ch, dstT, fn) in enumerate(((12, twT, AF.Tanh), (13, daT, AF.Copy), (14, sgT[:, 0, :], AF.Sigmoid), (15, sgT[:, 1, :], AF.Sigmoid))):
        def cons(half, dstT=dstT, fn=fn):
            A(lambda e: e.activation(out=dstT[:, half * HS:(half + 1) * HS], in_=SH[:, :], func=fn), ["SH"], ["lora_act"])
        load_shift(ch, 0, 128, parl[:, i, 2:3], parl[:, i, 0:1], parl[:, i, 1:2], cons)

    def head(hh):
        ch_r, ch_k, ch_v = hh // 2, 4 + hh // 2, 8 + hh // 2
        r0 = (hh % 2) * 64
        pc = lambda j: par[:, hh, j:j + 1]
        for (ch, i, dst) in ((ch_r, 0, Rb16), (ch_k, 1, Kb16), (ch_v, 2, Vb16)):
            def cons(half, dst=dst):
                A(lambda e: e.activation(out=dst[:, half * HS:(half + 1) * HS], in_=SH[0:64, :], func=AF.Copy), ["SH"], ["rkv"])
            load_shift(ch, r0, 64, pc(16 + i), pc(2 * i), pc(2 * i + 1), cons)
        for half in range(2):
            pi = state["pt"]; state["pt"] ^= 1
            for j in range(16):
                blk = half * 16 + j
                T(lambda e, blk=blk, j=j, pi=pi: e.transpose(out=pst[pi][:, j * 64:(j + 1) * 64], in_=Vb16[:, blk * 128:(blk + 1) * 128], identity=id64), ["rkv", "ident"], [f"ps{6 + pi}"])
            V(lambda e, half=half, pi=pi: e.tensor_copy(out=Vt[:, half * 16:(half + 1) * 16, :], in_=pst[pi][:, :].rearrange("p (k t) -> p k t", k=16)), [f"ps{6 + pi}"], ["rVt"])
        for b in range(NB):
            bs = slice(b * BLK, (b + 1) * BLK)
            V(lambda e, bs=bs: e.tensor_scalar(out=T1[:], in0=Kb16[:, bs], scalar1=pc(10), scalar2=None, op0=ALU.mult), ["rkv", "par"], ["T1"])
            A(lambda e: e.activation(out=T2[:], in_=T1[:], func=AF.Square), ["T1"], ["T2"])
            pi = next_ps()
            T(lambda e, pi=pi: e.matmul(psum[pi][0:64, :], lhsT=ones64, rhs=T2[:], start=True, stop=True), ["T2", "ones"], [f"ps{pi}"])
            A(lambda e, pi=pi: e.activation(out=KD[:], in_=psum[pi][0:64, :], func=AF.Sqrt, bias=tiny[:, 0:1]), [f"ps{pi}", "gnb"], ["KD"])
            V(lambda e: e.reciprocal(out=KD[:], in_=KD[:]), ["KD"], ["KD"])
            V(lambda e, bs=bs: e.tensor_tensor(out=KKN[:, bs], in0=T1[:], in1=KD[:], op=ALU.mult), ["T1", "KD"], ["KKN"])
        def direction(d):
            V(lambda e: e.memset(Hf[:], 0.0), [], ["Hf"])
            V(lambda e: e.memset(Hb[:], 0.0), [], ["Hb"])
            blocks = range(NB) if d == 0 else range(NB - 1, -1, -1)
            def block(bi, b):
                bs = slice(b * BLK, (b + 1) * BLK)
                sl = bi % 2
                O4, K3, EC = ops4[sl], tok3[sl], Eci[sl]
                hs = slice(hh * 64, (hh + 1) * 64)
                pi = 6
                T(lambda e, pi=pi, bs=bs, hs=hs: e.matmul(psum[pi][0:64, :], lhsT=lw2b[:, d, hs], rhs=twT[0:96, bs], start=True, stop=True), ["lw2b", "lora_act"], [f"ps{pi}"])
                A(lambda e, pi=pi: e.activation(out=LW[:], in_=psum[pi][0:64, :], func=AF.Sigmoid, bias=pc(6 + d)), [f"ps{pi}", "par"], ["LW"])
                V(lambda e: e.tensor_scalar(out=LW[:], in0=LW[:], scalar1=DEC, scalar2=None, op0=ALU.mult), ["LW"], ["LW"])
                V(lambda e: e.tensor_tensor_scan(out=CI[:], data0=mreset[:], data1=LW[:], initial=0.0, op0=ALU.mult, op1=ALU.add), ["LW", "mreset"], ["CI"])
                if d == 0:
                    V(lambda e: e.tensor_tensor(out=CE[:], in0=CI[:], in1=LW[:], op=ALU.subtract), ["CI", "LW"], ["CE"])
                else:
                    for c in range(4):
                        cs = slice(c * C, (c + 1) * C)
                        V(lambda e, cs=cs, c=c: e.tensor_scalar(out=CE[:, cs], in0=CI[:, cs], scalar1=-1.0, scalar2=CI[:, c * C + C - 1:c * C + C], op0=ALU.mult, op1=ALU.add), ["CI"], ["CE"])
                    V(lambda e: e.tensor_tensor(out=CI[:], in0=CE[:], in1=LW[:], op=ALU.add), ["CE", "LW"], ["CI"])
                A(lambda e: e.activation(out=Ece[:], in_=CE[:], func=AF.Exp), ["CE"], ["Ece"])
                A(lambda e: e.activation(out=Enci[:], in_=CI[:], func=AF.Exp, scale=-1.0), ["CI"], ["Enci"])
                A(lambda e, EC=EC: e.activation(out=EC[:], in_=CI[:], func=AF.Exp), ["CI"], [f"Eci{sl}"])
                pi = 7
                T(lambda e, pi=pi, bs=bs, hs=hs: e.matmul(psum[pi][0:64, :], lhsT=lw2b[:, 2 + d, hs], rhs=daT[0:96, bs], start=True, stop=True), ["lw2b", "lora_act"], [f"ps{pi}"])
                A(lambda e, pi=pi: e.activation(out=At[:], in_=psum[pi][0:64, :], func=AF.Sigmoid, bias=pc(8 + d)), [f"ps{pi}", "par"], ["At"])
                V(lambda e: e.tensor_scalar(out=T1[:], in0=At[:], scalar1=-1.0, scalar2=pc(11), op0=ALU.add, op1=ALU.mult), ["At", "par"], ["T1"])
                V(lambda e, bs=bs: e.scalar_tensor_tensor(out=KD[:], in0=T1[:], scalar=1.0, in1=Kb16[:, bs], op0=ALU.add, op1=ALU.mult), ["T1", "rkv"], ["KD"])
                V(lambda e, bs=bs: e.tensor_tensor(out=At[:], in0=At[:], in1=KKN[:, bs], op=ALU.mult), ["At", "KKN"], ["At"])
                V(lambda e, bs=bs, O4=O4: e.tensor_tensor(out=O4[:, 0, :], in0=KKN[:, bs], in1=Ece[:], op=ALU.mult), ["KKN", "Ece"], [f"ops4_{sl}"])
                V(lambda e, O4=O4: e.tensor_tensor(out=O4[:, 1, :], in0=At[:], in1=Enci[:], op=ALU.mult), ["At", "Enci"], [f"ops4_{sl}"])
                V(lambda e, O4=O4: e.tensor_tensor(out=O4[:, 2, :], in0=KD[:], in1=Enci[:], op=ALU.mult), ["KD", "Enci"], [f"ops4_{sl}"])
                V(lambda e, bs=bs, O4=O4, EC=EC: e.tensor_tensor(out=O4[:, 3, :], in0=Rb16[:, bs], in1=EC[:], op=ALU.mult), ["rkv", f"Eci{sl}"], [f"ops4_{sl}"])
                V(lambda e, bs=bs: e.scalar_tensor_tensor(out=T2[:], in0=Rb16[:, bs], scalar=pc(12), in1=KD[:], op0=ALU.mult, op1=ALU.mult), ["rkv", "KD", "par"], ["T2"])
                pi = 6
                T(lambda e, pi=pi: e.matmul(psum[pi][0:64, :], lhsT=ones64, rhs=T2[:], start=True, stop=True), ["T2", "ones"], [f"ps{pi}"])
                V(lambda e, pi=pi, bs=bs: e.scalar_tensor_tensor(out=T1[:], in0=psum[pi][0:64, :], scalar=0.5, in1=Vb16[:, bs], op0=ALU.mult, op1=ALU.mult), [f"ps{pi}", "rkv"], ["T1"])
                if d == 0:
                    V(lambda e, bs=bs: e.tensor_copy(out=BONV[:, bs], in_=T1[:]), ["T1"], ["BONV"])
                else:
                    V(lambda e, bs=bs: e.tensor_tensor(out=BONV[:, bs], in0=BONV[:, bs], in1=T1[:], op=ALU.add), ["T1", "BONV"], ["BONV"])
                pi = state["pt"]; state["pt"] ^= 1
                for o in range(3):
                    for c in range(4):
                        T(lambda e, o=o, c=c, pi=pi, O4=O4: e.transpose(out=pst[pi][:, (o * 4 + c) * 64:(o * 4 + c + 1) * 64], in_=O4[:, o, c * C:(c + 1) * C], identity=id64),
                          [f"ops4_{sl}", "ident"], [f"ps{6 + pi}"])
                V(lambda e, pi=pi, K3=K3: e.tensor_copy(out=K3[:, :, :], in_=pst[pi][:, 0:768].rearrange("p (k t) -> p k t", k=12)), [f"ps{6 + pi}"], [f"tok3_{sl}"])
                chunks = list(range(4)) if d == 0 else list(range(3, -1, -1))
                ok = [f"ops4_{sl}"]
                cst_ = {}

                def opsof(c):
                    cs = slice(c * C, (c + 1) * C)
                    return O4[:, 0, cs], O4[:, 1, cs], O4[:, 2, cs], O4[:, 3, cs]

                kx = lambda c: sl * 4 + c

                def gram1(c):
                    Ab_c, Bb_c, Kb_c, Rb_c = opsof(c)
                    p1, p2 = next_ps(), next_ps()
                    cst_[c] = dict(p1=p1, p2=p2)
                    T(lambda e: e.matmul(psum[p1][:, 0:128], lhsT=Bb_c, rhs=Ab_c, start=True, stop=True), ok, [f"ps{p1}"])
                    T(lambda e: e.matmul(psum[p1][:, 128:256], lhsT=Kb_c, rhs=Ab_c, start=True, stop=True), ok, [f"ps{p1}"])
                    T(lambda e: e.matmul(psum[p1][:, 256:384], lhsT=Ab_c, rhs=Bb_c, start=True, stop=True), ok, [f"ps{p1}"])
                    T(lambda e: e.matmul(psum[p2][:, 0:128], lhsT=Bb_c, rhs=Rb_c, start=True, stop=True), ok, [f"ps{p2}"])
                    T(lambda e: e.matmul(psum[p2][:, 128:256], lhsT=Kb_c, rhs=Rb_c, start=True, stop=True), ok, [f"ps{p2}"])

                def evac1(c):
                    p1, p2 = cst_[c]["p1"], cst_[c]["p2"]
                    V(lambda e: e.tensor_tensor(out=G1s[kx(c)][:], in0=psum[p1][:, 0:384], in1=masks[:, d, 0:384], op=ALU.mult), [f"ps{p1}", "masks"], [f"G1s{kx(c)}"])
                    V(lambda e: e.tensor_tensor(out=G2s[kx(c)][:], in0=psum[p2][:, 0:256], in1=masks[:, d, 384:640], op=ALU.mult), [f"ps{p2}", "masks"], [f"G2s{kx(c)}"])
                    V(lambda e: e.tensor_tensor(out=Qs[kx(c)][0][:], in0=G1s[kx(c)][:, 0:128], in1=ident[:], op=ALU.add), [f"G1s{kx(c)}", "ident"], [f"Qs{kx(c)}_0"])
                    cst_[c].update(M=G1s[kx(c)][:, 256:384], MT=G1s[kx(c)][:, 0:128], mk=f"G1s{kx(c)}", qi=0)

                def levelA(c, lev):
                    stc = cst_[c]
                    Mprev, MTprev, mk = stc["M"], stc["MT"], stc["mk"]
                    mi = lev % 2
                    pm = next_ps()
                    T(lambda e: e.matmul(psum[pm][:, 0:128], lhsT=MTprev, rhs=Mprev, start=True, stop=True), [mk], [f"ps{pm}"])
                    if lev < 6:
                        T(lambda e: e.matmul(psum[pm][:, 128:256], lhsT=Mprev, rhs=MTprev, start=True, stop=True), [mk], [f"ps{pm}"])
                    A(lambda e: e.activation(out=MM[kx(c)][mi][:], in_=psum[pm][:, 0:256], func=AF.Copy), [f"ps{pm}"], [f"MM{kx(c)}_{mi}"])
                    V(lambda e: e.tensor_tensor(out=IMs[kx(c)][mi][:], in0=MM[kx(c)][mi][:, 0:128], in1=ident[:], op=ALU.add), [f"MM{kx(c)}_{mi}", "ident"], [f"IM{kx(c)}_{mi}"])
                    stc["M"], stc["MT"], stc["mk"] = MM[kx(c)][mi][:, 0:128], MM[kx(c)][mi][:, 128:256], f"MM{kx(c)}_{mi}"

                def levelB(c, lev):
                    stc = cst_[c]
                    qi = stc["qi"]
                    mi = lev % 2
                    pq = next_ps()
                    T(lambda e: e.matmul(psum[pq][:, 0:128], lhsT=IMs[kx(c)][mi][:], rhs=Qs[kx(c)][qi][:], start=True, stop=True), [f"Qs{kx(c)}_{qi}", f"IM{kx(c)}_{mi}"], [f"ps{pq}"])
                    V(lambda e: e.tensor_copy(out=Qs[kx(c)][1 - qi][:], in_=psum[pq][:, 0:128]), [f"ps{pq}"], [f"Qs{kx(c)}_{1 - qi}"])
                    stc["qi"] = 1 - qi

                def w1x(c):
                    gc = b * 4 + c
                    qi = cst_[c]["qi"]
                    Q, qk = Qs[kx(c)][qi], f"Qs{kx(c)}_{qi}"
                    Abt = K3[:, 0 + c, :]
                    pw = next_ps()
                    T(lambda e: e.matmul(psum[pw][0:64, 0:128], lhsT=Abt, rhs=Q[:], start=True, stop=True), [f"tok3_{sl}", qk], [f"ps{pw}"])
                    A(lambda e: e.activation(out=nW1T[kx(c)][:], in_=psum[pw][0:64, 0:128], func=AF.Copy, scale=-1.0), [f"ps{pw}"], [f"nW1T{kx(c)}"])
                    px = next_ps()
                    T(lambda e: e.matmul(psum[px][:, 0:64], lhsT=G1s[kx(c)][:, 128:256], rhs=Vt[:, gc, :], start=True, stop=True), [f"G1s{kx(c)}", "rVt"], [f"ps{px}"])
                    A(lambda e: e.activation(out=nXs[kx(c)][:], in_=psum[px][:, 0:64], func=AF.Copy, scale=-1.0), [f"ps{px}"], [f"nXs{kx(c)}"])

                def seq(c):
                    gc = b * 4 + c
                    qi = cst_[c]["qi"]
                    Q, qk = Qs[kx(c)][qi], f"Qs{kx(c)}_{qi}"
                    Ab_c, Bb_c, Kb_c, Rb_c = opsof(c)
                    Bbt, Kbt = K3[:, 4 + c, :], K3[:, 8 + c, :]
                    Vtc = Vt[:, gc, :]
                    pu = next_ps()
                    T(lambda e: e.matmul(psum[pu][:, 0:64], lhsT=Q[:], rhs=nXs[kx(c)][:], start=True, stop=False), [qk, f"nXs{kx(c)}"], [f"ps{pu}"])
                    T(lambda e: e.matmul(psum[pu][:, 0:64], lhsT=nW1T[kx(c)][:], rhs=Hb[:], start=False, stop=True), [f"nW1T{kx(c)}", "Hb"], [f"ps{pu}"])
                    V(lambda e: e.tensor_copy(out=Us[:], in_=psum[pu][:, 0:64]), [f"ps{pu}"], ["Us"])
                    po = next_ps()
                    T(lambda e: e.matmul(psum[po][0:64, 0:128], lhsT=Hb[:], rhs=Rb_c, start=True, stop=False), ["Hb"] + ok, [f"ps{po}"])
                    T(lambda e: e.matmul(psum[po][0:64, 0:128], lhsT=Us[:], rhs=G2s[kx(c)][:, 0:128], start=False, stop=False), ["Us", f"G2s{kx(c)}"], [f"ps{po}"])
                    T(lambda e: e.matmul(psum[po][0:64, 0:128], lhsT=Vtc, rhs=G2s[kx(c)][:, 128:256], start=False, stop=True), ["rVt", f"G2s{kx(c)}"], [f"ps{po}"])
                    gs = slice(gc * C, (gc + 1) * C)
                    if d == 0:
                        A(lambda e: e.activation(out=OT[:, gs], in_=psum[po][0:64, 0:128], func=AF.Copy), [f"ps{po}"], ["OT"])
                    else:
                        V(lambda e: e.tensor_tensor(out=OT[:, gs], in0=OT[:, gs], in1=psum[po][0:64, 0:128], op=ALU.add), [f"ps{po}", "OT"], ["OT"])
                    ph = next_ps()
                    T(lambda e: e.matmul(psum[ph][0:64, 0:64], lhsT=Bbt, rhs=Us[:], start=True, stop=False), [f"tok3_{sl}", "Us"], [f"ps{ph}"])
                    T(lambda e: e.matmul(psum[ph][0:64, 0:64], lhsT=Kbt, rhs=Vtc, start=False, stop=True), [f"tok3_{sl}", "rVt"], [f"ps{ph}"])
                    gidx = c * C + C - 1 if d == 0 else c * C
                    gam = EC[:, gidx:gidx + 1]
                    V(lambda e: e.tensor_scalar(out=Hf[:], in0=Hf[:], scalar1=gam, scalar2=None, op0=ALU.mult), ["Hf", f"Eci{sl}"], ["Hf"])
                    V(lambda e: e.scalar_tensor_tensor(out=Hf[:], in0=psum[ph][0:64, 0:64], scalar=gam, in1=Hf[:], op0=ALU.mult, op1=ALU.add), ["Hf", f"Eci{sl}", f"ps{ph}"], ["Hf"])
                    A(lambda e: e.activation(out=Hb[:], in_=Hf[:], func=AF.Copy), ["Hf"], ["Hb"])

                stages = []
                for cg in (chunks[0:2], chunks[2:4]):
                    for c in cg:
                        stages.append(lambda c=c: gram1(c))
                    for c in cg:
                        stages.append(lambda c=c: evac1(c))
                for lev in range(1, 7):
                    for c in chunks:
                        stages.append(lambda c=c, lev=lev: levelA(c, lev))
                    for c in chunks:
                        stages.append(lambda c=c, lev=lev: levelB(c, lev))
                for c in chunks:
                    stages.append(lambda c=c: w1x(c))
                seqs = [(lambda c=c: seq(c)) for c in chunks]
                return stages, seqs
            prev = None
            for bi, b in enumerate(blocks):
                stages, seqs = block(bi, b)
                if prev is None or RWMODE != 3:
                    if prev is not None:
                        for f in prev:
                            f()
                    for f in stages:
                        f()
                else:
                    n = len(stages)
                    marks = {int((k + 1) * n / 5): k for k in range(4)}
                    for i, f in enumerate(stages):
                        f()
                        if (i + 1) in marks:
                            prev[marks[i + 1]]()
                prev = seqs
            for f in prev:
                f()
        P.fence()
        direction(0)
        direction(1)
        P.fence()
        for b in range(NB):
            bs = slice(b * BLK, (b + 1) * BLK)
            A(lambda e, bs=bs: e.activation(out=ob[:], in_=OT[:, bs], func=AF.Copy), ["OT"], ["ob"])
            pi = next_ps()
            T(lambda e, pi=pi: e.matmul(psum[pi][0:64, :], lhsT=ones64, rhs=ob[:], start=True, stop=True), ["ob", "ones"], [f"ps{pi}"])
            V(lambda e, pi=pi, bs=bs: e.scalar_tensor_tensor(out=dd[:], in0=psum[pi][0:64, :], scalar=-1.0 / 64, in1=OT[:, bs], op0=ALU.mult, op1=ALU.add), [f"ps{pi}", "OT"], ["dd"])
            A(lambda e: e.activation(out=ob[:], in_=dd[:], func=AF.Square), ["dd"], ["ob"])
            pi = next_ps()
            T(lambda e, pi=pi: e.matmul(psum[pi][0:64, :], lhsT=ones64, rhs=ob[:], start=True, stop=True), ["ob", "ones"], [f"ps{pi}"])
            A(lambda e, pi=pi: e.activation(out=T1[:], in_=psum[pi][0:64, :], func=AF.Sqrt, scale=1.0 / 64, bias=gnb[:, 0:1]), [f"ps{pi}", "gnb"], ["T1"])
            V(lambda e: e.reciprocal(out=T1[:], in_=T1[:]), ["T1"], ["T1"])
            V(lambda e: e.tensor_tensor(out=dd[:], in0=dd[:], in1=T1[:], op=ALU.mult), ["dd", "T1"], ["dd"])
            V(lambda e: e.tensor_scalar(out=dd[:], in0=dd[:], scalar1=pc(13), scalar2=pc(14), op0=ALU.mult, op1=ALU.add), ["dd", "par"], ["dd"])
            V(lambda e, bs=bs: e.tensor_tensor(out=dd[:], in0=dd[:], in1=BONV[:, bs], op=ALU.add), ["dd", "BONV"], ["dd"])
            pi = next_ps()
            for kc in range(2):
                T(lambda e, pi=pi, kc=kc, bs=bs: e.matmul(psum[pi][0:64, :], lhsT=g2b[:, kc, hh * 64:(hh + 1) * 64], rhs=sgT[:, kc, bs], start=(kc == 0), stop=(kc == 1)), ["g2b", "lora_act"], [f"ps{pi}"])
            V(lambda e, pi=pi: e.tensor_tensor(out=yo[:], in0=dd[:], in1=psum[pi][0:64, :], op=ALU.mult), ["dd", f"ps{pi}"], ["ryo"])
            P.op("sync", lambda e, bs=bs: e.dma_start(out=yT[hh // 2][(hh % 2) * 64:(hh % 2) * 64 + 64, bs], in_=yo[:]), reads=["ryo"], writes=["yT"], dma_key="ryo")

    for hh in range(rw_heads):
        head(hh)


def relayout_rwkv(inp, c, l=0):
    b, g = c // 4, c % 4
    chs = slice(512 * g, 512 * (g + 1))
    par = np.zeros((64, 8, 16), np.float32)
    sp, sn = inp["shift_prev"][l], inp["shift_next"][l]
    for hh in range(8):
        cg = slice(512 * g + hh * 64, 512 * g + (hh + 1) * 64)
        for i in range(3):
            par[:, hh, 2 * i] = sp[i * 2048:(i + 1) * 2048][cg]
            par[:, hh, 2 * i + 1] = sn[i * 2048:(i + 1) * 2048][cg]
        par[:, hh, 6] = inp["decay_bias_fwd"][l][cg]; par[:, hh, 7] = inp["decay_bias_bwd"][l][cg]
        par[:, hh, 8] = inp["iclr_bias_fwd"][l][cg]; par[:, hh, 9] = inp["iclr_bias_bwd"][l][cg]
        par[:, hh, 10] = inp["k_k"][l][cg]; par[:, hh, 11] = inp["k_a"][l][cg]
        par[:, hh, 12] = inp["r_k"][l].reshape(-1)[cg]
        par[:, hh, 13] = inp["ln_x_gain"][l][cg]; par[:, hh, 14] = inp["ln_x_bias"][l][cg]
    parl = np.zeros((128, 4, 2), np.float32)
    lo = 3 * 2048
    for i, (a0, n) in enumerate(((lo, 96), (lo + 96, 96), (lo + 192, 128), (lo + 320, 128))):
        parl[:n, i, 0] = sp[a0:a0 + n]; parl[:n, i, 1] = sn[a0:a0 + n]
    lw2 = np.stack([inp[k][l][:, chs] for k in ("decay_up_fwd", "decay_up_bwd", "iclr_up_fwd", "iclr_up_bwd")], axis=1)
    g2 = np.ascontiguousarray(inp["gate_up"][l][:, chs].reshape(2, 128, 512).transpose(1, 0, 2))
    ii = np.arange(128)
    row, col = ii[:, None], ii[None, :]
    masks = np.zeros((128, 2, 640), np.float32)
    for d in range(2):
        lt = (row < col) if d == 0 else (row > col)
        le = (row <= col) if d == 0 else (row >= col)
        masks[:, d, 0:128] = -lt.astype(np.float32)
        masks[:, d, 128:256] = lt
        masks[:, d, 256:384] = -lt.T.astype(np.float32)
        masks[:, d, 384:512] = le
        masks[:, d, 512:640] = le
    mreset = np.ones((64, BLK), np.float32)
    mreset[:, ::C] = 0.0
    return dict(par=par, parl=parl, lw2=np.ascontiguousarray(lw2).astype(np.float32), g2=g2.astype(np.float32), masks=masks, mreset=mreset)


F32 = mybir.dt.float32
BF16 = mybir.dt.bfloat16
AF = mybir.ActivationFunctionType
ALU = mybir.AluOpType
D = 4096
S = 4096
EPS = 1e-6
NCH = 28
NR = 6592
LAMBDA_INIT = 0.8 - 0.6
WARMN = 16
NFILL = 1


def body_A(nc, P, st, sh, yT, do_proj=True, do_attn=True, do_rwkv=True, n_tb=8, attn_heads=4, attn_qb=8, rw_heads=8):
    xb = nc.dram_tensor("xb", [S, D], F32, kind="ExternalInput").ap()
    wA = nc.dram_tensor("wA", [NCH, 128, 32, 128], F32, kind="ExternalInput").ap()
    abias = nc.dram_tensor("abias", [4, 5, 128, 512], F32, kind="ExternalInput").ap()
    acst = nc.dram_tensor("acst", [128, 4 * 64], F32, kind="ExternalInput").ap()
    lam_d = nc.dram_tensor("lam", [128, 4, 64], F32, kind="ExternalInput").ap()
    subg = nc.dram_tensor("subg", [128, 1], F32, kind="ExternalInput").ap()
    PT_d = nc.dram_tensor("PT_d", [NCH, 128, S], F32).ap()
    gain = sh["gains"][0]
    ident, ones, small, epsb = sh["ident"], sh["ones"], sh["small"], sh["epsb"]
    psum, pst, state, next_ps = sh["psum"], sh["pst"], sh["state"], sh["next_ps"]
    if True:
        if do_proj:
            with ExitStack() as st2:
                bufAs = [st2.enter_context(nc.sbuf_tensor(f"bufA{i}", [128, 32, 512], BF16)) for i in range(2)]
                xt = st2.enter_context(nc.sbuf_tensor("xt", [128, D], F32))
                gt = st2.enter_context(nc.sbuf_tensor("gt", [128, D], F32))
                hb = st2.enter_context(nc.sbuf_tensor("hb", [128, D], BF16))
                wbuf = [st2.enter_context(nc.sbuf_tensor(f"wb{i}", [128, 32, 128], BF16)) for i in range(3)]
                ot = [st2.enter_context(nc.sbuf_tensor(f"ot{i}", [128, 512], F32)) for i in range(3)]
                P.op("sync", lambda e: e.dma_start(out=gt[:], in_=gain), writes=["gt"], dma_key="gt")
                for tb in range(n_tb):
                    bufA = bufAs[tb % 2]
                    bk = f"bufA{tb % 2}"
                    for tt in range(4):
                        r0 = tb * 512 + tt * 128
                        P.op("sync", lambda e, r0=r0: e.dma_start(out=xt[:], in_=xb[r0:r0 + 128, :]), writes=["xt"], dma_key="xt")
                        P.op("vector", lambda e: e.memset(small[:, 0:1], 0.0), writes=["sm0"])
                        P.op("scalar", lambda e: e.activation(out=hb[:], in_=xt[:], func=AF.Square, accum_out=small[:, 0:1]), reads=["xt", "sm0"], writes=["hb", "sm0"])
                        P.op("scalar", lambda e: e.activation(out=small[:, 0:1], in_=small[:, 0:1], func=AF.Sqrt, scale=1.0 / D, bias=epsb[:, 0:1]), reads=["sm0", "epsb"], writes=["sm0"])
                        P.op("vector", lambda e: e.reciprocal(out=small[:, 0:1], in_=small[:, 0:1]), reads=["sm0"], writes=["sm0"])
                        P.op("vector", lambda e: e.scalar_tensor_tensor(out=hb[:], in0=xt[:], scalar=small[:, 0:1], in1=gt[:], op0=ALU.mult, op1=ALU.mult),
                             reads=["xt", "sm0", "gt"], writes=["hb"])
                        for k8 in range(4):
                            pi = state["pt"]; state["pt"] ^= 1
                            for j in range(8):
                                kc = k8 * 8 + j
                                P.op("tensor", lambda e, kc=kc, j=j, pi=pi: e.transpose(out=pst[pi][:, j * 128:(j + 1) * 128], in_=hb[:, kc * 128:(kc + 1) * 128], identity=ident[:]),
                                     reads=["hb", "ident"], writes=[f"ps{6 + pi}"])
                            dst = bufA[:, k8 * 8:(k8 + 1) * 8, tt * 128:(tt + 1) * 128]
                            srcp = pst[pi][:, :].rearrange("p (k t) -> p k t", k=8)
                            if k8 % 2 == 0:
                                P.op("scalar", lambda e, dst=dst, srcp=srcp: e.activation(out=dst, in_=srcp, func=AF.Copy), reads=[f"ps{6 + pi}"], writes=[bk])
                            else:
                                P.op("vector", lambda e, dst=dst, srcp=srcp: e.tensor_copy(out=dst, in_=srcp), reads=[f"ps{6 + pi}"], writes=[bk])
                    for ch in range(NCH):
                        s = ch % 3
                        for half in range(2):
                            P.op("gpsimd", lambda e, s=s, ch=ch, half=half: e.dma_start(out=wbuf[s][:, half * 16:(half + 1) * 16, :], in_=wA[ch][:, half * 16:(half + 1) * 16, :], max_dma_last_dim=4096),
                                 writes=[f"wb{s}"], dma_key=f"wb{s}")
                        pi = next_ps()
                        for kc in range(32):
                            P.op("tensor", lambda e, kc=kc, pi=pi, s=s, bufA=bufA: e.matmul(psum[pi][:], lhsT=wbuf[s][:, kc, :], rhs=bufA[:, kc, :], start=(kc == 0), stop=(kc == 31)),
                                 reads=[f"wb{s}", bk], writes=[f"ps{pi}"])
                        if ch % 2 == 0:
                            P.op("scalar", lambda e, pi=pi, s=s: e.activation(out=ot[s][:], in_=psum[pi][:], func=AF.Copy), reads=[f"ps{pi}"], writes=[f"ot{s}"])
                        else:
                            P.op("vector", lambda e, pi=pi, s=s: e.tensor_copy(out=ot[s][:], in_=psum[pi][:]), reads=[f"ps{pi}"], writes=[f"ot{s}"])
                        P.op("sync", lambda e, s=s, ch=ch, tb=tb: e.dma_start(out=PT_d[ch][:, tb * 512:(tb + 1) * 512], in_=ot[s][:]), reads=[f"ot{s}"], writes=[f"PT{ch}"], dma_key=f"ot{s}")
                P.fence()
        if do_attn:
            with ExitStack() as st2:
                def sb2(name, shape, dt):
                    return st2.enter_context(nc.sbuf_tensor(name, shape, dt))
                LA = 2
                NSL = LA + 1
                qk32 = sb2("qk32", [128, S], F32)
                QTs = [sb2(f"QT{i}", [64, 2, S], BF16) for i in range(2)]
                KTs = [sb2(f"KT{i}", [64, 2, S], BF16) for i in range(2)]
                Vts = [sb2(f"Vt{i}", [128, 32, 128], BF16) for i in range(2)]
                vb = sb2("vb", [128, S], BF16)
                bts = [sb2(f"bt{i}", [128, 5, 512], F32) for i in range(2)]
                cst = sb2("cst", [128, 256], F32)
                lamt = sb2("lamt", [128, 4, 64], F32)
                lsm = sb2("lsm", [128, 8], F32)
                sgt = sb2("sgt", [128, 1], F32)
                tmp = [sb2(f"atmp{i}", [128, 512], F32) for i in range(NSL)]
                Eb = [sb2(f"Eb{i}", [128, 512], BF16) for i in range(NSL)]
                o0 = sb2("o0", [128, 512], F32)
                o1 = sb2("o1", [128, 512], F32)
                rr = sb2("rr", [128, 512], F32)
                rr2 = sb2("rr2", [128, 512], F32)
                sq = sb2("sq", [128, 512], BF16)
                yo = sb2("yo", [128, 512], BF16)
                wz = sb2("wz", [128, 512], BF16)
                P.op("vector", lambda e: e.memset(wz[:], 0.0), writes=["wz"])

                def warm(n):
                    for _ in range(n):
                        P.op("tensor", lambda e: e.matmul(psum[7][:], lhsT=ones[:], rhs=wz[:], start=True, stop=True), reads=["wz", "ones"], writes=["ps7"])
                P.op("sync", lambda e: e.dma_start(out=cst[:], in_=acst), writes=["cst"], dma_key="cst")
                P.op("sync", lambda e: e.dma_start(out=lamt[:], in_=lam_d), writes=["lamt"], dma_key="lamt")
                P.op("sync", lambda e: e.dma_start(out=sgt[:], in_=subg), writes=["sgt"], dma_key="sgt")
                for i in range(2):
                    P.op("vector", lambda e, i=i: e.tensor_tensor(out=lamt[:, 2 * i, :], in0=lamt[:, 2 * i, :], in1=lamt[:, 2 * i + 1, :], op=ALU.mult), reads=["lamt"], writes=["lamt"])
                    P.op("vector", lambda e, i=i: e.tensor_reduce(out=lsm[:, i:i + 1], in_=lamt[:, 2 * i, :], axis=mybir.AxisListType.X, op=ALU.add), reads=["lamt"], writes=["lsm"])
                    P.op("scalar", lambda e, i=i: e.activation(out=lsm[:, i:i + 1], in_=lsm[:, i:i + 1], func=AF.Exp), reads=["lsm"], writes=["lsm"])
                P.op("vector", lambda e: e.tensor_tensor(out=lsm[:, 2:3], in0=lsm[:, 1:2], in1=lsm[:, 0:1], op=ALU.subtract), reads=["lsm"], writes=["lsm"])
                P.op("vector", lambda e: e.tensor_scalar(out=lsm[:, 2:3], in0=lsm[:, 2:3], scalar1=-LAMBDA_INIT, scalar2=None, op0=ALU.add), reads=["lsm"], writes=["lsm"])
                P.op("vector", lambda e: e.tensor_scalar(out=sgt[:], in0=sgt[:], scalar1=1.0 - LAMBDA_INIT, scalar2=None, op0=ALU.mult), reads=["sgt"], writes=["sgt"])

                def load_head(hd):
                    hs = hd % 2
                    QT, KT, Vt, bt = QTs[hs], KTs[hs], Vts[hs], bts[hs]
                    P.op("sync", lambda e: e.dma_start(out=bt[:], in_=abias[hd].rearrange("f p q -> p f q")), writes=[f"bt{hs}"], dma_key=f"bt{hs}")
                    for (dstT, ch, nm) in ((QT, 16 + hd, f"QT{hs}"), (KT, 20 + hd, f"KT{hs}")):
                        for c in range(2):
                            P.op("sync", lambda e, ch=ch, c=c: e.dma_start(out=qk32[0:64, :], in_=PT_d[ch][c * 64:(c + 1) * 64, :]), reads=[f"PT{ch}"], writes=["qk32"], dma_key="qk32")
                            P.op("scalar", lambda e, dstT=dstT, c=c: e.activation(out=dstT[:, c, :], in_=qk32[0:64, :], func=AF.Copy), reads=["qk32"], writes=[nm])
                    P.op("sync", lambda e: e.dma_start(out=qk32[:], in_=PT_d[24 + hd]), reads=[f"PT{24 + hd}"], writes=["qk32"], dma_key="qk32")
                    P.op("vector", lambda e: e.tensor_copy(out=vb[:], in_=qk32[:]), reads=["qk32"], writes=["vb"])
                    for k8 in range(4):
                        pi = state["pt"]; state["pt"] ^= 1
                        for j in range(8):
                            blk = k8 * 8 + j
                            P.op("tensor", lambda e, blk=blk, j=j, pi=pi: e.transpose(out=pst[pi][:, j * 128:(j + 1) * 128], in_=vb[:, blk * 128:(blk + 1) * 128], identity=ident[:]),
                                 reads=["vb", "ident"], writes=[f"ps{6 + pi}"])
                        P.op("vector", lambda e, k8=k8, pi=pi: e.tensor_copy(out=Vt[:, k8 * 8:(k8 + 1) * 8, :], in_=pst[pi][:, :].rearrange("p (k t) -> p k t", k=8)), reads=[f"ps{6 + pi}"], writes=[f"Vt{hs}"])

                pacc = [0, 1, 2, 3]
                pending = [None]

                def unit_front(hd, qb, i):
                    hs = hd % 2
                    kb, c = i // 2, i % 2
                    delta = kb - 4 * qb
                    pi = 4 + (i % NSL)
                    ti = i % NSL
                    P.op("tensor", lambda e: e.matmul(psum[pi][:], lhsT=KTs[hs][:, c, kb * 128:(kb + 1) * 128], rhs=QTs[hs][:, c, qb * 512:(qb + 1) * 512], start=True, stop=True),
                         reads=[f"QT{hs}", f"KT{hs}"], writes=[f"ps{pi}"])
                    if delta >= 4:
                        bti, op1 = 0, ALU.add
                    elif delta < 0:
                        bti, op1 = 0, ALU.subtract
                    else:
                        bti, op1 = 1 + delta, ALU.add
                    P.op("vector", lambda e: e.scalar_tensor_tensor(out=tmp[ti][:], in0=psum[pi][:], scalar=0.125, in1=bts[hs][:, bti, :], op0=ALU.mult, op1=op1),
                         reads=[f"ps{pi}", f"bt{hs}"], writes=[f"atmp{ti}"])
                    ci = hd * 64 + (delta + 32)
                    P.op("scalar", lambda e: e.activation(out=Eb[ti][:], in_=tmp[ti][:], func=AF.Exp, bias=cst[:, ci:ci + 1]), reads=[f"atmp{ti}", "cst"], writes=[f"Eb{ti}"])

                def unit_back(hd, qb, i):
                    hs = hd % 2
                    kb, c = i // 2, i % 2
                    ti = i % NSL
                    P.op("tensor", lambda e: e.matmul(psum[pacc[2 * c]][:], lhsT=Vts[hs][:, kb, :], rhs=Eb[ti][:], start=(kb == 0), stop=(kb == 31)),
                         reads=[f"Eb{ti}", f"Vt{hs}"], writes=[f"ps{pacc[2 * c]}"])
                    P.op("tensor", lambda e: e.matmul(psum[pacc[2 * c + 1]][:], lhsT=ones[:], rhs=Eb[ti][:], start=(kb == 0), stop=(kb == 31)),
                         reads=[f"Eb{ti}", "ones"], writes=[f"ps{pacc[2 * c + 1]}"])

                def fin1():
                    P.op("vector", lambda e: e.reciprocal(out=rr[:], in_=psum[pacc[1]][:]), reads=[f"ps{pacc[1]}"], writes=["rr"])
                    P.op("vector", lambda e: e.tensor_tensor(out=o0[:], in0=psum[pacc[0]][:], in1=rr[:], op=ALU.mult), reads=[f"ps{pacc[0]}", "rr"], writes=["o0"])
                    P.op("vector", lambda e: e.reciprocal(out=rr[:], in_=psum[pacc[3]][:]), reads=[f"ps{pacc[3]}"], writes=["rr"])
                    P.op("vector", lambda e: e.tensor_tensor(out=o1[:], in0=psum[pacc[2]][:], in1=rr[:], op=ALU.mult), reads=[f"ps{pacc[2]}", "rr"], writes=["o1"])
                    P.op("vector", lambda e: e.scalar_tensor_tensor(out=o0[:], in0=o1[:], scalar=lsm[:, 2:3], in1=o0[:], op0=ALU.mult, op1=ALU.add), reads=["o0", "o1", "lsm"], writes=["o0"])
                    P.op("scalar", lambda e: e.activation(out=sq[:], in_=o0[:], func=AF.Square), reads=["o0"], writes=["sq"])

                def fin2(hd, qb, pi):
                    P.op("tensor", lambda e: e.matmul(psum[pi][:], lhsT=ones[:], rhs=sq[:], start=True, stop=True), reads=["sq", "ones"], writes=[f"ps{pi}"])
                    P.op("scalar", lambda e: e.activation(out=rr2[:], in_=psum[pi][:], func=AF.Sqrt, scale=1.0 / 128, bias=epsb[:, 1:2]), reads=[f"ps{pi}", "epsb"], writes=["rr2"])
                    P.op("vector", lambda e: e.reciprocal(out=rr2[:], in_=rr2[:]), reads=["rr2"], writes=["rr2"])
                    P.op("vector", lambda e: e.scalar_tensor_tensor(out=yo[:], in0=o0[:], scalar=sgt[:, 0:1], in1=rr2[:], op0=ALU.mult, op1=ALU.mult), reads=["o0", "sgt", "rr2"], writes=["yo"])
                    P.op("sync", lambda e: e.dma_start(out=yT[4 + hd][:, qb * 512:(qb + 1) * 512], in_=yo[:]), reads=["yo"], writes=["yT"], dma_key="yo")

                load_head(0)
                NU = 64
                for hd in range(attn_heads):
                    for qb in range(attn_qb):
                        warm(WARMN)
                        for i in range(NU + LA):
                            if i < NU:
                                unit_front(hd, qb, i)
                            if i >= LA:
                                unit_back(hd, qb, i - LA)
                                warm(NFILL)
                            if i == LA + 1 and pending[0] is not None:
                                ph, pq = pending[0]
                                pending[0] = None
                                fin2(ph, pq, 7)
                        fin1()
                        pending[0] = (hd, qb)
                        if qb == 1 and hd + 1 < attn_heads:
                            load_head(hd + 1)
                        if hd == attn_heads - 1 and qb == attn_qb - 1:
                            fin2(hd, qb, 7)
                            pending[0] = None
                P.fence()
        if do_rwkv:
            with ExitStack() as st3:
                emit_rwkv(nc, P, st3, PT_d, yT, psum, pst, state, next_ps, ident, ones, epsb, rw_heads)
            P.fence()


def build_A(**kw):
    nc = bass.Bass("TRN2", target_bir_lowering=False)
    yT = nc.dram_tensor("yT", [8, 128, S], BF16, kind="ExternalOutput").ap()
    P = Prog(nc)
    with ExitStack() as st:
        sh = make_shared(nc, P, st)
        body_A(nc, P, st, sh, yT, **kw)
        counts = P.emit(st)
        print("A ops", counts, "waits", P.n_waits)
    return nc


def build_fused():
    nc = bass.Bass("TRN2", target_bir_lowering=False)
    yTi = nc.dram_tensor("yTi", [8, 128, S], BF16).ap()
    G = nc.dram_tensor("Gy", [8, 4, 128, S], BF16).ap()
    sel = nc.dram_tensor("sel", [128, 4], F32, kind="ExternalInput").ap()
    P = Prog(nc)
    with ExitStack() as st:
        sh = make_shared(nc, P, st)
        body_A(nc, P, st, sh, yTi)
        for k in range(8):
            P.op("gpsimd", lambda e, k=k: e.collective_compute("AllGather", ALU.bypass, replica_groups=[[0, 1, 2, 3], [4, 5, 6, 7]],
                                                               ins=[yTi[k].opt()], outs=[G[k].rearrange("g p t -> (g p) t").opt()]),
                 reads=["yT"], writes=["G"], dma_key="cc", inc=1)
        with ExitStack() as st4:
            body_B(nc, P, st4, sh, ("gather", G, sel))
        counts = P.emit(st)
        print("fused ops", counts, "waits", P.n_waits)
    return nc


def slopes():
    H = 16
    return np.exp2(-8.0 * np.arange(1, H + 1, dtype=np.float32) / H).astype(np.float32)


def relayout_A(inp, c, l=0):
    b, g = c // 4, c % 4
    w_in = inp["w_in"][l]
    cols = []
    for part in range(3):
        cols.append(np.arange(part * 2048 + 512 * g, part * 2048 + 512 * (g + 1)))
    lo = 3 * 2048
    cols.append(np.arange(lo, lo + 96)); pad1 = 32
    cols.append(np.arange(lo + 96, lo + 192)); pad2 = 32
    cols.append(np.arange(lo + 192, lo + 448))
    for part in range(3):
        cols.append(np.arange(NR + part * 2048 + 512 * g, NR + part * 2048 + 512 * (g + 1)))
    W = np.zeros((D, NCH * 128), np.float32)
    W[:, 0:1536] = w_in[:, np.concatenate(cols[0:3])]
    W[:, 1536:1536 + 96] = w_in[:, cols[3]]
    W[:, 1664:1664 + 96] = w_in[:, cols[4]]
    W[:, 1792:2048] = w_in[:, cols[5]]
    W[:, 2048:3584] = w_in[:, np.concatenate(cols[6:9])]
    wA = np.ascontiguousarray(W.reshape(32, 128, NCH, 128).transpose(2, 1, 0, 3))
    gain = np.ascontiguousarray(np.broadcast_to(inp["attn_pre_norm"][l][None, :], (128, D))).astype(np.float32)
    ident = np.eye(128, dtype=np.float32).astype(ml_dtypes.bfloat16)
    ones = np.ones((128, 128), np.float32).astype(ml_dtypes.bfloat16)
    sl = slopes()
    kk = np.arange(128, dtype=np.float32)[:, None]
    qq = np.arange(512, dtype=np.float32)[None, :]
    abias = np.zeros((4, 5, 128, 512), np.float32)
    acst = np.zeros((128, 256), np.float32)
    for hd in range(4):
        s_ = sl[4 * g + hd]
        abias[hd, 0] = -s_ * (kk - qq)
        for dl in range(4):
            abias[hd, 1 + dl] = -s_ * np.abs(128.0 * dl + kk - qq)
        for delta in range(-32, 32):
            if delta >= 4:
                v = -s_ * 128.0 * delta
            elif delta < 0:
                v = s_ * 128.0 * delta
            else:
                v = 0.0
            acst[:, hd * 64 + delta + 32] = v
    lam = np.stack([inp[k][l] for k in ("lambda_q1", "lambda_k1", "lambda_q2", "lambda_k2")])
    lam = np.ascontiguousarray(np.broadcast_to(lam[None], (128, 4, 64))).astype(np.float32)
    subg = np.ascontiguousarray(inp["subln_gain"][l].reshape(128, 1)).astype(np.float32)
    return dict(xb=np.ascontiguousarray(inp["x"][b]), wA=wA, abias=abias, acst=acst, lam=lam, subg=subg)


def kernel(**inp):
    inp = {k: np.asarray(v) for k, v in inp.items()}
    n = 8
    nc = build_fused()
    W = relayout_B(inp)
    in_maps = []
    for c in range(n):
        b, g = c // 4, c % 4
        im = dict(W)
        im.update(relayout_A(inp, c))
        im.update(relayout_rwkv(inp, c))
        im["xo"] = np.ascontiguousarray(inp["x"][b, 1024 * g:1024 * (g + 1)])
        sel = np.zeros((128, 4), np.float32)
        sel[:, g] = 1.0
        im["sel"] = sel
        in_maps.append(im)
    res = run_bass_kernel_spmd(nc, in_maps, core_ids=list(range(n)))
    out = np.zeros((2, S, D), np.float32)
    for c in range(n):
        b, g = c // 4, c % 4
        out[b, 1024 * g:1024 * (g + 1)] = res.results[c]["out"]
    return out
```

```python
import math
import bisect
import numpy as np
import ml_dtypes
from contextlib import ExitStack
import concourse.bass as bass
import concourse.mybir as mybir
from concourse.bass_utils import run_bass_kernel_spmd


ENGS = ("tensor", "vector", "scalar", "gpsimd", "sync")
ROT = 12000


class Prog:
    def __init__(self, nc):
        self.nc = nc
        self.ops = []

    def op(self, eng, fn, reads=(), writes=(), dma_key=None, inc=None):
        self.ops.append(dict(eng=eng, fn=fn, reads=tuple(reads), writes=tuple(writes),
                             dma=dma_key is not None, key=dma_key, inc=inc))
        return len(self.ops) - 1

    def fence(self, eng="vector"):
        self.ops.append(dict(eng=eng, fn=self.fence_fn, reads=(), writes="ALL", dma=False, key=None, inc=None))

    def emit(self, stack):
        nc = self.nc
        ops = self.ops
        n = len(ops)
        allkeys = set()
        for o in ops:
            if o["writes"] != "ALL":
                allkeys.update(o["reads"]); allkeys.update(o["writes"])
        allkeys = tuple(sorted(allkeys, key=str))
        for o in ops:
            if o["writes"] == "ALL":
                o["writes"] = allkeys
        last_w = {}
        readers = {}
        deps = [None] * n
        for i, o in enumerate(ops):
            d = set()
            for r in o["reads"]:
                if r in last_w:
                    d.add(last_w[r])
            for w in o["writes"]:
                if w in last_w:
                    d.add(last_w[w])
                for j in readers.get(w, ()):
                    d.add(j)
            d.discard(i)
            dd = []
            for j in d:
                oj = ops[j]
                if (not oj["dma"]) and (not o["dma"]) and oj["eng"] == o["eng"] == "tensor":
                    continue
                dd.append(j)
            deps[i] = dd
            for r in o["reads"]:
                readers.setdefault(r, []).append(i)
            for w in o["writes"]:
                last_w[w] = i
                readers[w] = []
        needed = [False] * n
        for i in range(n):
            for j in deps[i]:
                needed[j] = True
        sem_handles = {}

        def get_sem(name):
            if name not in sem_handles:
                sem_handles[name] = stack.enter_context(nc.semaphore(name))
            return sem_handles[name]

        cnt = {}
        sig = [None] * n
        dma_cum_at = {}
        for i, o in enumerate(ops):
            if o["dma"]:
                base = "d_" + str(o["key"])
                inc = o["inc"] or 16
                lim = ROT
            else:
                if not needed[i]:
                    continue
                base = "e_" + o["eng"]
                inc = 1
                lim = ROT
            g, c = cnt.get(base, (0, 0))
            if c + inc > lim * (16 if o["dma"] else 1):
                g, c = g + 1, 0
            c += inc
            cnt[base] = (g, c)
            sig[i] = (base + "_" + str(g), c)
            if o["dma"]:
                dma_cum_at.setdefault(base, []).append((i, sig[i][0], c))
        import bisect
        dma_idx = {k: [t[0] for t in v] for k, v in dma_cum_at.items()}
        waits = [None] * n
        waited = {e: {} for e in ENGS}
        for i, o in enumerate(ops):
            need = {}
            for j in deps[i]:
                oj = ops[j]
                if oj["dma"]:
                    base = "d_" + str(oj["key"])
                    lst = dma_cum_at[base]
                    pos = bisect.bisect_left(dma_idx[base], i) - 1
                    sname_j, vj = sig[j]
                    k = pos
                    while lst[k][1] != sname_j:
                        k -= 1
                    sname, val = lst[k][1], lst[k][2]
                else:
                    sname, val = sig[j]
                if need.get(sname, 0) < val:
                    need[sname] = val
            wl = []
            wd = waited[o["eng"]]
            for sname, val in need.items():
                if wd.get(sname, 0) >= val:
                    continue
                wd[sname] = val
                wl.append((sname, val))
            waits[i] = wl
        self.n_waits = sum(len(w) for w in waits)
        per_eng = {e: [i for i, o in enumerate(ops) if o["eng"] == e] for e in ENGS}
        block = stack.enter_context(nc.Block())

        def body(engname):
            def f(eng):
                for i in per_eng[engname]:
                    for sname, val in waits[i]:
                        eng.wait_ge(get_sem(sname), val)
                    inst = ops[i]["fn"](eng)
                    if sig[i] is not None:
                        inst.then_inc(get_sem(sig[i][0]), (ops[i]["inc"] or 16) if ops[i]["dma"] else 1)
            return f

        for i in range(n):
            if sig[i] is not None:
                get_sem(sig[i][0])
        block.tensor(body("tensor"))
        block.vector(body("vector"))
        block.scalar(body("scalar"))
        block.gpsimd(body("gpsimd"))
        block.sync(body("sync"))
        return {e: len(v) for e, v in per_eng.items()}


F32 = mybir.dt.float32
BF16 = mybir.dt.bfloat16
AF = mybir.ActivationFunctionType
ALU = mybir.AluOpType
D = 4096
DFF = 16384
EPS = 1e-6
NTOK = 1024
TP = 512
FB = 256
NFB = DFF // FB


def make_shared(nc, P, st):
    ident_d = nc.dram_tensor("ident", [128, 128], BF16, kind="ExternalInput").ap()
    ones_d = nc.dram_tensor("ones", [128, 128], BF16, kind="ExternalInput").ap()
    gains = nc.dram_tensor("gains", [4, 128, D], F32, kind="ExternalInput").ap()
    ident = st.enter_context(nc.sbuf_tensor("ident_s", [128, 128], BF16))
    ones = st.enter_context(nc.sbuf_tensor("ones_s", [128, 128], BF16))
    small = st.enter_context(nc.sbuf_tensor("small", [128, 16], F32))
    dummy = st.enter_context(nc.sbuf_tensor("fdummy", [128, 8], F32))
    epsb = st.enter_context(nc.sbuf_tensor("epsb", [128, 2], F32))
    P.fence_fn = lambda e: e.memset(dummy[:], 0.0)
    P.op("vector", lambda e: e.memset(epsb[:, 0:1], EPS), writes=["epsb"])
    P.op("vector", lambda e: e.memset(epsb[:, 1:2], 1e-5), writes=["epsb"])
    P.op("sync", lambda e: e.dma_start(out=ident[:], in_=ident_d), writes=["ident"], dma_key="ident")
    P.op("sync", lambda e: e.dma_start(out=ones[:], in_=ones_d), writes=["ones"], dma_key="ones")
    psum = [st.enter_context(nc.psum_tensor(f"ps{i}", [128, 512], F32)) for i in range(8)]
    pst = [psum[6 + i][:, :].bitcast(BF16) for i in range(2)]
    state = dict(ps=0, pt=0)

    def next_ps():
        i = state["ps"]; state["ps"] = (i + 1) % 6
        return i
    return dict(ident=ident, ones=ones, small=small, epsb=epsb, psum=psum, pst=pst, state=state, next_ps=next_ps, gains=gains)


def body_B(nc, P, st, sh, ysrc, npass=2, stages=(0, 1, 2, 3, 4, 5), nfb=NFB):
    x = nc.dram_tensor("xo", [NTOK, D], F32, kind="ExternalInput").ap()
    wg = nc.dram_tensor("wg", [64, 128, 32, 128], F32, kind="ExternalInput").ap()
    wu = nc.dram_tensor("wu", [64, 128, 16, 128], F32, kind="ExternalInput").ap()
    wo = nc.dram_tensor("wo", [16, 128, 32, 256], F32, kind="ExternalInput").ap()
    w1 = nc.dram_tensor("w1", [NFB, 128, 32, FB], F32, kind="ExternalInput").ap()
    w2 = nc.dram_tensor("w2", [NFB, 128, FB // 128, D], F32, kind="ExternalInput").ap()
    out = nc.dram_tensor("out", [NTOK, D], F32, kind="ExternalOutput").ap()
    x1_d = nc.dram_tensor("x1_d", [NTOK, D], F32).ap()
    gains, ident, small, epsb = sh["gains"], sh["ident"], sh["small"], sh["epsb"]
    psum, pst, state, next_ps = sh["psum"], sh["pst"], sh["state"], sh["next_ps"]
    if True:
        arena = st.enter_context(nc.sbuf_tensor("arena", [128, 172 * 256], F32))
        if ysrc[0] == "gather":
            selt = st.enter_context(nc.sbuf_tensor("selt", [128, 4], F32))
            P.op("sync", lambda e: e.dma_start(out=selt[:], in_=ysrc[2]), writes=["selt"], dma_key="selt")

        def AV(off_kib, size_kib, dt):
            v = arena[:, off_kib * 256:(off_kib + size_kib) * 256]
            return v.bitcast(BF16) if dt == BF16 else v

        bufA = AV(0, 32, BF16).rearrange("p (k t) -> p k t", k=32)
        bufB = AV(32, 32, BF16).rearrange("p (k t) -> p k t", k=32)
        bufM = AV(64, 32, BF16).rearrange("p (k t) -> p k t", k=32)
        bufZ = AV(96, 64, F32).rearrange("p (a d) -> p a d", a=4)
        wbuf = [AV(96 + 24 * s, 24, BF16) for s in range(2)]
        wo_v = [AV(32 + 16 * s, 16, BF16).rearrange("p (k j) -> p k j", k=32) for s in range(2)]
        w1_v = [AV(64 + 16 * s, 16, BF16).rearrange("p (k j) -> p k j", k=32) for s in range(2)]
        w2_v = [AV(32 + 16 * s, 16, BF16).rearrange("p (k j) -> p k j", k=FB // 128) for s in range(2)]
        xt_R3, gt_R3 = AV(64, 16, F32), AV(80, 16, F32)
        xt_R2, gt_R2 = AV(32, 16, F32), AV(48, 16, F32)
        hb_R4 = AV(144, 8, BF16)
        hb_R3 = AV(64, 8, BF16)
        t1 = [AV(160 + 2 * i, 2, F32) for i in range(2)]
        t2 = [AV(164 + 2 * i, 2, F32) for i in range(2)]
        ub = [AV(168 + 2 * i, 2, BF16).rearrange("p (k t) -> p k t", k=FB // 128) for i in range(2)]
        def load_gain(idx, gt):
            P.op("sync", lambda e: e.dma_start(out=gt, in_=gains[idx]), reads=["gains"], writes=["gt"], dma_key="gt")


        def rstd_from(src_ap, src_key, col, hb):
            P.op("vector", lambda e: e.memset(small[:, col:col + 1], 0.0), writes=[f"sm{col}"])
            P.op("scalar", lambda e: e.activation(out=hb, in_=src_ap, func=AF.Square, accum_out=small[:, col:col + 1]),
                 reads=[src_key, f"sm{col}"], writes=["hb", f"sm{col}"])
            P.op("scalar", lambda e: e.activation(out=small[:, col:col + 1], in_=small[:, col:col + 1], func=AF.Sqrt, scale=1.0 / D, bias=epsb[:, 0:1]),
                 reads=[f"sm{col}", "epsb"], writes=[f"sm{col}"])
            P.op("vector", lambda e: e.reciprocal(out=small[:, col:col + 1], in_=small[:, col:col + 1]), reads=[f"sm{col}"], writes=[f"sm{col}"])

        def norm_transpose(src_ap, src_key, col, gt, hb, tt):
            P.op("vector", lambda e: e.scalar_tensor_tensor(out=hb, in0=src_ap, scalar=small[:, col:col + 1], in1=gt,
                                                            op0=ALU.mult, op1=ALU.mult), reads=[src_key, f"sm{col}", "gt"], writes=["hb"])
            for k8 in range(4):
                pi = state["pt"]; state["pt"] ^= 1
                for j in range(8):
                    kc = k8 * 8 + j
                    P.op("tensor", lambda e, kc=kc, j=j, pi=pi: e.transpose(out=pst[pi][:, j * 128:(j + 1) * 128], in_=hb[:, kc * 128:(kc + 1) * 128], identity=ident[:]),
                         reads=["hb", "ident"], writes=[f"ps{6 + pi}"])
                dst = bufA[:, k8 * 8:(k8 + 1) * 8, tt * 128:(tt + 1) * 128]
                srcp = pst[pi][:, :].rearrange("p (k t) -> p k t", k=8)
                if k8 % 2 == 0:
                    P.op("scalar", lambda e, dst=dst, srcp=srcp: e.activation(out=dst, in_=srcp, func=AF.Copy), reads=[f"ps{6 + pi}"], writes=["bufA"])
                else:
                    P.op("vector", lambda e, dst=dst, srcp=srcp: e.tensor_copy(out=dst, in_=srcp), reads=[f"ps{6 + pi}"], writes=["bufA"])

        for ps_i in range(npass):
            tok0 = ps_i * TP
            if ps_i > 0 or ysrc[0] != "gather":
                P.fence()
            if 0 in stages:
                xt, gt, hb = xt_R3, gt_R3, hb_R4
                load_gain(0, gt)
                for tt in range(4):
                    r0 = tok0 + tt * 128
                    P.op("sync", lambda e, r0=r0, xt=xt: e.dma_start(out=xt, in_=x[r0:r0 + 128, :]), writes=["xt"], dma_key="xt")
                    rstd_from(xt, "xt", 0, hb)
                    norm_transpose(xt, "xt", 0, gt, hb, tt)
                if ysrc[0] == "input":
                    P.op("sync", lambda e, tok0=tok0: e.dma_start(out=bufB, in_=ysrc[1][:, :, tok0:tok0 + TP].rearrange("k p t -> p k t")), writes=["bufB"], dma_key="bufB")
                else:
                    G = ysrc[1]
                    cand = AV(96, 32, BF16).rearrange("p (k t) -> p k t", k=32)
                    for q in range(4):
                        t0 = 1024 * q + tok0
                        for part in range(2):
                            dstv = cand[:, 16 * part:16 * (part + 1), :].rearrange("p (g k) t -> p g k t", g=4) if part == 0 else \
                                cand[:, 16 * part:16 * (part + 1), :].rearrange("p (k g) t -> p g k t", g=4)
                            for gq in range(4):
                                P.op("sync", lambda e, dstv=dstv, part=part, t0=t0, gq=gq: e.dma_start(out=dstv[:, gq, :, :], in_=G[4 * part:4 * part + 4, gq, :, t0:t0 + TP].rearrange("k p t -> p k t")),
                                     reads=["G"], writes=["cand"], dma_key="cand")
                        if q == 0:
                            P.op("vector", lambda e: e.tensor_scalar(out=bufB, in0=cand, scalar1=selt[:, 0:1], scalar2=None, op0=ALU.mult), reads=["cand", "selt"], writes=["bufB"])
                        else:
                            P.op("vector", lambda e, q=q: e.scalar_tensor_tensor(out=bufB, in0=cand, scalar=selt[:, q:q + 1], in1=bufB, op0=ALU.mult, op1=ALU.add), reads=["cand", "selt", "bufB"], writes=["bufB"])
            P.fence()
            if 1 in stages:
                for cc in range(32):
                    s = cc % 2
                    wb = wbuf[s]
                    vgA = wb[:, 0:4096].rearrange("p (k j) -> p k j", k=32)
                    vgB = wb[:, 4096:8192].rearrange("p (k j) -> p k j", k=32)
                    vuA = wb[:, 8192:10240].rearrange("p (k j) -> p k j", k=16)
                    vuB = wb[:, 10240:12288].rearrange("p (k j) -> p k j", k=16)
                    for (dst, src) in ((vgA, wg[cc]), (vgB, wg[32 + cc]), (vuA, wu[cc]), (vuB, wu[32 + cc])):
                        P.op("gpsimd", lambda e, dst=dst, src=src: e.dma_start(out=dst, in_=src, max_dma_last_dim=4096), writes=[f"wbuf{s}"], dma_key=f"wbuf{s}")
                    pgA, pgB, puA, puB = next_ps(), next_ps(), next_ps(), next_ps()
                    for (pi, wv, nk, src, koff) in ((pgA, vgA, 32, bufA, 0), (puA, vuA, 16, bufB, 0), (pgB, vgB, 32, bufA, 0), (puB, vuB, 16, bufB, 16)):
                        for kc in range(nk):
                            P.op("tensor", lambda e, kc=kc, pi=pi, wv=wv, nk=nk, src=src, koff=koff: e.matmul(psum[pi][:], lhsT=wv[:, kc, :], rhs=src[:, koff + kc, :], start=(kc == 0), stop=(kc == nk - 1)),
                                 reads=[f"wbuf{s}", "bufA", "bufB"], writes=[f"ps{pi}"])
                    P.op("scalar", lambda e, pgA=pgA, s=s: e.activation(out=t1[s], in_=psum[pgA][:], func=AF.Sigmoid), reads=[f"ps{pgA}"], writes=[f"t1_{s}"])
                    P.op("vector", lambda e, puA=puA, s=s: e.tensor_tensor(out=t1[s], in0=t1[s], in1=psum[puA][:], op=ALU.mult), reads=[f"ps{puA}", f"t1_{s}"], writes=[f"t1_{s}"])
                    P.op("scalar", lambda e, pgB=pgB, s=s: e.activation(out=t2[s], in_=psum[pgB][:], func=AF.Sigmoid), reads=[f"ps{pgB}"], writes=[f"t2_{s}"])
                    P.op("vector", lambda e, puB=puB, s=s: e.tensor_tensor(out=t2[s], in0=t2[s], in1=psum[puB][:], op=ALU.mult), reads=[f"ps{puB}", f"t2_{s}"], writes=[f"t2_{s}"])
                    P.op("vector", lambda e, cc=cc, s=s: e.tensor_tensor(out=bufM[:, cc, :], in0=t1[s], in1=t2[s], op=ALU.add), reads=[f"t1_{s}", f"t2_{s}"], writes=["bufM"])
            P.fence()
            if 2 in stages:
                for nb in range(16):
                    s = nb % 2
                    wv = wo_v[s]
                    for half in range(2):
                        P.op("gpsimd", lambda e, wv=wv, nb=nb, half=half: e.dma_start(out=wv[:, half * 16:(half + 1) * 16, :], in_=wo[nb][:, half * 16:(half + 1) * 16, :], max_dma_last_dim=4096),
                             writes=[f"wo{s}"], dma_key=f"wo{s}")
                    for tt in range(4):
                        pi = next_ps()
                        for kc in range(32):
                            P.op("tensor", lambda e, kc=kc, pi=pi, wv=wv, tt=tt: e.matmul(psum[pi][:, 0:256], lhsT=bufM[:, kc, tt * 128:(tt + 1) * 128], rhs=wv[:, kc, :], start=(kc == 0), stop=(kc == 31)),
                                 reads=[f"wo{s}", "bufM"], writes=[f"ps{pi}"])
                        dst = bufZ[:, tt, nb * 256:(nb + 1) * 256]
                        if (tt + nb) % 2 == 0:
                            P.op("scalar", lambda e, dst=dst, pi=pi: e.activation(out=dst, in_=psum[pi][:, 0:256], func=AF.Copy), reads=[f"ps{pi}"], writes=[f"bufZ{tt}_{nb % 8}"])
                        else:
                            P.op("vector", lambda e, dst=dst, pi=pi: e.tensor_copy(out=dst, in_=psum[pi][:, 0:256]), reads=[f"ps{pi}"], writes=[f"bufZ{tt}_{nb % 8}"])
            P.fence()
            if 3 in stages:
                xt, gt, hb = xt_R2, gt_R2, hb_R3
                load_gain(1, gt)
                for tt in range(4):
                    r0 = tok0 + tt * 128
                    zt = bufZ[:, tt, :]
                    rstd_from(zt, f"bufZ{tt}", 1, hb)
                    P.op("sync", lambda e, r0=r0, xt=xt: e.dma_start(out=xt, in_=x[r0:r0 + 128, :]), writes=["xt"], dma_key="xt")
                    P.op("vector", lambda e, zt=zt, gt=gt: e.scalar_tensor_tensor(out=zt, in0=zt, scalar=small[:, 1:2], in1=gt, op0=ALU.mult, op1=ALU.mult),
                         reads=[f"bufZ{tt}", "sm1", "gt"], writes=[f"bufZ{tt}"])
                    P.op("vector", lambda e, zt=zt, xt=xt: e.tensor_tensor(out=zt, in0=zt, in1=xt, op=ALU.add), reads=[f"bufZ{tt}", "xt"], writes=[f"bufZ{tt}"])
                    P.op("sync", lambda e, r0=r0, zt=zt: e.dma_start(out=x1_d[r0:r0 + 128, :], in_=zt), reads=[f"bufZ{tt}"], writes=[f"x1d{ps_i}_{tt}"], dma_key=f"x1st{tt}")
                load_gain(2, gt)
                for tt in range(4):
                    zt = bufZ[:, tt, :]
                    rstd_from(zt, f"bufZ{tt}", 2, hb)
                    norm_transpose(zt, f"bufZ{tt}", 2, gt, hb, tt)
            P.fence()
            if 4 in stages:
                for fb in range(nfb):
                    s = fb % 2
                    w1v = w1_v[s]
                    w2v = w2_v[s]
                    for half in range(2):
                        P.op("gpsimd", lambda e, w1v=w1v, fb=fb, half=half: e.dma_start(out=w1v[:, half * 16:(half + 1) * 16, :], in_=w1[fb][:, half * 16:(half + 1) * 16, :], max_dma_last_dim=4096),
                             writes=[f"w1_{s}"], dma_key=f"w1_{s}")
                    for kc2 in range(FB // 128):
                        P.op("gpsimd", lambda e, w2v=w2v, fb=fb, kc2=kc2: e.dma_start(out=w2v[:, kc2, :], in_=w2[fb][:, kc2, :], max_dma_last_dim=4096),
                             writes=[f"w2_{s}"], dma_key=f"w2_{s}")
                    for fc in range(FB // 128):
                        pi = next_ps()
                        for kc in range(32):
                            P.op("tensor", lambda e, kc=kc, pi=pi, w1v=w1v, fc=fc: e.matmul(psum[pi][:], lhsT=w1v[:, kc, fc * 128:(fc + 1) * 128], rhs=bufA[:, kc, :], start=(kc == 0), stop=(kc == 31)),
                                 reads=[f"w1_{s}", "bufA"], writes=[f"ps{pi}"])
                        P.op("scalar", lambda e, pi=pi, s=s: e.activation(out=t1[s], in_=psum[pi][:], func=AF.Relu), reads=[f"ps{pi}"], writes=[f"t1_{s}"])
                        P.op("vector", lambda e, s=s, fc=fc: e.tensor_tensor(out=ub[s][:, fc, :], in0=t1[s], in1=t1[s], op=ALU.mult), reads=[f"t1_{s}"], writes=[f"ub{s}"])
                    for tt in range(4):
                        for nb in range(8):
                            pi = next_ps()
                            for kc2 in range(FB // 128):
                                P.op("tensor", lambda e, kc2=kc2, pi=pi, w2v=w2v, tt=tt, nb=nb, s=s: e.matmul(psum[pi][:], lhsT=ub[s][:, kc2, tt * 128:(tt + 1) * 128], rhs=w2v[:, kc2, nb * 512:(nb + 1) * 512],
                                                                                                          start=(kc2 == 0), stop=(kc2 == FB // 128 - 1)),
                                     reads=[f"w2_{s}", f"ub{s}"], writes=[f"ps{pi}"])
                            dst = bufZ[:, tt, nb * 512:(nb + 1) * 512]
                            key = f"bufZ{tt}_{nb}"
                            if fb == 0:
                                P.op("vector", lambda e, dst=dst, pi=pi: e.tensor_copy(out=dst, in_=psum[pi][:]), reads=[f"ps{pi}"], writes=[key])
                            else:
                                P.op("vector", lambda e, dst=dst, pi=pi: e.tensor_tensor(out=dst, in0=dst, in1=psum[pi][:], op=ALU.add), reads=[f"ps{pi}", key], writes=[key])
            P.fence()
            if 5 in stages:
                xt, gt, hb = xt_R3, gt_R3, AV(32, 8, BF16)
                load_gain(3, gt)
                for tt in range(4):
                    r0 = tok0 + tt * 128
                    zt = bufZ[:, tt, :]
                    rstd_from(zt, f"bufZ{tt}", 3, hb)
                    P.op("sync", lambda e, r0=r0, xt=xt: e.dma_start(out=xt, in_=x1_d[r0:r0 + 128, :]), reads=[f"x1d{ps_i}_{tt}"], writes=["xt"], dma_key="xt")
                    P.op("vector", lambda e, zt=zt, gt=gt: e.scalar_tensor_tensor(out=zt, in0=zt, scalar=small[:, 3:4], in1=gt, op0=ALU.mult, op1=ALU.mult),
                         reads=[f"bufZ{tt}", "sm3", "gt"], writes=[f"bufZ{tt}"])
                    P.op("vector", lambda e, zt=zt, xt=xt: e.tensor_tensor(out=zt, in0=zt, in1=xt, op=ALU.add), reads=[f"bufZ{tt}", "xt"], writes=[f"bufZ{tt}"])
                    P.op("sync", lambda e, r0=r0, zt=zt: e.dma_start(out=out[r0:r0 + 128, :], in_=zt), reads=[f"bufZ{tt}"], writes=["out"], dma_key=f"x1st{tt}")
        P.fence()


def build_B(npass=2, stages=(0, 1, 2, 3, 4, 5), nfb=NFB):
    nc = bass.Bass("TRN2", target_bir_lowering=False)
    yT = nc.dram_tensor("yT", [32, 128, NTOK], BF16, kind="ExternalInput").ap()
    P = Prog(nc)
    with ExitStack() as st:
        sh = make_shared(nc, P, st)
        body_B(nc, P, st, sh, ("input", yT), npass=npass, stages=stages, nfb=nfb)
        counts = P.emit(st)
        print("B ops", counts, "waits", P.n_waits)
    return nc


def relayout_B(inp, l=0):
    w_in = inp["w_in"][l]
    NR = 6592
    gcol0 = NR + 3 * 2048
    Wg = w_in[:, gcol0:gcol0 + 8192]
    wg = np.ascontiguousarray(Wg.reshape(32, 128, 64, 128).transpose(2, 1, 0, 3))
    Wu = np.concatenate([inp["w_up_rwkv"][l], inp["w_up_diff"][l]], axis=1)
    wu = np.ascontiguousarray(Wu.reshape(16, 128, 64, 128).transpose(2, 1, 0, 3))
    wo = np.ascontiguousarray(inp["w_out"][l].reshape(32, 128, 16, 256).transpose(2, 1, 0, 3))
    w1 = np.ascontiguousarray(inp["w_mlp_in"][l].reshape(32, 128, NFB, FB).transpose(2, 1, 0, 3))
    w2 = np.ascontiguousarray(inp["w_mlp_out"][l].reshape(NFB, FB // 128, 128, D).transpose(0, 2, 1, 3))
    gains = np.stack([np.broadcast_to(inp[k][l][None, :], (128, D)) for k in ("attn_pre_norm", "attn_post_norm", "mlp_pre_norm", "mlp_post_norm")]).astype(np.float32)
    ident = np.eye(128, dtype=np.float32).astype(ml_dtypes.bfloat16)
    ones = np.ones((128, 128), np.float32).astype(ml_dtypes.bfloat16)
    return dict(wg=wg, wu=wu, wo=wo, w1=w1, w2=w2, gains=np.ascontiguousarray(gains), ident=ident, ones=ones)


F32 = mybir.dt.float32
BF16 = mybir.dt.bfloat16
AF = mybir.ActivationFunctionType
ALU = mybir.AluOpType
S = 4096
C = 128
BLK = 512
NB = S // BLK
EPS_GN = 64e-5
DEC = -0.6065306597126334
RWMODE = 3


def emit_rwkv(nc, P, st, PT_d, yT, psum, pst, state, next_ps, ident, ones, epsb, rw_heads):
    par_d = nc.dram_tensor("par", [64, 8, 16], F32, kind="ExternalInput").ap()
    parl_d = nc.dram_tensor("parl", [128, 4, 2], F32, kind="ExternalInput").ap()
    lw2_d = nc.dram_tensor("lw2", [96, 4, 512], F32, kind="ExternalInput").ap()
    g2_d = nc.dram_tensor("g2", [128, 2, 512], F32, kind="ExternalInput").ap()
    masks_d = nc.dram_tensor("masks", [128, 2, 640], F32, kind="ExternalInput").ap()
    mreset_d = nc.dram_tensor("mreset", [64, BLK], F32, kind="ExternalInput").ap()

    def sb(name, shape, dt):
        return st.enter_context(nc.sbuf_tensor(name, shape, dt))
    par = sb("par_s", [64, 8, 20], F32)
    parl = sb("parl_s", [128, 4, 3], F32)
    lw2b = sb("lw2b", [96, 4, 512], BF16)
    g2b = sb("g2b", [128, 2, 512], BF16)
    masks = sb("masks_s", [128, 2, 640], F32)
    mreset = sb("mreset_s", [64, BLK], F32)
    twT = sb("twT", [128, S], BF16)
    daT = sb("daT", [128, S], BF16)
    sgT = sb("sgT", [128, 2, S], BF16)
    RAW = sb("RAW", [128, S // 2 + 2], F32)
    SH = sb("SH", [128, S // 2], F32)
    Rb16 = sb("R16", [64, S], BF16)
    Kb16 = sb("K16", [64, S], BF16)
    Vb16 = sb("V16", [64, S], BF16)
    Vt = sb("rVt", [128, 32, 64], BF16)
    KKN = sb("KKN", [64, S], BF16)
    OT = sb("OT", [64, S], F32)
    BONV = sb("BONV", [64, S], BF16)
    gnb = sb("gnb", [64, 1], F32)
    tiny = sb("tinyb", [64, 1], F32)
    LW = sb("LW", [64, BLK], F32)
    CI = sb("CI", [64, BLK], F32)
    CE = sb("CE", [64, BLK], F32)
    Ece = sb("Ece", [64, BLK], F32)
    Enci = sb("Enci", [64, BLK], F32)
    Eci = [sb(f"Eci{i}", [64, BLK], F32) for i in range(2)]
    At = sb("At", [64, BLK], F32)
    T1 = sb("T1", [64, BLK], F32)
    KD = sb("KD", [64, BLK], F32)
    T2 = sb("T2b", [64, BLK], BF16)
    ops4 = [sb(f"ops4_{i}", [64, 4, BLK], BF16) for i in range(2)]
    tok3 = [sb(f"tok3_{i}", [128, 12, 64], BF16) for i in range(2)]
    G1s = [sb(f"G1s{c}", [128, 384], BF16) for c in range(8)]
    G2s = [sb(f"G2s{c}", [128, 256], BF16) for c in range(8)]
    MM = [[sb(f"MM{c}_{i}", [128, 256], BF16) for i in range(2)] for c in range(8)]
    Qs = [[sb(f"Qs{c}_{i}", [128, 128], BF16) for i in range(2)] for c in range(8)]
    IMs = [[sb(f"IM{c}_{i}", [128, 128], BF16) for i in range(2)] for c in range(8)]
    nW1T = [sb(f"nW1T{c}", [64, 128], BF16) for c in range(8)]
    nXs = [sb(f"nXs{c}", [128, 64], BF16) for c in range(8)]
    Us = sb("Us", [128, 64], BF16)
    Hf = sb("Hf", [64, 64], F32)
    Hb = sb("Hb", [64, 64], BF16)
    yo = sb("ryo", [64, BLK], BF16)
    ob = sb("ob", [64, BLK], BF16)
    dd = sb("dd", [64, BLK], F32)
    ones64 = ones[0:64, 0:64]
    id64 = ident[0:64, 0:64]

    def V(fn, reads, writes):
        P.op("vector", fn, reads=reads, writes=writes)

    def A(fn, reads, writes):
        P.op("scalar", fn, reads=reads, writes=writes)

    def T(fn, reads, writes):
        P.op("tensor", fn, reads=reads, writes=writes)

    for (dst, src, k, q) in ((par[:, :, 0:16], par_d, "par", "sync"), (parl[:, :, 0:2], parl_d, "parl", "sync"), (masks[:], masks_d, "masks", "sync"),
                             (mreset[:], mreset_d, "mreset", "sync"), (lw2b[:], lw2_d, "lw2b", "gpsimd"), (g2b[:], g2_d, "g2b", "gpsimd")):
        P.op(q, lambda e, dst=dst, src=src: e.dma_start(out=dst, in_=src), writes=[k], dma_key=k)
    V(lambda e: e.memset(gnb[:], EPS_GN), [], ["gnb"])
    V(lambda e: e.memset(tiny[:], 1e-24), [], ["gnb"])
    for i in range(3):
        V(lambda e, i=i: e.tensor_tensor(out=par[:, :, 16 + i], in0=par[:, :, 2 * i], in1=par[:, :, 2 * i + 1], op=ALU.add), ["par"], ["par"])
        V(lambda e, i=i: e.tensor_scalar(out=par[:, :, 16 + i], in0=par[:, :, 16 + i], scalar1=-1.0, scalar2=1.0, op0=ALU.mult, op1=ALU.add), ["par"], ["par"])
    V(lambda e: e.tensor_tensor(out=parl[:, :, 2], in0=parl[:, :, 0], in1=parl[:, :, 1], op=ALU.add), ["parl"], ["parl"])
    V(lambda e: e.tensor_scalar(out=parl[:, :, 2], in0=parl[:, :, 2], scalar1=-1.0, scalar2=1.0, op0=ALU.mult, op1=ALU.add), ["parl"], ["parl"])

    HS = S // 2

    def load_shift(ch, r0, nrow, c0, mup, mun, consume):
        for half in range(2):
            t0 = half * HS
            lo = max(t0 - 1, 0)
            hi = min(t0 + HS + 1, S)
            off = lo - (t0 - 1)
            if half == 0:
                V(lambda e: e.memset(RAW[0:nrow, 0:1], 0.0), [], ["RAW"])
            else:
                V(lambda e: e.memset(RAW[0:nrow, HS + 1:HS + 2], 0.0), [], ["RAW"])
            P.op("sync", lambda e, lo=lo, hi=hi, off=off: e.dma_start(out=RAW[0:nrow, off:off + (hi - lo)], in_=PT_d[ch][r0:r0 + nrow, lo:hi]), reads=[f"PT{ch}"], writes=["RAW"], dma_key="RAW")
            V(lambda e: e.tensor_scalar(out=SH[0:nrow, :], in0=RAW[0:nrow, 1:HS + 1], scalar1=c0, scalar2=None, op0=ALU.mult), ["RAW", "par", "parl"], ["SH"])
            V(lambda e: e.scalar_tensor_tensor(out=SH[0:nrow, :], in0=RAW[0:nrow, 0:HS], scalar=mup, in1=SH[0:nrow, :], op0=ALU.mult, op1=ALU.add), ["RAW", "SH", "par", "parl"], ["SH"])
            V(lambda e: e.scalar_tensor_tensor(out=SH[0:nrow, :], in0=RAW[0:nrow, 2:HS + 2], scalar=mun, in1=SH[0:nrow, :], op0=ALU.mult, op1=ALU.add), ["RAW", "SH", "par", "parl"], ["SH"])
            consume(half)

    for i, (ch, dstT, fn) in enumerate(((12, twT, AF.Tanh), (13, daT, AF.Copy), (14, sgT[:, 0, :], AF.Sigmoid), (15, sgT[:, 1, :], AF.Sigmoid))):
        def cons(half, dstT=dstT, fn=fn):
            A(lambda e: e.activation(out=dstT[:, half * HS:(half + 1) * HS], in_=SH[:, :], func=fn), ["SH"], ["lora_act"])
        load_shift(ch, 0, 128, parl[:, i, 2:3], parl[:, i, 0:1], parl[:, i, 1:2], cons)

    def head(hh):
        ch_r, ch_k, ch_v = hh // 2, 4 + hh // 2, 8 + hh // 2
        r0 = (hh % 2) * 64
        pc = lambda j: par[:, hh, j:j + 1]
        for (ch, i, dst) in ((ch_r, 0, Rb16), (ch_k, 1, Kb16), (ch_v, 2, Vb16)):
            def cons(half, dst=dst):
                A(lambda e: e.activation(out=dst[:, half * HS:(half + 1) * HS], in_=SH[0:64, :], func=AF.Copy), ["SH"], ["rkv"])
            load_shift(ch, r0, 64, pc(16 + i), pc(2 * i), pc(2 * i + 1), cons)
        for half in range(2):
            pi = state["pt"]; state["pt"] ^= 1
            for j in range(16):
                blk = half * 16 + j
                T(lambda e, blk=blk, j=j, pi=pi: e.transpose(out=pst[pi][:, j * 64:(j + 1) * 64], in_=Vb16[:, blk * 128:(blk + 1) * 128], identity=id64), ["rkv", "ident"], [f"ps{6 + pi}"])
            V(lambda e, half=half, pi=pi: e.tensor_copy(out=Vt[:, half * 16:(half + 1) * 16, :], in_=pst[pi][:, :].rearrange("p (k t) -> p k t", k=16)), [f"ps{6 + pi}"], ["rVt"])
        for b in range(NB):
            bs = slice(b * BLK, (b + 1) * BLK)
            V(lambda e, bs=bs: e.tensor_scalar(out=T1[:], in0=Kb16[:, bs], scalar1=pc(10), scalar2=None, op0=ALU.mult), ["rkv", "par"], ["T1"])
            A(lambda e: e.activation(out=T2[:], in_=T1[:], func=AF.Square), ["T1"], ["T2"])
            pi = next_ps()
            T(lambda e, pi=pi: e.matmul(psum[pi][0:64, :], lhsT=ones64, rhs=T2[:], start=True, stop=True), ["T2", "ones"], [f"ps{pi}"])
            A(lambda e, pi=pi: e.activation(out=KD[:], in_=psum[pi][0:64, :], func=AF.Sqrt, bias=tiny[:, 0:1]), [f"ps{pi}", "gnb"], ["KD"])
            V(lambda e: e.reciprocal(out=KD[:], in_=KD[:]), ["KD"], ["KD"])
            V(lambda e, bs=bs: e.tensor_tensor(out=KKN[:, bs], in0=T1[:], in1=KD[:], op=ALU.mult), ["T1", "KD"], ["KKN"])
        def direction(d):
            V(lambda e: e.memset(Hf[:], 0.0), [], ["Hf"])
            V(lambda e: e.memset(Hb[:], 0.0), [], ["Hb"])
            blocks = range(NB) if d == 0 else range(NB - 1, -1, -1)
            def block(bi, b):
                bs = slice(b * BLK, (b + 1) * BLK)
                sl = bi % 2
                O4, K3, EC = ops4[sl], tok3[sl], Eci[sl]
                hs = slice(hh * 64, (hh + 1) * 64)
                pi = 6
                T(lambda e, pi=pi, bs=bs, hs=hs: e.matmul(psum[pi][0:64, :], lhsT=lw2b[:, d, hs], rhs=twT[0:96, bs], start=True, stop=True), ["lw2b", "lora_act"], [f"ps{pi}"])
                A(lambda e, pi=pi: e.activation(out=LW[:], in_=psum[pi][0:64, :], func=AF.Sigmoid, bias=pc(6 + d)), [f"ps{pi}", "par"], ["LW"])
                V(lambda e: e.tensor_scalar(out=LW[:], in0=LW[:], scalar1=DEC, scalar2=None, op0=ALU.mult), ["LW"], ["LW"])
                V(lambda e: e.tensor_tensor_scan(out=CI[:], data0=mreset[:], data1=LW[:], initial=0.0, op0=ALU.mult, op1=ALU.add), ["LW", "mreset"], ["CI"])
                if d == 0:
                    V(lambda e: e.tensor_tensor(out=CE[:], in0=CI[:], in1=LW[:], op=ALU.subtract), ["CI", "LW"], ["CE"])
                else:
                    for c in range(4):
                        cs = slice(c * C, (c + 1) * C)
                        V(lambda e, cs=cs, c=c: e.tensor_scalar(out=CE[:, cs], in0=CI[:, cs], scalar1=-1.0, scalar2=CI[:, c * C + C - 1:c * C + C], op0=ALU.mult, op1=ALU.add), ["CI"], ["CE"])
                    V(lambda e: e.tensor_tensor(out=CI[:], in0=CE[:], in1=LW[:], op=ALU.add), ["CE", "LW"], ["CI"])
                A(lambda e: e.activation(out=Ece[:], in_=CE[:], func=AF.Exp), ["CE"], ["Ece"])
                A(lambda e: e.activation(out=Enci[:], in_=CI[:], func=AF.Exp, scale=-1.0), ["CI"], ["Enci"])
                A(lambda e, EC=EC: e.activation(out=EC[:], in_=CI[:], func=AF.Exp), ["CI"], [f"Eci{sl}"])
                pi = 7
                T(lambda e, pi=pi, bs=bs, hs=hs: e.matmul(psum[pi][0:64, :], lhsT=lw2b[:, 2 + d, hs], rhs=daT[0:96, bs], start=True, stop=True), ["lw2b", "lora_act"], [f"ps{pi}"])
                A(lambda e, pi=pi: e.activation(out=At[:], in_=psum[pi][0:64, :], func=AF.Sigmoid, bias=pc(8 + d)), [f"ps{pi}", "par"], ["At"])
                V(lambda e: e.tensor_scalar(out=T1[:], in0=At[:], scalar1=-1.0, scalar2=pc(11), op0=ALU.add, op1=ALU.mult), ["At", "par"], ["T1"])
                V(lambda e, bs=bs: e.scalar_tensor_tensor(out=KD[:], in0=T1[:], scalar=1.0, in1=Kb16[:, bs], op0=ALU.add, op1=ALU.mult), ["T1", "rkv"], ["KD"])
                V(lambda e, bs=bs: e.tensor_tensor(out=At[:], in0=At[:], in1=KKN[:, bs], op=ALU.mult), ["At", "KKN"], ["At"])
                V(lambda e, bs=bs, O4=O4: e.tensor_tensor(out=O4[:, 0, :], in0=KKN[:, bs], in1=Ece[:], op=ALU.mult), ["KKN", "Ece"], [f"ops4_{sl}"])
                V(lambda e, O4=O4: e.tensor_tensor(out=O4[:, 1, :], in0=At[:], in1=Enci[:], op=ALU.mult), ["At", "Enci"], [f"ops4_{sl}"])
                V(lambda e, O4=O4: e.tensor_tensor(out=O4[:, 2, :], in0=KD[:], in1=Enci[:], op=ALU.mult), ["KD", "Enci"], [f"ops4_{sl}"])
                V(lambda e, bs=bs, O4=O4, EC=EC: e.tensor_tensor(out=O4[:, 3, :], in0=Rb16[:, bs], in1=EC[:], op=ALU.mult), ["rkv", f"Eci{sl}"], [f"ops4_{sl}"])
                V(lambda e, bs=bs: e.scalar_tensor_tensor(out=T2[:], in0=Rb16[:, bs], scalar=pc(12), in1=KD[:], op0=ALU.mult, op1=ALU.mult), ["rkv", "KD", "par"], ["T2"])
                pi = 6
                T(lambda e, pi=pi: e.matmul(psum[pi][0:64, :], lhsT=ones64, rhs=T2[:], start=True, stop=True), ["T2", "ones"], [f"ps{pi}"])
                V(lambda e, pi=pi, bs=bs: e.scalar_tensor_tensor(out=T1[:], in0=psum[pi][0:64, :], scalar=0.5, in1=Vb16[:, bs], op0=ALU.mult, op1=ALU.mult), [f"ps{pi}", "rkv"], ["T1"])
                if d == 0:
                    V(lambda e, bs=bs: e.tensor_copy(out=BONV[:, bs], in_=T1[:]), ["T1"], ["BONV"])
                else:
                    V(lambda e, bs=bs: e.tensor_tensor(out=BONV[:, bs], in0=BONV[:, bs], in1=T1[:], op=ALU.add), ["T1", "BONV"], ["BONV"])
                pi = state["pt"]; state["pt"] ^= 1
                for o in range(3):
                    for c in range(4):
                        T(lambda e, o=o, c=c, pi=pi, O4=O4: e.transpose(out=pst[pi][:, (o * 4 + c) * 64:(o * 4 + c + 1) * 64], in_=O4[:, o, c * C:(c + 1) * C], identity=id64),
                          [f"ops4_{sl}", "ident"], [f"ps{6 + pi}"])
                V(lambda e, pi=pi, K3=K3: e.tensor_copy(out=K3[:, :, :], in_=pst[pi][:, 0:768].rearrange("p (k t) -> p k t", k=12)), [f"ps{6 + pi}"], [f"tok3_{sl}"])
                chunks = list(range(4)) if d == 0 else list(range(3, -1, -1))
                ok = [f"ops4_{sl}"]
                cst_ = {}

                def opsof(c):
                    cs = slice(c * C, (c + 1) * C)
                    return O4[:, 0, cs], O4[:, 1, cs], O4[:, 2, cs], O4[:, 3, cs]

                kx = lambda c: sl * 4 + c

                def gram1(c):
                    Ab_c, Bb_c, Kb_c, Rb_c = opsof(c)
                    p1, p2 = next_ps(), next_ps()
                    cst_[c] = dict(p1=p1, p2=p2)
                    T(lambda e: e.matmul(psum[p1][:, 0:128], lhsT=Bb_c, rhs=Ab_c, start=True, stop=True), ok, [f"ps{p1}"])
                    T(lambda e: e.matmul(psum[p1][:, 128:256], lhsT=Kb_c, rhs=Ab_c, start=True, stop=True), ok, [f"ps{p1}"])
                    T(lambda e: e.matmul(psum[p1][:, 256:384], lhsT=Ab_c, rhs=Bb_c, start=True, stop=True), ok, [f"ps{p1}"])
                    T(lambda e: e.matmul(psum[p2][:, 0:128], lhsT=Bb_c, rhs=Rb_c, start=True, stop=True), ok, [f"ps{p2}"])
                    T(lambda e: e.matmul(psum[p2][:, 128:256], lhsT=Kb_c, rhs=Rb_c, start=True, stop=True), ok, [f"ps{p2}"])

                def evac1(c):
                    p1, p2 = cst_[c]["p1"], cst_[c]["p2"]
                    V(lambda e: e.tensor_tensor(out=G1s[kx(c)][:], in0=psum[p1][:, 0:384], in1=masks[:, d, 0:384], op=ALU.mult), [f"ps{p1}", "masks"], [f"G1s{kx(c)}"])
                    V(lambda e: e.tensor_tensor(out=G2s[kx(c)][:], in0=psum[p2][:, 0:256], in1=masks[:, d, 384:640], op=ALU.mult), [f"ps{p2}", "masks"], [f"G2s{kx(c)}"])
                    V(lambda e: e.tensor_tensor(out=Qs[kx(c)][0][:], in0=G1s[kx(c)][:, 0:128], in1=ident[:], op=ALU.add), [f"G1s{kx(c)}", "ident"], [f"Qs{kx(c)}_0"])
                    cst_[c].update(M=G1s[kx(c)][:, 256:384], MT=G1s[kx(c)][:, 0:128], mk=f"G1s{kx(c)}", qi=0)

                def levelA(c, lev):
                    stc = cst_[c]
                    Mprev, MTprev, mk = stc["M"], stc["MT"], stc["mk"]
                    mi = lev % 2
                    pm = next_ps()
                    T(lambda e: e.matmul(psum[pm][:, 0:128], lhsT=MTprev, rhs=Mprev, start=True, stop=True), [mk], [f"ps{pm}"])
                    if lev < 6:
                        T(lambda e: e.matmul(psum[pm][:, 128:256], lhsT=Mprev, rhs=MTprev, start=True, stop=True), [mk], [f"ps{pm}"])
                    A(lambda e: e.activation(out=MM[kx(c)][mi][:], in_=psum[pm][:, 0:256], func=AF.Copy), [f"ps{pm}"], [f"MM{kx(c)}_{mi}"])
                    V(lambda e: e.tensor_tensor(out=IMs[kx(c)][mi][:], in0=MM[kx(c)][mi][:, 0:128], in1=ident[:], op=ALU.add), [f"MM{kx(c)}_{mi}", "ident"], [f"IM{kx(c)}_{mi}"])
                    stc["M"], stc["MT"], stc["mk"] = MM[kx(c)][mi][:, 0:128], MM[kx(c)][mi][:, 128:256], f"MM{kx(c)}_{mi}"

                def levelB(c, lev):
                    stc = cst_[c]
                    qi = stc["qi"]
                    mi = lev % 2
                    pq = next_ps()
                    T(lambda e: e.matmul(psum[pq][:, 0:128], lhsT=IMs[kx(c)][mi][:], rhs=Qs[kx(c)][qi][:], start=True, stop=True), [f"Qs{kx(c)}_{qi}", f"IM{kx(c)}_{mi}"], [f"ps{pq}"])
                    V(lambda e: e.tensor_copy(out=Qs[kx(c)][1 - qi][:], in_=psum[pq][:, 0:128]), [f"ps{pq}"], [f"Qs{kx(c)}_{1 - qi}"])
                    stc["qi"] = 1 - qi

                def w1x(c):
                    gc = b * 4 + c
                    qi = cst_[c]["qi"]
                    Q, qk = Qs[kx(c)][qi], f"Qs{kx(c)}_{qi}"
                    Abt = K3[:, 0 + c, :]
                    pw = next_ps()
                    T(lambda e: e.matmul(psum[pw][0:64, 0:128], lhsT=Abt, rhs=Q[:], start=True, stop=True), [f"tok3_{sl}", qk], [f"ps{pw}"])
                    A(lambda e: e.activation(out=nW1T[kx(c)][:], in_=psum[pw][0:64, 0:128], func=AF.Copy, scale=-1.0), [f"ps{pw}"], [f"nW1T{kx(c)}"])
                    px = next_ps()
                    T(lambda e: e.matmul(psum[px][:, 0:64], lhsT=G1s[kx(c)][:, 128:256], rhs=Vt[:, gc, :], start=True, stop=True), [f"G1s{kx(c)}", "rVt"], [f"ps{px}"])
                    A(lambda e: e.activation(out=nXs[kx(c)][:], in_=psum[px][:, 0:64], func=AF.Copy, scale=-1.0), [f"ps{px}"], [f"nXs{kx(c)}"])

                def seq(c):
                    gc = b * 4 + c
                    qi = cst_[c]["qi"]
                    Q, qk = Qs[kx(c)][qi], f"Qs{kx(c)}_{qi}"
                    Ab_c, Bb_c, Kb_c, Rb_c = opsof(c)
                    Bbt, Kbt = K3[:, 4 + c, :], K3[:, 8 + c, :]
                    Vtc = Vt[:, gc, :]
                    pu = next_ps()
                    T(lambda e: e.matmul(psum[pu][:, 0:64], lhsT=Q[:], rhs=nXs[kx(c)][:], start=True, stop=False), [qk, f"nXs{kx(c)}"], [f"ps{pu}"])
                    T(lambda e: e.matmul(psum[pu][:, 0:64], lhsT=nW1T[kx(c)][:], rhs=Hb[:], start=False, stop=True), [f"nW1T{kx(c)}", "Hb"], [f"ps{pu}"])
                    V(lambda e: e.tensor_copy(out=Us[:], in_=psum[pu][:, 0:64]), [f"ps{pu}"], ["Us"])
                    po = next_ps()
                    T(lambda e: e.matmul(psum[po][0:64, 0:128], lhsT=Hb[:], rhs=Rb_c, start=True, stop=False), ["Hb"] + ok, [f"ps{po}"])
                    T(lambda e: e.matmul(psum[po][0:64, 0:128], lhsT=Us[:], rhs=G2s[kx(c)][:, 0:128], start=False, stop=False), ["Us", f"G2s{kx(c)}"], [f"ps{po}"])
                    T(lambda e: e.matmul(psum[po][0:64, 0:128], lhsT=Vtc, rhs=G2s[kx(c)][:, 128:256], start=False, stop=True), ["rVt", f"G2s{kx(c)}"], [f"ps{po}"])
                    gs = slice(gc * C, (gc + 1) * C)
                    if d == 0:
                        A(lambda e: e.activation(out=OT[:, gs], in_=psum[po][0:64, 0:128], func=AF.Copy), [f"ps{po}"], ["OT"])
                    else:
                        V(lambda e: e.tensor_tensor(out=OT[:, gs], in0=OT[:, gs], in1=psum[po][0:64, 0:128], op=ALU.add), [f"ps{po}", "OT"], ["OT"])
                    ph = next_ps()
                    T(lambda e: e.matmul(psum[ph][0:64, 0:64], lhsT=Bbt, rhs=Us[:], start=True, stop=False), [f"tok3_{sl}", "Us"], [f"ps{ph}"])
                    T(lambda e: e.matmul(psum[ph][0:64, 0:64], lhsT=Kbt, rhs=Vtc, start=False, stop=True), [f"tok3_{sl}", "rVt"], [f"ps{ph}"])
                    gidx = c * C + C - 1 if d == 0 else c * C
                    gam = EC[:, gidx:gidx + 1]
                    V(lambda e: e.tensor_scalar(out=Hf[:], in0=Hf[:], scalar1=gam, scalar2=None, op0=ALU.mult), ["Hf", f"Eci{sl}"], ["Hf"])
                    V(lambda e: e.scalar_tensor_tensor(out=Hf[:], in0=psum[ph][0:64, 0:64], scalar=gam, in1=Hf[:], op0=ALU.mult, op1=ALU.add), ["Hf", f"Eci{sl}", f"ps{ph}"], ["Hf"])
                    A(lambda e: e.activation(out=Hb[:], in_=Hf[:], func=AF.Copy), ["Hf"], ["Hb"])

                stages = []
                for cg in (chunks[0:2], chunks[2:4]):
                    for c in cg:
                        stages.append(lambda c=c: gram1(c))
                    for c in cg:
                        stages.append(lambda c=c: evac1(c))
                for lev in range(1, 7):
                    for c in chunks:
                        stages.append(lambda c=c, lev=lev: levelA(c, lev))
                    for c in chunks:
                        stages.append(lambda c=c, lev=lev: levelB(c, lev))
                for c in chunks:
                    stages.append(lambda c=c: w1x(c))
                seqs = [(lambda c=c: seq(c)) for c in chunks]
                return stages, seqs
            prev = None
            for bi, b in enumerate(blocks):
                stages, seqs = block(bi, b)
                if prev is None or RWMODE != 3:
                    if prev is not None:
                        for f in prev:
                            f()
                    for f in stages:
                        f()
                else:
                    n = len(stages)
                    marks = {int((k + 1) * n / 5): k for k in range(4)}
                    for i, f in enumerate(stages):
                        f()
                        if (i + 1) in marks:
                            prev[marks[i + 1]]()
                prev = seqs
            for f in prev:
                f()
        P.fence()
        direction(0)
        direction(1)
        P.fence()
        for b in range(NB):
            bs = slice(b * BLK, (b + 1) * BLK)
            A(lambda e, bs=bs: e.activation(out=ob[:], in_=OT[:, bs], func=AF.Copy), ["OT"], ["ob"])
            pi = next_ps()
            T(lambda e, pi=pi: e.matmul(psum[pi][0:64, :], lhsT=ones64, rhs=ob[:], start=True, stop=True), ["ob", "ones"], [f"ps{pi}"])
            V(lambda e, pi=pi, bs=bs: e.scalar_tensor_tensor(out=dd[:], in0=psum[pi][0:64, :], scalar=-1.0 / 64, in1=OT[:, bs], op0=ALU.mult, op1=ALU.add), [f"ps{pi}", "OT"], ["dd"])
            A(lambda e: e.activation(out=ob[:], in_=dd[:], func=AF.Square), ["dd"], ["ob"])
            pi = next_ps()
            T(lambda e, pi=pi: e.matmul(psum[pi][0:64, :], lhsT=ones64, rhs=ob[:], start=True, stop=True), ["ob", "ones"], [f"ps{pi}"])
            A(lambda e, pi=pi: e.activation(out=T1[:], in_=psum[pi][0:64, :], func=AF.Sqrt, scale=1.0 / 64, bias=gnb[:, 0:1]), [f"ps{pi}", "gnb"], ["T1"])
            V(lambda e: e.reciprocal(out=T1[:], in_=T1[:]), ["T1"], ["T1"])
            V(lambda e: e.tensor_tensor(out=dd[:], in0=dd[:], in1=T1[:], op=ALU.mult), ["dd", "T1"], ["dd"])
            V(lambda e: e.tensor_scalar(out=dd[:], in0=dd[:], scalar1=pc(13), scalar2=pc(14), op0=ALU.mult, op1=ALU.add), ["dd", "par"], ["dd"])
            V(lambda e, bs=bs: e.tensor_tensor(out=dd[:], in0=dd[:], in1=BONV[:, bs], op=ALU.add), ["dd", "BONV"], ["dd"])
            pi = next_ps()
            for kc in range(2):
                T(lambda e, pi=pi, kc=kc, bs=bs: e.matmul(psum[pi][0:64, :], lhsT=g2b[:, kc, hh * 64:(hh + 1) * 64], rhs=sgT[:, kc, bs], start=(kc == 0), stop=(kc == 1)), ["g2b", "lora_act"], [f"ps{pi}"])
            V(lambda e, pi=pi: e.tensor_tensor(out=yo[:], in0=dd[:], in1=psum[pi][0:64, :], op=ALU.mult), ["dd", f"ps{pi}"], ["ryo"])
            P.op("sync", lambda e, bs=bs: e.dma_start(out=yT[hh // 2][(hh % 2) * 64:(hh % 2) * 64 + 64, bs], in_=yo[:]), reads=["ryo"], writes=["yT"], dma_key="ryo")

    for hh in range(rw_heads):
        head(hh)


def relayout_rwkv(inp, c, l=0):
    b, g = c // 4, c % 4
    chs = slice(512 * g, 512 * (g + 1))
    par = np.zeros((64, 8, 16), np.float32)
    sp, sn = inp["shift_prev"][l], inp["shift_next"][l]
    for hh in range(8):
        cg = slice(512 * g + hh * 64, 512 * g + (hh + 1) * 64)
        for i in range(3):
            par[:, hh, 2 * i] = sp[i * 2048:(i + 1) * 2048][cg]
            par[:, hh, 2 * i + 1] = sn[i * 2048:(i + 1) * 2048][cg]
        par[:, hh, 6] = inp["decay_bias_fwd"][l][cg]; par[:, hh, 7] = inp["decay_bias_bwd"][l][cg]
        par[:, hh, 8] = inp["iclr_bias_fwd"][l][cg]; par[:, hh, 9] = inp["iclr_bias_bwd"][l][cg]
        par[:, hh, 10] = inp["k_k"][l][cg]; par[:, hh, 11] = inp["k_a"][l][cg]
        par[:, hh, 12] = inp["r_k"][l].reshape(-1)[cg]
        par[:, hh, 13] = inp["ln_x_gain"][l][cg]; par[:, hh, 14] = inp["ln_x_bias"][l][cg]
    parl = np.zeros((128, 4, 2), np.float32)
    lo = 3 * 2048
    for i, (a0, n) in enumerate(((lo, 96), (lo + 96, 96), (lo + 192, 128), (lo + 320, 128))):
        parl[:n, i, 0] = sp[a0:a0 + n]; parl[:n, i, 1] = sn[a0:a0 + n]
    lw2 = np.stack([inp[k][l][:, chs] for k in ("decay_up_fwd", "decay_up_bwd", "iclr_up_fwd", "iclr_up_bwd")], axis=1)
    g2 = np.ascontiguousarray(inp["gate_up"][l][:, chs].reshape(2, 128, 512).transpose(1, 0, 2))
    ii = np.arange(128)
    row, col = ii[:, None], ii[None, :]
    masks = np.zeros((128, 2, 640), np.float32)
    for d in range(2):
        lt = (row < col) if d == 0 else (row > col)
        le = (row <= col) if d == 0 else (row >= col)
        masks[:, d, 0:128] = -lt.astype(np.float32)
        masks[:, d, 128:256] = lt
        masks[:, d, 256:384] = -lt.T.astype(np.float32)
        masks[:, d, 384:512] = le
        masks[:, d, 512:640] = le
    mreset = np.ones((64, BLK), np.float32)
    mreset[:, ::C] = 0.0
    return dict(par=par, parl=parl, lw2=np.ascontiguousarray(lw2).astype(np.float32), g2=g2.astype(np.float32), masks=masks, mreset=mreset)


F32 = mybir.dt.float32
BF16 = mybir.dt.bfloat16
AF = mybir.ActivationFunctionType
ALU = mybir.AluOpType
D = 4096
S = 4096
EPS = 1e-6
NCH = 28
NR = 6592
LAMBDA_INIT = 0.8 - 0.6
WARMN = 16
NFILL = 1


def body_A(nc, P, st, sh, yT, do_proj=True, do_attn=True, do_rwkv=True, n_tb=8, attn_heads=4, attn_qb=8, rw_heads=8):
    xb = nc.dram_tensor("xb", [S, D], F32, kind="ExternalInput").ap()
    wA = nc.dram_tensor("wA", [NCH, 128, 32, 128], F32, kind="ExternalInput").ap()
    abias = nc.dram_tensor("abias", [4, 5, 128, 512], F32, kind="ExternalInput").ap()
    acst = nc.dram_tensor("acst", [128, 4 * 64], F32, kind="ExternalInput").ap()
    lam_d = nc.dram_tensor("lam", [128, 4, 64], F32, kind="ExternalInput").ap()
    subg = nc.dram_tensor("subg", [128, 1], F32, kind="ExternalInput").ap()
    PT_d = nc.dram_tensor("PT_d", [NCH, 128, S], F32).ap()
    gain = sh["gains"][0]
    ident, ones, small, epsb = sh["ident"], sh["ones"], sh["small"], sh["epsb"]
    psum, pst, state, next_ps = sh["psum"], sh["pst"], sh["state"], sh["next_ps"]
    if True:
        if do_proj:
            with ExitStack() as st2:
                bufAs = [st2.enter_context(nc.sbuf_tensor(f"bufA{i}", [128, 32, 512], BF16)) for i in range(2)]
                xt = st2.enter_context(nc.sbuf_tensor("xt", [128, D], F32))
                gt = st2.enter_context(nc.sbuf_tensor("gt", [128, D], F32))
                hb = st2.enter_context(nc.sbuf_tensor("hb", [128, D], BF16))
                wbuf = [st2.enter_context(nc.sbuf_tensor(f"wb{i}", [128, 32, 128], BF16)) for i in range(3)]
                ot = [st2.enter_context(nc.sbuf_tensor(f"ot{i}", [128, 512], F32)) for i in range(3)]
                P.op("sync", lambda e: e.dma_start(out=gt[:], in_=gain), writes=["gt"], dma_key="gt")
                for tb in range(n_tb):
                    bufA = bufAs[tb % 2]
                    bk = f"bufA{tb % 2}"
                    for tt in range(4):
                        r0 = tb * 512 + tt * 128
                        P.op("sync", lambda e, r0=r0: e.dma_start(out=xt[:], in_=xb[r0:r0 + 128, :]), writes=["xt"], dma_key="xt")
                        P.op("vector", lambda e: e.memset(small[:, 0:1], 0.0), writes=["sm0"])
                        P.op("scalar", lambda e: e.activation(out=hb[:], in_=xt[:], func=AF.Square, accum_out=small[:, 0:1]), reads=["xt", "sm0"], writes=["hb", "sm0"])
                        P.op("scalar", lambda e: e.activation(out=small[:, 0:1], in_=small[:, 0:1], func=AF.Sqrt, scale=1.0 / D, bias=epsb[:, 0:1]), reads=["sm0", "epsb"], writes=["sm0"])
                        P.op("vector", lambda e: e.reciprocal(out=small[:, 0:1], in_=small[:, 0:1]), reads=["sm0"], writes=["sm0"])
                        P.op("vector", lambda e: e.scalar_tensor_tensor(out=hb[:], in0=xt[:], scalar=small[:, 0:1], in1=gt[:], op0=ALU.mult, op1=ALU.mult),
                             reads=["xt", "sm0", "gt"], writes=["hb"])
                        for k8 in range(4):
                            pi = state["pt"]; state["pt"] ^= 1
                            for j in range(8):
                                kc = k8 * 8 + j
                                P.op("tensor", lambda e, kc=kc, j=j, pi=pi: e.transpose(out=pst[pi][:, j * 128:(j + 1) * 128], in_=hb[:, kc * 128:(kc + 1) * 128], identity=ident[:]),
                                     reads=["hb", "ident"], writes=[f"ps{6 + pi}"])
                            dst = bufA[:, k8 * 8:(k8 + 1) * 8, tt * 128:(tt + 1) * 128]
                            srcp = pst[pi][:, :].rearrange("p (k t) -> p k t", k=8)
                            if k8 % 2 == 0:
                                P.op("scalar", lambda e, dst=dst, srcp=srcp: e.activation(out=dst, in_=srcp, func=AF.Copy), reads=[f"ps{6 + pi}"], writes=[bk])
                            else:
                                P.op("vector", lambda e, dst=dst, srcp=srcp: e.tensor_copy(out=dst, in_=srcp), reads=[f"ps{6 + pi}"], writes=[bk])
                    for ch in range(NCH):
                        s = ch % 3
                        for half in range(2):
                            P.op("gpsimd", lambda e, s=s, ch=ch, half=half: e.dma_start(out=wbuf[s][:, half * 16:(half + 1) * 16, :], in_=wA[ch][:, half * 16:(half + 1) * 16, :], max_dma_last_dim=4096),
                                 writes=[f"wb{s}"], dma_key=f"wb{s}")
                        pi = next_ps()
                        for kc in range(32):
                            P.op("tensor", lambda e, kc=kc, pi=pi, s=s, bufA=bufA: e.matmul(psum[pi][:], lhsT=wbuf[s][:, kc, :], rhs=bufA[:, kc, :], start=(kc == 0), stop=(kc == 31)),
                                 reads=[f"wb{s}", bk], writes=[f"ps{pi}"])
                        if ch % 2 == 0:
                            P.op("scalar", lambda e, pi=pi, s=s: e.activation(out=ot[s][:], in_=psum[pi][:], func=AF.Copy), reads=[f"ps{pi}"], writes=[f"ot{s}"])
                        else:
                            P.op("vector", lambda e, pi=pi, s=s: e.tensor_copy(out=ot[s][:], in_=psum[pi][:]), reads=[f"ps{pi}"], writes=[f"ot{s}"])
                        P.op("sync", lambda e, s=s, ch=ch, tb=tb: e.dma_start(out=PT_d[ch][:, tb * 512:(tb + 1) * 512], in_=ot[s][:]), reads=[f"ot{s}"], writes=[f"PT{ch}"], dma_key=f"ot{s}")
                P.fence()
        if do_attn:
            with ExitStack() as st2:
                def sb2(name, shape, dt):
                    return st2.enter_context(nc.sbuf_tensor(name, shape, dt))
                LA = 2
                NSL = LA + 1
                qk32 = sb2("qk32", [128, S], F32)
                QTs = [sb2(f"QT{i}", [64, 2, S], BF16) for i in range(2)]
                KTs = [sb2(f"KT{i}", [64, 2, S], BF16) for i in range(2)]
                Vts = [sb2(f"Vt{i}", [128, 32, 128], BF16) for i in range(2)]
                vb = sb2("vb", [128, S], BF16)
                bts = [sb2(f"bt{i}", [128, 5, 512], F32) for i in range(2)]
                cst = sb2("cst", [128, 256], F32)
                lamt = sb2("lamt", [128, 4, 64], F32)
                lsm = sb2("lsm", [128, 8], F32)
                sgt = sb2("sgt", [128, 1], F32)
                tmp = [sb2(f"atmp{i}", [128, 512], F32) for i in range(NSL)]
                Eb = [sb2(f"Eb{i}", [128, 512], BF16) for i in range(NSL)]
                o0 = sb2("o0", [128, 512], F32)
                o1 = sb2("o1", [128, 512], F32)
                rr = sb2("rr", [128, 512], F32)
                rr2 = sb2("rr2", [128, 512], F32)
                sq = sb2("sq", [128, 512], BF16)
                yo = sb2("yo", [128, 512], BF16)
                wz = sb2("wz", [128, 512], BF16)
                P.op("vector", lambda e: e.memset(wz[:], 0.0), writes=["wz"])

                def warm(n):
                    for _ in range(n):
                        P.op("tensor", lambda e: e.matmul(psum[7][:], lhsT=ones[:], rhs=wz[:], start=True, stop=True), reads=["wz", "ones"], writes=["ps7"])
                P.op("sync", lambda e: e.dma_start(out=cst[:], in_=acst), writes=["cst"], dma_key="cst")
                P.op("sync", lambda e: e.dma_start(out=lamt[:], in_=lam_d), writes=["lamt"], dma_key="lamt")
                P.op("sync", lambda e: e.dma_start(out=sgt[:], in_=subg), writes=["sgt"], dma_key="sgt")
                for i in range(2):
                    P.op("vector", lambda e, i=i: e.tensor_tensor(out=lamt[:, 2 * i, :], in0=lamt[:, 2 * i, :], in1=lamt[:, 2 * i + 1, :], op=ALU.mult), reads=["lamt"], writes=["lamt"])
                    P.op("vector", lambda e, i=i: e.tensor_reduce(out=lsm[:, i:i + 1], in_=lamt[:, 2 * i, :], axis=mybir.AxisListType.X, op=ALU.add), reads=["lamt"], writes=["lsm"])
                    P.op("scalar", lambda e, i=i: e.activation(out=lsm[:, i:i + 1], in_=lsm[:, i:i + 1], func=AF.Exp), reads=["lsm"], writes=["lsm"])
                P.op("vector", lambda e: e.tensor_tensor(out=lsm[:, 2:3], in0=lsm[:, 1:2], in1=lsm[:, 0:1], op=ALU.subtract), reads=["lsm"], writes=["lsm"])
                P.op("vector", lambda e: e.tensor_scalar(out=lsm[:, 2:3], in0=lsm[:, 2:3], scalar1=-LAMBDA_INIT, scalar2=None, op0=ALU.add), reads=["lsm"], writes=["lsm"])
                P.op("vector", lambda e: e.tensor_scalar(out=sgt[:], in0=sgt[:], scalar1=1.0 - LAMBDA_INIT, scalar2=None, op0=ALU.mult), reads=["sgt"], writes=["sgt"])

                def load_head(hd):
                    hs = hd % 2
                    QT, KT, Vt, bt = QTs[hs], KTs[hs], Vts[hs], bts[hs]
                    P.op("sync", lambda e: e.dma_start(out=bt[:], in_=abias[hd].rearrange("f p q -> p f q")), writes=[f"bt{hs}"], dma_key=f"bt{hs}")
                    for (dstT, ch, nm) in ((QT, 16 + hd, f"QT{hs}"), (KT, 20 + hd, f"KT{hs}")):
                        for c in range(2):
                            P.op("sync", lambda e, ch=ch, c=c: e.dma_start(out=qk32[0:64, :], in_=PT_d[ch][c * 64:(c + 1) * 64, :]), reads=[f"PT{ch}"], writes=["qk32"], dma_key="qk32")
                            P.op("scalar", lambda e, dstT=dstT, c=c: e.activation(out=dstT[:, c, :], in_=qk32[0:64, :], func=AF.Copy), reads=["qk32"], writes=[nm])
                    P.op("sync", lambda e: e.dma_start(out=qk32[:], in_=PT_d[24 + hd]), reads=[f"PT{24 + hd}"], writes=["qk32"], dma_key="qk32")
                    P.op("vector", lambda e: e.tensor_copy(out=vb[:], in_=qk32[:]), reads=["qk32"], writes=["vb"])
                    for k8 in range(4):
                        pi = state["pt"]; state["pt"] ^= 1
                        for j in range(8):
                            blk = k8 * 8 + j
                            P.op("tensor", lambda e, blk=blk, j=j, pi=pi: e.transpose(out=pst[pi][:, j * 128:(j + 1) * 128], in_=vb[:, blk * 128:(blk + 1) * 128], identity=ident[:]),
                                 reads=["vb", "ident"], writes=[f"ps{6 + pi}"])
                        P.op("vector", lambda e, k8=k8, pi=pi: e.tensor_copy(out=Vt[:, k8 * 8:(k8 + 1) * 8, :], in_=pst[pi][:, :].rearrange("p (k t) -> p k t", k=8)), reads=[f"ps{6 + pi}"], writes=[f"Vt{hs}"])

                pacc = [0, 1, 2, 3]
                pending = [None]

                def unit_front(hd, qb, i, ulist):
                    hs = hd % 2
                    kb, c = ulist[i]
                    delta = kb - 4 * qb
                    pi = 4 + (i % NSL)
                    ti = i % NSL
                    P.op("tensor", lambda e: e.matmul(psum[pi][:], lhsT=KTs[hs][:, c, kb * 128:(kb + 1) * 128], rhs=QTs[hs][:, c, qb * 512:(qb + 1) * 512], start=True, stop=True),
                         reads=[f"QT{hs}", f"KT{hs}"], writes=[f"ps{pi}"])
                    if delta >= 4:
                        bti, op1 = 0, ALU.add
                    elif delta < 0:
                        bti, op1 = 0, ALU.subtract
                    else:
                        bti, op1 = 1 + delta, ALU.add
                    P.op("vector", lambda e: e.scalar_tensor_tensor(out=tmp[ti][:], in0=psum[pi][:], scalar=0.125, in1=bts[hs][:, bti, :], op0=ALU.mult, op1=op1),
                         reads=[f"ps{pi}", f"bt{hs}"], writes=[f"atmp{ti}"])
                    ci = hd * 64 + (delta + 32)
                    P.op("scalar", lambda e: e.activation(out=Eb[ti][:], in_=tmp[ti][:], func=AF.Exp, bias=cst[:, ci:ci + 1]), reads=[f"atmp{ti}", "cst"], writes=[f"Eb{ti}"])

                def unit_back(hd, qb, i, ulist, kbs):
                    hs = hd % 2
                    kb, c = ulist[i]
                    ti = i % NSL
                    P.op("tensor", lambda e: e.matmul(psum[pacc[2 * c]][:], lhsT=Vts[hs][:, kb, :], rhs=Eb[ti][:], start=(kb == kbs[0]), stop=(kb == kbs[-1])),
                         reads=[f"Eb{ti}", f"Vt{hs}"], writes=[f"ps{pacc[2 * c]}"])
                    P.op("tensor", lambda e: e.matmul(psum[pacc[2 * c + 1]][:], lhsT=ones[:], rhs=Eb[ti][:], start=(kb == kbs[0]), stop=(kb == kbs[-1])),
                         reads=[f"Eb{ti}", "ones"], writes=[f"ps{pacc[2 * c + 1]}"])

                def fin1():
                    P.op("vector", lambda e: e.reciprocal(out=rr[:], in_=psum[pacc[1]][:]), reads=[f"ps{pacc[1]}"], writes=["rr"])
                    P.op("vector", lambda e: e.tensor_tensor(out=o0[:], in0=psum[pacc[0]][:], in1=rr[:], op=ALU.mult), reads=[f"ps{pacc[0]}", "rr"], writes=["o0"])
                    P.op("vector", lambda e: e.reciprocal(out=rr[:], in_=psum[pacc[3]][:]), reads=[f"ps{pacc[3]}"], writes=["rr"])
                    P.op("vector", lambda e: e.tensor_tensor(out=o1[:], in0=psum[pacc[2]][:], in1=rr[:], op=ALU.mult), reads=[f"ps{pacc[2]}", "rr"], writes=["o1"])
                    P.op("vector", lambda e: e.scalar_tensor_tensor(out=o0[:], in0=o1[:], scalar=lsm[:, 2:3], in1=o0[:], op0=ALU.mult, op1=ALU.add), reads=["o0", "o1", "lsm"], writes=["o0"])
                    P.op("scalar", lambda e: e.activation(out=sq[:], in_=o0[:], func=AF.Square), reads=["o0"], writes=["sq"])

                def fin2(hd, qb, pi):
                    P.op("tensor", lambda e: e.matmul(psum[pi][:], lhsT=ones[:], rhs=sq[:], start=True, stop=True), reads=["sq", "ones"], writes=[f"ps{pi}"])
                    P.op("scalar", lambda e: e.activation(out=rr2[:], in_=psum[pi][:], func=AF.Sqrt, scale=1.0 / 128, bias=epsb[:, 1:2]), reads=[f"ps{pi}", "epsb"], writes=["rr2"])
                    P.op("vector", lambda e: e.reciprocal(out=rr2[:], in_=rr2[:]), reads=["rr2"], writes=["rr2"])
                    P.op("vector", lambda e: e.scalar_tensor_tensor(out=yo[:], in0=o0[:], scalar=sgt[:, 0:1], in1=rr2[:], op0=ALU.mult, op1=ALU.mult), reads=["o0", "sgt", "rr2"], writes=["yo"])
                    P.op("sync", lambda e: e.dma_start(out=yT[4 + hd][:, qb * 512:(qb + 1) * 512], in_=yo[:]), reads=["yo"], writes=["yT"], dma_key="yo")

                load_head(0)

                def kept_kbs(hd, qb):
                    smin = 2.0 ** (-2.0 * (hd + 1))
                    out = []
                    for kb in range(32):
                        delta = kb - 4 * qb
                        dmin = 128 * (delta - 4) + 1 if delta >= 4 else (128 * (-delta - 1) + 1 if delta < 0 else 0)
                        if smin * dmin < 60.0:
                            out.append(kb)
                    return out
                for hd in range(attn_heads):
                    for qb in range(attn_qb):
                        kbs = kept_kbs(hd, qb)
                        ulist = [(kb, c) for kb in kbs for c in range(2)]
                        NU = len(ulist)
                        warm(WARMN)
                        for i in range(NU + LA):
                            if i < NU:
                                unit_front(hd, qb, i, ulist)
                            if i >= LA:
                                unit_back(hd, qb, i - LA, ulist, kbs)
                                warm(NFILL)
                            if i == LA + 1 and pending[0] is not None:
                                ph, pq = pending[0]
                                pending[0] = None
                                fin2(ph, pq, 7)
                        fin1()
                        pending[0] = (hd, qb)
                        if qb == 1 and hd + 1 < attn_heads:
                            load_head(hd + 1)
                        if hd == attn_heads - 1 and qb == attn_qb - 1:
                            fin2(hd, qb, 7)
                            pending[0] = None
                P.fence()
        if do_rwkv:
            with ExitStack() as st3:
                emit_rwkv(nc, P, st3, PT_d, yT, psum, pst, state, next_ps, ident, ones, epsb, rw_heads)
            P.fence()


def build_A(**kw):
    nc = bass.Bass("TRN2", target_bir_lowering=False)
    yT = nc.dram_tensor("yT", [8, 128, S], BF16, kind="ExternalOutput").ap()
    P = Prog(nc)
    with ExitStack() as st:
        sh = make_shared(nc, P, st)
        body_A(nc, P, st, sh, yT, **kw)
        counts = P.emit(st)
        print("A ops", counts, "waits", P.n_waits)
    return nc


def build_fused():
    nc = bass.Bass("TRN2", target_bir_lowering=False)
    yTi = nc.dram_tensor("yTi", [8, 128, S], BF16).ap()
    G = nc.dram_tensor("Gy", [8, 4, 128, S], BF16).ap()
    sel = nc.dram_tensor("sel", [128, 4], F32, kind="ExternalInput").ap()
    P = Prog(nc)
    with ExitStack() as st:
        sh = make_shared(nc, P, st)
        body_A(nc, P, st, sh, yTi)
        for k in range(8):
            P.op("gpsimd", lambda e, k=k: e.collective_compute("AllGather", ALU.bypass, replica_groups=[[0, 1, 2, 3], [4, 5, 6, 7]],
                                                               ins=[yTi[k].opt()], outs=[G[k].rearrange("g p t -> (g p) t").opt()]),
                 reads=["yT"], writes=["G"], dma_key="cc", inc=1)
        with ExitStack() as st4:
            body_B(nc, P, st4, sh, ("gather", G, sel))
        counts = P.emit(st)
        print("fused ops", counts, "waits", P.n_waits)
    return nc


def slopes():
    H = 16
    return np.exp2(-8.0 * np.arange(1, H + 1, dtype=np.float32) / H).astype(np.float32)


def relayout_A(inp, c, l=0):
    b, g = c // 4, c % 4
    w_in = inp["w_in"][l]
    cols = []
    for part in range(3):
        cols.append(np.arange(part * 2048 + 512 * g, part * 2048 + 512 * (g + 1)))
    lo = 3 * 2048
    cols.append(np.arange(lo, lo + 96)); pad1 = 32
    cols.append(np.arange(lo + 96, lo + 192)); pad2 = 32
    cols.append(np.arange(lo + 192, lo + 448))
    for part in range(3):
        cols.append(np.concatenate([np.arange(NR + part * 2048 + (4 * j + g) * 128, NR + part * 2048 + (4 * j + g + 1) * 128) for j in range(4)]))
    W = np.zeros((D, NCH * 128), np.float32)
    W[:, 0:1536] = w_in[:, np.concatenate(cols[0:3])]
    W[:, 1536:1536 + 96] = w_in[:, cols[3]]
    W[:, 1664:1664 + 96] = w_in[:, cols[4]]
    W[:, 1792:2048] = w_in[:, cols[5]]
    W[:, 2048:3584] = w_in[:, np.concatenate(cols[6:9])]
    wA = np.ascontiguousarray(W.reshape(32, 128, NCH, 128).transpose(2, 1, 0, 3))
    gain = np.ascontiguousarray(np.broadcast_to(inp["attn_pre_norm"][l][None, :], (128, D))).astype(np.float32)
    ident = np.eye(128, dtype=np.float32).astype(ml_dtypes.bfloat16)
    ones = np.ones((128, 128), np.float32).astype(ml_dtypes.bfloat16)
    sl = slopes()
    kk = np.arange(128, dtype=np.float32)[:, None]
    qq = np.arange(512, dtype=np.float32)[None, :]
    abias = np.zeros((4, 5, 128, 512), np.float32)
    acst = np.zeros((128, 256), np.float32)
    for hd in range(4):
        s_ = sl[4 * hd + g]
        abias[hd, 0] = -s_ * (kk - qq)
        for dl in range(4):
            abias[hd, 1 + dl] = -s_ * np.abs(128.0 * dl + kk - qq)
        for delta in range(-32, 32):
            if delta >= 4:
                v = -s_ * 128.0 * delta
            elif delta < 0:
                v = s_ * 128.0 * delta
            else:
                v = 0.0
            acst[:, hd * 64 + delta + 32] = v
    lam = np.stack([inp[k][l] for k in ("lambda_q1", "lambda_k1", "lambda_q2", "lambda_k2")])
    lam = np.ascontiguousarray(np.broadcast_to(lam[None], (128, 4, 64))).astype(np.float32)
    subg = np.ascontiguousarray(inp["subln_gain"][l].reshape(128, 1)).astype(np.float32)
    return dict(xb=np.ascontiguousarray(inp["x"][b]), wA=wA, abias=abias, acst=acst, lam=lam, subg=subg)


def kernel(**inp):
    inp = {k: np.asarray(v) for k, v in inp.items()}
    n = 8
    nc = build_fused()
    W = relayout_B(inp)
    in_maps = []
    for c in range(n):
        b, g = c // 4, c % 4
        im = dict(W)
        im.update(relayout_A(inp, c))
        im.update(relayout_rwkv(inp, c))
        im["xo"] = np.ascontiguousarray(inp["x"][b, 1024 * g:1024 * (g + 1)])
        sel = np.zeros((128, 4), np.float32)
        sel[:, g] = 1.0
        im["sel"] = sel
        in_maps.append(im)
    res = run_bass_kernel_spmd(nc, in_maps, core_ids=list(range(n)))
    out = np.zeros((2, S, D), np.float32)
    for c in range(n):
        b, g = c // 4, c % 4
        out[b, 1024 * g:1024 * (g + 1)] = res.results[c]["out"]
    return out
```

```python
import math
import bisect
import numpy as np
import ml_dtypes
from contextlib import ExitStack
import concourse.bass as bass
import concourse.mybir as mybir
from concourse.bass_utils import run_bass_kernel_spmd


ENGS = ("tensor", "vector", "scalar", "gpsimd", "sync")
ROT = 12000


class Prog:
    def __init__(self, nc):
        self.nc = nc
        self.ops = []

    def op(self, eng, fn, reads=(), writes=(), dma_key=None, inc=None):
        self.ops.append(dict(eng=eng, fn=fn, reads=tuple(reads), writes=tuple(writes),
                             dma=dma_key is not None, key=dma_key, inc=inc))
        return len(self.ops) - 1

    def fence(self, eng="vector"):
        self.ops.append(dict(eng=eng, fn=self.fence_fn, reads=(), writes="ALL", dma=False, key=None, inc=None))

    def emit(self, stack):
        nc = self.nc
        ops = self.ops
        n = len(ops)
        allkeys = set()
        for o in ops:
            if o["writes"] != "ALL":
                allkeys.update(o["reads"]); allkeys.update(o["writes"])
        allkeys = tuple(sorted(allkeys, key=str))
        for o in ops:
            if o["writes"] == "ALL":
                o["writes"] = allkeys
        last_w = {}
        readers = {}
        deps = [None] * n
        for i, o in enumerate(ops):
            d = set()
            for r in o["reads"]:
                if r in last_w:
                    d.add(last_w[r])
            for w in o["writes"]:
                if w in last_w:
                    d.add(last_w[w])
                for j in readers.get(w, ()):
                    d.add(j)
            d.discard(i)
            dd = []
            for j in d:
                oj = ops[j]
                if (not oj["dma"]) and (not o["dma"]) and oj["eng"] == o["eng"] == "tensor":
                    continue
                dd.append(j)
            deps[i] = dd
            for r in o["reads"]:
                readers.setdefault(r, []).append(i)
            for w in o["writes"]:
                last_w[w] = i
                readers[w] = []
        needed = [False] * n
        for i in range(n):
            for j in deps[i]:
                needed[j] = True
        sem_handles = {}

        def get_sem(name):
            if name not in sem_handles:
                sem_handles[name] = stack.enter_context(nc.semaphore(name))
            return sem_handles[name]

        cnt = {}
        sig = [None] * n
        dma_cum_at = {}
        for i, o in enumerate(ops):
            if o["dma"]:
                base = "d_" + str(o["key"])
                inc = o["inc"] or 16
                lim = ROT
            else:
                if not needed[i]:
                    continue
                base = "e_" + o["eng"]
                inc = 1
                lim = ROT
            g, c = cnt.get(base, (0, 0))
            if c + inc > lim * (16 if o["dma"] else 1):
                g, c = g + 1, 0
            c += inc
            cnt[base] = (g, c)
            sig[i] = (base + "_" + str(g), c)
            if o["dma"]:
                dma_cum_at.setdefault(base, []).append((i, sig[i][0], c))
        import bisect
        dma_idx = {k: [t[0] for t in v] for k, v in dma_cum_at.items()}
        waits = [None] * n
        waited = {e: {} for e in ENGS}
        for i, o in enumerate(ops):
            need = {}
            for j in deps[i]:
                oj = ops[j]
                if oj["dma"]:
                    base = "d_" + str(oj["key"])
                    lst = dma_cum_at[base]
                    pos = bisect.bisect_left(dma_idx[base], i) - 1
                    sname_j, vj = sig[j]
                    k = pos
                    while lst[k][1] != sname_j:
                        k -= 1
                    sname, val = lst[k][1], lst[k][2]
                else:
                    sname, val = sig[j]
                if need.get(sname, 0) < val:
                    need[sname] = val
            wl = []
            wd = waited[o["eng"]]
            for sname, val in need.items():
                if wd.get(sname, 0) >= val:
                    continue
                wd[sname] = val
                wl.append((sname, val))
            waits[i] = wl
        self.n_waits = sum(len(w) for w in waits)
        per_eng = {e: [i for i, o in enumerate(ops) if o["eng"] == e] for e in ENGS}
        block = stack.enter_context(nc.Block())

        def body(engname):
            def f(eng):
                for i in per_eng[engname]:
                    for sname, val in waits[i]:
                        eng.wait_ge(get_sem(sname), val)
                    inst = ops[i]["fn"](eng)
                    if sig[i] is not None:
                        inst.then_inc(get_sem(sig[i][0]), (ops[i]["inc"] or 16) if ops[i]["dma"] else 1)
            return f

        for i in range(n):
            if sig[i] is not None:
                get_sem(sig[i][0])
        block.tensor(body("tensor"))
        block.vector(body("vector"))
        block.scalar(body("scalar"))
        block.gpsimd(body("gpsimd"))
        block.sync(body("sync"))
        return {e: len(v) for e, v in per_eng.items()}


F32 = mybir.dt.float32
BF16 = mybir.dt.bfloat16
AF = mybir.ActivationFunctionType
ALU = mybir.AluOpType
D = 4096
DFF = 16384
EPS = 1e-6
NTOK = 1024
TP = 512
FB = 256
NFB = DFF // FB


def make_shared(nc, P, st):
    ident_d = nc.dram_tensor("ident", [128, 128], BF16, kind="ExternalInput").ap()
    ones_d = nc.dram_tensor("ones", [128, 128], BF16, kind="ExternalInput").ap()
    gains = nc.dram_tensor("gains", [4, 128, D], F32, kind="ExternalInput").ap()
    ident = st.enter_context(nc.sbuf_tensor("ident_s", [128, 128], BF16))
    ones = st.enter_context(nc.sbuf_tensor("ones_s", [128, 128], BF16))
    small = st.enter_context(nc.sbuf_tensor("small", [128, 16], F32))
    dummy = st.enter_context(nc.sbuf_tensor("fdummy", [128, 8], F32))
    epsb = st.enter_context(nc.sbuf_tensor("epsb", [128, 2], F32))
    P.fence_fn = lambda e: e.memset(dummy[:], 0.0)
    P.op("vector", lambda e: e.memset(epsb[:, 0:1], EPS), writes=["epsb"])
    P.op("vector", lambda e: e.memset(epsb[:, 1:2], 1e-5), writes=["epsb"])
    P.op("sync", lambda e: e.dma_start(out=ident[:], in_=ident_d), writes=["ident"], dma_key="ident")
    P.op("sync", lambda e: e.dma_start(out=ones[:], in_=ones_d), writes=["ones"], dma_key="ones")
    psum = [st.enter_context(nc.psum_tensor(f"ps{i}", [128, 512], F32)) for i in range(8)]
    pst = [psum[6 + i][:, :].bitcast(BF16) for i in range(2)]
    state = dict(ps=0, pt=0)

    def next_ps():
        i = state["ps"]; state["ps"] = (i + 1) % 6
        return i
    return dict(ident=ident, ones=ones, small=small, epsb=epsb, psum=psum, pst=pst, state=state, next_ps=next_ps, gains=gains)


def body_B(nc, P, st, sh, ysrc, npass=2, stages=(0, 1, 2, 3, 4, 5), nfb=NFB):
    x = nc.dram_tensor("xo", [NTOK, D], F32, kind="ExternalInput").ap()
    wg = nc.dram_tensor("wg", [64, 128, 32, 128], F32, kind="ExternalInput").ap()
    wu = nc.dram_tensor("wu", [64, 128, 16, 128], F32, kind="ExternalInput").ap()
    wo = nc.dram_tensor("wo", [16, 128, 32, 256], F32, kind="ExternalInput").ap()
    w1 = nc.dram_tensor("w1", [NFB, 128, 32, FB], F32, kind="ExternalInput").ap()
    w2 = nc.dram_tensor("w2", [NFB, 128, FB // 128, D], F32, kind="ExternalInput").ap()
    out = nc.dram_tensor("out", [NTOK, D], F32, kind="ExternalOutput").ap()
    x1_d = nc.dram_tensor("x1_d", [NTOK, D], F32).ap()
    gains, ident, small, epsb = sh["gains"], sh["ident"], sh["small"], sh["epsb"]
    psum, pst, state, next_ps = sh["psum"], sh["pst"], sh["state"], sh["next_ps"]
    if True:
        arena = st.enter_context(nc.sbuf_tensor("arena", [128, 172 * 256], F32))
        if ysrc[0] == "gather":
            selt = st.enter_context(nc.sbuf_tensor("selt", [128, 4], F32))
            P.op("sync", lambda e: e.dma_start(out=selt[:], in_=ysrc[2]), writes=["selt"], dma_key="selt")

        def AV(off_kib, size_kib, dt):
            v = arena[:, off_kib * 256:(off_kib + size_kib) * 256]
            return v.bitcast(BF16) if dt == BF16 else v

        bufA = AV(0, 32, BF16).rearrange("p (k t) -> p k t", k=32)
        bufB = AV(32, 32, BF16).rearrange("p (k t) -> p k t", k=32)
        bufM = AV(64, 32, BF16).rearrange("p (k t) -> p k t", k=32)
        bufZ = AV(96, 64, F32).rearrange("p (a d) -> p a d", a=4)
        wbuf = [AV(96 + 24 * s, 24, BF16) for s in range(2)]
        wo_v = [AV(32 + 16 * s, 16, BF16).rearrange("p (k j) -> p k j", k=32) for s in range(2)]
        w1_v = [AV(64 + 16 * s, 16, BF16).rearrange("p (k j) -> p k j", k=32) for s in range(2)]
        w2_v = [AV(32 + 16 * s, 16, BF16).rearrange("p (k j) -> p k j", k=FB // 128) for s in range(2)]
        xt_R3, gt_R3 = AV(64, 16, F32), AV(80, 16, F32)
        xt_R2, gt_R2 = AV(32, 16, F32), AV(48, 16, F32)
        hb_R4 = AV(144, 8, BF16)
        hb_R3 = AV(64, 8, BF16)
        t1 = [AV(160 + 2 * i, 2, F32) for i in range(2)]
        t2 = [AV(164 + 2 * i, 2, F32) for i in range(2)]
        ub = [AV(168 + 2 * i, 2, BF16).rearrange("p (k t) -> p k t", k=FB // 128) for i in range(2)]
        def load_gain(idx, gt):
            P.op("sync", lambda e: e.dma_start(out=gt, in_=gains[idx]), reads=["gains"], writes=["gt"], dma_key="gt")


        def rstd_from(src_ap, src_key, col, hb):
            P.op("vector", lambda e: e.memset(small[:, col:col + 1], 0.0), writes=[f"sm{col}"])
            P.op("scalar", lambda e: e.activation(out=hb, in_=src_ap, func=AF.Square, accum_out=small[:, col:col + 1]),
                 reads=[src_key, f"sm{col}"], writes=["hb", f"sm{col}"])
            P.op("scalar", lambda e: e.activation(out=small[:, col:col + 1], in_=small[:, col:col + 1], func=AF.Sqrt, scale=1.0 / D, bias=epsb[:, 0:1]),
                 reads=[f"sm{col}", "epsb"], writes=[f"sm{col}"])
            P.op("vector", lambda e: e.reciprocal(out=small[:, col:col + 1], in_=small[:, col:col + 1]), reads=[f"sm{col}"], writes=[f"sm{col}"])

        def norm_transpose(src_ap, src_key, col, gt, hb, tt):
            P.op("vector", lambda e: e.scalar_tensor_tensor(out=hb, in0=src_ap, scalar=small[:, col:col + 1], in1=gt,
                                                            op0=ALU.mult, op1=ALU.mult), reads=[src_key, f"sm{col}", "gt"], writes=["hb"])
            for k8 in range(4):
                pi = state["pt"]; state["pt"] ^= 1
                for j in range(8):
                    kc = k8 * 8 + j
                    P.op("tensor", lambda e, kc=kc, j=j, pi=pi: e.transpose(out=pst[pi][:, j * 128:(j + 1) * 128], in_=hb[:, kc * 128:(kc + 1) * 128], identity=ident[:]),
                         reads=["hb", "ident"], writes=[f"ps{6 + pi}"])
                dst = bufA[:, k8 * 8:(k8 + 1) * 8, tt * 128:(tt + 1) * 128]
                srcp = pst[pi][:, :].rearrange("p (k t) -> p k t", k=8)
                if k8 % 2 == 0:
                    P.op("scalar", lambda e, dst=dst, srcp=srcp: e.activation(out=dst, in_=srcp, func=AF.Copy), reads=[f"ps{6 + pi}"], writes=["bufA"])
                else:
                    P.op("vector", lambda e, dst=dst, srcp=srcp: e.tensor_copy(out=dst, in_=srcp), reads=[f"ps{6 + pi}"], writes=["bufA"])

        for ps_i in range(npass):
            tok0 = ps_i * TP
            if ps_i > 0 or ysrc[0] != "gather":
                P.fence()
            if 0 in stages:
                xt, gt, hb = xt_R3, gt_R3, hb_R4
                load_gain(0, gt)
                for tt in range(4):
                    r0 = tok0 + tt * 128
                    P.op("sync", lambda e, r0=r0, xt=xt: e.dma_start(out=xt, in_=x[r0:r0 + 128, :]), writes=["xt"], dma_key="xt")
                    rstd_from(xt, "xt", 0, hb)
                    norm_transpose(xt, "xt", 0, gt, hb, tt)
                if ysrc[0] == "input":
                    P.op("sync", lambda e, tok0=tok0: e.dma_start(out=bufB, in_=ysrc[1][:, :, tok0:tok0 + TP].rearrange("k p t -> p k t")), writes=["bufB"], dma_key="bufB")
                else:
                    G = ysrc[1]
                    cand = AV(96, 32, BF16).rearrange("p (k t) -> p k t", k=32)
                    for q in range(4):
                        t0 = 1024 * q + tok0
                        for part in range(2):
                            dstv = cand[:, 16 * part:16 * (part + 1), :].rearrange("p (g k) t -> p g k t", g=4) if part == 0 else \
                                cand[:, 16 * part:16 * (part + 1), :].rearrange("p (k g) t -> p g k t", g=4)
                            for gq in range(4):
                                P.op("sync", lambda e, dstv=dstv, part=part, t0=t0, gq=gq: e.dma_start(out=dstv[:, gq, :, :], in_=G[4 * part:4 * part + 4, gq, :, t0:t0 + TP].rearrange("k p t -> p k t")),
                                     reads=["G"], writes=["cand"], dma_key="cand")
                        if q == 0:
                            P.op("vector", lambda e: e.tensor_scalar(out=bufB, in0=cand, scalar1=selt[:, 0:1], scalar2=None, op0=ALU.mult), reads=["cand", "selt"], writes=["bufB"])
                        else:
                            P.op("vector", lambda e, q=q: e.scalar_tensor_tensor(out=bufB, in0=cand, scalar=selt[:, q:q + 1], in1=bufB, op0=ALU.mult, op1=ALU.add), reads=["cand", "selt", "bufB"], writes=["bufB"])
            P.fence()
            if 1 in stages:
                for cc in range(32):
                    s = cc % 2
                    wb = wbuf[s]
                    vgA = wb[:, 0:4096].rearrange("p (k j) -> p k j", k=32)
                    vgB = wb[:, 4096:8192].rearrange("p (k j) -> p k j", k=32)
                    vuA = wb[:, 8192:10240].rearrange("p (k j) -> p k j", k=16)
                    vuB = wb[:, 10240:12288].rearrange("p (k j) -> p k j", k=16)
                    for (dst, src) in ((vgA, wg[cc]), (vgB, wg[32 + cc]), (vuA, wu[cc]), (vuB, wu[32 + cc])):
                        P.op("gpsimd", lambda e, dst=dst, src=src: e.dma_start(out=dst, in_=src, max_dma_last_dim=4096), writes=[f"wbuf{s}"], dma_key=f"wbuf{s}")
                    pgA, pgB, puA, puB = next_ps(), next_ps(), next_ps(), next_ps()
                    for (pi, wv, nk, src, koff) in ((pgA, vgA, 32, bufA, 0), (puA, vuA, 16, bufB, 0), (pgB, vgB, 32, bufA, 0), (puB, vuB, 16, bufB, 16)):
                        for kc in range(nk):
                            P.op("tensor", lambda e, kc=kc, pi=pi, wv=wv, nk=nk, src=src, koff=koff: e.matmul(psum[pi][:], lhsT=wv[:, kc, :], rhs=src[:, koff + kc, :], start=(kc == 0), stop=(kc == nk - 1)),
                                 reads=[f"wbuf{s}", "bufA", "bufB"], writes=[f"ps{pi}"])
                    P.op("scalar", lambda e, pgA=pgA, s=s: e.activation(out=t1[s], in_=psum[pgA][:], func=AF.Sigmoid), reads=[f"ps{pgA}"], writes=[f"t1_{s}"])
                    P.op("vector", lambda e, puA=puA, s=s: e.tensor_tensor(out=t1[s], in0=t1[s], in1=psum[puA][:], op=ALU.mult), reads=[f"ps{puA}", f"t1_{s}"], writes=[f"t1_{s}"])
                    P.op("scalar", lambda e, pgB=pgB, s=s: e.activation(out=t2[s], in_=psum[pgB][:], func=AF.Sigmoid), reads=[f"ps{pgB}"], writes=[f"t2_{s}"])
                    P.op("vector", lambda e, puB=puB, s=s: e.tensor_tensor(out=t2[s], in0=t2[s], in1=psum[puB][:], op=ALU.mult), reads=[f"ps{puB}", f"t2_{s}"], writes=[f"t2_{s}"])
                    P.op("vector", lambda e, cc=cc, s=s: e.tensor_tensor(out=bufM[:, cc, :], in0=t1[s], in1=t2[s], op=ALU.add), reads=[f"t1_{s}", f"t2_{s}"], writes=["bufM"])
            P.fence()
            if 2 in stages:
                for nb in range(16):
                    s = nb % 2
                    wv = wo_v[s]
                    for half in range(2):
                        P.op("gpsimd", lambda e, wv=wv, nb=nb, half=half: e.dma_start(out=wv[:, half * 16:(half + 1) * 16, :], in_=wo[nb][:, half * 16:(half + 1) * 16, :], max_dma_last_dim=4096),
                             writes=[f"wo{s}"], dma_key=f"wo{s}")
                    for tt in range(4):
                        pi = next_ps()
                        for kc in range(32):
                            P.op("tensor", lambda e, kc=kc, pi=pi, wv=wv, tt=tt: e.matmul(psum[pi][:, 0:256], lhsT=bufM[:, kc, tt * 128:(tt + 1) * 128], rhs=wv[:, kc, :], start=(kc == 0), stop=(kc == 31)),
                                 reads=[f"wo{s}", "bufM"], writes=[f"ps{pi}"])
                        dst = bufZ[:, tt, nb * 256:(nb + 1) * 256]
                        if (tt + nb) % 2 == 0:
                            P.op("scalar", lambda e, dst=dst, pi=pi: e.activation(out=dst, in_=psum[pi][:, 0:256], func=AF.Copy), reads=[f"ps{pi}"], writes=[f"bufZ{tt}_{nb % 8}"])
                        else:
                            P.op("vector", lambda e, dst=dst, pi=pi: e.tensor_copy(out=dst, in_=psum[pi][:, 0:256]), reads=[f"ps{pi}"], writes=[f"bufZ{tt}_{nb % 8}"])
            P.fence()
            if 3 in stages:
                xt, gt, hb = xt_R2, gt_R2, hb_R3
                load_gain(1, gt)
                for tt in range(4):
                    r0 = tok0 + tt * 128
                    zt = bufZ[:, tt, :]
                    rstd_from(zt, f"bufZ{tt}", 1, hb)
                    P.op("sync", lambda e, r0=r0, xt=xt: e.dma_start(out=xt, in_=x[r0:r0 + 128, :]), writes=["xt"], dma_key="xt")
                    P.op("vector", lambda e, zt=zt, gt=gt: e.scalar_tensor_tensor(out=zt, in0=zt, scalar=small[:, 1:2], in1=gt, op0=ALU.mult, op1=ALU.mult),
                         reads=[f"bufZ{tt}", "sm1", "gt"], writes=[f"bufZ{tt}"])
                    P.op("vector", lambda e, zt=zt, xt=xt: e.tensor_tensor(out=zt, in0=zt, in1=xt, op=ALU.add), reads=[f"bufZ{tt}", "xt"], writes=[f"bufZ{tt}"])
                    P.op("sync", lambda e, r0=r0, zt=zt: e.dma_start(out=x1_d[r0:r0 + 128, :], in_=zt), reads=[f"bufZ{tt}"], writes=[f"x1d{ps_i}_{tt}"], dma_key=f"x1st{tt}")
                load_gain(2, gt)
                for tt in range(4):
                    zt = bufZ[:, tt, :]
                    rstd_from(zt, f"bufZ{tt}", 2, hb)
                    norm_transpose(zt, f"bufZ{tt}", 2, gt, hb, tt)
            P.fence()
            if 4 in stages:
                for fb in range(nfb):
                    s = fb % 2
                    w1v = w1_v[s]
                    w2v = w2_v[s]
                    for half in range(2):
                        P.op("gpsimd", lambda e, w1v=w1v, fb=fb, half=half: e.dma_start(out=w1v[:, half * 16:(half + 1) * 16, :], in_=w1[fb][:, half * 16:(half + 1) * 16, :], max_dma_last_dim=4096),
                             writes=[f"w1_{s}"], dma_key=f"w1_{s}")
                    for kc2 in range(FB // 128):
                        P.op("gpsimd", lambda e, w2v=w2v, fb=fb, kc2=kc2: e.dma_start(out=w2v[:, kc2, :], in_=w2[fb][:, kc2, :], max_dma_last_dim=4096),
                             writes=[f"w2_{s}"], dma_key=f"w2_{s}")
                    for fc in range(FB // 128):
                        pi = next_ps()
                        for kc in range(32):
                            P.op("tensor", lambda e, kc=kc, pi=pi, w1v=w1v, fc=fc: e.matmul(psum[pi][:], lhsT=w1v[:, kc, fc * 128:(fc + 1) * 128], rhs=bufA[:, kc, :], start=(kc == 0), stop=(kc == 31)),
                                 reads=[f"w1_{s}", "bufA"], writes=[f"ps{pi}"])
                        P.op("scalar", lambda e, pi=pi, s=s: e.activation(out=t1[s], in_=psum[pi][:], func=AF.Relu), reads=[f"ps{pi}"], writes=[f"t1_{s}"])
                        P.op("vector", lambda e, s=s, fc=fc: e.tensor_tensor(out=ub[s][:, fc, :], in0=t1[s], in1=t1[s], op=ALU.mult), reads=[f"t1_{s}"], writes=[f"ub{s}"])
                    for tt in range(4):
                        for nb in range(8):
                            pi = next_ps()
                            for kc2 in range(FB // 128):
                                P.op("tensor", lambda e, kc2=kc2, pi=pi, w2v=w2v, tt=tt, nb=nb, s=s: e.matmul(psum[pi][:], lhsT=ub[s][:, kc2, tt * 128:(tt + 1) * 128], rhs=w2v[:, kc2, nb * 512:(nb + 1) * 512],
                                                                                                          start=(kc2 == 0), stop=(kc2 == FB // 128 - 1)),
                                     reads=[f"w2_{s}", f"ub{s}"], writes=[f"ps{pi}"])
                            dst = bufZ[:, tt, nb * 512:(nb + 1) * 512]
                            key = f"bufZ{tt}_{nb}"
                            if fb == 0:
                                P.op("vector", lambda e, dst=dst, pi=pi: e.tensor_copy(out=dst, in_=psum[pi][:]), reads=[f"ps{pi}"], writes=[key])
                            else:
                                P.op("vector", lambda e, dst=dst, pi=pi: e.tensor_tensor(out=dst, in0=dst, in1=psum[pi][:], op=ALU.add), reads=[f"ps{pi}", key], writes=[key])
            P.fence()
            if 5 in stages:
                xt, gt, hb = xt_R3, gt_R3, AV(32, 8, BF16)
                load_gain(3, gt)
                for tt in range(4):
                    r0 = tok0 + tt * 128
                    zt = bufZ[:, tt, :]
                    rstd_from(zt, f"bufZ{tt}", 3, hb)
                    P.op("sync", lambda e, r0=r0, xt=xt: e.dma_start(out=xt, in_=x1_d[r0:r0 + 128, :]), reads=[f"x1d{ps_i}_{tt}"], writes=["xt"], dma_key="xt")
                    P.op("vector", lambda e, zt=zt, gt=gt: e.scalar_tensor_tensor(out=zt, in0=zt, scalar=small[:, 3:4], in1=gt, op0=ALU.mult, op1=ALU.mult),
                         reads=[f"bufZ{tt}", "sm3", "gt"], writes=[f"bufZ{tt}"])
                    P.op("vector", lambda e, zt=zt, xt=xt: e.tensor_tensor(out=zt, in0=zt, in1=xt, op=ALU.add), reads=[f"bufZ{tt}", "xt"], writes=[f"bufZ{tt}"])
                    P.op("sync", lambda e, r0=r0, zt=zt: e.dma_start(out=out[r0:r0 + 128, :], in_=zt), reads=[f"bufZ{tt}"], writes=["out"], dma_key=f"x1st{tt}")
        P.fence()


def build_B(npass=2, stages=(0, 1, 2, 3, 4, 5), nfb=NFB):
    nc = bass.Bass("TRN2", target_bir_lowering=False)
    yT = nc.dram_tensor("yT", [32, 128, NTOK], BF16, kind="ExternalInput").ap()
    P = Prog(nc)
    with ExitStack() as st:
        sh = make_shared(nc, P, st)
        body_B(nc, P, st, sh, ("input", yT), npass=npass, stages=stages, nfb=nfb)
        counts = P.emit(st)
        print("B ops", counts, "waits", P.n_waits)
    return nc


def relayout_B(inp, l=0):
    w_in = inp["w_in"][l]
    NR = 6592
    gcol0 = NR + 3 * 2048
    Wg = w_in[:, gcol0:gcol0 + 8192]
    wg = np.ascontiguousarray(Wg.reshape(32, 128, 64, 128).transpose(2, 1, 0, 3))
    Wu = np.concatenate([inp["w_up_rwkv"][l], inp["w_up_diff"][l]], axis=1)
    wu = np.ascontiguousarray(Wu.reshape(16, 128, 64, 128).transpose(2, 1, 0, 3))
    wo = np.ascontiguousarray(inp["w_out"][l].reshape(32, 128, 16, 256).transpose(2, 1, 0, 3))
    w1 = np.ascontiguousarray(inp["w_mlp_in"][l].reshape(32, 128, NFB, FB).transpose(2, 1, 0, 3))
    w2 = np.ascontiguousarray(inp["w_mlp_out"][l].reshape(NFB, FB // 128, 128, D).transpose(0, 2, 1, 3))
    gains = np.stack([np.broadcast_to(inp[k][l][None, :], (128, D)) for k in ("attn_pre_norm", "attn_post_norm", "mlp_pre_norm", "mlp_post_norm")]).astype(np.float32)
    ident = np.eye(128, dtype=np.float32).astype(ml_dtypes.bfloat16)
    ones = np.ones((128, 128), np.float32).astype(ml_dtypes.bfloat16)
    return dict(wg=wg, wu=wu, wo=wo, w1=w1, w2=w2, gains=np.ascontiguousarray(gains), ident=ident, ones=ones)


F32 = mybir.dt.float32
BF16 = mybir.dt.bfloat16
AF = mybir.ActivationFunctionType
ALU = mybir.AluOpType
S = 4096
C = 128
BLK = 512
NB = S // BLK
EPS_GN = 64e-5
DEC = -0.6065306597126334
RWMODE = 3


def emit_rwkv(nc, P, st, PT_d, yT, psum, pst, state, next_ps, ident, ones, epsb, rw_heads):
    par_d = nc.dram_tensor("par", [64, 8, 16], F32, kind="ExternalInput").ap()
    parl_d = nc.dram_tensor("parl", [128, 4, 2], F32, kind="ExternalInput").ap()
    lw2_d = nc.dram_tensor("lw2", [96, 4, 512], F32, kind="ExternalInput").ap()
    g2_d = nc.dram_tensor("g2", [128, 2, 512], F32, kind="ExternalInput").ap()
    masks_d = nc.dram_tensor("masks", [128, 2, 640], F32, kind="ExternalInput").ap()
    mreset_d = nc.dram_tensor("mreset", [64, BLK], F32, kind="ExternalInput").ap()

    def sb(name, shape, dt):
        return st.enter_context(nc.sbuf_tensor(name, shape, dt))
    par = sb("par_s", [64, 8, 20], F32)
    parl = sb("parl_s", [128, 4, 3], F32)
    lw2b = sb("lw2b", [96, 4, 512], BF16)
    g2b = sb("g2b", [128, 2, 512], BF16)
    masks = sb("masks_s", [128, 2, 640], F32)
    mreset = sb("mreset_s", [64, BLK], F32)
    twT = sb("twT", [128, S], BF16)
    daT = sb("daT", [128, S], BF16)
    sgT = sb("sgT", [128, 2, S], BF16)
    RAW = sb("RAW", [128, S // 2 + 2], F32)
    SH = sb("SH", [128, S // 2], F32)
    Rb16 = sb("R16", [64, S], BF16)
    Kb16 = sb("K16", [64, S], BF16)
    Vb16 = sb("V16", [64, S], BF16)
    Vt = sb("rVt", [128, 32, 64], BF16)
    KKN = sb("KKN", [64, S], BF16)
    OT = sb("OT", [64, S], F32)
    BONV = sb("BONV", [64, S], BF16)
    gnb = sb("gnb", [64, 1], F32)
    tiny = sb("tinyb", [64, 1], F32)
    LW = sb("LW", [64, BLK], F32)
    CI = sb("CI", [64, BLK], F32)
    CE = sb("CE", [64, BLK], F32)
    Ece = sb("Ece", [64, BLK], F32)
    Enci = sb("Enci", [64, BLK], F32)
    Eci = [sb(f"Eci{i}", [64, BLK], F32) for i in range(2)]
    At = sb("At", [64, BLK], F32)
    T1 = sb("T1", [64, BLK], F32)
    KD = sb("KD", [64, BLK], F32)
    T2 = sb("T2b", [64, BLK], BF16)
    ops4 = [sb(f"ops4_{i}", [64, 4, BLK], BF16) for i in range(2)]
    tok3 = [sb(f"tok3_{i}", [128, 12, 64], BF16) for i in range(2)]
    G1s = [sb(f"G1s{c}", [128, 384], BF16) for c in range(8)]
    G2s = [sb(f"G2s{c}", [128, 256], BF16) for c in range(8)]
    MM = [[sb(f"MM{c}_{i}", [128, 256], BF16) for i in range(2)] for c in range(8)]
    Qs = [[sb(f"Qs{c}_{i}", [128, 128], BF16) for i in range(2)] for c in range(8)]
    IMs = [[sb(f"IM{c}_{i}", [128, 128], BF16) for i in range(2)] for c in range(8)]
    nW1T = [sb(f"nW1T{c}", [64, 128], BF16) for c in range(8)]
    nXs = [sb(f"nXs{c}", [128, 64], BF16) for c in range(8)]
    Us = sb("Us", [128, 64], BF16)
    Hf = sb("Hf", [64, 64], F32)
    Hb = sb("Hb", [64, 64], BF16)
    yo = sb("ryo", [64, BLK], BF16)
    ob = sb("ob", [64, BLK], BF16)
    dd = sb("dd", [64, BLK], F32)
    ones64 = ones[0:64, 0:64]
    id64 = ident[0:64, 0:64]

    def V(fn, reads, writes):
        P.op("vector", fn, reads=reads, writes=writes)

    def A(fn, reads, writes):
        P.op("scalar", fn, reads=reads, writes=writes)

    def T(fn, reads, writes):
        P.op("tensor", fn, reads=reads, writes=writes)

    for (dst, src, k, q) in ((par[:, :, 0:16], par_d, "par", "sync"), (parl[:, :, 0:2], parl_d, "parl", "sync"), (masks[:], masks_d, "masks", "sync"),
                             (mreset[:], mreset_d, "mreset", "sync"), (lw2b[:], lw2_d, "lw2b", "gpsimd"), (g2b[:], g2_d, "g2b", "gpsimd")):
        P.op(q, lambda e, dst=dst, src=src: e.dma_start(out=dst, in_=src), writes=[k], dma_key=k)
    V(lambda e: e.memset(gnb[:], EPS_GN), [], ["gnb"])
    V(lambda e: e.memset(tiny[:], 1e-24), [], ["gnb"])
    for i in range(3):
        V(lambda e, i=i: e.tensor_tensor(out=par[:, :, 16 + i], in0=par[:, :, 2 * i], in1=par[:, :, 2 * i + 1], op=ALU.add), ["par"], ["par"])
        V(lambda e, i=i: e.tensor_scalar(out=par[:, :, 16 + i], in0=par[:, :, 16 + i], scalar1=-1.0, scalar2=1.0, op0=ALU.mult, op1=ALU.add), ["par"], ["par"])
    V(lambda e: e.tensor_tensor(out=parl[:, :, 2], in0=parl[:, :, 0], in1=parl[:, :, 1], op=ALU.add), ["parl"], ["parl"])
    V(lambda e: e.tensor_scalar(out=parl[:, :, 2], in0=parl[:, :, 2], scalar1=-1.0, scalar2=1.0, op0=ALU.mult, op1=ALU.add), ["parl"], ["parl"])

    HS = S // 2

    def load_shift(ch, r0, nrow, c0, mup, mun, consume):
        for half in range(2):
            t0 = half * HS
            lo = max(t0 - 1, 0)
            hi = min(t0 + HS + 1, S)
            off = lo - (t0 - 1)
            if half == 0:
                V(lambda e: e.memset(RAW[0:nrow, 0:1], 0.0), [], ["RAW"])
            else:
                V(lambda e: e.memset(RAW[0:nrow, HS + 1:HS + 2], 0.0), [], ["RAW"])
            P.op("sync", lambda e, lo=lo, hi=hi, off=off: e.dma_start(out=RAW[0:nrow, off:off + (hi - lo)], in_=PT_d[ch][r0:r0 + nrow, lo:hi]), reads=[f"PT{ch}"], writes=["RAW"], dma_key="RAW")
            V(lambda e: e.tensor_scalar(out=SH[0:nrow, :], in0=RAW[0:nrow, 1:HS + 1], scalar1=c0, scalar2=None, op0=ALU.mult), ["RAW", "par", "parl"], ["SH"])
            V(lambda e: e.scalar_tensor_tensor(out=SH[0:nrow, :], in0=RAW[0:nrow, 0:HS], scalar=mup, in1=SH[0:nrow, :], op0=ALU.mult, op1=ALU.add), ["RAW", "SH", "par", "parl"], ["SH"])
            V(lambda e: e.scalar_tensor_tensor(out=SH[0:nrow, :], in0=RAW[0:nrow, 2:HS + 2], scalar=mun, in1=SH[0:nrow, :], op0=ALU.mult, op1=ALU.add), ["RAW", "SH", "par", "parl"], ["SH"])
            consume(half)

    for i, (ch, dstT, fn) in enumerate(((12, twT, AF.Tanh), (13, daT, AF.Copy), (14, sgT[:, 0, :], AF.Sigmoid), (15, sgT[:, 1, :], AF.Sigmoid))):
        def cons(half, dstT=dstT, fn=fn):
            A(lambda e: e.activation(out=dstT[:, half * HS:(half + 1) * HS], in_=SH[:, :], func=fn), ["SH"], ["lora_act"])
        load_shift(ch, 0, 128, parl[:, i, 2:3], parl[:, i, 0:1], parl[:, i, 1:2], cons)

    def head(hh):
        ch_r, ch_k, ch_v = hh // 2, 4 + hh // 2, 8 + hh // 2
        r0 = (hh % 2) * 64
        pc = lambda j: par[:, hh, j:j + 1]
        for (ch, i, dst) in ((ch_r, 0, Rb16), (ch_k, 1, Kb16), (ch_v, 2, Vb16)):
            def cons(half, dst=dst):
                A(lambda e: e.activation(out=dst[:, half * HS:(half + 1) * HS], in_=SH[0:64, :], func=AF.Copy), ["SH"], ["rkv"])
            load_shift(ch, r0, 64, pc(16 + i), pc(2 * i), pc(2 * i + 1), cons)
        for half in range(2):
            pi = state["pt"]; state["pt"] ^= 1
            for j in range(16):
                blk = half * 16 + j
                T(lambda e, blk=blk, j=j, pi=pi: e.transpose(out=pst[pi][:, j * 64:(j + 1) * 64], in_=Vb16[:, blk * 128:(blk + 1) * 128], identity=id64), ["rkv", "ident"], [f"ps{6 + pi}"])
            V(lambda e, half=half, pi=pi: e.tensor_copy(out=Vt[:, half * 16:(half + 1) * 16, :], in_=pst[pi][:, :].rearrange("p (k t) -> p k t", k=16)), [f"ps{6 + pi}"], ["rVt"])
        for b in range(NB):
            bs = slice(b * BLK, (b + 1) * BLK)
            V(lambda e, bs=bs: e.tensor_scalar(out=T1[:], in0=Kb16[:, bs], scalar1=pc(10), scalar2=None, op0=ALU.mult), ["rkv", "par"], ["T1"])
            A(lambda e: e.activation(out=T2[:], in_=T1[:], func=AF.Square), ["T1"], ["T2"])
            pi = next_ps()
            T(lambda e, pi=pi: e.matmul(psum[pi][0:64, :], lhsT=ones64, rhs=T2[:], start=True, stop=True), ["T2", "ones"], [f"ps{pi}"])
            A(lambda e, pi=pi: e.activation(out=KD[:], in_=psum[pi][0:64, :], func=AF.Sqrt, bias=tiny[:, 0:1]), [f"ps{pi}", "gnb"], ["KD"])
            V(lambda e: e.reciprocal(out=KD[:], in_=KD[:]), ["KD"], ["KD"])
            V(lambda e, bs=bs: e.tensor_tensor(out=KKN[:, bs], in0=T1[:], in1=KD[:], op=ALU.mult), ["T1", "KD"], ["KKN"])
        def direction(d):
            V(lambda e: e.memset(Hf[:], 0.0), [], ["Hf"])
            V(lambda e: e.memset(Hb[:], 0.0), [], ["Hb"])
            blocks = range(NB) if d == 0 else range(NB - 1, -1, -1)
            def block(bi, b):
                bs = slice(b * BLK, (b + 1) * BLK)
                sl = bi % 2
                O4, K3, EC = ops4[sl], tok3[sl], Eci[sl]
                hs = slice(hh * 64, (hh + 1) * 64)
                pi = 6
                T(lambda e, pi=pi, bs=bs, hs=hs: e.matmul(psum[pi][0:64, :], lhsT=lw2b[:, d, hs], rhs=twT[0:96, bs], start=True, stop=True), ["lw2b", "lora_act"], [f"ps{pi}"])
                A(lambda e, pi=pi: e.activation(out=LW[:], in_=psum[pi][0:64, :], func=AF.Sigmoid, bias=pc(6 + d)), [f"ps{pi}", "par"], ["LW"])
                V(lambda e: e.tensor_scalar(out=LW[:], in0=LW[:], scalar1=DEC, scalar2=None, op0=ALU.mult), ["LW"], ["LW"])
                V(lambda e: e.tensor_tensor_scan(out=CI[:], data0=mreset[:], data1=LW[:], initial=0.0, op0=ALU.mult, op1=ALU.add), ["LW", "mreset"], ["CI"])
                if d == 0:
                    V(lambda e: e.tensor_tensor(out=CE[:], in0=CI[:], in1=LW[:], op=ALU.subtract), ["CI", "LW"], ["CE"])
                else:
                    for c in range(4):
                        cs = slice(c * C, (c + 1) * C)
                        V(lambda e, cs=cs, c=c: e.tensor_scalar(out=CE[:, cs], in0=CI[:, cs], scalar1=-1.0, scalar2=CI[:, c * C + C - 1:c * C + C], op0=ALU.mult, op1=ALU.add), ["CI"], ["CE"])
                    V(lambda e: e.tensor_tensor(out=CI[:], in0=CE[:], in1=LW[:], op=ALU.add), ["CE", "LW"], ["CI"])
                A(lambda e: e.activation(out=Ece[:], in_=CE[:], func=AF.Exp), ["CE"], ["Ece"])
                A(lambda e: e.activation(out=Enci[:], in_=CI[:], func=AF.Exp, scale=-1.0), ["CI"], ["Enci"])
                A(lambda e, EC=EC: e.activation(out=EC[:], in_=CI[:], func=AF.Exp), ["CI"], [f"Eci{sl}"])
                pi = 7
                T(lambda e, pi=pi, bs=bs, hs=hs: e.matmul(psum[pi][0:64, :], lhsT=lw2b[:, 2 + d, hs], rhs=daT[0:96, bs], start=True, stop=True), ["lw2b", "lora_act"], [f"ps{pi}"])
                A(lambda e, pi=pi: e.activation(out=At[:], in_=psum[pi][0:64, :], func=AF.Sigmoid, bias=pc(8 + d)), [f"ps{pi}", "par"], ["At"])
                V(lambda e: e.tensor_scalar(out=T1[:], in0=At[:], scalar1=-1.0, scalar2=pc(11), op0=ALU.add, op1=ALU.mult), ["At", "par"], ["T1"])
                V(lambda e, bs=bs: e.scalar_tensor_tensor(out=KD[:], in0=T1[:], scalar=1.0, in1=Kb16[:, bs], op0=ALU.add, op1=ALU.mult), ["T1", "rkv"], ["KD"])
                V(lambda e, bs=bs: e.tensor_tensor(out=At[:], in0=At[:], in1=KKN[:, bs], op=ALU.mult), ["At", "KKN"], ["At"])
                V(lambda e, bs=bs, O4=O4: e.tensor_tensor(out=O4[:, 0, :], in0=KKN[:, bs], in1=Ece[:], op=ALU.mult), ["KKN", "Ece"], [f"ops4_{sl}"])
                V(lambda e, O4=O4: e.tensor_tensor(out=O4[:, 1, :], in0=At[:], in1=Enci[:], op=ALU.mult), ["At", "Enci"], [f"ops4_{sl}"])
                V(lambda e, O4=O4: e.tensor_tensor(out=O4[:, 2, :], in0=KD[:], in1=Enci[:], op=ALU.mult), ["KD", "Enci"], [f"ops4_{sl}"])
                V(lambda e, bs=bs, O4=O4, EC=EC: e.tensor_tensor(out=O4[:, 3, :], in0=Rb16[:, bs], in1=EC[:], op=ALU.mult), ["rkv", f"Eci{sl}"], [f"ops4_{sl}"])
                V(lambda e, bs=bs: e.scalar_tensor_tensor(out=T2[:], in0=Rb16[:, bs], scalar=pc(12), in1=KD[:], op0=ALU.mult, op1=ALU.mult), ["rkv", "KD", "par"], ["T2"])
                pi = 6
                T(lambda e, pi=pi: e.matmul(psum[pi][0:64, :], lhsT=ones64, rhs=T2[:], start=True, stop=True), ["T2", "ones"], [f"ps{pi}"])
                V(lambda e, pi=pi, bs=bs: e.scalar_tensor_tensor(out=T1[:], in0=psum[pi][0:64, :], scalar=0.5, in1=Vb16[:, bs], op0=ALU.mult, op1=ALU.mult), [f"ps{pi}", "rkv"], ["T1"])
                if d == 0:
                    V(lambda e, bs=bs: e.tensor_copy(out=BONV[:, bs], in_=T1[:]), ["T1"], ["BONV"])
                else:
                    V(lambda e, bs=bs: e.tensor_tensor(out=BONV[:, bs], in0=BONV[:, bs], in1=T1[:], op=ALU.add), ["T1", "BONV"], ["BONV"])
                pi = state["pt"]; state["pt"] ^= 1
                for o in range(3):
                    for c in range(4):
                        T(lambda e, o=o, c=c, pi=pi, O4=O4: e.transpose(out=pst[pi][:, (o * 4 + c) * 64:(o * 4 + c + 1) * 64], in_=O4[:, o, c * C:(c + 1) * C], identity=id64),
                          [f"ops4_{sl}", "ident"], [f"ps{6 + pi}"])
                V(lambda e, pi=pi, K3=K3: e.tensor_copy(out=K3[:, :, :], in_=pst[pi][:, 0:768].rearrange("p (k t) -> p k t", k=12)), [f"ps{6 + pi}"], [f"tok3_{sl}"])
                chunks = list(range(4)) if d == 0 else list(range(3, -1, -1))
                ok = [f"ops4_{sl}"]
                cst_ = {}

                def opsof(c):
                    cs = slice(c * C, (c + 1) * C)
                    return O4[:, 0, cs], O4[:, 1, cs], O4[:, 2, cs], O4[:, 3, cs]

                kx = lambda c: sl * 4 + c

                def gram1(c):
                    Ab_c, Bb_c, Kb_c, Rb_c = opsof(c)
                    p1, p2 = next_ps(), next_ps()
                    cst_[c] = dict(p1=p1, p2=p2)
                    T(lambda e: e.matmul(psum[p1][:, 0:128], lhsT=Bb_c, rhs=Ab_c, start=True, stop=True), ok, [f"ps{p1}"])
                    T(lambda e: e.matmul(psum[p1][:, 128:256], lhsT=Kb_c, rhs=Ab_c, start=True, stop=True), ok, [f"ps{p1}"])
                    T(lambda e: e.matmul(psum[p1][:, 256:384], lhsT=Ab_c, rhs=Bb_c, start=True, stop=True), ok, [f"ps{p1}"])
                    T(lambda e: e.matmul(psum[p2][:, 0:128], lhsT=Bb_c, rhs=Rb_c, start=True, stop=True), ok, [f"ps{p2}"])
                    T(lambda e: e.matmul(psum[p2][:, 128:256], lhsT=Kb_c, rhs=Rb_c, start=True, stop=True), ok, [f"ps{p2}"])

                def evac1(c):
                    p1, p2 = cst_[c]["p1"], cst_[c]["p2"]
                    V(lambda e: e.tensor_tensor(out=G1s[kx(c)][:], in0=psum[p1][:, 0:384], in1=masks[:, d, 0:384], op=ALU.mult), [f"ps{p1}", "masks"], [f"G1s{kx(c)}"])
                    V(lambda e: e.tensor_tensor(out=G2s[kx(c)][:], in0=psum[p2][:, 0:256], in1=masks[:, d, 384:640], op=ALU.mult), [f"ps{p2}", "masks"], [f"G2s{kx(c)}"])
                    V(lambda e: e.tensor_tensor(out=Qs[kx(c)][0][:], in0=G1s[kx(c)][:, 0:128], in1=ident[:], op=ALU.add), [f"G1s{kx(c)}", "ident"], [f"Qs{kx(c)}_0"])
                    cst_[c].update(M=G1s[kx(c)][:, 256:384], MT=G1s[kx(c)][:, 0:128], mk=f"G1s{kx(c)}", qi=0)

                def levelA(c, lev):
                    stc = cst_[c]
                    Mprev, MTprev, mk = stc["M"], stc["MT"], stc["mk"]
                    mi = lev % 2
                    pm = next_ps()
                    T(lambda e: e.matmul(psum[pm][:, 0:128], lhsT=MTprev, rhs=Mprev, start=True, stop=True), [mk], [f"ps{pm}"])
                    if lev < 6:
                        T(lambda e: e.matmul(psum[pm][:, 128:256], lhsT=Mprev, rhs=MTprev, start=True, stop=True), [mk], [f"ps{pm}"])
                    A(lambda e: e.activation(out=MM[kx(c)][mi][:], in_=psum[pm][:, 0:256], func=AF.Copy), [f"ps{pm}"], [f"MM{kx(c)}_{mi}"])
                    V(lambda e: e.tensor_tensor(out=IMs[kx(c)][mi][:], in0=MM[kx(c)][mi][:, 0:128], in1=ident[:], op=ALU.add), [f"MM{kx(c)}_{mi}", "ident"], [f"IM{kx(c)}_{mi}"])
                    stc["M"], stc["MT"], stc["mk"] = MM[kx(c)][mi][:, 0:128], MM[kx(c)][mi][:, 128:256], f"MM{kx(c)}_{mi}"

                def levelB(c, lev):
                    stc = cst_[c]
                    qi = stc["qi"]
                    mi = lev % 2
                    pq = next_ps()
                    T(lambda e: e.matmul(psum[pq][:, 0:128], lhsT=IMs[kx(c)][mi][:], rhs=Qs[kx(c)][qi][:], start=True, stop=True), [f"Qs{kx(c)}_{qi}", f"IM{kx(c)}_{mi}"], [f"ps{pq}"])
                    V(lambda e: e.tensor_copy(out=Qs[kx(c)][1 - qi][:], in_=psum[pq][:, 0:128]), [f"ps{pq}"], [f"Qs{kx(c)}_{1 - qi}"])
                    stc["qi"] = 1 - qi

                def w1x(c):
                    gc = b * 4 + c
                    qi = cst_[c]["qi"]
                    Q, qk = Qs[kx(c)][qi], f"Qs{kx(c)}_{qi}"
                    Abt = K3[:, 0 + c, :]
                    pw = next_ps()
                    T(lambda e: e.matmul(psum[pw][0:64, 0:128], lhsT=Abt, rhs=Q[:], start=True, stop=True), [f"tok3_{sl}", qk], [f"ps{pw}"])
                    A(lambda e: e.activation(out=nW1T[kx(c)][:], in_=psum[pw][0:64, 0:128], func=AF.Copy, scale=-1.0), [f"ps{pw}"], [f"nW1T{kx(c)}"])
                    px = next_ps()
                    T(lambda e: e.matmul(psum[px][:, 0:64], lhsT=G1s[kx(c)][:, 128:256], rhs=Vt[:, gc, :], start=True, stop=True), [f"G1s{kx(c)}", "rVt"], [f"ps{px}"])
                    A(lambda e: e.activation(out=nXs[kx(c)][:], in_=psum[px][:, 0:64], func=AF.Copy, scale=-1.0), [f"ps{px}"], [f"nXs{kx(c)}"])

                def seq(c):
                    gc = b * 4 + c
                    qi = cst_[c]["qi"]
                    Q, qk = Qs[kx(c)][qi], f"Qs{kx(c)}_{qi}"
                    Ab_c, Bb_c, Kb_c, Rb_c = opsof(c)
                    Bbt, Kbt = K3[:, 4 + c, :], K3[:, 8 + c, :]
                    Vtc = Vt[:, gc, :]
                    pu = next_ps()
                    T(lambda e: e.matmul(psum[pu][:, 0:64], lhsT=Q[:], rhs=nXs[kx(c)][:], start=True, stop=False), [qk, f"nXs{kx(c)}"], [f"ps{pu}"])
                    T(lambda e: e.matmul(psum[pu][:, 0:64], lhsT=nW1T[kx(c)][:], rhs=Hb[:], start=False, stop=True), [f"nW1T{kx(c)}", "Hb"], [f"ps{pu}"])
                    V(lambda e: e.tensor_copy(out=Us[:], in_=psum[pu][:, 0:64]), [f"ps{pu}"], ["Us"])
                    po = next_ps()
                    T(lambda e: e.matmul(psum[po][0:64, 0:128], lhsT=Hb[:], rhs=Rb_c, start=True, stop=False), ["Hb"] + ok, [f"ps{po}"])
                    T(lambda e: e.matmul(psum[po][0:64, 0:128], lhsT=Us[:], rhs=G2s[kx(c)][:, 0:128], start=False, stop=False), ["Us", f"G2s{kx(c)}"], [f"ps{po}"])
                    T(lambda e: e.matmul(psum[po][0:64, 0:128], lhsT=Vtc, rhs=G2s[kx(c)][:, 128:256], start=False, stop=True), ["rVt", f"G2s{kx(c)}"], [f"ps{po}"])
                    gs = slice(gc * C, (gc + 1) * C)
                    if d == 0:
                        A(lambda e: e.activation(out=OT[:, gs], in_=psum[po][0:64, 0:128], func=AF.Copy), [f"ps{po}"], ["OT"])
                    else:
                        V(lambda e: e.tensor_tensor(out=OT[:, gs], in0=OT[:, gs], in1=psum[po][0:64, 0:128], op=ALU.add), [f"ps{po}", "OT"], ["OT"])
                    ph = next_ps()
                    T(lambda e: e.matmul(psum[ph][0:64, 0:64], lhsT=Bbt, rhs=Us[:], start=True, stop=False), [f"tok3_{sl}", "Us"], [f"ps{ph}"])
                    T(lambda e: e.matmul(psum[ph][0:64, 0:64], lhsT=Kbt, rhs=Vtc, start=False, stop=True), [f"tok3_{sl}", "rVt"], [f"ps{ph}"])
                    gidx = c * C + C - 1 if d == 0 else c * C
                    gam = EC[:, gidx:gidx + 1]
                    V(lambda e: e.tensor_scalar(out=Hf[:], in0=Hf[:], scalar1=gam, scalar2=None, op0=ALU.mult), ["Hf", f"Eci{sl}"], ["Hf"])
                    V(lambda e: e.scalar_tensor_tensor(out=Hf[:], in0=psum[ph][0:64, 0:64], scalar=gam, in1=Hf[:], op0=ALU.mult, op1=ALU.add), ["Hf", f"Eci{sl}", f"ps{ph}"], ["Hf"])
                    A(lambda e: e.activation(out=Hb[:], in_=Hf[:], func=AF.Copy), ["Hf"], ["Hb"])

                stages = []
                for cg in (chunks[0:2], chunks[2:4]):
                    for c in cg:
                        stages.append(lambda c=c: gram1(c))
                    for c in cg:
                        stages.append(lambda c=c: evac1(c))
                for lev in range(1, 7):
                    for c in chunks:
                        stages.append(lambda c=c, lev=lev: levelA(c, lev))
                    for c in chunks:
                        stages.append(lambda c=c, lev=lev: levelB(c, lev))
                for c in chunks:
                    stages.append(lambda c=c: w1x(c))
                seqs = [(lambda c=c: seq(c)) for c in chunks]
                return stages, seqs
            prev = None
            for bi, b in enumerate(blocks):
                stages, seqs = block(bi, b)
                if prev is None or RWMODE != 3:
                    if prev is not None:
                        for f in prev:
                            f()
                    for f in stages:
                        f()
                else:
                    n = len(stages)
                    marks = {int((k + 1) * n / 5): k for k in range(4)}
                    for i, f in enumerate(stages):
                        f()
                        if (i + 1) in marks:
                            prev[marks[i + 1]]()
                prev = seqs
            for f in prev:
                f()
        P.fence()
        direction(0)
        direction(1)
        P.fence()
        for b in range(NB):
            bs = slice(b * BLK, (b + 1) * BLK)
            A(lambda e, bs=bs: e.activation(out=ob[:], in_=OT[:, bs], func=AF.Copy), ["OT"], ["ob"])
            pi = next_ps()
            T(lambda e, pi=pi: e.matmul(psum[pi][0:64, :], lhsT=ones64, rhs=ob[:], start=True, stop=True), ["ob", "ones"], [f"ps{pi}"])
            V(lambda e, pi=pi, bs=bs: e.scalar_tensor_tensor(out=dd[:], in0=psum[pi][0:64, :], scalar=-1.0 / 64, in1=OT[:, bs], op0=ALU.mult, op1=ALU.add), [f"ps{pi}", "OT"], ["dd"])
            A(lambda e: e.activation(out=ob[:], in_=dd[:], func=AF.Square), ["dd"], ["ob"])
            pi = next_ps()
            T(lambda e, pi=pi: e.matmul(psum[pi][0:64, :], lhsT=ones64, rhs=ob[:], start=True, stop=True), ["ob", "ones"], [f"ps{pi}"])
            A(lambda e, pi=pi: e.activation(out=T1[:], in_=psum[pi][0:64, :], func=AF.Sqrt, scale=1.0 / 64, bias=gnb[:, 0:1]), [f"ps{pi}", "gnb"], ["T1"])
            V(lambda e: e.reciprocal(out=T1[:], in_=T1[:]), ["T1"], ["T1"])
            V(lambda e: e.tensor_tensor(out=dd[:], in0=dd[:], in1=T1[:], op=ALU.mult), ["dd", "T1"], ["dd"])
            V(lambda e: e.tensor_scalar(out=dd[:], in0=dd[:], scalar1=pc(13), scalar2=pc(14), op0=ALU.mult, op1=ALU.add), ["dd", "par"], ["dd"])
            V(lambda e, bs=bs: e.tensor_tensor(out=dd[:], in0=dd[:], in1=BONV[:, bs], op=ALU.add), ["dd", "BONV"], ["dd"])
            pi = next_ps()
            for kc in range(2):
                T(lambda e, pi=pi, kc=kc, bs=bs: e.matmul(psum[pi][0:64, :], lhsT=g2b[:, kc, hh * 64:(hh + 1) * 64], rhs=sgT[:, kc, bs], start=(kc == 0), stop=(kc == 1)), ["g2b", "lora_act"], [f"ps{pi}"])
            V(lambda e, pi=pi: e.tensor_tensor(out=yo[:], in0=dd[:], in1=psum[pi][0:64, :], op=ALU.mult), ["dd", f"ps{pi}"], ["ryo"])
            P.op("sync", lambda e, bs=bs: e.dma_start(out=yT[hh // 2][(hh % 2) * 64:(hh % 2) * 64 + 64, bs], in_=yo[:]), reads=["ryo"], writes=["yT"], dma_key="ryo")

    for hh in range(rw_heads):
        head(hh)


def relayout_rwkv(inp, c, l=0):
    b, g = c // 4, c % 4
    chs = slice(512 * g, 512 * (g + 1))
    par = np.zeros((64, 8, 16), np.float32)
    sp, sn = inp["shift_prev"][l], inp["shift_next"][l]
    for hh in range(8):
        cg = slice(512 * g + hh * 64, 512 * g + (hh + 1) * 64)
        for i in range(3):
            par[:, hh, 2 * i] = sp[i * 2048:(i + 1) * 2048][cg]
            par[:, hh, 2 * i + 1] = sn[i * 2048:(i + 1) * 2048][cg]
        par[:, hh, 6] = inp["decay_bias_fwd"][l][cg]; par[:, hh, 7] = inp["decay_bias_bwd"][l][cg]
        par[:, hh, 8] = inp["iclr_bias_fwd"][l][cg]; par[:, hh, 9] = inp["iclr_bias_bwd"][l][cg]
        par[:, hh, 10] = inp["k_k"][l][cg]; par[:, hh, 11] = inp["k_a"][l][cg]
        par[:, hh, 12] = inp["r_k"][l].reshape(-1)[cg]
        par[:, hh, 13] = inp["ln_x_gain"][l][cg]; par[:, hh, 14] = inp["ln_x_bias"][l][cg]
    parl = np.zeros((128, 4, 2), np.float32)
    lo = 3 * 2048
    for i, (a0, n) in enumerate(((lo, 96), (lo + 96, 96), (lo + 192, 128), (lo + 320, 128))):
        parl[:n, i, 0] = sp[a0:a0 + n]; parl[:n, i, 1] = sn[a0:a0 + n]
    lw2 = np.stack([inp[k][l][:, chs] for k in ("decay_up_fwd", "decay_up_bwd", "iclr_up_fwd", "iclr_up_bwd")], axis=1)
    g2 = np.ascontiguousarray(inp["gate_up"][l][:, chs].reshape(2, 128, 512).transpose(1, 0, 2))
    ii = np.arange(128)
    row, col = ii[:, None], ii[None, :]
    masks = np.zeros((128, 2, 640), np.float32)
    for d in range(2):
        lt = (row < col) if d == 0 else (row > col)
        le = (row <= col) if d == 0 else (row >= col)
        masks[:, d, 0:128] = -lt.astype(np.float32)
        masks[:, d, 128:256] = lt
        masks[:, d, 256:384] = -lt.T.astype(np.float32)
        masks[:, d, 384:512] = le
        masks[:, d, 512:640] = le
    mreset = np.ones((64, BLK), np.float32)
    mreset[:, ::C] = 0.0
    return dict(par=par, parl=parl, lw2=np.ascontiguousarray(lw2).astype(np.float32), g2=g2.astype(np.float32), masks=masks, mreset=mreset)


F32 = mybir.dt.float32
BF16 = mybir.dt.bfloat16
AF = mybir.ActivationFunctionType
ALU = mybir.AluOpType
D = 4096
S = 4096
EPS = 1e-6
NCH = 28
NR = 6592
LAMBDA_INIT = 0.8 - 0.6
WARMN = 0
NFILL = 1


def body_A(nc, P, st, sh, yT, do_proj=True, do_attn=True, do_rwkv=True, n_tb=8, attn_heads=4, attn_qb=8, rw_heads=8):
    xb = nc.dram_tensor("xb", [S, D], F32, kind="ExternalInput").ap()
    wA = nc.dram_tensor("wA", [NCH, 128, 32, 128], F32, kind="ExternalInput").ap()
    abias = nc.dram_tensor("abias", [4, 5, 128, 512], F32, kind="ExternalInput").ap()
    acst = nc.dram_tensor("acst", [128, 4 * 64], F32, kind="ExternalInput").ap()
    lam_d = nc.dram_tensor("lam", [128, 4, 64], F32, kind="ExternalInput").ap()
    subg = nc.dram_tensor("subg", [128, 1], F32, kind="ExternalInput").ap()
    PT_d = nc.dram_tensor("PT_d", [NCH, 128, S], F32).ap()
    gain = sh["gains"][0]
    ident, ones, small, epsb = sh["ident"], sh["ones"], sh["small"], sh["epsb"]
    psum, pst, state, next_ps = sh["psum"], sh["pst"], sh["state"], sh["next_ps"]
    if True:
        if do_proj:
            with ExitStack() as st2:
                bufAs = [st2.enter_context(nc.sbuf_tensor(f"bufA{i}", [128, 32, 512], BF16)) for i in range(2)]
                xt = st2.enter_context(nc.sbuf_tensor("xt", [128, D], F32))
                gt = st2.enter_context(nc.sbuf_tensor("gt", [128, D], F32))
                hb = st2.enter_context(nc.sbuf_tensor("hb", [128, D], BF16))
                wbuf = [st2.enter_context(nc.sbuf_tensor(f"wb{i}", [128, 32, 128], BF16)) for i in range(3)]
                ot = [st2.enter_context(nc.sbuf_tensor(f"ot{i}", [128, 512], F32)) for i in range(3)]
                P.op("sync", lambda e: e.dma_start(out=gt[:], in_=gain), writes=["gt"], dma_key="gt")
                for tb in range(n_tb):
                    bufA = bufAs[tb % 2]
                    bk = f"bufA{tb % 2}"
                    for tt in range(4):
                        r0 = tb * 512 + tt * 128
                        P.op("sync", lambda e, r0=r0: e.dma_start(out=xt[:], in_=xb[r0:r0 + 128, :]), writes=["xt"], dma_key="xt")
                        P.op("vector", lambda e: e.memset(small[:, 0:1], 0.0), writes=["sm0"])
                        P.op("scalar", lambda e: e.activation(out=hb[:], in_=xt[:], func=AF.Square, accum_out=small[:, 0:1]), reads=["xt", "sm0"], writes=["hb", "sm0"])
                        P.op("scalar", lambda e: e.activation(out=small[:, 0:1], in_=small[:, 0:1], func=AF.Sqrt, scale=1.0 / D, bias=epsb[:, 0:1]), reads=["sm0", "epsb"], writes=["sm0"])
                        P.op("vector", lambda e: e.reciprocal(out=small[:, 0:1], in_=small[:, 0:1]), reads=["sm0"], writes=["sm0"])
                        P.op("vector", lambda e: e.scalar_tensor_tensor(out=hb[:], in0=xt[:], scalar=small[:, 0:1], in1=gt[:], op0=ALU.mult, op1=ALU.mult),
                             reads=["xt", "sm0", "gt"], writes=["hb"])
                        for k8 in range(4):
                            pi = state["pt"]; state["pt"] ^= 1
                            for j in range(8):
                                kc = k8 * 8 + j
                                P.op("tensor", lambda e, kc=kc, j=j, pi=pi: e.transpose(out=pst[pi][:, j * 128:(j + 1) * 128], in_=hb[:, kc * 128:(kc + 1) * 128], identity=ident[:]),
                                     reads=["hb", "ident"], writes=[f"ps{6 + pi}"])
                            dst = bufA[:, k8 * 8:(k8 + 1) * 8, tt * 128:(tt + 1) * 128]
                            srcp = pst[pi][:, :].rearrange("p (k t) -> p k t", k=8)
                            if k8 % 2 == 0:
                                P.op("scalar", lambda e, dst=dst, srcp=srcp: e.activation(out=dst, in_=srcp, func=AF.Copy), reads=[f"ps{6 + pi}"], writes=[bk])
                            else:
                                P.op("vector", lambda e, dst=dst, srcp=srcp: e.tensor_copy(out=dst, in_=srcp), reads=[f"ps{6 + pi}"], writes=[bk])
                    for ch in range(NCH):
                        s = ch % 3
                        for half in range(2):
                            P.op("gpsimd", lambda e, s=s, ch=ch, half=half: e.dma_start(out=wbuf[s][:, half * 16:(half + 1) * 16, :], in_=wA[ch][:, half * 16:(half + 1) * 16, :], max_dma_last_dim=4096),
                                 writes=[f"wb{s}"], dma_key=f"wb{s}")
                        pi = next_ps()
                        for kc in range(32):
                            P.op("tensor", lambda e, kc=kc, pi=pi, s=s, bufA=bufA: e.matmul(psum[pi][:], lhsT=wbuf[s][:, kc, :], rhs=bufA[:, kc, :], start=(kc == 0), stop=(kc == 31)),
                                 reads=[f"wb{s}", bk], writes=[f"ps{pi}"])
                        if ch % 2 == 0:
                            P.op("scalar", lambda e, pi=pi, s=s: e.activation(out=ot[s][:], in_=psum[pi][:], func=AF.Copy), reads=[f"ps{pi}"], writes=[f"ot{s}"])
                        else:
                            P.op("vector", lambda e, pi=pi, s=s: e.tensor_copy(out=ot[s][:], in_=psum[pi][:]), reads=[f"ps{pi}"], writes=[f"ot{s}"])
                        P.op("sync", lambda e, s=s, ch=ch, tb=tb: e.dma_start(out=PT_d[ch][:, tb * 512:(tb + 1) * 512], in_=ot[s][:]), reads=[f"ot{s}"], writes=[f"PT{ch}"], dma_key=f"ot{s}")
                P.fence()
        if do_attn:
            with ExitStack() as st2:
                def sb2(name, shape, dt):
                    return st2.enter_context(nc.sbuf_tensor(name, shape, dt))
                LA = 2
                NSL = LA + 1
                qk32 = sb2("qk32", [128, S], F32)
                QTs = [sb2(f"QT{i}", [64, 2, S], BF16) for i in range(2)]
                KTs = [sb2(f"KT{i}", [64, 2, S], BF16) for i in range(2)]
                Vts = [sb2(f"Vt{i}", [128, 32, 128], BF16) for i in range(2)]
                vb = sb2("vb", [128, S], BF16)
                bts = [sb2(f"bt{i}", [128, 5, 512], F32) for i in range(2)]
                cst = sb2("cst", [128, 256], F32)
                lamt = sb2("lamt", [128, 4, 64], F32)
                lsm = sb2("lsm", [128, 8], F32)
                sgt = sb2("sgt", [128, 1], F32)
                tmp = [sb2(f"atmp{i}", [128, 512], F32) for i in range(NSL)]
                Eb = [sb2(f"Eb{i}", [128, 512], BF16) for i in range(NSL)]
                o0 = sb2("o0", [128, 512], F32)
                o1 = sb2("o1", [128, 512], F32)
                rr = sb2("rr", [128, 512], F32)
                rr2 = sb2("rr2", [128, 512], F32)
                sq = sb2("sq", [128, 512], BF16)
                yo = sb2("yo", [128, 512], BF16)
                wz = sb2("wz", [128, 512], BF16)
                P.op("vector", lambda e: e.memset(wz[:], 0.0), writes=["wz"])

                def warm(n):
                    for _ in range(n):
                        P.op("tensor", lambda e: e.matmul(psum[7][:], lhsT=ones[:], rhs=wz[:], start=True, stop=True), reads=["wz", "ones"], writes=["ps7"])
                P.op("sync", lambda e: e.dma_start(out=cst[:], in_=acst), writes=["cst"], dma_key="cst")
                P.op("sync", lambda e: e.dma_start(out=lamt[:], in_=lam_d), writes=["lamt"], dma_key="lamt")
                P.op("sync", lambda e: e.dma_start(out=sgt[:], in_=subg), writes=["sgt"], dma_key="sgt")
                for i in range(2):
                    P.op("vector", lambda e, i=i: e.tensor_tensor(out=lamt[:, 2 * i, :], in0=lamt[:, 2 * i, :], in1=lamt[:, 2 * i + 1, :], op=ALU.mult), reads=["lamt"], writes=["lamt"])
                    P.op("vector", lambda e, i=i: e.tensor_reduce(out=lsm[:, i:i + 1], in_=lamt[:, 2 * i, :], axis=mybir.AxisListType.X, op=ALU.add), reads=["lamt"], writes=["lsm"])
                    P.op("scalar", lambda e, i=i: e.activation(out=lsm[:, i:i + 1], in_=lsm[:, i:i + 1], func=AF.Exp), reads=["lsm"], writes=["lsm"])
                P.op("vector", lambda e: e.tensor_tensor(out=lsm[:, 2:3], in0=lsm[:, 1:2], in1=lsm[:, 0:1], op=ALU.subtract), reads=["lsm"], writes=["lsm"])
                P.op("vector", lambda e: e.tensor_scalar(out=lsm[:, 2:3], in0=lsm[:, 2:3], scalar1=-LAMBDA_INIT, scalar2=None, op0=ALU.add), reads=["lsm"], writes=["lsm"])
                P.op("vector", lambda e: e.tensor_scalar(out=sgt[:], in0=sgt[:], scalar1=1.0 - LAMBDA_INIT, scalar2=None, op0=ALU.mult), reads=["sgt"], writes=["sgt"])

                def load_head(hd):
                    hs = hd % 2
                    QT, KT, Vt, bt = QTs[hs], KTs[hs], Vts[hs], bts[hs]
                    P.op("sync", lambda e: e.dma_start(out=bt[:], in_=abias[hd].rearrange("f p q -> p f q")), writes=[f"bt{hs}"], dma_key=f"bt{hs}")
                    for (dstT, ch, nm) in ((QT, 16 + hd, f"QT{hs}"), (KT, 20 + hd, f"KT{hs}")):
                        for c in range(2):
                            P.op("sync", lambda e, ch=ch, c=c: e.dma_start(out=qk32[0:64, :], in_=PT_d[ch][c * 64:(c + 1) * 64, :]), reads=[f"PT{ch}"], writes=["qk32"], dma_key="qk32")
                            P.op("scalar", lambda e, dstT=dstT, c=c: e.activation(out=dstT[:, c, :], in_=qk32[0:64, :], func=AF.Copy), reads=["qk32"], writes=[nm])
                    P.op("sync", lambda e: e.dma_start(out=qk32[:], in_=PT_d[24 + hd]), reads=[f"PT{24 + hd}"], writes=["qk32"], dma_key="qk32")
                    P.op("vector", lambda e: e.tensor_copy(out=vb[:], in_=qk32[:]), reads=["qk32"], writes=["vb"])
                    for k8 in range(4):
                        pi = state["pt"]; state["pt"] ^= 1
                        for j in range(8):
                            blk = k8 * 8 + j
                            P.op("tensor", lambda e, blk=blk, j=j, pi=pi: e.transpose(out=pst[pi][:, j * 128:(j + 1) * 128], in_=vb[:, blk * 128:(blk + 1) * 128], identity=ident[:]),
                                 reads=["vb", "ident"], writes=[f"ps{6 + pi}"])
                        P.op("vector", lambda e, k8=k8, pi=pi: e.tensor_copy(out=Vt[:, k8 * 8:(k8 + 1) * 8, :], in_=pst[pi][:, :].rearrange("p (k t) -> p k t", k=8)), reads=[f"ps{6 + pi}"], writes=[f"Vt{hs}"])

                pacc = [0, 1, 2, 3]
                pending = [None]

                def unit_front(hd, qb, i, ulist):
                    hs = hd % 2
                    kb, c = ulist[i]
                    delta = kb - 4 * qb
                    pi = 4 + (i % NSL)
                    ti = i % NSL
                    P.op("tensor", lambda e: e.matmul(psum[pi][:], lhsT=KTs[hs][:, c, kb * 128:(kb + 1) * 128], rhs=QTs[hs][:, c, qb * 512:(qb + 1) * 512], start=True, stop=True),
                         reads=[f"QT{hs}", f"KT{hs}"], writes=[f"ps{pi}"])
                    if delta >= 4:
                        bti, op1 = 0, ALU.add
                    elif delta < 0:
                        bti, op1 = 0, ALU.subtract
                    else:
                        bti, op1 = 1 + delta, ALU.add
                    P.op("vector", lambda e: e.scalar_tensor_tensor(out=tmp[ti][:], in0=psum[pi][:], scalar=0.125, in1=bts[hs][:, bti, :], op0=ALU.mult, op1=op1),
                         reads=[f"ps{pi}", f"bt{hs}"], writes=[f"atmp{ti}"])
                    ci = hd * 64 + (delta + 32)
                    P.op("scalar", lambda e: e.activation(out=Eb[ti][:], in_=tmp[ti][:], func=AF.Exp, bias=cst[:, ci:ci + 1]), reads=[f"atmp{ti}", "cst"], writes=[f"Eb{ti}"])

                def unit_back(hd, qb, i, ulist, kbs):
                    hs = hd % 2
                    kb, c = ulist[i]
                    ti = i % NSL
                    P.op("tensor", lambda e: e.matmul(psum[pacc[2 * c]][:], lhsT=Vts[hs][:, kb, :], rhs=Eb[ti][:], start=(kb == kbs[0]), stop=(kb == kbs[-1])),
                         reads=[f"Eb{ti}", f"Vt{hs}"], writes=[f"ps{pacc[2 * c]}"])
                    P.op("tensor", lambda e: e.matmul(psum[pacc[2 * c + 1]][:], lhsT=ones[:], rhs=Eb[ti][:], start=(kb == kbs[0]), stop=(kb == kbs[-1])),
                         reads=[f"Eb{ti}", "ones"], writes=[f"ps{pacc[2 * c + 1]}"])

                def fin1():
                    P.op("vector", lambda e: e.reciprocal(out=rr[:], in_=psum[pacc[1]][:]), reads=[f"ps{pacc[1]}"], writes=["rr"])
                    P.op("vector", lambda e: e.tensor_tensor(out=o0[:], in0=psum[pacc[0]][:], in1=rr[:], op=ALU.mult), reads=[f"ps{pacc[0]}", "rr"], writes=["o0"])
                    P.op("vector", lambda e: e.reciprocal(out=rr[:], in_=psum[pacc[3]][:]), reads=[f"ps{pacc[3]}"], writes=["rr"])
                    P.op("vector", lambda e: e.tensor_tensor(out=o1[:], in0=psum[pacc[2]][:], in1=rr[:], op=ALU.mult), reads=[f"ps{pacc[2]}", "rr"], writes=["o1"])
                    P.op("vector", lambda e: e.scalar_tensor_tensor(out=o0[:], in0=o1[:], scalar=lsm[:, 2:3], in1=o0[:], op0=ALU.mult, op1=ALU.add), reads=["o0", "o1", "lsm"], writes=["o0"])
                    P.op("scalar", lambda e: e.activation(out=sq[:], in_=o0[:], func=AF.Square), reads=["o0"], writes=["sq"])

                def fin2(hd, qb, pi):
                    P.op("tensor", lambda e: e.matmul(psum[pi][:], lhsT=ones[:], rhs=sq[:], start=True, stop=True), reads=["sq", "ones"], writes=[f"ps{pi}"])
                    P.op("scalar", lambda e: e.activation(out=rr2[:], in_=psum[pi][:], func=AF.Sqrt, scale=1.0 / 128, bias=epsb[:, 1:2]), reads=[f"ps{pi}", "epsb"], writes=["rr2"])
                    P.op("vector", lambda e: e.reciprocal(out=rr2[:], in_=rr2[:]), reads=["rr2"], writes=["rr2"])
                    P.op("vector", lambda e: e.scalar_tensor_tensor(out=yo[:], in0=o0[:], scalar=sgt[:, 0:1], in1=rr2[:], op0=ALU.mult, op1=ALU.mult), reads=["o0", "sgt", "rr2"], writes=["yo"])
                    P.op("sync", lambda e: e.dma_start(out=yT[4 + hd][:, qb * 512:(qb + 1) * 512], in_=yo[:]), reads=["yo"], writes=["yT"], dma_key="yo")

                load_head(0)

                def kept_kbs(hd, qb):
                    smin = 2.0 ** (-2.0 * (hd + 1))
                    out = []
                    for kb in range(32):
                        delta = kb - 4 * qb
                        dmin = 128 * (delta - 4) + 1 if delta >= 4 else (128 * (-delta - 1) + 1 if delta < 0 else 0)
                        if smin * dmin < 60.0:
                            out.append(kb)
                    return out
                for hd in range(attn_heads):
                    for qb in range(attn_qb):
                        kbs = kept_kbs(hd, qb)
                        ulist = [(kb, c) for kb in kbs for c in range(2)]
                        NU = len(ulist)
                        warm(WARMN)
                        for i in range(NU + LA):
                            if i < NU:
                                unit_front(hd, qb, i, ulist)
                            if i >= LA:
                                unit_back(hd, qb, i - LA, ulist, kbs)
                                warm(NFILL)
                            if i == LA + 1 and pending[0] is not None:
                                ph, pq = pending[0]
                                pending[0] = None
                                fin2(ph, pq, 7)
                        fin1()
                        pending[0] = (hd, qb)
                        if qb == 1 and hd + 1 < attn_heads:
                            load_head(hd + 1)
                        if hd == attn_heads - 1 and qb == attn_qb - 1:
                            fin2(hd, qb, 7)
                            pending[0] = None
                P.fence()
        if do_rwkv:
            with ExitStack() as st3:
                emit_rwkv(nc, P, st3, PT_d, yT, psum, pst, state, next_ps, ident, ones, epsb, rw_heads)
            P.fence()


def build_A(**kw):
    nc = bass.Bass("TRN2", target_bir_lowering=False)
    yT = nc.dram_tensor("yT", [8, 128, S], BF16, kind="ExternalOutput").ap()
    P = Prog(nc)
    with ExitStack() as st:
        sh = make_shared(nc, P, st)
        body_A(nc, P, st, sh, yT, **kw)
        counts = P.emit(st)
        print("A ops", counts, "waits", P.n_waits)
    return nc


def build_fused():
    nc = bass.Bass("TRN2", target_bir_lowering=False)
    yTi = nc.dram_tensor("yTi", [8, 128, S], BF16).ap()
    G = nc.dram_tensor("Gy", [8, 4, 128, S], BF16).ap()
    sel = nc.dram_tensor("sel", [128, 4], F32, kind="ExternalInput").ap()
    P = Prog(nc)
    with ExitStack() as st:
        sh = make_shared(nc, P, st)
        body_A(nc, P, st, sh, yTi)
        for k in range(8):
            P.op("gpsimd", lambda e, k=k: e.collective_compute("AllGather", ALU.bypass, replica_groups=[[0, 1, 2, 3], [4, 5, 6, 7]],
                                                               ins=[yTi[k].opt()], outs=[G[k].rearrange("g p t -> (g p) t").opt()]),
                 reads=["yT"], writes=["G"], dma_key="cc", inc=1)
        with ExitStack() as st4:
            body_B(nc, P, st4, sh, ("gather", G, sel))
        counts = P.emit(st)
        print("fused ops", counts, "waits", P.n_waits)
    return nc


def slopes():
    H = 16
    return np.exp2(-8.0 * np.arange(1, H + 1, dtype=np.float32) / H).astype(np.float32)


def relayout_A(inp, c, l=0):
    b, g = c // 4, c % 4
    w_in = inp["w_in"][l]
    cols = []
    for part in range(3):
        cols.append(np.arange(part * 2048 + 512 * g, part * 2048 + 512 * (g + 1)))
    lo = 3 * 2048
    cols.append(np.arange(lo, lo + 96)); pad1 = 32
    cols.append(np.arange(lo + 96, lo + 192)); pad2 = 32
    cols.append(np.arange(lo + 192, lo + 448))
    for part in range(3):
        cols.append(np.concatenate([np.arange(NR + part * 2048 + (4 * j + g) * 128, NR + part * 2048 + (4 * j + g + 1) * 128) for j in range(4)]))
    W = np.zeros((D, NCH * 128), np.float32)
    W[:, 0:1536] = w_in[:, np.concatenate(cols[0:3])]
    W[:, 1536:1536 + 96] = w_in[:, cols[3]]
    W[:, 1664:1664 + 96] = w_in[:, cols[4]]
    W[:, 1792:2048] = w_in[:, cols[5]]
    W[:, 2048:3584] = w_in[:, np.concatenate(cols[6:9])]
    wA = np.ascontiguousarray(W.reshape(32, 128, NCH, 128).transpose(2, 1, 0, 3))
    gain = np.ascontiguousarray(np.broadcast_to(inp["attn_pre_norm"][l][None, :], (128, D))).astype(np.float32)
    ident = np.eye(128, dtype=np.float32).astype(ml_dtypes.bfloat16)
    ones = np.ones((128, 128), np.float32).astype(ml_dtypes.bfloat16)
    sl = slopes()
    kk = np.arange(128, dtype=np.float32)[:, None]
    qq = np.arange(512, dtype=np.float32)[None, :]
    abias = np.zeros((4, 5, 128, 512), np.float32)
    acst = np.zeros((128, 256), np.float32)
    for hd in range(4):
        s_ = sl[4 * hd + g]
        abias[hd, 0] = -s_ * (kk - qq)
        for dl in range(4):
            abias[hd, 1 + dl] = -s_ * np.abs(128.0 * dl + kk - qq)
        for delta in range(-32, 32):
            if delta >= 4:
                v = -s_ * 128.0 * delta
            elif delta < 0:
                v = s_ * 128.0 * delta
            else:
                v = 0.0
            acst[:, hd * 64 + delta + 32] = v
    lam = np.stack([inp[k][l] for k in ("lambda_q1", "lambda_k1", "lambda_q2", "lambda_k2")])
    lam = np.ascontiguousarray(np.broadcast_to(lam[None], (128, 4, 64))).astype(np.float32)
    subg = np.ascontiguousarray(inp["subln_gain"][l].reshape(128, 1)).astype(np.float32)
    return dict(xb=np.ascontiguousarray(inp["x"][b]), wA=wA, abias=abias, acst=acst, lam=lam, subg=subg)


def kernel(**inp):
    inp = {k: np.asarray(v) for k, v in inp.items()}
    n = 8
    nc = build_fused()
    W = relayout_B(inp)
    in_maps = []
    for c in range(n):
        b, g = c // 4, c % 4
        im = dict(W)
        im.update(relayout_A(inp, c))
        im.update(relayout_rwkv(inp, c))
        im["xo"] = np.ascontiguousarray(inp["x"][b, 1024 * g:1024 * (g + 1)])
        sel = np.zeros((128, 4), np.float32)
        sel[:, g] = 1.0
        im["sel"] = sel
        in_maps.append(im)
    res = run_bass_kernel_spmd(nc, in_maps, core_ids=list(range(n)))
    out = np.zeros((2, S, D), np.float32)
    for c in range(n):
        b, g = c // 4, c % 4
        out[b, 1024 * g:1024 * (g + 1)] = res.results[c]["out"]
    return out
```

```python
import math
import bisect
import numpy as np
import ml_dtypes
from contextlib import ExitStack
import concourse.bass as bass
import concourse.mybir as mybir
from concourse.bass_utils import run_bass_kernel_spmd


ENGS = ("tensor", "vector", "scalar", "gpsimd", "sync")
ROT = 12000


class Prog:
    def __init__(self, nc):
        self.nc = nc
        self.ops = []

    def op(self, eng, fn, reads=(), writes=(), dma_key=None, inc=None):
        self.ops.append(dict(eng=eng, fn=fn, reads=tuple(reads), writes=tuple(writes),
                             dma=dma_key is not None, key=dma_key, inc=inc))
        return len(self.ops) - 1

    def fence(self, eng="vector"):
        self.ops.append(dict(eng=eng, fn=self.fence_fn, reads=(), writes="ALL", dma=False, key=None, inc=None))

    def emit(self, stack):
        nc = self.nc
        ops = self.ops
        n = len(ops)
        allkeys = set()
        for o in ops:
            if o["writes"] != "ALL":
                allkeys.update(o["reads"]); allkeys.update(o["writes"])
        allkeys = tuple(sorted(allkeys, key=str))
        for o in ops:
            if o["writes"] == "ALL":
                o["writes"] = allkeys
        last_w = {}
        readers = {}
        deps = [None] * n
        for i, o in enumerate(ops):
            d = set()
            for r in o["reads"]:
                if r in last_w:
                    d.add(last_w[r])
            for w in o["writes"]:
                if w in last_w:
                    d.add(last_w[w])
                for j in readers.get(w, ()):
                    d.add(j)
            d.discard(i)
            dd = []
            for j in d:
                oj = ops[j]
                if (not oj["dma"]) and (not o["dma"]) and oj["eng"] == o["eng"] == "tensor":
                    continue
                dd.append(j)
            deps[i] = dd
            for r in o["reads"]:
                readers.setdefault(r, []).append(i)
            for w in o["writes"]:
                last_w[w] = i
                readers[w] = []
        needed = [False] * n
        for i in range(n):
            for j in deps[i]:
                needed[j] = True
        sem_handles = {}

        def get_sem(name):
            if name not in sem_handles:
                sem_handles[name] = stack.enter_context(nc.semaphore(name))
            return sem_handles[name]

        cnt = {}
        sig = [None] * n
        dma_cum_at = {}
        for i, o in enumerate(ops):
            if o["dma"]:
                base = "d_" + str(o["key"])
                inc = o["inc"] or 16
                lim = ROT
            else:
                if not needed[i]:
                    continue
                base = "e_" + o["eng"]
                inc = 1
                lim = ROT
            g, c = cnt.get(base, (0, 0))
            if c + inc > lim * (16 if o["dma"] else 1):
                g, c = g + 1, 0
            c += inc
            cnt[base] = (g, c)
            sig[i] = (base + "_" + str(g), c)
            if o["dma"]:
                dma_cum_at.setdefault(base, []).append((i, sig[i][0], c))
        import bisect
        dma_idx = {k: [t[0] for t in v] for k, v in dma_cum_at.items()}
        waits = [None] * n
        waited = {e: {} for e in ENGS}
        for i, o in enumerate(ops):
            need = {}
            for j in deps[i]:
                oj = ops[j]
                if oj["dma"]:
                    base = "d_" + str(oj["key"])
                    lst = dma_cum_at[base]
                    pos = bisect.bisect_left(dma_idx[base], i) - 1
                    sname_j, vj = sig[j]
                    k = pos
                    while lst[k][1] != sname_j:
                        k -= 1
                    sname, val = lst[k][1], lst[k][2]
                else:
                    sname, val = sig[j]
                if need.get(sname, 0) < val:
                    need[sname] = val
            wl = []
            wd = waited[o["eng"]]
            for sname, val in need.items():
                if wd.get(sname, 0) >= val:
                    continue
                wd[sname] = val
                wl.append((sname, val))
            waits[i] = wl
        self.n_waits = sum(len(w) for w in waits)
        per_eng = {e: [i for i, o in enumerate(ops) if o["eng"] == e] for e in ENGS}
        block = stack.enter_context(nc.Block())

        def body(engname):
            def f(eng):
                for i in per_eng[engname]:
                    for sname, val in waits[i]:
                        eng.wait_ge(get_sem(sname), val)
                    inst = ops[i]["fn"](eng)
                    if sig[i] is not None:
                        inst.then_inc(get_sem(sig[i][0]), (ops[i]["inc"] or 16) if ops[i]["dma"] else 1)
            return f

        for i in range(n):
            if sig[i] is not None:
                get_sem(sig[i][0])
        block.tensor(body("tensor"))
        block.vector(body("vector"))
        block.scalar(body("scalar"))
        block.gpsimd(body("gpsimd"))
        block.sync(body("sync"))
        return {e: len(v) for e, v in per_eng.items()}


F32 = mybir.dt.float32
BF16 = mybir.dt.bfloat16
AF = mybir.ActivationFunctionType
ALU = mybir.AluOpType
D = 4096
DFF = 16384
EPS = 1e-6
NTOK = 1024
TP = 512
FB = 256
NFB = DFF // FB


def make_shared(nc, P, st):
    ident_d = nc.dram_tensor("ident", [128, 128], BF16, kind="ExternalInput").ap()
    ones_d = nc.dram_tensor("ones", [128, 128], BF16, kind="ExternalInput").ap()
    gains = nc.dram_tensor("gains", [4, 128, D], F32, kind="ExternalInput").ap()
    ident = st.enter_context(nc.sbuf_tensor("ident_s", [128, 128], BF16))
    ones = st.enter_context(nc.sbuf_tensor("ones_s", [128, 128], BF16))
    small = st.enter_context(nc.sbuf_tensor("small", [128, 16], F32))
    dummy = st.enter_context(nc.sbuf_tensor("fdummy", [128, 8], F32))
    epsb = st.enter_context(nc.sbuf_tensor("epsb", [128, 2], F32))
    P.fence_fn = lambda e: e.memset(dummy[:], 0.0)
    P.op("vector", lambda e: e.memset(epsb[:, 0:1], EPS), writes=["epsb"])
    P.op("vector", lambda e: e.memset(epsb[:, 1:2], 1e-5), writes=["epsb"])
    P.op("sync", lambda e: e.dma_start(out=ident[:], in_=ident_d), writes=["ident"], dma_key="ident")
    P.op("sync", lambda e: e.dma_start(out=ones[:], in_=ones_d), writes=["ones"], dma_key="ones")
    psum = [st.enter_context(nc.psum_tensor(f"ps{i}", [128, 512], F32)) for i in range(8)]
    pst = [psum[6 + i][:, :].bitcast(BF16) for i in range(2)]
    state = dict(ps=0, pt=0)

    def next_ps():
        i = state["ps"]; state["ps"] = (i + 1) % 6
        return i
    return dict(ident=ident, ones=ones, small=small, epsb=epsb, psum=psum, pst=pst, state=state, next_ps=next_ps, gains=gains)


def body_B(nc, P, st, sh, ysrc, npass=2, stages=(0, 1, 2, 3, 4, 5), nfb=NFB):
    x = nc.dram_tensor("xo", [NTOK, D], F32, kind="ExternalInput").ap()
    wg = nc.dram_tensor("wg", [64, 128, 32, 128], F32, kind="ExternalInput").ap()
    wu = nc.dram_tensor("wu", [64, 128, 16, 128], F32, kind="ExternalInput").ap()
    wo = nc.dram_tensor("wo", [16, 128, 32, 256], F32, kind="ExternalInput").ap()
    w1 = nc.dram_tensor("w1", [NFB, 128, 32, FB], F32, kind="ExternalInput").ap()
    w2 = nc.dram_tensor("w2", [NFB, 128, FB // 128, D], F32, kind="ExternalInput").ap()
    out = nc.dram_tensor("out", [NTOK, D], F32, kind="ExternalOutput").ap()
    x1_d = nc.dram_tensor("x1_d", [NTOK, D], F32).ap()
    gains, ident, small, epsb = sh["gains"], sh["ident"], sh["small"], sh["epsb"]
    psum, pst, state, next_ps = sh["psum"], sh["pst"], sh["state"], sh["next_ps"]
    if True:
        arena = st.enter_context(nc.sbuf_tensor("arena", [128, 172 * 256], F32))
        if ysrc[0] == "gather":
            selt = st.enter_context(nc.sbuf_tensor("selt", [128, 4], F32))
            P.op("sync", lambda e: e.dma_start(out=selt[:], in_=ysrc[2]), writes=["selt"], dma_key="selt")

        def AV(off_kib, size_kib, dt):
            v = arena[:, off_kib * 256:(off_kib + size_kib) * 256]
            return v.bitcast(BF16) if dt == BF16 else v

        bufA = AV(0, 32, BF16).rearrange("p (k t) -> p k t", k=32)
        bufB = AV(32, 32, BF16).rearrange("p (k t) -> p k t", k=32)
        bufM = AV(64, 32, BF16).rearrange("p (k t) -> p k t", k=32)
        bufZ = AV(96, 64, F32).rearrange("p (a d) -> p a d", a=4)
        wbuf = [AV(96 + 24 * s, 24, BF16) for s in range(2)]
        wo_v = [AV(32 + 16 * s, 16, BF16).rearrange("p (k j) -> p k j", k=32) for s in range(2)]
        w1_v = [AV(64 + 16 * s, 16, BF16).rearrange("p (k j) -> p k j", k=32) for s in range(2)]
        w2_v = [AV(32 + 16 * s, 16, BF16).rearrange("p (k j) -> p k j", k=FB // 128) for s in range(2)]
        xt_R3, gt_R3 = AV(64, 16, F32), AV(80, 16, F32)
        xt_R2, gt_R2 = AV(32, 16, F32), AV(48, 16, F32)
        hb_R4 = AV(144, 8, BF16)
        hb_R3 = AV(64, 8, BF16)
        t1 = [AV(160 + 2 * i, 2, F32) for i in range(2)]
        t2 = [AV(164 + 2 * i, 2, F32) for i in range(2)]
        ub = [AV(168 + 2 * i, 2, BF16).rearrange("p (k t) -> p k t", k=FB // 128) for i in range(2)]
        def load_gain(idx, gt):
            P.op("sync", lambda e: e.dma_start(out=gt, in_=gains[idx]), reads=["gains"], writes=["gt"], dma_key="gt")


        def rstd_from(src_ap, src_key, col, hb):
            P.op("vector", lambda e: e.memset(small[:, col:col + 1], 0.0), writes=[f"sm{col}"])
            P.op("scalar", lambda e: e.activation(out=hb, in_=src_ap, func=AF.Square, accum_out=small[:, col:col + 1]),
                 reads=[src_key, f"sm{col}"], writes=["hb", f"sm{col}"])
            P.op("scalar", lambda e: e.activation(out=small[:, col:col + 1], in_=small[:, col:col + 1], func=AF.Sqrt, scale=1.0 / D, bias=epsb[:, 0:1]),
                 reads=[f"sm{col}", "epsb"], writes=[f"sm{col}"])
            P.op("vector", lambda e: e.reciprocal(out=small[:, col:col + 1], in_=small[:, col:col + 1]), reads=[f"sm{col}"], writes=[f"sm{col}"])

        def norm_transpose(src_ap, src_key, col, gt, hb, tt):
            P.op("vector", lambda e: e.scalar_tensor_tensor(out=hb, in0=src_ap, scalar=small[:, col:col + 1], in1=gt,
                                                            op0=ALU.mult, op1=ALU.mult), reads=[src_key, f"sm{col}", "gt"], writes=["hb"])
            for k8 in range(4):
                pi = state["pt"]; state["pt"] ^= 1
                for j in range(8):
                    kc = k8 * 8 + j
                    P.op("tensor", lambda e, kc=kc, j=j, pi=pi: e.transpose(out=pst[pi][:, j * 128:(j + 1) * 128], in_=hb[:, kc * 128:(kc + 1) * 128], identity=ident[:]),
                         reads=["hb", "ident"], writes=[f"ps{6 + pi}"])
                dst = bufA[:, k8 * 8:(k8 + 1) * 8, tt * 128:(tt + 1) * 128]
                srcp = pst[pi][:, :].rearrange("p (k t) -> p k t", k=8)
                if k8 % 2 == 0:
                    P.op("scalar", lambda e, dst=dst, srcp=srcp: e.activation(out=dst, in_=srcp, func=AF.Copy), reads=[f"ps{6 + pi}"], writes=["bufA"])
                else:
                    P.op("vector", lambda e, dst=dst, srcp=srcp: e.tensor_copy(out=dst, in_=srcp), reads=[f"ps{6 + pi}"], writes=["bufA"])

        for ps_i in range(npass):
            tok0 = ps_i * TP
            if ps_i > 0 or ysrc[0] != "gather":
                P.fence()
            if 0 in stages:
                xt, gt, hb = xt_R3, gt_R3, hb_R4
                load_gain(0, gt)
                for tt in range(4):
                    r0 = tok0 + tt * 128
                    P.op("sync", lambda e, r0=r0, xt=xt: e.dma_start(out=xt, in_=x[r0:r0 + 128, :]), writes=["xt"], dma_key="xt")
                    rstd_from(xt, "xt", 0, hb)
                    norm_transpose(xt, "xt", 0, gt, hb, tt)
                if ysrc[0] == "input":
                    P.op("sync", lambda e, tok0=tok0: e.dma_start(out=bufB, in_=ysrc[1][:, :, tok0:tok0 + TP].rearrange("k p t -> p k t")), writes=["bufB"], dma_key="bufB")
                else:
                    G = ysrc[1]
                    cand = AV(96, 32, BF16).rearrange("p (k t) -> p k t", k=32)
                    for q in range(4):
                        t0 = 1024 * q + tok0
                        for part in range(2):
                            dstv = cand[:, 16 * part:16 * (part + 1), :].rearrange("p (g k) t -> p g k t", g=4) if part == 0 else \
                                cand[:, 16 * part:16 * (part + 1), :].rearrange("p (k g) t -> p g k t", g=4)
                            for gq in range(4):
                                P.op("sync", lambda e, dstv=dstv, part=part, t0=t0, gq=gq: e.dma_start(out=dstv[:, gq, :, :], in_=G[4 * part:4 * part + 4, gq, :, t0:t0 + TP].rearrange("k p t -> p k t")),
                                     reads=["G"], writes=["cand"], dma_key="cand")
                        if q == 0:
                            P.op("vector", lambda e: e.tensor_scalar(out=bufB, in0=cand, scalar1=selt[:, 0:1], scalar2=None, op0=ALU.mult), reads=["cand", "selt"], writes=["bufB"])
                        else:
                            P.op("vector", lambda e, q=q: e.scalar_tensor_tensor(out=bufB, in0=cand, scalar=selt[:, q:q + 1], in1=bufB, op0=ALU.mult, op1=ALU.add), reads=["cand", "selt", "bufB"], writes=["bufB"])
            P.fence()
            if 1 in stages:
                for cc in range(32):
                    s = cc % 2
                    wb = wbuf[s]
                    vgA = wb[:, 0:4096].rearrange("p (k j) -> p k j", k=32)
                    vgB = wb[:, 4096:8192].rearrange("p (k j) -> p k j", k=32)
                    vuA = wb[:, 8192:10240].rearrange("p (k j) -> p k j", k=16)
                    vuB = wb[:, 10240:12288].rearrange("p (k j) -> p k j", k=16)
                    for (dst, src) in ((vgA, wg[cc]), (vgB, wg[32 + cc]), (vuA, wu[cc]), (vuB, wu[32 + cc])):
                        P.op("gpsimd", lambda e, dst=dst, src=src: e.dma_start(out=dst, in_=src, max_dma_last_dim=4096), writes=[f"wbuf{s}"], dma_key=f"wbuf{s}")
                    pgA, pgB, puA, puB = next_ps(), next_ps(), next_ps(), next_ps()
                    for (pi, wv, nk, src, koff) in ((pgA, vgA, 32, bufA, 0), (puA, vuA, 16, bufB, 0), (pgB, vgB, 32, bufA, 0), (puB, vuB, 16, bufB, 16)):
                        for kc in range(nk):
                            P.op("tensor", lambda e, kc=kc, pi=pi, wv=wv, nk=nk, src=src, koff=koff: e.matmul(psum[pi][:], lhsT=wv[:, kc, :], rhs=src[:, koff + kc, :], start=(kc == 0), stop=(kc == nk - 1)),
                                 reads=[f"wbuf{s}", "bufA", "bufB"], writes=[f"ps{pi}"])
                    P.op("scalar", lambda e, pgA=pgA, s=s: e.activation(out=t1[s], in_=psum[pgA][:], func=AF.Sigmoid), reads=[f"ps{pgA}"], writes=[f"t1_{s}"])
                    P.op("vector", lambda e, puA=puA, s=s: e.tensor_tensor(out=t1[s], in0=t1[s], in1=psum[puA][:], op=ALU.mult), reads=[f"ps{puA}", f"t1_{s}"], writes=[f"t1_{s}"])
                    P.op("scalar", lambda e, pgB=pgB, s=s: e.activation(out=t2[s], in_=psum[pgB][:], func=AF.Sigmoid), reads=[f"ps{pgB}"], writes=[f"t2_{s}"])
                    P.op("vector", lambda e, puB=puB, s=s: e.tensor_tensor(out=t2[s], in0=t2[s], in1=psum[puB][:], op=ALU.mult), reads=[f"ps{puB}", f"t2_{s}"], writes=[f"t2_{s}"])
                    P.op("vector", lambda e, cc=cc, s=s: e.tensor_tensor(out=bufM[:, cc, :], in0=t1[s], in1=t2[s], op=ALU.add), reads=[f"t1_{s}", f"t2_{s}"], writes=["bufM"])
            P.fence()
            if 2 in stages:
                for nb in range(16):
                    s = nb % 2
                    wv = wo_v[s]
                    for half in range(2):
                        P.op("gpsimd", lambda e, wv=wv, nb=nb, half=half: e.dma_start(out=wv[:, half * 16:(half + 1) * 16, :], in_=wo[nb][:, half * 16:(half + 1) * 16, :], max_dma_last_dim=4096),
                             writes=[f"wo{s}"], dma_key=f"wo{s}")
                    for tt in range(4):
                        pi = next_ps()
                        for kc in range(32):
                            P.op("tensor", lambda e, kc=kc, pi=pi, wv=wv, tt=tt: e.matmul(psum[pi][:, 0:256], lhsT=bufM[:, kc, tt * 128:(tt + 1) * 128], rhs=wv[:, kc, :], start=(kc == 0), stop=(kc == 31)),
                                 reads=[f"wo{s}", "bufM"], writes=[f"ps{pi}"])
                        dst = bufZ[:, tt, nb * 256:(nb + 1) * 256]
                        if (tt + nb) % 2 == 0:
                            P.op("scalar", lambda e, dst=dst, pi=pi: e.activation(out=dst, in_=psum[pi][:, 0:256], func=AF.Copy), reads=[f"ps{pi}"], writes=[f"bufZ{tt}_{nb % 8}"])
                        else:
                            P.op("vector", lambda e, dst=dst, pi=pi: e.tensor_copy(out=dst, in_=psum[pi][:, 0:256]), reads=[f"ps{pi}"], writes=[f"bufZ{tt}_{nb % 8}"])
            P.fence()
            if 3 in stages:
                xt, gt, hb = xt_R2, gt_R2, hb_R3
                load_gain(1, gt)
                for tt in range(4):
                    r0 = tok0 + tt * 128
                    zt = bufZ[:, tt, :]
                    rstd_from(zt, f"bufZ{tt}", 1, hb)
                    P.op("sync", lambda e, r0=r0, xt=xt: e.dma_start(out=xt, in_=x[r0:r0 + 128, :]), writes=["xt"], dma_key="xt")
                    P.op("vector", lambda e, zt=zt, gt=gt: e.scalar_tensor_tensor(out=zt, in0=zt, scalar=small[:, 1:2], in1=gt, op0=ALU.mult, op1=ALU.mult),
                         reads=[f"bufZ{tt}", "sm1", "gt"], writes=[f"bufZ{tt}"])
                    P.op("vector", lambda e, zt=zt, xt=xt: e.tensor_tensor(out=zt, in0=zt, in1=xt, op=ALU.add), reads=[f"bufZ{tt}", "xt"], writes=[f"bufZ{tt}"])
                    P.op("sync", lambda e, r0=r0, zt=zt: e.dma_start(out=x1_d[r0:r0 + 128, :], in_=zt), reads=[f"bufZ{tt}"], writes=[f"x1d{ps_i}_{tt}"], dma_key=f"x1st{tt}")
                load_gain(2, gt)
                for tt in range(4):
                    zt = bufZ[:, tt, :]
                    rstd_from(zt, f"bufZ{tt}", 2, hb)
                    norm_transpose(zt, f"bufZ{tt}", 2, gt, hb, tt)
            P.fence()
            if 4 in stages:
                for fb in range(nfb):
                    s = fb % 2
                    w1v = w1_v[s]
                    w2v = w2_v[s]
                    for half in range(2):
                        P.op("gpsimd", lambda e, w1v=w1v, fb=fb, half=half: e.dma_start(out=w1v[:, half * 16:(half + 1) * 16, :], in_=w1[fb][:, half * 16:(half + 1) * 16, :], max_dma_last_dim=4096),
                             writes=[f"w1_{s}"], dma_key=f"w1_{s}")
                    for kc2 in range(FB // 128):
                        P.op("gpsimd", lambda e, w2v=w2v, fb=fb, kc2=kc2: e.dma_start(out=w2v[:, kc2, :], in_=w2[fb][:, kc2, :], max_dma_last_dim=4096),
                             writes=[f"w2_{s}"], dma_key=f"w2_{s}")
                    for fc in range(FB // 128):
                        pi = next_ps()
                        for kc in range(32):
                            P.op("tensor", lambda e, kc=kc, pi=pi, w1v=w1v, fc=fc: e.matmul(psum[pi][:], lhsT=w1v[:, kc, fc * 128:(fc + 1) * 128], rhs=bufA[:, kc, :], start=(kc == 0), stop=(kc == 31)),
                                 reads=[f"w1_{s}", "bufA"], writes=[f"ps{pi}"])
                        P.op("scalar", lambda e, pi=pi, s=s: e.activation(out=t1[s], in_=psum[pi][:], func=AF.Relu), reads=[f"ps{pi}"], writes=[f"t1_{s}"])
                        P.op("vector", lambda e, s=s, fc=fc: e.tensor_tensor(out=ub[s][:, fc, :], in0=t1[s], in1=t1[s], op=ALU.mult), reads=[f"t1_{s}"], writes=[f"ub{s}"])
                    for tt in range(4):
                        for nb in range(8):
                            pi = next_ps()
                            for kc2 in range(FB // 128):
                                P.op("tensor", lambda e, kc2=kc2, pi=pi, w2v=w2v, tt=tt, nb=nb, s=s: e.matmul(psum[pi][:], lhsT=ub[s][:, kc2, tt * 128:(tt + 1) * 128], rhs=w2v[:, kc2, nb * 512:(nb + 1) * 512],
                                                                                                          start=(kc2 == 0), stop=(kc2 == FB // 128 - 1)),
                                     reads=[f"w2_{s}", f"ub{s}"], writes=[f"ps{pi}"])
                            dst = bufZ[:, tt, nb * 512:(nb + 1) * 512]
                            key = f"bufZ{tt}_{nb}"
                            if fb == 0:
                                P.op("vector", lambda e, dst=dst, pi=pi: e.tensor_copy(out=dst, in_=psum[pi][:]), reads=[f"ps{pi}"], writes=[key])
                            else:
                                P.op("vector", lambda e, dst=dst, pi=pi: e.tensor_tensor(out=dst, in0=dst, in1=psum[pi][:], op=ALU.add), reads=[f"ps{pi}", key], writes=[key])
            P.fence()
            if 5 in stages:
                xt, gt, hb = xt_R3, gt_R3, AV(32, 8, BF16)
                load_gain(3, gt)
                for tt in range(4):
                    r0 = tok0 + tt * 128
                    zt = bufZ[:, tt, :]
                    rstd_from(zt, f"bufZ{tt}", 3, hb)
                    P.op("sync", lambda e, r0=r0, xt=xt: e.dma_start(out=xt, in_=x1_d[r0:r0 + 128, :]), reads=[f"x1d{ps_i}_{tt}"], writes=["xt"], dma_key="xt")
                    P.op("vector", lambda e, zt=zt, gt=gt: e.scalar_tensor_tensor(out=zt, in0=zt, scalar=small[:, 3:4], in1=gt, op0=ALU.mult, op1=ALU.mult),
                         reads=[f"bufZ{tt}", "sm3", "gt"], writes=[f"bufZ{tt}"])
                    P.op("vector", lambda e, zt=zt, xt=xt: e.tensor_tensor(out=zt, in0=zt, in1=xt, op=ALU.add), reads=[f"bufZ{tt}", "xt"], writes=[f"bufZ{tt}"])
                    P.op("sync", lambda e, r0=r0, zt=zt: e.dma_start(out=out[r0:r0 + 128, :], in_=zt), reads=[f"bufZ{tt}"], writes=["out"], dma_key=f"x1st{tt}")
        P.fence()


def build_B(npass=2, stages=(0, 1, 2, 3, 4, 5), nfb=NFB):
    nc = bass.Bass("TRN2", target_bir_lowering=False)
    yT = nc.dram_tensor("yT", [32, 128, NTOK], BF16, kind="ExternalInput").ap()
    P = Prog(nc)
    with ExitStack() as st:
        sh = make_shared(nc, P, st)
        body_B(nc, P, st, sh, ("input", yT), npass=npass, stages=stages, nfb=nfb)
        counts = P.emit(st)
        print("B ops", counts, "waits", P.n_waits)
    return nc


def relayout_B(inp, l=0):
    w_in = inp["w_in"][l]
    NR = 6592
    gcol0 = NR + 3 * 2048
    Wg = w_in[:, gcol0:gcol0 + 8192]
    wg = np.ascontiguousarray(Wg.reshape(32, 128, 64, 128).transpose(2, 1, 0, 3))
    Wu = np.concatenate([inp["w_up_rwkv"][l], inp["w_up_diff"][l]], axis=1)
    wu = np.ascontiguousarray(Wu.reshape(16, 128, 64, 128).transpose(2, 1, 0, 3))
    wo = np.ascontiguousarray(inp["w_out"][l].reshape(32, 128, 16, 256).transpose(2, 1, 0, 3))
    w1 = np.ascontiguousarray(inp["w_mlp_in"][l].reshape(32, 128, NFB, FB).transpose(2, 1, 0, 3))
    w2 = np.ascontiguousarray(inp["w_mlp_out"][l].reshape(NFB, FB // 128, 128, D).transpose(0, 2, 1, 3))
    gains = np.stack([np.broadcast_to(inp[k][l][None, :], (128, D)) for k in ("attn_pre_norm", "attn_post_norm", "mlp_pre_norm", "mlp_post_norm")]).astype(np.float32)
    ident = np.eye(128, dtype=np.float32).astype(ml_dtypes.bfloat16)
    ones = np.ones((128, 128), np.float32).astype(ml_dtypes.bfloat16)
    return dict(wg=wg, wu=wu, wo=wo, w1=w1, w2=w2, gains=np.ascontiguousarray(gains), ident=ident, ones=ones)


F32 = mybir.dt.float32
BF16 = mybir.dt.bfloat16
AF = mybir.ActivationFunctionType
ALU = mybir.AluOpType
S = 4096
C = 128
BLK = 512
NB = S // BLK
EPS_GN = 64e-5
DEC = -0.6065306597126334
RWMODE = 3


def emit_rwkv(nc, P, st, PT_d, yT, psum, pst, state, next_ps, ident, ones, epsb, rw_heads):
    par_d = nc.dram_tensor("par", [64, 8, 16], F32, kind="ExternalInput").ap()
    parl_d = nc.dram_tensor("parl", [128, 4, 2], F32, kind="ExternalInput").ap()
    lw2_d = nc.dram_tensor("lw2", [96, 4, 512], F32, kind="ExternalInput").ap()
    g2_d = nc.dram_tensor("g2", [128, 2, 512], F32, kind="ExternalInput").ap()
    masks_d = nc.dram_tensor("masks", [128, 2, 640], F32, kind="ExternalInput").ap()
    mreset_d = nc.dram_tensor("mreset", [64, BLK], F32, kind="ExternalInput").ap()

    def sb(name, shape, dt):
        return st.enter_context(nc.sbuf_tensor(name, shape, dt))
    par = sb("par_s", [64, 8, 20], F32)
    parl = sb("parl_s", [128, 4, 3], F32)
    lw2b = sb("lw2b", [96, 4, 512], BF16)
    g2b = sb("g2b", [128, 2, 512], BF16)
    masks = sb("masks_s", [128, 2, 640], F32)
    mreset = sb("mreset_s", [64, BLK], F32)
    twT = sb("twT", [128, S], BF16)
    daT = sb("daT", [128, S], BF16)
    sgT = sb("sgT", [128, 2, S], BF16)
    RAW = sb("RAW", [128, S // 2 + 2], F32)
    SH = sb("SH", [128, S // 2], F32)
    Rb16 = sb("R16", [64, S], BF16)
    Kb16 = sb("K16", [64, S], BF16)
    Vb16 = sb("V16", [64, S], BF16)
    Vt = sb("rVt", [128, 32, 64], BF16)
    KKN = sb("KKN", [64, S], BF16)
    OT = sb("OT", [64, S], F32)
    BONV = sb("BONV", [64, S], BF16)
    gnb = sb("gnb", [64, 1], F32)
    tiny = sb("tinyb", [64, 1], F32)
    LW = sb("LW", [64, BLK], F32)
    CI = sb("CI", [64, BLK], F32)
    CE = sb("CE", [64, BLK], F32)
    Ece = sb("Ece", [64, BLK], F32)
    Enci = sb("Enci", [64, BLK], F32)
    Eci = [sb(f"Eci{i}", [64, BLK], F32) for i in range(2)]
    At = sb("At", [64, BLK], F32)
    T1 = sb("T1", [64, BLK], F32)
    KD = sb("KD", [64, BLK], F32)
    T2 = sb("T2b", [64, BLK], BF16)
    ops4 = [sb(f"ops4_{i}", [64, 4, BLK], BF16) for i in range(2)]
    tok3 = [sb(f"tok3_{i}", [128, 12, 64], BF16) for i in range(2)]
    G1s = [sb(f"G1s{c}", [128, 384], BF16) for c in range(8)]
    G2s = [sb(f"G2s{c}", [128, 256], BF16) for c in range(8)]
    MM = [[sb(f"MM{c}_{i}", [128, 256], BF16) for i in range(2)] for c in range(8)]
    Qs = [[sb(f"Qs{c}_{i}", [128, 128], BF16) for i in range(2)] for c in range(8)]
    IMs = [[sb(f"IM{c}_{i}", [128, 128], BF16) for i in range(2)] for c in range(8)]
    nW1T = [sb(f"nW1T{c}", [64, 128], BF16) for c in range(8)]
    nXs = [sb(f"nXs{c}", [128, 64], BF16) for c in range(8)]
    Us = sb("Us", [128, 64], BF16)
    Hf = sb("Hf", [64, 64], F32)
    Hb = sb("Hb", [64, 64], BF16)
    yo = sb("ryo", [64, BLK], BF16)
    ob = sb("ob", [64, BLK], BF16)
    dd = sb("dd", [64, BLK], F32)
    ones64 = ones[0:64, 0:64]
    id64 = ident[0:64, 0:64]

    def V(fn, reads, writes):
        P.op("vector", fn, reads=reads, writes=writes)

    def A(fn, reads, writes):
        P.op("scalar", fn, reads=reads, writes=writes)

    def T(fn, reads, writes):
        P.op("tensor", fn, reads=reads, writes=writes)

    for (dst, src, k, q) in ((par[:, :, 0:16], par_d, "par", "sync"), (parl[:, :, 0:2], parl_d, "parl", "sync"), (masks[:], masks_d, "masks", "sync"),
                             (mreset[:], mreset_d, "mreset", "sync"), (lw2b[:], lw2_d, "lw2b", "gpsimd"), (g2b[:], g2_d, "g2b", "gpsimd")):
        P.op(q, lambda e, dst=dst, src=src: e.dma_start(out=dst, in_=src), writes=[k], dma_key=k)
    V(lambda e: e.memset(gnb[:], EPS_GN), [], ["gnb"])
    V(lambda e: e.memset(tiny[:], 1e-24), [], ["gnb"])
    for i in range(3):
        V(lambda e, i=i: e.tensor_tensor(out=par[:, :, 16 + i], in0=par[:, :, 2 * i], in1=par[:, :, 2 * i + 1], op=ALU.add), ["par"], ["par"])
        V(lambda e, i=i: e.tensor_scalar(out=par[:, :, 16 + i], in0=par[:, :, 16 + i], scalar1=-1.0, scalar2=1.0, op0=ALU.mult, op1=ALU.add), ["par"], ["par"])
    V(lambda e: e.tensor_tensor(out=parl[:, :, 2], in0=parl[:, :, 0], in1=parl[:, :, 1], op=ALU.add), ["parl"], ["parl"])
    V(lambda e: e.tensor_scalar(out=parl[:, :, 2], in0=parl[:, :, 2], scalar1=-1.0, scalar2=1.0, op0=ALU.mult, op1=ALU.add), ["parl"], ["parl"])

    HS = S // 2

    def load_shift(ch, r0, nrow, c0, mup, mun, consume):
        for half in range(2):
            t0 = half * HS
            lo = max(t0 - 1, 0)
            hi = min(t0 + HS + 1, S)
            off = lo - (t0 - 1)
            if half == 0:
                V(lambda e: e.memset(RAW[0:nrow, 0:1], 0.0), [], ["RAW"])
            else:
                V(lambda e: e.memset(RAW[0:nrow, HS + 1:HS + 2], 0.0), [], ["RAW"])
            P.op("sync", lambda e, lo=lo, hi=hi, off=off: e.dma_start(out=RAW[0:nrow, off:off + (hi - lo)], in_=PT_d[ch][r0:r0 + nrow, lo:hi]), reads=[f"PT{ch}"], writes=["RAW"], dma_key="RAW")
            V(lambda e: e.tensor_scalar(out=SH[0:nrow, :], in0=RAW[0:nrow, 1:HS + 1], scalar1=c0, scalar2=None, op0=ALU.mult), ["RAW", "par", "parl"], ["SH"])
            V(lambda e: e.scalar_tensor_tensor(out=SH[0:nrow, :], in0=RAW[0:nrow, 0:HS], scalar=mup, in1=SH[0:nrow, :], op0=ALU.mult, op1=ALU.add), ["RAW", "SH", "par", "parl"], ["SH"])
            V(lambda e: e.scalar_tensor_tensor(out=SH[0:nrow, :], in0=RAW[0:nrow, 2:HS + 2], scalar=mun, in1=SH[0:nrow, :], op0=ALU.mult, op1=ALU.add), ["RAW", "SH", "par", "parl"], ["SH"])
            consume(half)

    for i, (ch, dstT, fn) in enumerate(((12, twT, AF.Tanh), (13, daT, AF.Copy), (14, sgT[:, 0, :], AF.Sigmoid), (15, sgT[:, 1, :], AF.Sigmoid))):
        def cons(half, dstT=dstT, fn=fn):
            A(lambda e: e.activation(out=dstT[:, half * HS:(half + 1) * HS], in_=SH[:, :], func=fn), ["SH"], ["lora_act"])
        load_shift(ch, 0, 128, parl[:, i, 2:3], parl[:, i, 0:1], parl[:, i, 1:2], cons)

    def head(hh):
        ch_r, ch_k, ch_v = hh // 2, 4 + hh // 2, 8 + hh // 2
        r0 = (hh % 2) * 64
        pc = lambda j: par[:, hh, j:j + 1]
        for (ch, i, dst) in ((ch_r, 0, Rb16), (ch_k, 1, Kb16), (ch_v, 2, Vb16)):
            def cons(half, dst=dst):
                A(lambda e: e.activation(out=dst[:, half * HS:(half + 1) * HS], in_=SH[0:64, :], func=AF.Copy), ["SH"], ["rkv"])
            load_shift(ch, r0, 64, pc(16 + i), pc(2 * i), pc(2 * i + 1), cons)
        for half in range(2):
            pi = state["pt"]; state["pt"] ^= 1
            for j in range(16):
                blk = half * 16 + j
                T(lambda e, blk=blk, j=j, pi=pi: e.transpose(out=pst[pi][:, j * 64:(j + 1) * 64], in_=Vb16[:, blk * 128:(blk + 1) * 128], identity=id64), ["rkv", "ident"], [f"ps{6 + pi}"])
            V(lambda e, half=half, pi=pi: e.tensor_copy(out=Vt[:, half * 16:(half + 1) * 16, :], in_=pst[pi][:, :].rearrange("p (k t) -> p k t", k=16)), [f"ps{6 + pi}"], ["rVt"])
        for b in range(NB):
            bs = slice(b * BLK, (b + 1) * BLK)
            V(lambda e, bs=bs: e.tensor_scalar(out=T1[:], in0=Kb16[:, bs], scalar1=pc(10), scalar2=None, op0=ALU.mult), ["rkv", "par"], ["T1"])
            A(lambda e: e.activation(out=T2[:], in_=T1[:], func=AF.Square), ["T1"], ["T2"])
            pi = next_ps()
            T(lambda e, pi=pi: e.matmul(psum[pi][0:64, :], lhsT=ones64, rhs=T2[:], start=True, stop=True), ["T2", "ones"], [f"ps{pi}"])
            A(lambda e, pi=pi: e.activation(out=KD[:], in_=psum[pi][0:64, :], func=AF.Sqrt, bias=tiny[:, 0:1]), [f"ps{pi}", "gnb"], ["KD"])
            V(lambda e: e.reciprocal(out=KD[:], in_=KD[:]), ["KD"], ["KD"])
            V(lambda e, bs=bs: e.tensor_tensor(out=KKN[:, bs], in0=T1[:], in1=KD[:], op=ALU.mult), ["T1", "KD"], ["KKN"])
        def direction(d):
            V(lambda e: e.memset(Hf[:], 0.0), [], ["Hf"])
            V(lambda e: e.memset(Hb[:], 0.0), [], ["Hb"])
            blocks = range(NB) if d == 0 else range(NB - 1, -1, -1)
            def block(bi, b):
                bs = slice(b * BLK, (b + 1) * BLK)
                sl = bi % 2
                O4, K3, EC = ops4[sl], tok3[sl], Eci[sl]
                hs = slice(hh * 64, (hh + 1) * 64)
                pi = 6
                T(lambda e, pi=pi, bs=bs, hs=hs: e.matmul(psum[pi][0:64, :], lhsT=lw2b[:, d, hs], rhs=twT[0:96, bs], start=True, stop=True), ["lw2b", "lora_act"], [f"ps{pi}"])
                A(lambda e, pi=pi: e.activation(out=LW[:], in_=psum[pi][0:64, :], func=AF.Sigmoid, bias=pc(6 + d)), [f"ps{pi}", "par"], ["LW"])
                V(lambda e: e.tensor_scalar(out=LW[:], in0=LW[:], scalar1=DEC, scalar2=None, op0=ALU.mult), ["LW"], ["LW"])
                V(lambda e: e.tensor_tensor_scan(out=CI[:], data0=mreset[:], data1=LW[:], initial=0.0, op0=ALU.mult, op1=ALU.add), ["LW", "mreset"], ["CI"])
                if d == 0:
                    V(lambda e: e.tensor_tensor(out=CE[:], in0=CI[:], in1=LW[:], op=ALU.subtract), ["CI", "LW"], ["CE"])
                else:
                    for c in range(4):
                        cs = slice(c * C, (c + 1) * C)
                        V(lambda e, cs=cs, c=c: e.tensor_scalar(out=CE[:, cs], in0=CI[:, cs], scalar1=-1.0, scalar2=CI[:, c * C + C - 1:c * C + C], op0=ALU.mult, op1=ALU.add), ["CI"], ["CE"])
                    V(lambda e: e.tensor_tensor(out=CI[:], in0=CE[:], in1=LW[:], op=ALU.add), ["CE", "LW"], ["CI"])
                A(lambda e: e.activation(out=Ece[:], in_=CE[:], func=AF.Exp), ["CE"], ["Ece"])
                A(lambda e: e.activation(out=Enci[:], in_=CI[:], func=AF.Exp, scale=-1.0), ["CI"], ["Enci"])
                A(lambda e, EC=EC: e.activation(out=EC[:], in_=CI[:], func=AF.Exp), ["CI"], [f"Eci{sl}"])
                pi = 7
                T(lambda e, pi=pi, bs=bs, hs=hs: e.matmul(psum[pi][0:64, :], lhsT=lw2b[:, 2 + d, hs], rhs=daT[0:96, bs], start=True, stop=True), ["lw2b", "lora_act"], [f"ps{pi}"])
                A(lambda e, pi=pi: e.activation(out=At[:], in_=psum[pi][0:64, :], func=AF.Sigmoid, bias=pc(8 + d)), [f"ps{pi}", "par"], ["At"])
                V(lambda e: e.tensor_scalar(out=T1[:], in0=At[:], scalar1=-1.0, scalar2=pc(11), op0=ALU.add, op1=ALU.mult), ["At", "par"], ["T1"])
                V(lambda e, bs=bs: e.scalar_tensor_tensor(out=KD[:], in0=T1[:], scalar=1.0, in1=Kb16[:, bs], op0=ALU.add, op1=ALU.mult), ["T1", "rkv"], ["KD"])
                V(lambda e, bs=bs: e.tensor_tensor(out=At[:], in0=At[:], in1=KKN[:, bs], op=ALU.mult), ["At", "KKN"], ["At"])
                V(lambda e, bs=bs, O4=O4: e.tensor_tensor(out=O4[:, 0, :], in0=KKN[:, bs], in1=Ece[:], op=ALU.mult), ["KKN", "Ece"], [f"ops4_{sl}"])
                V(lambda e, O4=O4: e.tensor_tensor(out=O4[:, 1, :], in0=At[:], in1=Enci[:], op=ALU.mult), ["At", "Enci"], [f"ops4_{sl}"])
                V(lambda e, O4=O4: e.tensor_tensor(out=O4[:, 2, :], in0=KD[:], in1=Enci[:], op=ALU.mult), ["KD", "Enci"], [f"ops4_{sl}"])
                V(lambda e, bs=bs, O4=O4, EC=EC: e.tensor_tensor(out=O4[:, 3, :], in0=Rb16[:, bs], in1=EC[:], op=ALU.mult), ["rkv", f"Eci{sl}"], [f"ops4_{sl}"])
                V(lambda e, bs=bs: e.scalar_tensor_tensor(out=T2[:], in0=Rb16[:, bs], scalar=pc(12), in1=KD[:], op0=ALU.mult, op1=ALU.mult), ["rkv", "KD", "par"], ["T2"])
                pi = 6
                T(lambda e, pi=pi: e.matmul(psum[pi][0:64, :], lhsT=ones64, rhs=T2[:], start=True, stop=True), ["T2", "ones"], [f"ps{pi}"])
                V(lambda e, pi=pi, bs=bs: e.scalar_tensor_tensor(out=T1[:], in0=psum[pi][0:64, :], scalar=0.5, in1=Vb16[:, bs], op0=ALU.mult, op1=ALU.mult), [f"ps{pi}", "rkv"], ["T1"])
                if d == 0:
                    V(lambda e, bs=bs: e.tensor_copy(out=BONV[:, bs], in_=T1[:]), ["T1"], ["BONV"])
                else:
                    V(lambda e, bs=bs: e.tensor_tensor(out=BONV[:, bs], in0=BONV[:, bs], in1=T1[:], op=ALU.add), ["T1", "BONV"], ["BONV"])
                pi = state["pt"]; state["pt"] ^= 1
                for o in range(3):
                    for c in range(4):
                        T(lambda e, o=o, c=c, pi=pi, O4=O4: e.transpose(out=pst[pi][:, (o * 4 + c) * 64:(o * 4 + c + 1) * 64], in_=O4[:, o, c * C:(c + 1) * C], identity=id64),
                          [f"ops4_{sl}", "ident"], [f"ps{6 + pi}"])
                V(lambda e, pi=pi, K3=K3: e.tensor_copy(out=K3[:, :, :], in_=pst[pi][:, 0:768].rearrange("p (k t) -> p k t", k=12)), [f"ps{6 + pi}"], [f"tok3_{sl}"])
                chunks = list(range(4)) if d == 0 else list(range(3, -1, -1))
                ok = [f"ops4_{sl}"]
                cst_ = {}

                def opsof(c):
                    cs = slice(c * C, (c + 1) * C)
                    return O4[:, 0, cs], O4[:, 1, cs], O4[:, 2, cs], O4[:, 3, cs]

                kx = lambda c: sl * 4 + c

                def gram1(c):
                    Ab_c, Bb_c, Kb_c, Rb_c = opsof(c)
                    p1, p2 = next_ps(), next_ps()
                    cst_[c] = dict(p1=p1, p2=p2)
                    T(lambda e: e.matmul(psum[p1][:, 0:128], lhsT=Bb_c, rhs=Ab_c, start=True, stop=True), ok, [f"ps{p1}"])
                    T(lambda e: e.matmul(psum[p1][:, 128:256], lhsT=Kb_c, rhs=Ab_c, start=True, stop=True), ok, [f"ps{p1}"])
                    T(lambda e: e.matmul(psum[p1][:, 256:384], lhsT=Ab_c, rhs=Bb_c, start=True, stop=True), ok, [f"ps{p1}"])
                    T(lambda e: e.matmul(psum[p2][:, 0:128], lhsT=Bb_c, rhs=Rb_c, start=True, stop=True), ok, [f"ps{p2}"])
                    T(lambda e: e.matmul(psum[p2][:, 128:256], lhsT=Kb_c, rhs=Rb_c, start=True, stop=True), ok, [f"ps{p2}"])

                def evac1(c):
                    p1, p2 = cst_[c]["p1"], cst_[c]["p2"]
                    V(lambda e: e.tensor_tensor(out=G1s[kx(c)][:], in0=psum[p1][:, 0:384], in1=masks[:, d, 0:384], op=ALU.mult), [f"ps{p1}", "masks"], [f"G1s{kx(c)}"])
                    V(lambda e: e.tensor_tensor(out=G2s[kx(c)][:], in0=psum[p2][:, 0:256], in1=masks[:, d, 384:640], op=ALU.mult), [f"ps{p2}", "masks"], [f"G2s{kx(c)}"])
                    V(lambda e: e.tensor_tensor(out=Qs[kx(c)][0][:], in0=G1s[kx(c)][:, 0:128], in1=ident[:], op=ALU.add), [f"G1s{kx(c)}", "ident"], [f"Qs{kx(c)}_0"])
                    cst_[c].update(M=G1s[kx(c)][:, 256:384], MT=G1s[kx(c)][:, 0:128], mk=f"G1s{kx(c)}", qi=0)

                def levelA(c, lev):
                    stc = cst_[c]
                    Mprev, MTprev, mk = stc["M"], stc["MT"], stc["mk"]
                    mi = lev % 2
                    pm = next_ps()
                    T(lambda e: e.matmul(psum[pm][:, 0:128], lhsT=MTprev, rhs=Mprev, start=True, stop=True), [mk], [f"ps{pm}"])
                    if lev < 6:
                        T(lambda e: e.matmul(psum[pm][:, 128:256], lhsT=Mprev, rhs=MTprev, start=True, stop=True), [mk], [f"ps{pm}"])
                    A(lambda e: e.activation(out=MM[kx(c)][mi][:], in_=psum[pm][:, 0:256], func=AF.Copy), [f"ps{pm}"], [f"MM{kx(c)}_{mi}"])
                    V(lambda e: e.tensor_tensor(out=IMs[kx(c)][mi][:], in0=MM[kx(c)][mi][:, 0:128], in1=ident[:], op=ALU.add), [f"MM{kx(c)}_{mi}", "ident"], [f"IM{kx(c)}_{mi}"])
                    stc["M"], stc["MT"], stc["mk"] = MM[kx(c)][mi][:, 0:128], MM[kx(c)][mi][:, 128:256], f"MM{kx(c)}_{mi}"

                def levelB(c, lev):
                    stc = cst_[c]
                    qi = stc["qi"]
                    mi = lev % 2
                    pq = next_ps()
                    T(lambda e: e.matmul(psum[pq][:, 0:128], lhsT=IMs[kx(c)][mi][:], rhs=Qs[kx(c)][qi][:], start=True, stop=True), [f"Qs{kx(c)}_{qi}", f"IM{kx(c)}_{mi}"], [f"ps{pq}"])
                    V(lambda e: e.tensor_copy(out=Qs[kx(c)][1 - qi][:], in_=psum[pq][:, 0:128]), [f"ps{pq}"], [f"Qs{kx(c)}_{1 - qi}"])
                    stc["qi"] = 1 - qi

                def w1x(c):
                    gc = b * 4 + c
                    qi = cst_[c]["qi"]
                    Q, qk = Qs[kx(c)][qi], f"Qs{kx(c)}_{qi}"
                    Abt = K3[:, 0 + c, :]
                    pw = next_ps()
                    T(lambda e: e.matmul(psum[pw][0:64, 0:128], lhsT=Abt, rhs=Q[:], start=True, stop=True), [f"tok3_{sl}", qk], [f"ps{pw}"])
                    A(lambda e: e.activation(out=nW1T[kx(c)][:], in_=psum[pw][0:64, 0:128], func=AF.Copy, scale=-1.0), [f"ps{pw}"], [f"nW1T{kx(c)}"])
                    px = next_ps()
                    T(lambda e: e.matmul(psum[px][:, 0:64], lhsT=G1s[kx(c)][:, 128:256], rhs=Vt[:, gc, :], start=True, stop=True), [f"G1s{kx(c)}", "rVt"], [f"ps{px}"])
                    A(lambda e: e.activation(out=nXs[kx(c)][:], in_=psum[px][:, 0:64], func=AF.Copy, scale=-1.0), [f"ps{px}"], [f"nXs{kx(c)}"])

                def seq(c):
                    gc = b * 4 + c
                    qi = cst_[c]["qi"]
                    Q, qk = Qs[kx(c)][qi], f"Qs{kx(c)}_{qi}"
                    Ab_c, Bb_c, Kb_c, Rb_c = opsof(c)
                    Bbt, Kbt = K3[:, 4 + c, :], K3[:, 8 + c, :]
                    Vtc = Vt[:, gc, :]
                    pu = next_ps()
                    T(lambda e: e.matmul(psum[pu][:, 0:64], lhsT=Q[:], rhs=nXs[kx(c)][:], start=True, stop=False), [qk, f"nXs{kx(c)}"], [f"ps{pu}"])
                    T(lambda e: e.matmul(psum[pu][:, 0:64], lhsT=nW1T[kx(c)][:], rhs=Hb[:], start=False, stop=True), [f"nW1T{kx(c)}", "Hb"], [f"ps{pu}"])
                    V(lambda e: e.tensor_copy(out=Us[:], in_=psum[pu][:, 0:64]), [f"ps{pu}"], ["Us"])
                    po = next_ps()
                    T(lambda e: e.matmul(psum[po][0:64, 0:128], lhsT=Hb[:], rhs=Rb_c, start=True, stop=False), ["Hb"] + ok, [f"ps{po}"])
                    T(lambda e: e.matmul(psum[po][0:64, 0:128], lhsT=Us[:], rhs=G2s[kx(c)][:, 0:128], start=False, stop=False), ["Us", f"G2s{kx(c)}"], [f"ps{po}"])
                    T(lambda e: e.matmul(psum[po][0:64, 0:128], lhsT=Vtc, rhs=G2s[kx(c)][:, 128:256], start=False, stop=True), ["rVt", f"G2s{kx(c)}"], [f"ps{po}"])
                    gs = slice(gc * C, (gc + 1) * C)
                    if d == 0:
                        A(lambda e: e.activation(out=OT[:, gs], in_=psum[po][0:64, 0:128], func=AF.Copy), [f"ps{po}"], ["OT"])
                    else:
                        V(lambda e: e.tensor_tensor(out=OT[:, gs], in0=OT[:, gs], in1=psum[po][0:64, 0:128], op=ALU.add), [f"ps{po}", "OT"], ["OT"])
                    ph = next_ps()
                    T(lambda e: e.matmul(psum[ph][0:64, 0:64], lhsT=Bbt, rhs=Us[:], start=True, stop=False), [f"tok3_{sl}", "Us"], [f"ps{ph}"])
                    T(lambda e: e.matmul(psum[ph][0:64, 0:64], lhsT=Kbt, rhs=Vtc, start=False, stop=True), [f"tok3_{sl}", "rVt"], [f"ps{ph}"])
                    gidx = c * C + C - 1 if d == 0 else c * C
                    gam = EC[:, gidx:gidx + 1]
                    V(lambda e: e.tensor_scalar(out=Hf[:], in0=Hf[:], scalar1=gam, scalar2=None, op0=ALU.mult), ["Hf", f"Eci{sl}"], ["Hf"])
                    V(lambda e: e.scalar_tensor_tensor(out=Hf[:], in0=psum[ph][0:64, 0:64], scalar=gam, in1=Hf[:], op0=ALU.mult, op1=ALU.add), ["Hf", f"Eci{sl}", f"ps{ph}"], ["Hf"])
                    A(lambda e: e.activation(out=Hb[:], in_=Hf[:], func=AF.Copy), ["Hf"], ["Hb"])

                stages = []
                for cg in (chunks[0:2], chunks[2:4]):
                    for c in cg:
                        stages.append(lambda c=c: gram1(c))
                    for c in cg:
                        stages.append(lambda c=c: evac1(c))
                for lev in range(1, 7):
                    for c in chunks:
                        stages.append(lambda c=c, lev=lev: levelA(c, lev))
                    for c in chunks:
                        stages.append(lambda c=c, lev=lev: levelB(c, lev))
                for c in chunks:
                    stages.append(lambda c=c: w1x(c))
                seqs = [(lambda c=c: seq(c)) for c in chunks]
                return stages, seqs
            prev = None
            for bi, b in enumerate(blocks):
                stages, seqs = block(bi, b)
                if prev is None or RWMODE != 3:
                    if prev is not None:
                        for f in prev:
                            f()
                    for f in stages:
                        f()
                else:
                    n = len(stages)
                    marks = {int((k + 1) * n / 5): k for k in range(4)}
                    for i, f in enumerate(stages):
                        f()
                        if (i + 1) in marks:
                            prev[marks[i + 1]]()
                prev = seqs
            for f in prev:
                f()
        direction(0)
        direction(1)
        for b in range(NB):
            bs = slice(b * BLK, (b + 1) * BLK)
            A(lambda e, bs=bs: e.activation(out=ob[:], in_=OT[:, bs], func=AF.Copy), ["OT"], ["ob"])
            pi = next_ps()
            T(lambda e, pi=pi: e.matmul(psum[pi][0:64, :], lhsT=ones64, rhs=ob[:], start=True, stop=True), ["ob", "ones"], [f"ps{pi}"])
            V(lambda e, pi=pi, bs=bs: e.scalar_tensor_tensor(out=dd[:], in0=psum[pi][0:64, :], scalar=-1.0 / 64, in1=OT[:, bs], op0=ALU.mult, op1=ALU.add), [f"ps{pi}", "OT"], ["dd"])
            A(lambda e: e.activation(out=ob[:], in_=dd[:], func=AF.Square), ["dd"], ["ob"])
            pi = next_ps()
            T(lambda e, pi=pi: e.matmul(psum[pi][0:64, :], lhsT=ones64, rhs=ob[:], start=True, stop=True), ["ob", "ones"], [f"ps{pi}"])
            A(lambda e, pi=pi: e.activation(out=T1[:], in_=psum[pi][0:64, :], func=AF.Sqrt, scale=1.0 / 64, bias=gnb[:, 0:1]), [f"ps{pi}", "gnb"], ["T1"])
            V(lambda e: e.reciprocal(out=T1[:], in_=T1[:]), ["T1"], ["T1"])
            V(lambda e: e.tensor_tensor(out=dd[:], in0=dd[:], in1=T1[:], op=ALU.mult), ["dd", "T1"], ["dd"])
            V(lambda e: e.tensor_scalar(out=dd[:], in0=dd[:], scalar1=pc(13), scalar2=pc(14), op0=ALU.mult, op1=ALU.add), ["dd", "par"], ["dd"])
            V(lambda e, bs=bs: e.tensor_tensor(out=dd[:], in0=dd[:], in1=BONV[:, bs], op=ALU.add), ["dd", "BONV"], ["dd"])
            pi = next_ps()
            for kc in range(2):
                T(lambda e, pi=pi, kc=kc, bs=bs: e.matmul(psum[pi][0:64, :], lhsT=g2b[:, kc, hh * 64:(hh + 1) * 64], rhs=sgT[:, kc, bs], start=(kc == 0), stop=(kc == 1)), ["g2b", "lora_act"], [f"ps{pi}"])
            V(lambda e, pi=pi: e.tensor_tensor(out=yo[:], in0=dd[:], in1=psum[pi][0:64, :], op=ALU.mult), ["dd", f"ps{pi}"], ["ryo"])
            P.op("sync", lambda e, bs=bs: e.dma_start(out=yT[hh // 2][(hh % 2) * 64:(hh % 2) * 64 + 64, bs], in_=yo[:]), reads=["ryo"], writes=["yT"], dma_key="ryo")

    for hh in range(rw_heads):
        head(hh)


def relayout_rwkv(inp, c, l=0):
    b, g = c // 4, c % 4
    chs = slice(512 * g, 512 * (g + 1))
    par = np.zeros((64, 8, 16), np.float32)
    sp, sn = inp["shift_prev"][l], inp["shift_next"][l]
    for hh in range(8):
        cg = slice(512 * g + hh * 64, 512 * g + (hh + 1) * 64)
        for i in range(3):
            par[:, hh, 2 * i] = sp[i * 2048:(i + 1) * 2048][cg]
            par[:, hh, 2 * i + 1] = sn[i * 2048:(i + 1) * 2048][cg]
        par[:, hh, 6] = inp["decay_bias_fwd"][l][cg]; par[:, hh, 7] = inp["decay_bias_bwd"][l][cg]
        par[:, hh, 8] = inp["iclr_bias_fwd"][l][cg]; par[:, hh, 9] = inp["iclr_bias_bwd"][l][cg]
        par[:, hh, 10] = inp["k_k"][l][cg]; par[:, hh, 11] = inp["k_a"][l][cg]
        par[:, hh, 12] = inp["r_k"][l].reshape(-1)[cg]
        par[:, hh, 13] = inp["ln_x_gain"][l][cg]; par[:, hh, 14] = inp["ln_x_bias"][l][cg]
    parl = np.zeros((128, 4, 2), np.float32)
    lo = 3 * 2048
    for i, (a0, n) in enumerate(((lo, 96), (lo + 96, 96), (lo + 192, 128), (lo + 320, 128))):
        parl[:n, i, 0] = sp[a0:a0 + n]; parl[:n, i, 1] = sn[a0:a0 + n]
    lw2 = np.stack([inp[k][l][:, chs] for k in ("decay_up_fwd", "decay_up_bwd", "iclr_up_fwd", "iclr_up_bwd")], axis=1)
    g2 = np.ascontiguousarray(inp["gate_up"][l][:, chs].reshape(2, 128, 512).transpose(1, 0, 2))
    ii = np.arange(128)
    row, col = ii[:, None], ii[None, :]
    masks = np.zeros((128, 2, 640), np.float32)
    for d in range(2):
        lt = (row < col) if d == 0 else (row > col)
        le = (row <= col) if d == 0 else (row >= col)
        masks[:, d, 0:128] = -lt.astype(np.float32)
        masks[:, d, 128:256] = lt
        masks[:, d, 256:384] = -lt.T.astype(np.float32)
        masks[:, d, 384:512] = le
        masks[:, d, 512:640] = le
    mreset = np.ones((64, BLK), np.float32)
    mreset[:, ::C] = 0.0
    return dict(par=par, parl=parl, lw2=np.ascontiguousarray(lw2).astype(np.float32), g2=g2.astype(np.float32), masks=masks, mreset=mreset)


F32 = mybir.dt.float32
BF16 = mybir.dt.bfloat16
AF = mybir.ActivationFunctionType
ALU = mybir.AluOpType
D = 4096
S = 4096
EPS = 1e-6
NCH = 28
NR = 6592
LAMBDA_INIT = 0.8 - 0.6
WARMN = 0
NFILL = 1


def body_A(nc, P, st, sh, yT, do_proj=True, do_attn=True, do_rwkv=True, n_tb=8, attn_heads=4, attn_qb=8, rw_heads=8):
    xb = nc.dram_tensor("xb", [S, D], F32, kind="ExternalInput").ap()
    wA = nc.dram_tensor("wA", [NCH, 128, 32, 128], F32, kind="ExternalInput").ap()
    abias = nc.dram_tensor("abias", [4, 5, 128, 512], F32, kind="ExternalInput").ap()
    acst = nc.dram_tensor("acst", [128, 4 * 64], F32, kind="ExternalInput").ap()
    lam_d = nc.dram_tensor("lam", [128, 4, 64], F32, kind="ExternalInput").ap()
    subg = nc.dram_tensor("subg", [128, 1], F32, kind="ExternalInput").ap()
    PT_d = nc.dram_tensor("PT_d", [NCH, 128, S], F32).ap()
    gain = sh["gains"][0]
    ident, ones, small, epsb = sh["ident"], sh["ones"], sh["small"], sh["epsb"]
    psum, pst, state, next_ps = sh["psum"], sh["pst"], sh["state"], sh["next_ps"]
    if True:
        if do_proj:
            with ExitStack() as st2:
                bufAs = [st2.enter_context(nc.sbuf_tensor(f"bufA{i}", [128, 32, 512], BF16)) for i in range(2)]
                xt = st2.enter_context(nc.sbuf_tensor("xt", [128, D], F32))
                gt = st2.enter_context(nc.sbuf_tensor("gt", [128, D], F32))
                hb = st2.enter_context(nc.sbuf_tensor("hb", [128, D], BF16))
                wbuf = [st2.enter_context(nc.sbuf_tensor(f"wb{i}", [128, 32, 128], BF16)) for i in range(3)]
                ot = [st2.enter_context(nc.sbuf_tensor(f"ot{i}", [128, 512], F32)) for i in range(3)]
                P.op("sync", lambda e: e.dma_start(out=gt[:], in_=gain), writes=["gt"], dma_key="gt")
                for tb in range(n_tb):
                    bufA = bufAs[tb % 2]
                    bk = f"bufA{tb % 2}"
                    for tt in range(4):
                        r0 = tb * 512 + tt * 128
                        P.op("sync", lambda e, r0=r0: e.dma_start(out=xt[:], in_=xb[r0:r0 + 128, :]), writes=["xt"], dma_key="xt")
                        P.op("vector", lambda e: e.memset(small[:, 0:1], 0.0), writes=["sm0"])
                        P.op("scalar", lambda e: e.activation(out=hb[:], in_=xt[:], func=AF.Square, accum_out=small[:, 0:1]), reads=["xt", "sm0"], writes=["hb", "sm0"])
                        P.op("scalar", lambda e: e.activation(out=small[:, 0:1], in_=small[:, 0:1], func=AF.Sqrt, scale=1.0 / D, bias=epsb[:, 0:1]), reads=["sm0", "epsb"], writes=["sm0"])
                        P.op("vector", lambda e: e.reciprocal(out=small[:, 0:1], in_=small[:, 0:1]), reads=["sm0"], writes=["sm0"])
                        P.op("vector", lambda e: e.scalar_tensor_tensor(out=hb[:], in0=xt[:], scalar=small[:, 0:1], in1=gt[:], op0=ALU.mult, op1=ALU.mult),
                             reads=["xt", "sm0", "gt"], writes=["hb"])
                        for k8 in range(4):
                            pi = state["pt"]; state["pt"] ^= 1
                            for j in range(8):
                                kc = k8 * 8 + j
                                P.op("tensor", lambda e, kc=kc, j=j, pi=pi: e.transpose(out=pst[pi][:, j * 128:(j + 1) * 128], in_=hb[:, kc * 128:(kc + 1) * 128], identity=ident[:]),
                                     reads=["hb", "ident"], writes=[f"ps{6 + pi}"])
                            dst = bufA[:, k8 * 8:(k8 + 1) * 8, tt * 128:(tt + 1) * 128]
                            srcp = pst[pi][:, :].rearrange("p (k t) -> p k t", k=8)
                            if k8 % 2 == 0:
                                P.op("scalar", lambda e, dst=dst, srcp=srcp: e.activation(out=dst, in_=srcp, func=AF.Copy), reads=[f"ps{6 + pi}"], writes=[bk])
                            else:
                                P.op("vector", lambda e, dst=dst, srcp=srcp: e.tensor_copy(out=dst, in_=srcp), reads=[f"ps{6 + pi}"], writes=[bk])
                    for ch in range(NCH):
                        s = ch % 3
                        for half in range(2):
                            P.op("gpsimd", lambda e, s=s, ch=ch, half=half: e.dma_start(out=wbuf[s][:, half * 16:(half + 1) * 16, :], in_=wA[ch][:, half * 16:(half + 1) * 16, :], max_dma_last_dim=4096),
                                 writes=[f"wb{s}"], dma_key=f"wb{s}")
                        pi = next_ps()
                        for kc in range(32):
                            P.op("tensor", lambda e, kc=kc, pi=pi, s=s, bufA=bufA: e.matmul(psum[pi][:], lhsT=wbuf[s][:, kc, :], rhs=bufA[:, kc, :], start=(kc == 0), stop=(kc == 31)),
                                 reads=[f"wb{s}", bk], writes=[f"ps{pi}"])
                        if ch % 2 == 0:
                            P.op("scalar", lambda e, pi=pi, s=s: e.activation(out=ot[s][:], in_=psum[pi][:], func=AF.Copy), reads=[f"ps{pi}"], writes=[f"ot{s}"])
                        else:
                            P.op("vector", lambda e, pi=pi, s=s: e.tensor_copy(out=ot[s][:], in_=psum[pi][:]), reads=[f"ps{pi}"], writes=[f"ot{s}"])
                        P.op("sync", lambda e, s=s, ch=ch, tb=tb: e.dma_start(out=PT_d[ch][:, tb * 512:(tb + 1) * 512], in_=ot[s][:]), reads=[f"ot{s}"], writes=[f"PT{ch}"], dma_key=f"ot{s}")
                P.fence()
        if do_attn:
            with ExitStack() as st2:
                def sb2(name, shape, dt):
                    return st2.enter_context(nc.sbuf_tensor(name, shape, dt))
                LA = 2
                NSL = LA + 1
                qk32 = sb2("qk32", [128, S], F32)
                QTs = [sb2(f"QT{i}", [64, 2, S], BF16) for i in range(2)]
                KTs = [sb2(f"KT{i}", [64, 2, S], BF16) for i in range(2)]
                Vts = [sb2(f"Vt{i}", [128, 32, 128], BF16) for i in range(2)]
                vb = sb2("vb", [128, S], BF16)
                bts = [sb2(f"bt{i}", [128, 5, 512], F32) for i in range(2)]
                cst = sb2("cst", [128, 256], F32)
                lamt = sb2("lamt", [128, 4, 64], F32)
                lsm = sb2("lsm", [128, 8], F32)
                sgt = sb2("sgt", [128, 1], F32)
                tmp = [sb2(f"atmp{i}", [128, 512], F32) for i in range(NSL)]
                Eb = [sb2(f"Eb{i}", [128, 512], BF16) for i in range(NSL)]
                o0 = sb2("o0", [128, 512], F32)
                o1 = sb2("o1", [128, 512], F32)
                rr = sb2("rr", [128, 512], F32)
                rr2 = sb2("rr2", [128, 512], F32)
                sq = sb2("sq", [128, 512], BF16)
                yo = sb2("yo", [128, 512], BF16)
                wz = sb2("wz", [128, 512], BF16)
                P.op("vector", lambda e: e.memset(wz[:], 0.0), writes=["wz"])

                def warm(n):
                    for _ in range(n):
                        P.op("tensor", lambda e: e.matmul(psum[7][:], lhsT=ones[:], rhs=wz[:], start=True, stop=True), reads=["wz", "ones"], writes=["ps7"])
                P.op("sync", lambda e: e.dma_start(out=cst[:], in_=acst), writes=["cst"], dma_key="cst")
                P.op("sync", lambda e: e.dma_start(out=lamt[:], in_=lam_d), writes=["lamt"], dma_key="lamt")
                P.op("sync", lambda e: e.dma_start(out=sgt[:], in_=subg), writes=["sgt"], dma_key="sgt")
                for i in range(2):
                    P.op("vector", lambda e, i=i: e.tensor_tensor(out=lamt[:, 2 * i, :], in0=lamt[:, 2 * i, :], in1=lamt[:, 2 * i + 1, :], op=ALU.mult), reads=["lamt"], writes=["lamt"])
                    P.op("vector", lambda e, i=i: e.tensor_reduce(out=lsm[:, i:i + 1], in_=lamt[:, 2 * i, :], axis=mybir.AxisListType.X, op=ALU.add), reads=["lamt"], writes=["lsm"])
                    P.op("scalar", lambda e, i=i: e.activation(out=lsm[:, i:i + 1], in_=lsm[:, i:i + 1], func=AF.Exp), reads=["lsm"], writes=["lsm"])
                P.op("vector", lambda e: e.tensor_tensor(out=lsm[:, 2:3], in0=lsm[:, 1:2], in1=lsm[:, 0:1], op=ALU.subtract), reads=["lsm"], writes=["lsm"])
                P.op("vector", lambda e: e.tensor_scalar(out=lsm[:, 2:3], in0=lsm[:, 2:3], scalar1=-LAMBDA_INIT, scalar2=None, op0=ALU.add), reads=["lsm"], writes=["lsm"])
                P.op("vector", lambda e: e.tensor_scalar(out=sgt[:], in0=sgt[:], scalar1=1.0 - LAMBDA_INIT, scalar2=None, op0=ALU.mult), reads=["sgt"], writes=["sgt"])

                def load_head(hd):
                    hs = hd % 2
                    QT, KT, Vt, bt = QTs[hs], KTs[hs], Vts[hs], bts[hs]
                    P.op("sync", lambda e: e.dma_start(out=bt[:], in_=abias[hd].rearrange("f p q -> p f q")), writes=[f"bt{hs}"], dma_key=f"bt{hs}")
                    for (dstT, ch, nm) in ((QT, 16 + hd, f"QT{hs}"), (KT, 20 + hd, f"KT{hs}")):
                        for c in range(2):
                            P.op("sync", lambda e, ch=ch, c=c: e.dma_start(out=qk32[0:64, :], in_=PT_d[ch][c * 64:(c + 1) * 64, :]), reads=[f"PT{ch}"], writes=["qk32"], dma_key="qk32")
                            P.op("scalar", lambda e, dstT=dstT, c=c: e.activation(out=dstT[:, c, :], in_=qk32[0:64, :], func=AF.Copy), reads=["qk32"], writes=[nm])
                    P.op("sync", lambda e: e.dma_start(out=qk32[:], in_=PT_d[24 + hd]), reads=[f"PT{24 + hd}"], writes=["qk32"], dma_key="qk32")
                    P.op("vector", lambda e: e.tensor_copy(out=vb[:], in_=qk32[:]), reads=["qk32"], writes=["vb"])
                    for k8 in range(4):
                        pi = state["pt"]; state["pt"] ^= 1
                        for j in range(8):
                            blk = k8 * 8 + j
                            P.op("tensor", lambda e, blk=blk, j=j, pi=pi: e.transpose(out=pst[pi][:, j * 128:(j + 1) * 128], in_=vb[:, blk * 128:(blk + 1) * 128], identity=ident[:]),
                                 reads=["vb", "ident"], writes=[f"ps{6 + pi}"])
                        P.op("vector", lambda e, k8=k8, pi=pi: e.tensor_copy(out=Vt[:, k8 * 8:(k8 + 1) * 8, :], in_=pst[pi][:, :].rearrange("p (k t) -> p k t", k=8)), reads=[f"ps{6 + pi}"], writes=[f"Vt{hs}"])

                pacc = [0, 1, 2, 3]
                pending = [None]

                def unit_front(hd, qb, i, ulist):
                    hs = hd % 2
                    kb, c = ulist[i]
                    delta = kb - 4 * qb
                    pi = 4 + (i % NSL)
                    ti = i % NSL
                    P.op("tensor", lambda e: e.matmul(psum[pi][:], lhsT=KTs[hs][:, c, kb * 128:(kb + 1) * 128], rhs=QTs[hs][:, c, qb * 512:(qb + 1) * 512], start=True, stop=True),
                         reads=[f"QT{hs}", f"KT{hs}"], writes=[f"ps{pi}"])
                    if delta >= 4:
                        bti, op1 = 0, ALU.add
                    elif delta < 0:
                        bti, op1 = 0, ALU.subtract
                    else:
                        bti, op1 = 1 + delta, ALU.add
                    P.op("vector", lambda e: e.scalar_tensor_tensor(out=tmp[ti][:], in0=psum[pi][:], scalar=0.125, in1=bts[hs][:, bti, :], op0=ALU.mult, op1=op1),
                         reads=[f"ps{pi}", f"bt{hs}"], writes=[f"atmp{ti}"])
                    ci = hd * 64 + (delta + 32)
                    P.op("scalar", lambda e: e.activation(out=Eb[ti][:], in_=tmp[ti][:], func=AF.Exp, bias=cst[:, ci:ci + 1]), reads=[f"atmp{ti}", "cst"], writes=[f"Eb{ti}"])

                def unit_back(hd, qb, i, ulist, kbs):
                    hs = hd % 2
                    kb, c = ulist[i]
                    ti = i % NSL
                    P.op("tensor", lambda e: e.matmul(psum[pacc[2 * c]][:], lhsT=Vts[hs][:, kb, :], rhs=Eb[ti][:], start=(kb == kbs[0]), stop=(kb == kbs[-1])),
                         reads=[f"Eb{ti}", f"Vt{hs}"], writes=[f"ps{pacc[2 * c]}"])
                    P.op("tensor", lambda e: e.matmul(psum[pacc[2 * c + 1]][:], lhsT=ones[:], rhs=Eb[ti][:], start=(kb == kbs[0]), stop=(kb == kbs[-1])),
                         reads=[f"Eb{ti}", "ones"], writes=[f"ps{pacc[2 * c + 1]}"])

                def fin1():
                    P.op("vector", lambda e: e.reciprocal(out=rr[:], in_=psum[pacc[1]][:]), reads=[f"ps{pacc[1]}"], writes=["rr"])
                    P.op("vector", lambda e: e.tensor_tensor(out=o0[:], in0=psum[pacc[0]][:], in1=rr[:], op=ALU.mult), reads=[f"ps{pacc[0]}", "rr"], writes=["o0"])
                    P.op("vector", lambda e: e.reciprocal(out=rr[:], in_=psum[pacc[3]][:]), reads=[f"ps{pacc[3]}"], writes=["rr"])
                    P.op("vector", lambda e: e.tensor_tensor(out=o1[:], in0=psum[pacc[2]][:], in1=rr[:], op=ALU.mult), reads=[f"ps{pacc[2]}", "rr"], writes=["o1"])
                    P.op("vector", lambda e: e.scalar_tensor_tensor(out=o0[:], in0=o1[:], scalar=lsm[:, 2:3], in1=o0[:], op0=ALU.mult, op1=ALU.add), reads=["o0", "o1", "lsm"], writes=["o0"])
                    P.op("scalar", lambda e: e.activation(out=sq[:], in_=o0[:], func=AF.Square), reads=["o0"], writes=["sq"])

                def fin2(hd, qb, pi):
                    P.op("tensor", lambda e: e.matmul(psum[pi][:], lhsT=ones[:], rhs=sq[:], start=True, stop=True), reads=["sq", "ones"], writes=[f"ps{pi}"])
                    P.op("scalar", lambda e: e.activation(out=rr2[:], in_=psum[pi][:], func=AF.Sqrt, scale=1.0 / 128, bias=epsb[:, 1:2]), reads=[f"ps{pi}", "epsb"], writes=["rr2"])
                    P.op("vector", lambda e: e.reciprocal(out=rr2[:], in_=rr2[:]), reads=["rr2"], writes=["rr2"])
                    P.op("vector", lambda e: e.scalar_tensor_tensor(out=yo[:], in0=o0[:], scalar=sgt[:, 0:1], in1=rr2[:], op0=ALU.mult, op1=ALU.mult), reads=["o0", "sgt", "rr2"], writes=["yo"])
                    P.op("sync", lambda e: e.dma_start(out=yT[4 + hd][:, qb * 512:(qb + 1) * 512], in_=yo[:]), reads=["yo"], writes=["yT"], dma_key="yo")

                load_head(0)

                def kept_kbs(hd, qb):
                    smin = 2.0 ** (-2.0 * (hd + 1))
                    out = []
                    for kb in range(32):
                        delta = kb - 4 * qb
                        dmin = 128 * (delta - 4) + 1 if delta >= 4 else (128 * (-delta - 1) + 1 if delta < 0 else 0)
                        if smin * dmin < 60.0:
                            out.append(kb)
                    return out
                for hd in range(attn_heads):
                    for qb in range(attn_qb):
                        kbs = kept_kbs(hd, qb)
                        ulist = [(kb, c) for kb in kbs for c in range(2)]
                        NU = len(ulist)
                        warm(WARMN)
                        for i in range(NU + LA):
                            if i < NU:
                                unit_front(hd, qb, i, ulist)
                            if i >= LA:
                                unit_back(hd, qb, i - LA, ulist, kbs)
                                warm(NFILL)
                            if i == LA + 1 and pending[0] is not None:
                                ph, pq = pending[0]
                                pending[0] = None
                                fin2(ph, pq, 7)
                        fin1()
                        pending[0] = (hd, qb)
                        if qb == 1 and hd + 1 < attn_heads:
                            load_head(hd + 1)
                        if hd == attn_heads - 1 and qb == attn_qb - 1:
                            fin2(hd, qb, 7)
                            pending[0] = None
                P.fence()
        if do_rwkv:
            with ExitStack() as st3:
                emit_rwkv(nc, P, st3, PT_d, yT, psum, pst, state, next_ps, ident, ones, epsb, rw_heads)
            P.fence()


def build_A(**kw):
    nc = bass.Bass("TRN2", target_bir_lowering=False)
    yT = nc.dram_tensor("yT", [8, 128, S], BF16, kind="ExternalOutput").ap()
    P = Prog(nc)
    with ExitStack() as st:
        sh = make_shared(nc, P, st)
        body_A(nc, P, st, sh, yT, **kw)
        counts = P.emit(st)
        print("A ops", counts, "waits", P.n_waits)
    return nc


def build_fused():
    nc = bass.Bass("TRN2", target_bir_lowering=False)
    yTi = nc.dram_tensor("yTi", [8, 128, S], BF16).ap()
    G = nc.dram_tensor("Gy", [8, 4, 128, S], BF16).ap()
    sel = nc.dram_tensor("sel", [128, 4], F32, kind="ExternalInput").ap()
    P = Prog(nc)
    with ExitStack() as st:
        sh = make_shared(nc, P, st)
        body_A(nc, P, st, sh, yTi)
        for k in range(8):
            P.op("gpsimd", lambda e, k=k: e.collective_compute("AllGather", ALU.bypass, replica_groups=[[0, 1, 2, 3], [4, 5, 6, 7]],
                                                               ins=[yTi[k].opt()], outs=[G[k].rearrange("g p t -> (g p) t").opt()]),
                 reads=["yT"], writes=["G"], dma_key="cc", inc=1)
        with ExitStack() as st4:
            body_B(nc, P, st4, sh, ("gather", G, sel))
        counts = P.emit(st)
        print("fused ops", counts, "waits", P.n_waits)
    return nc


def slopes():
    H = 16
    return np.exp2(-8.0 * np.arange(1, H + 1, dtype=np.float32) / H).astype(np.float32)


def relayout_A(inp, c, l=0):
    b, g = c // 4, c % 4
    w_in = inp["w_in"][l]
    cols = []
    for part in range(3):
        cols.append(np.arange(part * 2048 + 512 * g, part * 2048 + 512 * (g + 1)))
    lo = 3 * 2048
    cols.append(np.arange(lo, lo + 96)); pad1 = 32
    cols.append(np.arange(lo + 96, lo + 192)); pad2 = 32
    cols.append(np.arange(lo + 192, lo + 448))
    for part in range(3):
        cols.append(np.concatenate([np.arange(NR + part * 2048 + (4 * j + g) * 128, NR + part * 2048 + (4 * j + g + 1) * 128) for j in range(4)]))
    W = np.zeros((D, NCH * 128), np.float32)
    W[:, 0:1536] = w_in[:, np.concatenate(cols[0:3])]
    W[:, 1536:1536 + 96] = w_in[:, cols[3]]
    W[:, 1664:1664 + 96] = w_in[:, cols[4]]
    W[:, 1792:2048] = w_in[:, cols[5]]
    W[:, 2048:3584] = w_in[:, np.concatenate(cols[6:9])]
    wA = np.ascontiguousarray(W.reshape(32, 128, NCH, 128).transpose(2, 1, 0, 3))
    gain = np.ascontiguousarray(np.broadcast_to(inp["attn_pre_norm"][l][None, :], (128, D))).astype(np.float32)
    ident = np.eye(128, dtype=np.float32).astype(ml_dtypes.bfloat16)
    ones = np.ones((128, 128), np.float32).astype(ml_dtypes.bfloat16)
    sl = slopes()
    kk = np.arange(128, dtype=np.float32)[:, None]
    qq = np.arange(512, dtype=np.float32)[None, :]
    abias = np.zeros((4, 5, 128, 512), np.float32)
    acst = np.zeros((128, 256), np.float32)
    for hd in range(4):
        s_ = sl[4 * hd + g]
        abias[hd, 0] = -s_ * (kk - qq)
        for dl in range(4):
            abias[hd, 1 + dl] = -s_ * np.abs(128.0 * dl + kk - qq)
        for delta in range(-32, 32):
            if delta >= 4:
                v = -s_ * 128.0 * delta
            elif delta < 0:
                v = s_ * 128.0 * delta
            else:
                v = 0.0
            acst[:, hd * 64 + delta + 32] = v
    lam = np.stack([inp[k][l] for k in ("lambda_q1", "lambda_k1", "lambda_q2", "lambda_k2")])
    lam = np.ascontiguousarray(np.broadcast_to(lam[None], (128, 4, 64))).astype(np.float32)
    subg = np.ascontiguousarray(inp["subln_gain"][l].reshape(128, 1)).astype(np.float32)
    return dict(xb=np.ascontiguousarray(inp["x"][b]), wA=wA, abias=abias, acst=acst, lam=lam, subg=subg)


def kernel(**inp):
    inp = {k: np.asarray(v) for k, v in inp.items()}
    n = 8
    nc = build_fused()
    W = relayout_B(inp)
    in_maps = []
    for c in range(n):
        b, g = c // 4, c % 4
        im = dict(W)
        im.update(relayout_A(inp, c))
        im.update(relayout_rwkv(inp, c))
        im["xo"] = np.ascontiguousarray(inp["x"][b, 1024 * g:1024 * (g + 1)])
        sel = np.zeros((128, 4), np.float32)
        sel[:, g] = 1.0
        im["sel"] = sel
        in_maps.append(im)
    res = run_bass_kernel_spmd(nc, in_maps, core_ids=list(range(n)))
    out = np.zeros((2, S, D), np.float32)
    for c in range(n):
        b, g = c // 4, c % 4
        out[b, 1024 * g:1024 * (g + 1)] = res.results[c]["out"]
    return out
```

```python
import math
import bisect
import numpy as np
import ml_dtypes
from contextlib import ExitStack
import concourse.bass as bass
import concourse.mybir as mybir
from concourse.bass_utils import run_bass_kernel_spmd


ENGS = ("tensor", "vector", "scalar", "gpsimd", "sync")
ROT = 12000


class Prog:
    def __init__(self, nc):
        self.nc = nc
        self.ops = []

    def op(self, eng, fn, reads=(), writes=(), dma_key=None, inc=None):
        self.ops.append(dict(eng=eng, fn=fn, reads=tuple(reads), writes=tuple(writes),
                             dma=dma_key is not None, key=dma_key, inc=inc))
        return len(self.ops) - 1

    def fence(self, eng="vector"):
        self.ops.append(dict(eng=eng, fn=self.fence_fn, reads=(), writes="ALL", dma=False, key=None, inc=None))

    def emit(self, stack):
        nc = self.nc
        ops = self.ops
        n = len(ops)
        allkeys = set()
        for o in ops:
            if o["writes"] != "ALL":
                allkeys.update(o["reads"]); allkeys.update(o["writes"])
        allkeys = tuple(sorted(allkeys, key=str))
        for o in ops:
            if o["writes"] == "ALL":
                o["writes"] = allkeys
        last_w = {}
        readers = {}
        deps = [None] * n
        for i, o in enumerate(ops):
            d = set()
            for r in o["reads"]:
                if r in last_w:
                    d.add(last_w[r])
            for w in o["writes"]:
                if w in last_w:
                    d.add(last_w[w])
                for j in readers.get(w, ()):
                    d.add(j)
            d.discard(i)
            dd = []
            for j in d:
                oj = ops[j]
                if (not oj["dma"]) and (not o["dma"]) and oj["eng"] == o["eng"] == "tensor":
                    continue
                dd.append(j)
            deps[i] = dd
            for r in o["reads"]:
                readers.setdefault(r, []).append(i)
            for w in o["writes"]:
                last_w[w] = i
                readers[w] = []
        needed = [False] * n
        for i in range(n):
            for j in deps[i]:
                needed[j] = True
        sem_handles = {}

        def get_sem(name):
            if name not in sem_handles:
                sem_handles[name] = stack.enter_context(nc.semaphore(name))
            return sem_handles[name]

        cnt = {}
        sig = [None] * n
        dma_cum_at = {}
        for i, o in enumerate(ops):
            if o["dma"]:
                base = "d_" + str(o["key"])
                inc = o["inc"] or 16
                lim = ROT
            else:
                if not needed[i]:
                    continue
                base = "e_" + o["eng"]
                inc = 1
                lim = ROT
            g, c = cnt.get(base, (0, 0))
            if c + inc > lim * (16 if o["dma"] else 1):
                g, c = g + 1, 0
            c += inc
            cnt[base] = (g, c)
            sig[i] = (base + "_" + str(g), c)
            if o["dma"]:
                dma_cum_at.setdefault(base, []).append((i, sig[i][0], c))
        import bisect
        dma_idx = {k: [t[0] for t in v] for k, v in dma_cum_at.items()}
        waits = [None] * n
        waited = {e: {} for e in ENGS}
        for i, o in enumerate(ops):
            need = {}
            for j in deps[i]:
                oj = ops[j]
                if oj["dma"]:
                    base = "d_" + str(oj["key"])
                    lst = dma_cum_at[base]
                    pos = bisect.bisect_left(dma_idx[base], i) - 1
                    sname_j, vj = sig[j]
                    k = pos
                    while lst[k][1] != sname_j:
                        k -= 1
                    sname, val = lst[k][1], lst[k][2]
                else:
                    sname, val = sig[j]
                if need.get(sname, 0) < val:
                    need[sname] = val
            wl = []
            wd = waited[o["eng"]]
            for sname, val in need.items():
                if wd.get(sname, 0) >= val:
                    continue
                wd[sname] = val
                wl.append((sname, val))
            waits[i] = wl
        self.n_waits = sum(len(w) for w in waits)
        per_eng = {e: [i for i, o in enumerate(ops) if o["eng"] == e] for e in ENGS}
        block = stack.enter_context(nc.Block())

        def body(engname):
            def f(eng):
                for i in per_eng[engname]:
                    for sname, val in waits[i]:
                        eng.wait_ge(get_sem(sname), val)
                    inst = ops[i]["fn"](eng)
                    if sig[i] is not None:
                        inst.then_inc(get_sem(sig[i][0]), (ops[i]["inc"] or 16) if ops[i]["dma"] else 1)
            return f

        for i in range(n):
            if sig[i] is not None:
                get_sem(sig[i][0])
        block.tensor(body("tensor"))
        block.vector(body("vector"))
        block.scalar(body("scalar"))
        block.gpsimd(body("gpsimd"))
        block.sync(body("sync"))
        return {e: len(v) for e, v in per_eng.items()}


F32 = mybir.dt.float32
BF16 = mybir.dt.bfloat16
AF = mybir.ActivationFunctionType
ALU = mybir.AluOpType
D = 4096
DFF = 16384
EPS = 1e-6
NTOK = 1024
TP = 512
FB = 256
NFB = DFF // FB


def make_shared(nc, P, st):
    ident_d = nc.dram_tensor("ident", [128, 128], BF16, kind="ExternalInput").ap()
    ones_d = nc.dram_tensor("ones", [128, 128], BF16, kind="ExternalInput").ap()
    gains = nc.dram_tensor("gains", [4, 128, D], F32, kind="ExternalInput").ap()
    ident = st.enter_context(nc.sbuf_tensor("ident_s", [128, 128], BF16))
    ones = st.enter_context(nc.sbuf_tensor("ones_s", [128, 128], BF16))
    small = st.enter_context(nc.sbuf_tensor("small", [128, 16], F32))
    dummy = st.enter_context(nc.sbuf_tensor("fdummy", [128, 8], F32))
    epsb = st.enter_context(nc.sbuf_tensor("epsb", [128, 2], F32))
    P.fence_fn = lambda e: e.memset(dummy[:], 0.0)
    P.op("vector", lambda e: e.memset(epsb[:, 0:1], EPS), writes=["epsb"])
    P.op("vector", lambda e: e.memset(epsb[:, 1:2], 1e-5), writes=["epsb"])
    P.op("sync", lambda e: e.dma_start(out=ident[:], in_=ident_d), writes=["ident"], dma_key="ident")
    P.op("sync", lambda e: e.dma_start(out=ones[:], in_=ones_d), writes=["ones"], dma_key="ones")
    psum = [st.enter_context(nc.psum_tensor(f"ps{i}", [128, 512], F32)) for i in range(8)]
    pst = [psum[6 + i][:, :].bitcast(BF16) for i in range(2)]
    state = dict(ps=0, pt=0)

    def next_ps():
        i = state["ps"]; state["ps"] = (i + 1) % 6
        return i
    return dict(ident=ident, ones=ones, small=small, epsb=epsb, psum=psum, pst=pst, state=state, next_ps=next_ps, gains=gains)


def body_B(nc, P, st, sh, ysrc, npass=2, stages=(0, 1, 2, 3, 4, 5), nfb=NFB):
    x = nc.dram_tensor("xo", [NTOK, D], F32, kind="ExternalInput").ap()
    wg = nc.dram_tensor("wg", [64, 128, 32, 128], F32, kind="ExternalInput").ap()
    wu = nc.dram_tensor("wu", [64, 128, 16, 128], F32, kind="ExternalInput").ap()
    wo = nc.dram_tensor("wo", [16, 128, 32, 256], F32, kind="ExternalInput").ap()
    w1 = nc.dram_tensor("w1", [NFB, 128, 32, FB], F32, kind="ExternalInput").ap()
    w2 = nc.dram_tensor("w2", [NFB, 128, FB // 128, D], F32, kind="ExternalInput").ap()
    out = nc.dram_tensor("out", [NTOK, D], F32, kind="ExternalOutput").ap()
    x1_d = nc.dram_tensor("x1_d", [NTOK, D], F32).ap()
    gains, ident, small, epsb = sh["gains"], sh["ident"], sh["small"], sh["epsb"]
    psum, pst, state, next_ps = sh["psum"], sh["pst"], sh["state"], sh["next_ps"]
    if True:
        arena = st.enter_context(nc.sbuf_tensor("arena", [128, 172 * 256], F32))
        if ysrc[0] == "gather":
            selt = st.enter_context(nc.sbuf_tensor("selt", [128, 4], F32))
            P.op("sync", lambda e: e.dma_start(out=selt[:], in_=ysrc[2]), writes=["selt"], dma_key="selt")

        def AV(off_kib, size_kib, dt):
            v = arena[:, off_kib * 256:(off_kib + size_kib) * 256]
            return v.bitcast(BF16) if dt == BF16 else v

        bufA = AV(0, 32, BF16).rearrange("p (k t) -> p k t", k=32)
        bufB = AV(32, 32, BF16).rearrange("p (k t) -> p k t", k=32)
        bufM = AV(64, 32, BF16).rearrange("p (k t) -> p k t", k=32)
        bufZ = AV(96, 64, F32).rearrange("p (a d) -> p a d", a=4)
        wbuf = [AV(96 + 24 * s, 24, BF16) for s in range(2)]
        wo_v = [AV(32 + 16 * s, 16, BF16).rearrange("p (k j) -> p k j", k=32) for s in range(2)]
        w1_v = [AV(64 + 16 * s, 16, BF16).rearrange("p (k j) -> p k j", k=32) for s in range(2)]
        w2_v = [AV(32 + 16 * s, 16, BF16).rearrange("p (k j) -> p k j", k=FB // 128) for s in range(2)]
        xt_R3, gt_R3 = AV(64, 16, F32), AV(80, 16, F32)
        xt_R2, gt_R2 = AV(32, 16, F32), AV(48, 16, F32)
        hb_R4 = AV(144, 8, BF16)
        hb_R3 = AV(64, 8, BF16)
        t1 = [AV(160 + 2 * i, 2, F32) for i in range(2)]
        t2 = [AV(164 + 2 * i, 2, F32) for i in range(2)]
        ub = [AV(168 + 2 * i, 2, BF16).rearrange("p (k t) -> p k t", k=FB // 128) for i in range(2)]
        def load_gain(idx, gt):
            P.op("sync", lambda e: e.dma_start(out=gt, in_=gains[idx]), reads=["gains"], writes=["gt"], dma_key="gt")


        def rstd_from(src_ap, src_key, col, hb):
            P.op("vector", lambda e: e.memset(small[:, col:col + 1], 0.0), writes=[f"sm{col}"])
            P.op("scalar", lambda e: e.activation(out=hb, in_=src_ap, func=AF.Square, accum_out=small[:, col:col + 1]),
                 reads=[src_key, f"sm{col}"], writes=["hb", f"sm{col}"])
            P.op("scalar", lambda e: e.activation(out=small[:, col:col + 1], in_=small[:, col:col + 1], func=AF.Sqrt, scale=1.0 / D, bias=epsb[:, 0:1]),
                 reads=[f"sm{col}", "epsb"], writes=[f"sm{col}"])
            P.op("vector", lambda e: e.reciprocal(out=small[:, col:col + 1], in_=small[:, col:col + 1]), reads=[f"sm{col}"], writes=[f"sm{col}"])

        def norm_transpose(src_ap, src_key, col, gt, hb, tt):
            P.op("vector", lambda e: e.scalar_tensor_tensor(out=hb, in0=src_ap, scalar=small[:, col:col + 1], in1=gt,
                                                            op0=ALU.mult, op1=ALU.mult), reads=[src_key, f"sm{col}", "gt"], writes=["hb"])
            for k8 in range(4):
                pi = state["pt"]; state["pt"] ^= 1
                for j in range(8):
                    kc = k8 * 8 + j
                    P.op("tensor", lambda e, kc=kc, j=j, pi=pi: e.transpose(out=pst[pi][:, j * 128:(j + 1) * 128], in_=hb[:, kc * 128:(kc + 1) * 128], identity=ident[:]),
                         reads=["hb", "ident"], writes=[f"ps{6 + pi}"])
                dst = bufA[:, k8 * 8:(k8 + 1) * 8, tt * 128:(tt + 1) * 128]
                srcp = pst[pi][:, :].rearrange("p (k t) -> p k t", k=8)
                if k8 % 2 == 0:
                    P.op("scalar", lambda e, dst=dst, srcp=srcp: e.activation(out=dst, in_=srcp, func=AF.Copy), reads=[f"ps{6 + pi}"], writes=["bufA"])
                else:
                    P.op("vector", lambda e, dst=dst, srcp=srcp: e.tensor_copy(out=dst, in_=srcp), reads=[f"ps{6 + pi}"], writes=["bufA"])

        for ps_i in range(npass):
            tok0 = ps_i * TP
            if ps_i > 0 or ysrc[0] != "gather":
                P.fence()
            if 0 in stages:
                xt, gt, hb = xt_R3, gt_R3, hb_R4
                load_gain(0, gt)
                for tt in range(4):
                    r0 = tok0 + tt * 128
                    P.op("sync", lambda e, r0=r0, xt=xt: e.dma_start(out=xt, in_=x[r0:r0 + 128, :]), writes=["xt"], dma_key="xt")
                    rstd_from(xt, "xt", 0, hb)
                    norm_transpose(xt, "xt", 0, gt, hb, tt)
                if ysrc[0] == "input":
                    P.op("sync", lambda e, tok0=tok0: e.dma_start(out=bufB, in_=ysrc[1][:, :, tok0:tok0 + TP].rearrange("k p t -> p k t")), writes=["bufB"], dma_key="bufB")
                else:
                    G = ysrc[1]
                    cand = AV(96, 32, BF16).rearrange("p (k t) -> p k t", k=32)
                    for q in range(4):
                        t0 = 1024 * q + tok0
                        for part in range(2):
                            dstv = cand[:, 16 * part:16 * (part + 1), :].rearrange("p (g k) t -> p g k t", g=4) if part == 0 else \
                                cand[:, 16 * part:16 * (part + 1), :].rearrange("p (k g) t -> p g k t", g=4)
                            for gq in range(4):
                                P.op("sync", lambda e, dstv=dstv, part=part, t0=t0, gq=gq: e.dma_start(out=dstv[:, gq, :, :], in_=G[4 * part:4 * part + 4, gq, :, t0:t0 + TP].rearrange("k p t -> p k t")),
                                     reads=["G"], writes=["cand"], dma_key="cand")
                        if q == 0:
                            P.op("vector", lambda e: e.tensor_scalar(out=bufB, in0=cand, scalar1=selt[:, 0:1], scalar2=None, op0=ALU.mult), reads=["cand", "selt"], writes=["bufB"])
                        else:
                            P.op("vector", lambda e, q=q: e.scalar_tensor_tensor(out=bufB, in0=cand, scalar=selt[:, q:q + 1], in1=bufB, op0=ALU.mult, op1=ALU.add), reads=["cand", "selt", "bufB"], writes=["bufB"])
            P.fence()
            if 1 in stages:
                for cc in range(32):
                    s = cc % 2
                    wb = wbuf[s]
                    vgA = wb[:, 0:4096].rearrange("p (k j) -> p k j", k=32)
                    vgB = wb[:, 4096:8192].rearrange("p (k j) -> p k j", k=32)
                    vuA = wb[:, 8192:10240].rearrange("p (k j) -> p k j", k=16)
                    vuB = wb[:, 10240:12288].rearrange("p (k j) -> p k j", k=16)
                    for (dst, src) in ((vgA, wg[cc]), (vgB, wg[32 + cc]), (vuA, wu[cc]), (vuB, wu[32 + cc])):
                        P.op("gpsimd", lambda e, dst=dst, src=src: e.dma_start(out=dst, in_=src, max_dma_last_dim=4096), writes=[f"wbuf{s}"], dma_key=f"wbuf{s}")
                    pgA, pgB, puA, puB = next_ps(), next_ps(), next_ps(), next_ps()
                    for (pi, wv, nk, src, koff) in ((pgA, vgA, 32, bufA, 0), (puA, vuA, 16, bufB, 0), (pgB, vgB, 32, bufA, 0), (puB, vuB, 16, bufB, 16)):
                        for kc in range(nk):
                            P.op("tensor", lambda e, kc=kc, pi=pi, wv=wv, nk=nk, src=src, koff=koff: e.matmul(psum[pi][:], lhsT=wv[:, kc, :], rhs=src[:, koff + kc, :], start=(kc == 0), stop=(kc == nk - 1)),
                                 reads=[f"wbuf{s}", "bufA", "bufB"], writes=[f"ps{pi}"])
                    P.op("scalar", lambda e, pgA=pgA, s=s: e.activation(out=t1[s], in_=psum[pgA][:], func=AF.Sigmoid), reads=[f"ps{pgA}"], writes=[f"t1_{s}"])
                    P.op("vector", lambda e, puA=puA, s=s: e.tensor_tensor(out=t1[s], in0=t1[s], in1=psum[puA][:], op=ALU.mult), reads=[f"ps{puA}", f"t1_{s}"], writes=[f"t1_{s}"])
                    P.op("scalar", lambda e, pgB=pgB, s=s: e.activation(out=t2[s], in_=psum[pgB][:], func=AF.Sigmoid), reads=[f"ps{pgB}"], writes=[f"t2_{s}"])
                    P.op("vector", lambda e, puB=puB, s=s: e.tensor_tensor(out=t2[s], in0=t2[s], in1=psum[puB][:], op=ALU.mult), reads=[f"ps{puB}", f"t2_{s}"], writes=[f"t2_{s}"])
                    P.op("vector", lambda e, cc=cc, s=s: e.tensor_tensor(out=bufM[:, cc, :], in0=t1[s], in1=t2[s], op=ALU.add), reads=[f"t1_{s}", f"t2_{s}"], writes=["bufM"])
            P.fence()
            if 2 in stages:
                for nb in range(16):
                    s = nb % 2
                    wv = wo_v[s]
                    for half in range(2):
                        P.op("gpsimd", lambda e, wv=wv, nb=nb, half=half: e.dma_start(out=wv[:, half * 16:(half + 1) * 16, :], in_=wo[nb][:, half * 16:(half + 1) * 16, :], max_dma_last_dim=4096),
                             writes=[f"wo{s}"], dma_key=f"wo{s}")
                    for tt in range(4):
                        pi = next_ps()
                        for kc in range(32):
                            P.op("tensor", lambda e, kc=kc, pi=pi, wv=wv, tt=tt: e.matmul(psum[pi][:, 0:256], lhsT=bufM[:, kc, tt * 128:(tt + 1) * 128], rhs=wv[:, kc, :], start=(kc == 0), stop=(kc == 31)),
                                 reads=[f"wo{s}", "bufM"], writes=[f"ps{pi}"])
                        dst = bufZ[:, tt, nb * 256:(nb + 1) * 256]
                        if (tt + nb) % 2 == 0:
                            P.op("scalar", lambda e, dst=dst, pi=pi: e.activation(out=dst, in_=psum[pi][:, 0:256], func=AF.Copy), reads=[f"ps{pi}"], writes=[f"bufZ{tt}_{nb % 8}"])
                        else:
                            P.op("vector", lambda e, dst=dst, pi=pi: e.tensor_copy(out=dst, in_=psum[pi][:, 0:256]), reads=[f"ps{pi}"], writes=[f"bufZ{tt}_{nb % 8}"])
            P.fence()
            if 3 in stages:
                xt, gt, hb = xt_R2, gt_R2, hb_R3
                load_gain(1, gt)
                for tt in range(4):
                    r0 = tok0 + tt * 128
                    zt = bufZ[:, tt, :]
                    rstd_from(zt, f"bufZ{tt}", 1, hb)
                    P.op("sync", lambda e, r0=r0, xt=xt: e.dma_start(out=xt, in_=x[r0:r0 + 128, :]), writes=["xt"], dma_key="xt")
                    P.op("vector", lambda e, zt=zt, gt=gt: e.scalar_tensor_tensor(out=zt, in0=zt, scalar=small[:, 1:2], in1=gt, op0=ALU.mult, op1=ALU.mult),
                         reads=[f"bufZ{tt}", "sm1", "gt"], writes=[f"bufZ{tt}"])
                    P.op("vector", lambda e, zt=zt, xt=xt: e.tensor_tensor(out=zt, in0=zt, in1=xt, op=ALU.add), reads=[f"bufZ{tt}", "xt"], writes=[f"bufZ{tt}"])
                    P.op("sync", lambda e, r0=r0, zt=zt: e.dma_start(out=x1_d[r0:r0 + 128, :], in_=zt), reads=[f"bufZ{tt}"], writes=[f"x1d{ps_i}_{tt}"], dma_key=f"x1st{tt}")
                load_gain(2, gt)
                for tt in range(4):
                    zt = bufZ[:, tt, :]
                    rstd_from(zt, f"bufZ{tt}", 2, hb)
                    norm_transpose(zt, f"bufZ{tt}", 2, gt, hb, tt)
            P.fence()
            if 4 in stages:
                for fb in range(nfb):
                    s = fb % 2
                    w1v = w1_v[s]
                    w2v = w2_v[s]
                    for half in range(2):
                        P.op("gpsimd", lambda e, w1v=w1v, fb=fb, half=half: e.dma_start(out=w1v[:, half * 16:(half + 1) * 16, :], in_=w1[fb][:, half * 16:(half + 1) * 16, :], max_dma_last_dim=4096),
                             writes=[f"w1_{s}"], dma_key=f"w1_{s}")
                    for kc2 in range(FB // 128):
                        P.op("gpsimd", lambda e, w2v=w2v, fb=fb, kc2=kc2: e.dma_start(out=w2v[:, kc2, :], in_=w2[fb][:, kc2, :], max_dma_last_dim=4096),
                             writes=[f"w2_{s}"], dma_key=f"w2_{s}")
                    for fc in range(FB // 128):
                        pi = next_ps()
                        for kc in range(32):
                            P.op("tensor", lambda e, kc=kc, pi=pi, w1v=w1v, fc=fc: e.matmul(psum[pi][:], lhsT=w1v[:, kc, fc * 128:(fc + 1) * 128], rhs=bufA[:, kc, :], start=(kc == 0), stop=(kc == 31)),
                                 reads=[f"w1_{s}", "bufA"], writes=[f"ps{pi}"])
                        P.op("scalar", lambda e, pi=pi, s=s: e.activation(out=t1[s], in_=psum[pi][:], func=AF.Relu), reads=[f"ps{pi}"], writes=[f"t1_{s}"])
                        P.op("vector", lambda e, s=s, fc=fc: e.tensor_tensor(out=ub[s][:, fc, :], in0=t1[s], in1=t1[s], op=ALU.mult), reads=[f"t1_{s}"], writes=[f"ub{s}"])
                    for tt in range(4):
                        for nb in range(8):
                            pi = next_ps()
                            for kc2 in range(FB // 128):
                                P.op("tensor", lambda e, kc2=kc2, pi=pi, w2v=w2v, tt=tt, nb=nb, s=s: e.matmul(psum[pi][:], lhsT=ub[s][:, kc2, tt * 128:(tt + 1) * 128], rhs=w2v[:, kc2, nb * 512:(nb + 1) * 512],
                                                                                                          start=(kc2 == 0), stop=(kc2 == FB // 128 - 1)),
                                     reads=[f"w2_{s}", f"ub{s}"], writes=[f"ps{pi}"])
                            dst = bufZ[:, tt, nb * 512:(nb + 1) * 512]
                            key = f"bufZ{tt}_{nb}"
                            if fb == 0:
                                P.op("vector", lambda e, dst=dst, pi=pi: e.tensor_copy(out=dst, in_=psum[pi][:]), reads=[f"ps{pi}"], writes=[key])
                            else:
                                P.op("vector", lambda e, dst=dst, pi=pi: e.tensor_tensor(out=dst, in0=dst, in1=psum[pi][:], op=ALU.add), reads=[f"ps{pi}", key], writes=[key])
            P.fence()
            if 5 in stages:
                xt, gt, hb = xt_R3, gt_R3, AV(32, 8, BF16)
                load_gain(3, gt)
                for tt in range(4):
                    r0 = tok0 + tt * 128
                    zt = bufZ[:, tt, :]
                    rstd_from(zt, f"bufZ{tt}", 3, hb)
                    P.op("sync", lambda e, r0=r0, xt=xt: e.dma_start(out=xt, in_=x1_d[r0:r0 + 128, :]), reads=[f"x1d{ps_i}_{tt}"], writes=["xt"], dma_key="xt")
                    P.op("vector", lambda e, zt=zt, gt=gt: e.scalar_tensor_tensor(out=zt, in0=zt, scalar=small[:, 3:4], in1=gt, op0=ALU.mult, op1=ALU.mult),
                         reads=[f"bufZ{tt}", "sm3", "gt"], writes=[f"bufZ{tt}"])
                    P.op("vector", lambda e, zt=zt, xt=xt: e.tensor_tensor(out=zt, in0=zt, in1=xt, op=ALU.add), reads=[f"bufZ{tt}", "xt"], writes=[f"bufZ{tt}"])
                    P.op("sync", lambda e, r0=r0, zt=zt: e.dma_start(out=out[r0:r0 + 128, :], in_=zt), reads=[f"bufZ{tt}"], writes=["out"], dma_key=f"x1st{tt}")
        P.fence()


def build_B(npass=2, stages=(0, 1, 2, 3, 4, 5), nfb=NFB):
    nc = bass.Bass("TRN2", target_bir_lowering=False)
    yT = nc.dram_tensor("yT", [32, 128, NTOK], BF16, kind="ExternalInput").ap()
    P = Prog(nc)
    with ExitStack() as st:
        sh = make_shared(nc, P, st)
        body_B(nc, P, st, sh, ("input", yT), npass=npass, stages=stages, nfb=nfb)
        counts = P.emit(st)
        print("B ops", counts, "waits", P.n_waits)
    return nc


def relayout_B(inp, l=0):
    w_in = inp["w_in"][l]
    NR = 6592
    gcol0 = NR + 3 * 2048
    Wg = w_in[:, gcol0:gcol0 + 8192]
    wg = np.ascontiguousarray(Wg.reshape(32, 128, 64, 128).transpose(2, 1, 0, 3))
    Wu = np.concatenate([inp["w_up_rwkv"][l], inp["w_up_diff"][l]], axis=1)
    wu = np.ascontiguousarray(Wu.reshape(16, 128, 64, 128).transpose(2, 1, 0, 3))
    wo = np.ascontiguousarray(inp["w_out"][l].reshape(32, 128, 16, 256).transpose(2, 1, 0, 3))
    w1 = np.ascontiguousarray(inp["w_mlp_in"][l].reshape(32, 128, NFB, FB).transpose(2, 1, 0, 3))
    w2 = np.ascontiguousarray(inp["w_mlp_out"][l].reshape(NFB, FB // 128, 128, D).transpose(0, 2, 1, 3))
    gains = np.stack([np.broadcast_to(inp[k][l][None, :], (128, D)) for k in ("attn_pre_norm", "attn_post_norm", "mlp_pre_norm", "mlp_post_norm")]).astype(np.float32)
    ident = np.eye(128, dtype=np.float32).astype(ml_dtypes.bfloat16)
    ones = np.ones((128, 128), np.float32).astype(ml_dtypes.bfloat16)
    return dict(wg=wg, wu=wu, wo=wo, w1=w1, w2=w2, gains=np.ascontiguousarray(gains), ident=ident, ones=ones)


F32 = mybir.dt.float32
BF16 = mybir.dt.bfloat16
AF = mybir.ActivationFunctionType
ALU = mybir.AluOpType
S = 4096
C = 128
BLK = 512
NB = S // BLK
EPS_GN = 64e-5
DEC = -0.6065306597126334
RWMODE = 3


def emit_rwkv(nc, P, st, PT_d, yT, psum, pst, state, next_ps, ident, ones, epsb, rw_heads):
    par_d = nc.dram_tensor("par", [64, 8, 16], F32, kind="ExternalInput").ap()
    parl_d = nc.dram_tensor("parl", [128, 4, 2], F32, kind="ExternalInput").ap()
    lw2_d = nc.dram_tensor("lw2", [96, 4, 512], F32, kind="ExternalInput").ap()
    g2_d = nc.dram_tensor("g2", [128, 2, 512], F32, kind="ExternalInput").ap()
    masks_d = nc.dram_tensor("masks", [128, 2, 640], F32, kind="ExternalInput").ap()
    mreset_d = nc.dram_tensor("mreset", [64, BLK], F32, kind="ExternalInput").ap()

    def sb(name, shape, dt):
        return st.enter_context(nc.sbuf_tensor(name, shape, dt))
    par = sb("par_s", [64, 8, 20], F32)
    parl = sb("parl_s", [128, 4, 3], F32)
    lw2b = sb("lw2b", [96, 4, 512], BF16)
    g2b = sb("g2b", [128, 2, 512], BF16)
    masks = sb("masks_s", [128, 2, 640], F32)
    mreset = sb("mreset_s", [64, BLK], F32)
    twT = sb("twT", [128, S], BF16)
    daT = sb("daT", [128, S], BF16)
    sgT = sb("sgT", [128, 2, S], BF16)
    RAW = sb("RAW", [128, S // 2 + 2], F32)
    SH = sb("SH", [128, S // 2], F32)
    Rb16 = sb("R16", [64, S], BF16)
    Kb16 = sb("K16", [64, S], BF16)
    Vb16 = sb("V16", [64, S], BF16)
    Vt = sb("rVt", [128, 32, 64], BF16)
    KKN = sb("KKN", [64, S], BF16)
    OT = sb("OT", [64, S], F32)
    BONV = sb("BONV", [64, S], BF16)
    gnb = sb("gnb", [64, 1], F32)
    tiny = sb("tinyb", [64, 1], F32)
    LW = sb("LW", [64, BLK], F32)
    CI = sb("CI", [64, BLK], F32)
    CE = sb("CE", [64, BLK], F32)
    Ece = sb("Ece", [64, BLK], F32)
    Enci = sb("Enci", [64, BLK], F32)
    Eci = [sb(f"Eci{i}", [64, BLK], F32) for i in range(2)]
    At = sb("At", [64, BLK], F32)
    T1 = sb("T1", [64, BLK], F32)
    KD = sb("KD", [64, BLK], F32)
    T2 = sb("T2b", [64, BLK], BF16)
    ops4 = [sb(f"ops4_{i}", [64, 4, BLK], BF16) for i in range(2)]
    tok3 = [sb(f"tok3_{i}", [128, 12, 64], BF16) for i in range(2)]
    G1s = [sb(f"G1s{c}", [128, 384], BF16) for c in range(8)]
    G2s = [sb(f"G2s{c}", [128, 256], BF16) for c in range(8)]
    MM = [[sb(f"MM{c}_{i}", [128, 256], BF16) for i in range(2)] for c in range(8)]
    Qs = [[sb(f"Qs{c}_{i}", [128, 128], BF16) for i in range(2)] for c in range(8)]
    IMs = [[sb(f"IM{c}_{i}", [128, 128], BF16) for i in range(2)] for c in range(8)]
    nW1T = [sb(f"nW1T{c}", [64, 128], BF16) for c in range(8)]
    nXs = [sb(f"nXs{c}", [128, 64], BF16) for c in range(8)]
    Us = sb("Us", [128, 64], BF16)
    Hf = sb("Hf", [64, 64], F32)
    Hb = sb("Hb", [64, 64], BF16)
    yo = sb("ryo", [64, BLK], BF16)
    ob = sb("ob", [64, BLK], BF16)
    dd = sb("dd", [64, BLK], F32)
    ones64 = ones[0:64, 0:64]
    id64 = ident[0:64, 0:64]

    def V(fn, reads, writes):
        P.op("vector", fn, reads=reads, writes=writes)

    def A(fn, reads, writes):
        P.op("scalar", fn, reads=reads, writes=writes)

    def T(fn, reads, writes):
        P.op("tensor", fn, reads=reads, writes=writes)

    for (dst, src, k, q) in ((par[:, :, 0:16], par_d, "par", "sync"), (parl[:, :, 0:2], parl_d, "parl", "sync"), (masks[:], masks_d, "masks", "sync"),
                             (mreset[:], mreset_d, "mreset", "sync"), (lw2b[:], lw2_d, "lw2b", "gpsimd"), (g2b[:], g2_d, "g2b", "gpsimd")):
        P.op(q, lambda e, dst=dst, src=src: e.dma_start(out=dst, in_=src), writes=[k], dma_key=k)
    V(lambda e: e.memset(gnb[:], EPS_GN), [], ["gnb"])
    V(lambda e: e.memset(tiny[:], 1e-24), [], ["gnb"])
    for i in range(3):
        V(lambda e, i=i: e.tensor_tensor(out=par[:, :, 16 + i], in0=par[:, :, 2 * i], in1=par[:, :, 2 * i + 1], op=ALU.add), ["par"], ["par"])
        V(lambda e, i=i: e.tensor_scalar(out=par[:, :, 16 + i], in0=par[:, :, 16 + i], scalar1=-1.0, scalar2=1.0, op0=ALU.mult, op1=ALU.add), ["par"], ["par"])
    V(lambda e: e.tensor_tensor(out=parl[:, :, 2], in0=parl[:, :, 0], in1=parl[:, :, 1], op=ALU.add), ["parl"], ["parl"])
    V(lambda e: e.tensor_scalar(out=parl[:, :, 2], in0=parl[:, :, 2], scalar1=-1.0, scalar2=1.0, op0=ALU.mult, op1=ALU.add), ["parl"], ["parl"])

    HS = S // 2

    def load_shift(ch, r0, nrow, c0, mup, mun, consume):
        for half in range(2):
            t0 = half * HS
            lo = max(t0 - 1, 0)
            hi = min(t0 + HS + 1, S)
            off = lo - (t0 - 1)
            if half == 0:
                V(lambda e: e.memset(RAW[0:nrow, 0:1], 0.0), [], ["RAW"])
            else:
                V(lambda e: e.memset(RAW[0:nrow, HS + 1:HS + 2], 0.0), [], ["RAW"])
            P.op("sync", lambda e, lo=lo, hi=hi, off=off: e.dma_start(out=RAW[0:nrow, off:off + (hi - lo)], in_=PT_d[ch][r0:r0 + nrow, lo:hi]), reads=[f"PT{ch}"], writes=["RAW"], dma_key="RAW")
            V(lambda e: e.tensor_scalar(out=SH[0:nrow, :], in0=RAW[0:nrow, 1:HS + 1], scalar1=c0, scalar2=None, op0=ALU.mult), ["RAW", "par", "parl"], ["SH"])
            V(lambda e: e.scalar_tensor_tensor(out=SH[0:nrow, :], in0=RAW[0:nrow, 0:HS], scalar=mup, in1=SH[0:nrow, :], op0=ALU.mult, op1=ALU.add), ["RAW", "SH", "par", "parl"], ["SH"])
            V(lambda e: e.scalar_tensor_tensor(out=SH[0:nrow, :], in0=RAW[0:nrow, 2:HS + 2], scalar=mun, in1=SH[0:nrow, :], op0=ALU.mult, op1=ALU.add), ["RAW", "SH", "par", "parl"], ["SH"])
            consume(half)

    for i, (ch, dstT, fn) in enumerate(((12, twT, AF.Tanh), (13, daT, AF.Copy), (14, sgT[:, 0, :], AF.Sigmoid), (15, sgT[:, 1, :], AF.Sigmoid))):
        def cons(half, dstT=dstT, fn=fn):
            A(lambda e: e.activation(out=dstT[:, half * HS:(half + 1) * HS], in_=SH[:, :], func=fn), ["SH"], ["lora_act"])
        load_shift(ch, 0, 128, parl[:, i, 2:3], parl[:, i, 0:1], parl[:, i, 1:2], cons)

    def head(hh):
        ch_r, ch_k, ch_v = hh // 2, 4 + hh // 2, 8 + hh // 2
        r0 = (hh % 2) * 64
        pc = lambda j: par[:, hh, j:j + 1]
        for (ch, i, dst) in ((ch_r, 0, Rb16), (ch_k, 1, Kb16), (ch_v, 2, Vb16)):
            def cons(half, dst=dst):
                A(lambda e: e.activation(out=dst[:, half * HS:(half + 1) * HS], in_=SH[0:64, :], func=AF.Copy), ["SH"], ["rkv"])
            load_shift(ch, r0, 64, pc(16 + i), pc(2 * i), pc(2 * i + 1), cons)
        for half in range(2):
            pi = state["pt"]; state["pt"] ^= 1
            for j in range(16):
                blk = half * 16 + j
                T(lambda e, blk=blk, j=j, pi=pi: e.transpose(out=pst[pi][:, j * 64:(j + 1) * 64], in_=Vb16[:, blk * 128:(blk + 1) * 128], identity=id64), ["rkv", "ident"], [f"ps{6 + pi}"])
            V(lambda e, half=half, pi=pi: e.tensor_copy(out=Vt[:, half * 16:(half + 1) * 16, :], in_=pst[pi][:, :].rearrange("p (k t) -> p k t", k=16)), [f"ps{6 + pi}"], ["rVt"])
        for b in range(NB):
            bs = slice(b * BLK, (b + 1) * BLK)
            V(lambda e, bs=bs: e.tensor_scalar(out=T1[:], in0=Kb16[:, bs], scalar1=pc(10), scalar2=None, op0=ALU.mult), ["rkv", "par"], ["T1"])
            A(lambda e: e.activation(out=T2[:], in_=T1[:], func=AF.Square), ["T1"], ["T2"])
            pi = next_ps()
            T(lambda e, pi=pi: e.matmul(psum[pi][0:64, :], lhsT=ones64, rhs=T2[:], start=True, stop=True), ["T2", "ones"], [f"ps{pi}"])
            A(lambda e, pi=pi: e.activation(out=KD[:], in_=psum[pi][0:64, :], func=AF.Sqrt, bias=tiny[:, 0:1]), [f"ps{pi}", "gnb"], ["KD"])
            V(lambda e: e.reciprocal(out=KD[:], in_=KD[:]), ["KD"], ["KD"])
            V(lambda e, bs=bs: e.tensor_tensor(out=KKN[:, bs], in0=T1[:], in1=KD[:], op=ALU.mult), ["T1", "KD"], ["KKN"])
        def direction(d):
            V(lambda e: e.memset(Hf[:], 0.0), [], ["Hf"])
            V(lambda e: e.memset(Hb[:], 0.0), [], ["Hb"])
            blocks = range(NB) if d == 0 else range(NB - 1, -1, -1)
            def block(bi, b):
                bs = slice(b * BLK, (b + 1) * BLK)
                sl = bi % 2
                O4, K3, EC = ops4[sl], tok3[sl], Eci[sl]
                hs = slice(hh * 64, (hh + 1) * 64)
                pi = 6
                T(lambda e, pi=pi, bs=bs, hs=hs: e.matmul(psum[pi][0:64, :], lhsT=lw2b[:, d, hs], rhs=twT[0:96, bs], start=True, stop=True), ["lw2b", "lora_act"], [f"ps{pi}"])
                A(lambda e, pi=pi: e.activation(out=LW[:], in_=psum[pi][0:64, :], func=AF.Sigmoid, bias=pc(6 + d)), [f"ps{pi}", "par"], ["LW"])
                V(lambda e: e.tensor_scalar(out=LW[:], in0=LW[:], scalar1=DEC, scalar2=None, op0=ALU.mult), ["LW"], ["LW"])
                V(lambda e: e.tensor_tensor_scan(out=CI[:], data0=mreset[:], data1=LW[:], initial=0.0, op0=ALU.mult, op1=ALU.add), ["LW", "mreset"], ["CI"])
                if d == 0:
                    V(lambda e: e.tensor_tensor(out=CE[:], in0=CI[:], in1=LW[:], op=ALU.subtract), ["CI", "LW"], ["CE"])
                else:
                    for c in range(4):
                        cs = slice(c * C, (c + 1) * C)
                        V(lambda e, cs=cs, c=c: e.tensor_scalar(out=CE[:, cs], in0=CI[:, cs], scalar1=-1.0, scalar2=CI[:, c * C + C - 1:c * C + C], op0=ALU.mult, op1=ALU.add), ["CI"], ["CE"])
                    V(lambda e: e.tensor_tensor(out=CI[:], in0=CE[:], in1=LW[:], op=ALU.add), ["CE", "LW"], ["CI"])
                A(lambda e: e.activation(out=Ece[:], in_=CE[:], func=AF.Exp), ["CE"], ["Ece"])
                A(lambda e: e.activation(out=Enci[:], in_=CI[:], func=AF.Exp, scale=-1.0), ["CI"], ["Enci"])
                A(lambda e, EC=EC: e.activation(out=EC[:], in_=CI[:], func=AF.Exp), ["CI"], [f"Eci{sl}"])
                pi = 7
                T(lambda e, pi=pi, bs=bs, hs=hs: e.matmul(psum[pi][0:64, :], lhsT=lw2b[:, 2 + d, hs], rhs=daT[0:96, bs], start=True, stop=True), ["lw2b", "lora_act"], [f"ps{pi}"])
                A(lambda e, pi=pi: e.activation(out=At[:], in_=psum[pi][0:64, :], func=AF.Sigmoid, bias=pc(8 + d)), [f"ps{pi}", "par"], ["At"])
                V(lambda e: e.tensor_scalar(out=T1[:], in0=At[:], scalar1=-1.0, scalar2=pc(11), op0=ALU.add, op1=ALU.mult), ["At", "par"], ["T1"])
                V(lambda e, bs=bs: e.scalar_tensor_tensor(out=KD[:], in0=T1[:], scalar=1.0, in1=Kb16[:, bs], op0=ALU.add, op1=ALU.mult), ["T1", "rkv"], ["KD"])
                V(lambda e, bs=bs: e.tensor_tensor(out=At[:], in0=At[:], in1=KKN[:, bs], op=ALU.mult), ["At", "KKN"], ["At"])
                V(lambda e, bs=bs, O4=O4: e.tensor_tensor(out=O4[:, 0, :], in0=KKN[:, bs], in1=Ece[:], op=ALU.mult), ["KKN", "Ece"], [f"ops4_{sl}"])
                V(lambda e, O4=O4: e.tensor_tensor(out=O4[:, 1, :], in0=At[:], in1=Enci[:], op=ALU.mult), ["At", "Enci"], [f"ops4_{sl}"])
                V(lambda e, O4=O4: e.tensor_tensor(out=O4[:, 2, :], in0=KD[:], in1=Enci[:], op=ALU.mult), ["KD", "Enci"], [f"ops4_{sl}"])
                V(lambda e, bs=bs, O4=O4, EC=EC: e.tensor_tensor(out=O4[:, 3, :], in0=Rb16[:, bs], in1=EC[:], op=ALU.mult), ["rkv", f"Eci{sl}"], [f"ops4_{sl}"])
                V(lambda e, bs=bs: e.scalar_tensor_tensor(out=T2[:], in0=Rb16[:, bs], scalar=pc(12), in1=KD[:], op0=ALU.mult, op1=ALU.mult), ["rkv", "KD", "par"], ["T2"])
                pi = 6
                T(lambda e, pi=pi: e.matmul(psum[pi][0:64, :], lhsT=ones64, rhs=T2[:], start=True, stop=True), ["T2", "ones"], [f"ps{pi}"])
                V(lambda e, pi=pi, bs=bs: e.scalar_tensor_tensor(out=T1[:], in0=psum[pi][0:64, :], scalar=0.5, in1=Vb16[:, bs], op0=ALU.mult, op1=ALU.mult), [f"ps{pi}", "rkv"], ["T1"])
                if d == 0:
                    V(lambda e, bs=bs: e.tensor_copy(out=BONV[:, bs], in_=T1[:]), ["T1"], ["BONV"])
                else:
                    V(lambda e, bs=bs: e.tensor_tensor(out=BONV[:, bs], in0=BONV[:, bs], in1=T1[:], op=ALU.add), ["T1", "BONV"], ["BONV"])
                pi = state["pt"]; state["pt"] ^= 1
                for o in range(3):
                    for c in range(4):
                        T(lambda e, o=o, c=c, pi=pi, O4=O4: e.transpose(out=pst[pi][:, (o * 4 + c) * 64:(o * 4 + c + 1) * 64], in_=O4[:, o, c * C:(c + 1) * C], identity=id64),
                          [f"ops4_{sl}", "ident"], [f"ps{6 + pi}"])
                V(lambda e, pi=pi, K3=K3: e.tensor_copy(out=K3[:, :, :], in_=pst[pi][:, 0:768].rearrange("p (k t) -> p k t", k=12)), [f"ps{6 + pi}"], [f"tok3_{sl}"])
                chunks = list(range(4)) if d == 0 else list(range(3, -1, -1))
                ok = [f"ops4_{sl}"]
                cst_ = {}

                def opsof(c):
                    cs = slice(c * C, (c + 1) * C)
                    return O4[:, 0, cs], O4[:, 1, cs], O4[:, 2, cs], O4[:, 3, cs]

                kx = lambda c: sl * 4 + c

                def gram1(c):
                    Ab_c, Bb_c, Kb_c, Rb_c = opsof(c)
                    p1, p2 = next_ps(), next_ps()
                    cst_[c] = dict(p1=p1, p2=p2)
                    T(lambda e: e.matmul(psum[p1][:, 0:128], lhsT=Bb_c, rhs=Ab_c, start=True, stop=True), ok, [f"ps{p1}"])
                    T(lambda e: e.matmul(psum[p1][:, 128:256], lhsT=Kb_c, rhs=Ab_c, start=True, stop=True), ok, [f"ps{p1}"])
                    T(lambda e: e.matmul(psum[p1][:, 256:384], lhsT=Ab_c, rhs=Bb_c, start=True, stop=True), ok, [f"ps{p1}"])
                    T(lambda e: e.matmul(psum[p2][:, 0:128], lhsT=Bb_c, rhs=Rb_c, start=True, stop=True), ok, [f"ps{p2}"])
                    T(lambda e: e.matmul(psum[p2][:, 128:256], lhsT=Kb_c, rhs=Rb_c, start=True, stop=True), ok, [f"ps{p2}"])

                def evac1(c):
                    p1, p2 = cst_[c]["p1"], cst_[c]["p2"]
                    V(lambda e: e.tensor_tensor(out=G1s[kx(c)][:], in0=psum[p1][:, 0:384], in1=masks[:, d, 0:384], op=ALU.mult), [f"ps{p1}", "masks"], [f"G1s{kx(c)}"])
                    V(lambda e: e.tensor_tensor(out=G2s[kx(c)][:], in0=psum[p2][:, 0:256], in1=masks[:, d, 384:640], op=ALU.mult), [f"ps{p2}", "masks"], [f"G2s{kx(c)}"])
                    V(lambda e: e.tensor_tensor(out=Qs[kx(c)][0][:], in0=G1s[kx(c)][:, 0:128], in1=ident[:], op=ALU.add), [f"G1s{kx(c)}", "ident"], [f"Qs{kx(c)}_0"])
                    cst_[c].update(M=G1s[kx(c)][:, 256:384], MT=G1s[kx(c)][:, 0:128], mk=f"G1s{kx(c)}", qi=0)

                def levelA(c, lev):
                    stc = cst_[c]
                    Mprev, MTprev, mk = stc["M"], stc["MT"], stc["mk"]
                    mi = lev % 2
                    pm = next_ps()
                    T(lambda e: e.matmul(psum[pm][:, 0:128], lhsT=MTprev, rhs=Mprev, start=True, stop=True), [mk], [f"ps{pm}"])
                    if lev < 6:
                        T(lambda e: e.matmul(psum[pm][:, 128:256], lhsT=Mprev, rhs=MTprev, start=True, stop=True), [mk], [f"ps{pm}"])
                    A(lambda e: e.activation(out=MM[kx(c)][mi][:], in_=psum[pm][:, 0:256], func=AF.Copy), [f"ps{pm}"], [f"MM{kx(c)}_{mi}"])
                    V(lambda e: e.tensor_tensor(out=IMs[kx(c)][mi][:], in0=MM[kx(c)][mi][:, 0:128], in1=ident[:], op=ALU.add), [f"MM{kx(c)}_{mi}", "ident"], [f"IM{kx(c)}_{mi}"])
                    stc["M"], stc["MT"], stc["mk"] = MM[kx(c)][mi][:, 0:128], MM[kx(c)][mi][:, 128:256], f"MM{kx(c)}_{mi}"

                def levelB(c, lev):
                    stc = cst_[c]
                    qi = stc["qi"]
                    mi = lev % 2
                    pq = next_ps()
                    T(lambda e: e.matmul(psum[pq][:, 0:128], lhsT=IMs[kx(c)][mi][:], rhs=Qs[kx(c)][qi][:], start=True, stop=True), [f"Qs{kx(c)}_{qi}", f"IM{kx(c)}_{mi}"], [f"ps{pq}"])
                    V(lambda e: e.tensor_copy(out=Qs[kx(c)][1 - qi][:], in_=psum[pq][:, 0:128]), [f"ps{pq}"], [f"Qs{kx(c)}_{1 - qi}"])
                    stc["qi"] = 1 - qi

                def w1x(c):
                    gc = b * 4 + c
                    qi = cst_[c]["qi"]
                    Q, qk = Qs[kx(c)][qi], f"Qs{kx(c)}_{qi}"
                    Abt = K3[:, 0 + c, :]
                    pw = next_ps()
                    T(lambda e: e.matmul(psum[pw][0:64, 0:128], lhsT=Abt, rhs=Q[:], start=True, stop=True), [f"tok3_{sl}", qk], [f"ps{pw}"])
                    A(lambda e: e.activation(out=nW1T[kx(c)][:], in_=psum[pw][0:64, 0:128], func=AF.Copy, scale=-1.0), [f"ps{pw}"], [f"nW1T{kx(c)}"])
                    px = next_ps()
                    T(lambda e: e.matmul(psum[px][:, 0:64], lhsT=G1s[kx(c)][:, 128:256], rhs=Vt[:, gc, :], start=True, stop=True), [f"G1s{kx(c)}", "rVt"], [f"ps{px}"])
                    A(lambda e: e.activation(out=nXs[kx(c)][:], in_=psum[px][:, 0:64], func=AF.Copy, scale=-1.0), [f"ps{px}"], [f"nXs{kx(c)}"])

                def seq(c):
                    gc = b * 4 + c
                    qi = cst_[c]["qi"]
                    Q, qk = Qs[kx(c)][qi], f"Qs{kx(c)}_{qi}"
                    Ab_c, Bb_c, Kb_c, Rb_c = opsof(c)
                    Bbt, Kbt = K3[:, 4 + c, :], K3[:, 8 + c, :]
                    Vtc = Vt[:, gc, :]
                    pu = next_ps()
                    T(lambda e: e.matmul(psum[pu][:, 0:64], lhsT=Q[:], rhs=nXs[kx(c)][:], start=True, stop=False), [qk, f"nXs{kx(c)}"], [f"ps{pu}"])
                    T(lambda e: e.matmul(psum[pu][:, 0:64], lhsT=nW1T[kx(c)][:], rhs=Hb[:], start=False, stop=True), [f"nW1T{kx(c)}", "Hb"], [f"ps{pu}"])
                    V(lambda e: e.tensor_copy(out=Us[:], in_=psum[pu][:, 0:64]), [f"ps{pu}"], ["Us"])
                    po = next_ps()
                    T(lambda e: e.matmul(psum[po][0:64, 0:128], lhsT=Hb[:], rhs=Rb_c, start=True, stop=False), ["Hb"] + ok, [f"ps{po}"])
                    T(lambda e: e.matmul(psum[po][0:64, 0:128], lhsT=Us[:], rhs=G2s[kx(c)][:, 0:128], start=False, stop=False), ["Us", f"G2s{kx(c)}"], [f"ps{po}"])
                    T(lambda e: e.matmul(psum[po][0:64, 0:128], lhsT=Vtc, rhs=G2s[kx(c)][:, 128:256], start=False, stop=True), ["rVt", f"G2s{kx(c)}"], [f"ps{po}"])
                    gs = slice(gc * C, (gc + 1) * C)
                    if d == 0:
                        A(lambda e: e.activation(out=OT[:, gs], in_=psum[po][0:64, 0:128], func=AF.Copy), [f"ps{po}"], ["OT"])
                    else:
                        V(lambda e: e.tensor_tensor(out=OT[:, gs], in0=OT[:, gs], in1=psum[po][0:64, 0:128], op=ALU.add), [f"ps{po}", "OT"], ["OT"])
                    ph = next_ps()
                    T(lambda e: e.matmul(psum[ph][0:64, 0:64], lhsT=Bbt, rhs=Us[:], start=True, stop=False), [f"tok3_{sl}", "Us"], [f"ps{ph}"])
                    T(lambda e: e.matmul(psum[ph][0:64, 0:64], lhsT=Kbt, rhs=Vtc, start=False, stop=True), [f"tok3_{sl}", "rVt"], [f"ps{ph}"])
                    gidx = c * C + C - 1 if d == 0 else c * C
                    gam = EC[:, gidx:gidx + 1]
                    V(lambda e: e.tensor_scalar(out=Hf[:], in0=Hf[:], scalar1=gam, scalar2=None, op0=ALU.mult), ["Hf", f"Eci{sl}"], ["Hf"])
                    V(lambda e: e.scalar_tensor_tensor(out=Hf[:], in0=psum[ph][0:64, 0:64], scalar=gam, in1=Hf[:], op0=ALU.mult, op1=ALU.add), ["Hf", f"Eci{sl}", f"ps{ph}"], ["Hf"])
                    A(lambda e: e.activation(out=Hb[:], in_=Hf[:], func=AF.Copy), ["Hf"], ["Hb"])

                stages = []
                for cg in (chunks[0:2], chunks[2:4]):
                    for c in cg:
                        stages.append(lambda c=c: gram1(c))
                    for c in cg:
                        stages.append(lambda c=c: evac1(c))
                for lev in range(1, 7):
                    for c in chunks:
                        stages.append(lambda c=c, lev=lev: levelA(c, lev))
                    for c in chunks:
                        stages.append(lambda c=c, lev=lev: levelB(c, lev))
                for c in chunks:
                    stages.append(lambda c=c: w1x(c))
                seqs = [(lambda c=c: seq(c)) for c in chunks]
                return stages, seqs
            prev = None
            for bi, b in enumerate(blocks):
                stages, seqs = block(bi, b)
                if prev is None or RWMODE != 3:
                    if prev is not None:
                        for f in prev:
                            f()
                    for f in stages:
                        f()
                else:
                    n = len(stages)
                    marks = {int((k + 1) * n / 5): k for k in range(4)}
                    for i, f in enumerate(stages):
                        f()
                        if (i + 1) in marks:
                            prev[marks[i + 1]]()
                prev = seqs
            for f in prev:
                f()
        direction(0)
        direction(1)
        for b in range(NB):
            bs = slice(b * BLK, (b + 1) * BLK)
            A(lambda e, bs=bs: e.activation(out=ob[:], in_=OT[:, bs], func=AF.Copy), ["OT"], ["ob"])
            pi = next_ps()
            T(lambda e, pi=pi: e.matmul(psum[pi][0:64, :], lhsT=ones64, rhs=ob[:], start=True, stop=True), ["ob", "ones"], [f"ps{pi}"])
            V(lambda e, pi=pi, bs=bs: e.scalar_tensor_tensor(out=dd[:], in0=psum[pi][0:64, :], scalar=-1.0 / 64, in1=OT[:, bs], op0=ALU.mult, op1=ALU.add), [f"ps{pi}", "OT"], ["dd"])
            A(lambda e: e.activation(out=ob[:], in_=dd[:], func=AF.Square), ["dd"], ["ob"])
            pi = next_ps()
            T(lambda e, pi=pi: e.matmul(psum[pi][0:64, :], lhsT=ones64, rhs=ob[:], start=True, stop=True), ["ob", "ones"], [f"ps{pi}"])
            A(lambda e, pi=pi: e.activation(out=T1[:], in_=psum[pi][0:64, :], func=AF.Sqrt, scale=1.0 / 64, bias=gnb[:, 0:1]), [f"ps{pi}", "gnb"], ["T1"])
            V(lambda e: e.reciprocal(out=T1[:], in_=T1[:]), ["T1"], ["T1"])
            V(lambda e: e.tensor_tensor(out=dd[:], in0=dd[:], in1=T1[:], op=ALU.mult), ["dd", "T1"], ["dd"])
            V(lambda e: e.tensor_scalar(out=dd[:], in0=dd[:], scalar1=pc(13), scalar2=pc(14), op0=ALU.mult, op1=ALU.add), ["dd", "par"], ["dd"])
            V(lambda e, bs=bs: e.tensor_tensor(out=dd[:], in0=dd[:], in1=BONV[:, bs], op=ALU.add), ["dd", "BONV"], ["dd"])
            pi = next_ps()
            for kc in range(2):
                T(lambda e, pi=pi, kc=kc, bs=bs: e.matmul(psum[pi][0:64, :], lhsT=g2b[:, kc, hh * 64:(hh + 1) * 64], rhs=sgT[:, kc, bs], start=(kc == 0), stop=(kc == 1)), ["g2b", "lora_act"], [f"ps{pi}"])
            V(lambda e, pi=pi: e.tensor_tensor(out=yo[:], in0=dd[:], in1=psum[pi][0:64, :], op=ALU.mult), ["dd", f"ps{pi}"], ["ryo"])
            P.op("sync", lambda e, bs=bs: e.dma_start(out=yT[hh // 2][(hh % 2) * 64:(hh % 2) * 64 + 64, bs], in_=yo[:]), reads=["ryo"], writes=["yT"], dma_key="ryo")

    for hh in range(rw_heads):
        head(hh)


def relayout_rwkv(inp, c, l=0):
    b, g = c // 4, c % 4
    chs = slice(512 * g, 512 * (g + 1))
    par = np.zeros((64, 8, 16), np.float32)
    sp, sn = inp["shift_prev"][l], inp["shift_next"][l]
    for hh in range(8):
        cg = slice(512 * g + hh * 64, 512 * g + (hh + 1) * 64)
        for i in range(3):
            par[:, hh, 2 * i] = sp[i * 2048:(i + 1) * 2048][cg]
            par[:, hh, 2 * i + 1] = sn[i * 2048:(i + 1) * 2048][cg]
        par[:, hh, 6] = inp["decay_bias_fwd"][l][cg]; par[:, hh, 7] = inp["decay_bias_bwd"][l][cg]
        par[:, hh, 8] = inp["iclr_bias_fwd"][l][cg]; par[:, hh, 9] = inp["iclr_bias_bwd"][l][cg]
        par[:, hh, 10] = inp["k_k"][l][cg]; par[:, hh, 11] = inp["k_a"][l][cg]
        par[:, hh, 12] = inp["r_k"][l].reshape(-1)[cg]
        par[:, hh, 13] = inp["ln_x_gain"][l][cg]; par[:, hh, 14] = inp["ln_x_bias"][l][cg]
    parl = np.zeros((128, 4, 2), np.float32)
    lo = 3 * 2048
    for i, (a0, n) in enumerate(((lo, 96), (lo + 96, 96), (lo + 192, 128), (lo + 320, 128))):
        parl[:n, i, 0] = sp[a0:a0 + n]; parl[:n, i, 1] = sn[a0:a0 + n]
    lw2 = np.stack([inp[k][l][:, chs] for k in ("decay_up_fwd", "decay_up_bwd", "iclr_up_fwd", "iclr_up_bwd")], axis=1)
    g2 = np.ascontiguousarray(inp["gate_up"][l][:, chs].reshape(2, 128, 512).transpose(1, 0, 2))
    ii = np.arange(128)
    row, col = ii[:, None], ii[None, :]
    masks = np.zeros((128, 2, 640), np.float32)
    for d in range(2):
        lt = (row < col) if d == 0 else (row > col)
        le = (row <= col) if d == 0 else (row >= col)
        masks[:, d, 0:128] = -lt.astype(np.float32)
        masks[:, d, 128:256] = lt
        masks[:, d, 256:384] = -lt.T.astype(np.float32)
        masks[:, d, 384:512] = le
        masks[:, d, 512:640] = le
    mreset = np.ones((64, BLK), np.float32)
    mreset[:, ::C] = 0.0
    return dict(par=par, parl=parl, lw2=np.ascontiguousarray(lw2).astype(np.float32), g2=g2.astype(np.float32), masks=masks, mreset=mreset)


F32 = mybir.dt.float32
BF16 = mybir.dt.bfloat16
AF = mybir.ActivationFunctionType
ALU = mybir.AluOpType
D = 4096
S = 4096
EPS = 1e-6
NCH = 28
NR = 6592
LAMBDA_INIT = 0.8 - 0.6
WARMN = 0
NFILL = 1


def body_A(nc, P, st, sh, yT, do_proj=True, do_attn=True, do_rwkv=True, n_tb=8, attn_heads=4, attn_qb=8, rw_heads=8):
    xb = nc.dram_tensor("xb", [S, D], F32, kind="ExternalInput").ap()
    wA = nc.dram_tensor("wA", [NCH, 128, 32, 128], F32, kind="ExternalInput").ap()
    abias = nc.dram_tensor("abias", [4, 5, 128, 512], F32, kind="ExternalInput").ap()
    acst = nc.dram_tensor("acst", [128, 4 * 64], F32, kind="ExternalInput").ap()
    lam_d = nc.dram_tensor("lam", [128, 4, 64], F32, kind="ExternalInput").ap()
    subg = nc.dram_tensor("subg", [128, 1], F32, kind="ExternalInput").ap()
    PT_d = nc.dram_tensor("PT_d", [NCH, 128, S], F32).ap()
    gain = sh["gains"][0]
    ident, ones, small, epsb = sh["ident"], sh["ones"], sh["small"], sh["epsb"]
    psum, pst, state, next_ps = sh["psum"], sh["pst"], sh["state"], sh["next_ps"]
    if True:
        if do_proj:
            with ExitStack() as st2:
                bufAs = [st2.enter_context(nc.sbuf_tensor(f"bufA{i}", [128, 32, 512], BF16)) for i in range(2)]
                xt = st2.enter_context(nc.sbuf_tensor("xt", [128, D], F32))
                gt = st2.enter_context(nc.sbuf_tensor("gt", [128, D], F32))
                hb = st2.enter_context(nc.sbuf_tensor("hb", [128, D], BF16))
                wbuf = [st2.enter_context(nc.sbuf_tensor(f"wb{i}", [128, 32, 128], BF16)) for i in range(3)]
                ot = [st2.enter_context(nc.sbuf_tensor(f"ot{i}", [128, 512], F32)) for i in range(3)]
                P.op("sync", lambda e: e.dma_start(out=gt[:], in_=gain), writes=["gt"], dma_key="gt")
                for tb in range(n_tb):
                    bufA = bufAs[tb % 2]
                    bk = f"bufA{tb % 2}"
                    for tt in range(4):
                        r0 = tb * 512 + tt * 128
                        P.op("sync", lambda e, r0=r0: e.dma_start(out=xt[:], in_=xb[r0:r0 + 128, :]), writes=["xt"], dma_key="xt")
                        P.op("vector", lambda e: e.memset(small[:, 0:1], 0.0), writes=["sm0"])
                        P.op("scalar", lambda e: e.activation(out=hb[:], in_=xt[:], func=AF.Square, accum_out=small[:, 0:1]), reads=["xt", "sm0"], writes=["hb", "sm0"])
                        P.op("scalar", lambda e: e.activation(out=small[:, 0:1], in_=small[:, 0:1], func=AF.Sqrt, scale=1.0 / D, bias=epsb[:, 0:1]), reads=["sm0", "epsb"], writes=["sm0"])
                        P.op("vector", lambda e: e.reciprocal(out=small[:, 0:1], in_=small[:, 0:1]), reads=["sm0"], writes=["sm0"])
                        P.op("vector", lambda e: e.scalar_tensor_tensor(out=hb[:], in0=xt[:], scalar=small[:, 0:1], in1=gt[:], op0=ALU.mult, op1=ALU.mult),
                             reads=["xt", "sm0", "gt"], writes=["hb"])
                        for k8 in range(4):
                            pi = state["pt"]; state["pt"] ^= 1
                            for j in range(8):
                                kc = k8 * 8 + j
                                P.op("tensor", lambda e, kc=kc, j=j, pi=pi: e.transpose(out=pst[pi][:, j * 128:(j + 1) * 128], in_=hb[:, kc * 128:(kc + 1) * 128], identity=ident[:]),
                                     reads=["hb", "ident"], writes=[f"ps{6 + pi}"])
                            dst = bufA[:, k8 * 8:(k8 + 1) * 8, tt * 128:(tt + 1) * 128]
                            srcp = pst[pi][:, :].rearrange("p (k t) -> p k t", k=8)
                            if k8 % 2 == 0:
                                P.op("scalar", lambda e, dst=dst, srcp=srcp: e.activation(out=dst, in_=srcp, func=AF.Copy), reads=[f"ps{6 + pi}"], writes=[bk])
                            else:
                                P.op("vector", lambda e, dst=dst, srcp=srcp: e.tensor_copy(out=dst, in_=srcp), reads=[f"ps{6 + pi}"], writes=[bk])
                    for ch in range(NCH):
                        s = ch % 3
                        for half in range(2):
                            P.op("gpsimd", lambda e, s=s, ch=ch, half=half: e.dma_start(out=wbuf[s][:, half * 16:(half + 1) * 16, :], in_=wA[ch][:, half * 16:(half + 1) * 16, :], max_dma_last_dim=4096),
                                 writes=[f"wb{s}"], dma_key=f"wb{s}")
                        pi = next_ps()
                        for kc in range(32):
                            P.op("tensor", lambda e, kc=kc, pi=pi, s=s, bufA=bufA: e.matmul(psum[pi][:], lhsT=wbuf[s][:, kc, :], rhs=bufA[:, kc, :], start=(kc == 0), stop=(kc == 31)),
                                 reads=[f"wb{s}", bk], writes=[f"ps{pi}"])
                        if ch % 2 == 0:
                            P.op("scalar", lambda e, pi=pi, s=s: e.activation(out=ot[s][:], in_=psum[pi][:], func=AF.Copy), reads=[f"ps{pi}"], writes=[f"ot{s}"])
                        else:
                            P.op("vector", lambda e, pi=pi, s=s: e.tensor_copy(out=ot[s][:], in_=psum[pi][:]), reads=[f"ps{pi}"], writes=[f"ot{s}"])
                        P.op("sync", lambda e, s=s, ch=ch, tb=tb: e.dma_start(out=PT_d[ch][:, tb * 512:(tb + 1) * 512], in_=ot[s][:]), reads=[f"ot{s}"], writes=[f"PT{ch}"], dma_key=f"ot{s}")
                P.fence()
        if do_attn:
            with ExitStack() as st2:
                def sb2(name, shape, dt):
                    return st2.enter_context(nc.sbuf_tensor(name, shape, dt))
                LA = 2
                NSL = LA + 1
                qk32 = sb2("qk32", [128, S], F32)
                QTs = [sb2(f"QT{i}", [64, 2, S], BF16) for i in range(2)]
                KTs = [sb2(f"KT{i}", [64, 2, S], BF16) for i in range(2)]
                Vts = [sb2(f"Vt{i}", [128, 32, 128], BF16) for i in range(2)]
                vb = sb2("vb", [128, S], BF16)
                bts = [sb2(f"bt{i}", [128, 5, 512], F32) for i in range(2)]
                cst = sb2("cst", [128, 256], F32)
                lamt = sb2("lamt", [128, 4, 64], F32)
                lsm = sb2("lsm", [128, 8], F32)
                sgt = sb2("sgt", [128, 1], F32)
                tmp = [sb2(f"atmp{i}", [128, 512], F32) for i in range(NSL)]
                Eb = [sb2(f"Eb{i}", [128, 512], BF16) for i in range(NSL)]
                o0 = sb2("o0", [128, 512], F32)
                o1 = sb2("o1", [128, 512], F32)
                rr = sb2("rr", [128, 512], F32)
                rr2 = sb2("rr2", [128, 512], F32)
                sq = sb2("sq", [128, 512], BF16)
                yo = sb2("yo", [128, 512], BF16)
                wz = sb2("wz", [128, 512], BF16)
                P.op("vector", lambda e: e.memset(wz[:], 0.0), writes=["wz"])

                def warm(n):
                    for _ in range(n):
                        P.op("tensor", lambda e: e.matmul(psum[7][:, 0:384], lhsT=ones[:], rhs=wz[:, 0:384], start=True, stop=True), reads=["wz", "ones"], writes=["ps7"])
                P.op("sync", lambda e: e.dma_start(out=cst[:], in_=acst), writes=["cst"], dma_key="cst")
                P.op("sync", lambda e: e.dma_start(out=lamt[:], in_=lam_d), writes=["lamt"], dma_key="lamt")
                P.op("sync", lambda e: e.dma_start(out=sgt[:], in_=subg), writes=["sgt"], dma_key="sgt")
                for i in range(2):
                    P.op("vector", lambda e, i=i: e.tensor_tensor(out=lamt[:, 2 * i, :], in0=lamt[:, 2 * i, :], in1=lamt[:, 2 * i + 1, :], op=ALU.mult), reads=["lamt"], writes=["lamt"])
                    P.op("vector", lambda e, i=i: e.tensor_reduce(out=lsm[:, i:i + 1], in_=lamt[:, 2 * i, :], axis=mybir.AxisListType.X, op=ALU.add), reads=["lamt"], writes=["lsm"])
                    P.op("scalar", lambda e, i=i: e.activation(out=lsm[:, i:i + 1], in_=lsm[:, i:i + 1], func=AF.Exp), reads=["lsm"], writes=["lsm"])
                P.op("vector", lambda e: e.tensor_tensor(out=lsm[:, 2:3], in0=lsm[:, 1:2], in1=lsm[:, 0:1], op=ALU.subtract), reads=["lsm"], writes=["lsm"])
                P.op("vector", lambda e: e.tensor_scalar(out=lsm[:, 2:3], in0=lsm[:, 2:3], scalar1=-LAMBDA_INIT, scalar2=None, op0=ALU.add), reads=["lsm"], writes=["lsm"])
                P.op("vector", lambda e: e.tensor_scalar(out=sgt[:], in0=sgt[:], scalar1=1.0 - LAMBDA_INIT, scalar2=None, op0=ALU.mult), reads=["sgt"], writes=["sgt"])

                def load_head(hd):
                    hs = hd % 2
                    QT, KT, Vt, bt = QTs[hs], KTs[hs], Vts[hs], bts[hs]
                    P.op("sync", lambda e: e.dma_start(out=bt[:], in_=abias[hd].rearrange("f p q -> p f q")), writes=[f"bt{hs}"], dma_key=f"bt{hs}")
                    for (dstT, ch, nm) in ((QT, 16 + hd, f"QT{hs}"), (KT, 20 + hd, f"KT{hs}")):
                        for c in range(2):
                            P.op("sync", lambda e, ch=ch, c=c: e.dma_start(out=qk32[0:64, :], in_=PT_d[ch][c * 64:(c + 1) * 64, :]), reads=[f"PT{ch}"], writes=["qk32"], dma_key="qk32")
                            P.op("scalar", lambda e, dstT=dstT, c=c: e.activation(out=dstT[:, c, :], in_=qk32[0:64, :], func=AF.Copy), reads=["qk32"], writes=[nm])
                    P.op("sync", lambda e: e.dma_start(out=qk32[:], in_=PT_d[24 + hd]), reads=[f"PT{24 + hd}"], writes=["qk32"], dma_key="qk32")
                    P.op("vector", lambda e: e.tensor_copy(out=vb[:], in_=qk32[:]), reads=["qk32"], writes=["vb"])
                    for k8 in range(4):
                        pi = state["pt"]; state["pt"] ^= 1
                        for j in range(8):
                            blk = k8 * 8 + j
                            P.op("tensor", lambda e, blk=blk, j=j, pi=pi: e.transpose(out=pst[pi][:, j * 128:(j + 1) * 128], in_=vb[:, blk * 128:(blk + 1) * 128], identity=ident[:]),
                                 reads=["vb", "ident"], writes=[f"ps{6 + pi}"])
                        P.op("vector", lambda e, k8=k8, pi=pi: e.tensor_copy(out=Vt[:, k8 * 8:(k8 + 1) * 8, :], in_=pst[pi][:, :].rearrange("p (k t) -> p k t", k=8)), reads=[f"ps{6 + pi}"], writes=[f"Vt{hs}"])

                pacc = [0, 1, 2, 3]
                pending = [None]

                def unit_front(hd, qb, i, ulist):
                    hs = hd % 2
                    kb, c = ulist[i]
                    delta = kb - 4 * qb
                    pi = 4 + (i % NSL)
                    ti = i % NSL
                    P.op("tensor", lambda e: e.matmul(psum[pi][:], lhsT=KTs[hs][:, c, kb * 128:(kb + 1) * 128], rhs=QTs[hs][:, c, qb * 512:(qb + 1) * 512], start=True, stop=True),
                         reads=[f"QT{hs}", f"KT{hs}"], writes=[f"ps{pi}"])
                    if delta >= 4:
                        bti, op1 = 0, ALU.add
                    elif delta < 0:
                        bti, op1 = 0, ALU.subtract
                    else:
                        bti, op1 = 1 + delta, ALU.add
                    P.op("vector", lambda e: e.scalar_tensor_tensor(out=tmp[ti][:], in0=psum[pi][:], scalar=0.125, in1=bts[hs][:, bti, :], op0=ALU.mult, op1=op1),
                         reads=[f"ps{pi}", f"bt{hs}"], writes=[f"atmp{ti}"])
                    ci = hd * 64 + (delta + 32)
                    P.op("scalar", lambda e: e.activation(out=Eb[ti][:], in_=tmp[ti][:], func=AF.Exp, bias=cst[:, ci:ci + 1]), reads=[f"atmp{ti}", "cst"], writes=[f"Eb{ti}"])

                def unit_back(hd, qb, i, ulist, kbs):
                    hs = hd % 2
                    kb, c = ulist[i]
                    ti = i % NSL
                    P.op("tensor", lambda e: e.matmul(psum[pacc[2 * c]][:], lhsT=Vts[hs][:, kb, :], rhs=Eb[ti][:], start=(kb == kbs[0]), stop=(kb == kbs[-1])),
                         reads=[f"Eb{ti}", f"Vt{hs}"], writes=[f"ps{pacc[2 * c]}"])
                    P.op("tensor", lambda e: e.matmul(psum[pacc[2 * c + 1]][:], lhsT=ones[:], rhs=Eb[ti][:], start=(kb == kbs[0]), stop=(kb == kbs[-1])),
                         reads=[f"Eb{ti}", "ones"], writes=[f"ps{pacc[2 * c + 1]}"])

                def fin1():
                    P.op("vector", lambda e: e.reciprocal(out=rr[:], in_=psum[pacc[1]][:]), reads=[f"ps{pacc[1]}"], writes=["rr"])
                    P.op("vector", lambda e: e.tensor_tensor(out=o0[:], in0=psum[pacc[0]][:], in1=rr[:], op=ALU.mult), reads=[f"ps{pacc[0]}", "rr"], writes=["o0"])
                    P.op("vector", lambda e: e.reciprocal(out=rr[:], in_=psum[pacc[3]][:]), reads=[f"ps{pacc[3]}"], writes=["rr"])
                    P.op("vector", lambda e: e.tensor_tensor(out=o1[:], in0=psum[pacc[2]][:], in1=rr[:], op=ALU.mult), reads=[f"ps{pacc[2]}", "rr"], writes=["o1"])
                    P.op("vector", lambda e: e.scalar_tensor_tensor(out=o0[:], in0=o1[:], scalar=lsm[:, 2:3], in1=o0[:], op0=ALU.mult, op1=ALU.add), reads=["o0", "o1", "lsm"], writes=["o0"])
                    P.op("scalar", lambda e: e.activation(out=sq[:], in_=o0[:], func=AF.Square), reads=["o0"], writes=["sq"])

                def fin2(hd, qb, pi):
                    P.op("tensor", lambda e: e.matmul(psum[pi][:], lhsT=ones[:], rhs=sq[:], start=True, stop=True), reads=["sq", "ones"], writes=[f"ps{pi}"])
                    P.op("scalar", lambda e: e.activation(out=rr2[:], in_=psum[pi][:], func=AF.Sqrt, scale=1.0 / 128, bias=epsb[:, 1:2]), reads=[f"ps{pi}", "epsb"], writes=["rr2"])
                    P.op("vector", lambda e: e.reciprocal(out=rr2[:], in_=rr2[:]), reads=["rr2"], writes=["rr2"])
                    P.op("vector", lambda e: e.scalar_tensor_tensor(out=yo[:], in0=o0[:], scalar=sgt[:, 0:1], in1=rr2[:], op0=ALU.mult, op1=ALU.mult), reads=["o0", "sgt", "rr2"], writes=["yo"])
                    P.op("sync", lambda e: e.dma_start(out=yT[4 + hd][:, qb * 512:(qb + 1) * 512], in_=yo[:]), reads=["yo"], writes=["yT"], dma_key="yo")

                load_head(0)

                def kept_kbs(hd, qb):
                    smin = 2.0 ** (-2.0 * (hd + 1))
                    out = []
                    for kb in range(32):
                        delta = kb - 4 * qb
                        dmin = 128 * (delta - 4) + 1 if delta >= 4 else (128 * (-delta - 1) + 1 if delta < 0 else 0)
                        if smin * dmin < 60.0:
                            out.append(kb)
                    return out
                for hd in range(attn_heads):
                    for qb in range(attn_qb):
                        kbs = kept_kbs(hd, qb)
                        ulist = [(kb, c) for kb in kbs for c in range(2)]
                        NU = len(ulist)
                        warm(WARMN)
                        for i in range(NU + LA):
                            if i < NU:
                                unit_front(hd, qb, i, ulist)
                            if i >= LA:
                                unit_back(hd, qb, i - LA, ulist, kbs)
                                warm(NFILL)
                            if i == LA + 1 and pending[0] is not None:
                                ph, pq = pending[0]
                                pending[0] = None
                                fin2(ph, pq, 7)
                        fin1()
                        pending[0] = (hd, qb)
                        if qb == 1 and hd + 1 < attn_heads:
                            load_head(hd + 1)
                        if hd == attn_heads - 1 and qb == attn_qb - 1:
                            fin2(hd, qb, 7)
                            pending[0] = None
                P.fence()
        if do_rwkv:
            with ExitStack() as st3:
                emit_rwkv(nc, P, st3, PT_d, yT, psum, pst, state, next_ps, ident, ones, epsb, rw_heads)
            P.fence()


def build_A(**kw):
    nc = bass.Bass("TRN2", target_bir_lowering=False)
    yT = nc.dram_tensor("yT", [8, 128, S], BF16, kind="ExternalOutput").ap()
    P = Prog(nc)
    with ExitStack() as st:
        sh = make_shared(nc, P, st)
        body_A(nc, P, st, sh, yT, **kw)
        counts = P.emit(st)
        print("A ops", counts, "waits", P.n_waits)
    return nc


def build_fused():
    nc = bass.Bass("TRN2", target_bir_lowering=False)
    yTi = nc.dram_tensor("yTi", [8, 128, S], BF16).ap()
    G = nc.dram_tensor("Gy", [8, 4, 128, S], BF16).ap()
    sel = nc.dram_tensor("sel", [128, 4], F32, kind="ExternalInput").ap()
    P = Prog(nc)
    with ExitStack() as st:
        sh = make_shared(nc, P, st)
        body_A(nc, P, st, sh, yTi)
        for k in range(8):
            P.op("gpsimd", lambda e, k=k: e.collective_compute("AllGather", ALU.bypass, replica_groups=[[0, 1, 2, 3], [4, 5, 6, 7]],
                                                               ins=[yTi[k].opt()], outs=[G[k].rearrange("g p t -> (g p) t").opt()]),
                 reads=["yT"], writes=["G"], dma_key="cc", inc=1)
        with ExitStack() as st4:
            body_B(nc, P, st4, sh, ("gather", G, sel))
        counts = P.emit(st)
        print("fused ops", counts, "waits", P.n_waits)
    return nc


def slopes():
    H = 16
    return np.exp2(-8.0 * np.arange(1, H + 1, dtype=np.float32) / H).astype(np.float32)


def relayout_A(inp, c, l=0):
    b, g = c // 4, c % 4
    w_in = inp["w_in"][l]
    cols = []
    for part in range(3):
        cols.append(np.arange(part * 2048 + 512 * g, part * 2048 + 512 * (g + 1)))
    lo = 3 * 2048
    cols.append(np.arange(lo, lo + 96)); pad1 = 32
    cols.append(np.arange(lo + 96, lo + 192)); pad2 = 32
    cols.append(np.arange(lo + 192, lo + 448))
    for part in range(3):
        cols.append(np.concatenate([np.arange(NR + part * 2048 + (4 * j + g) * 128, NR + part * 2048 + (4 * j + g + 1) * 128) for j in range(4)]))
    W = np.zeros((D, NCH * 128), np.float32)
    W[:, 0:1536] = w_in[:, np.concatenate(cols[0:3])]
    W[:, 1536:1536 + 96] = w_in[:, cols[3]]
    W[:, 1664:1664 + 96] = w_in[:, cols[4]]
    W[:, 1792:2048] = w_in[:, cols[5]]
    W[:, 2048:3584] = w_in[:, np.concatenate(cols[6:9])]
    wA = np.ascontiguousarray(W.reshape(32, 128, NCH, 128).transpose(2, 1, 0, 3))
    gain = np.ascontiguousarray(np.broadcast_to(inp["attn_pre_norm"][l][None, :], (128, D))).astype(np.float32)
    ident = np.eye(128, dtype=np.float32).astype(ml_dtypes.bfloat16)
    ones = np.ones((128, 128), np.float32).astype(ml_dtypes.bfloat16)
    sl = slopes()
    kk = np.arange(128, dtype=np.float32)[:, None]
    qq = np.arange(512, dtype=np.float32)[None, :]
    abias = np.zeros((4, 5, 128, 512), np.float32)
    acst = np.zeros((128, 256), np.float32)
    for hd in range(4):
        s_ = sl[4 * hd + g]
        abias[hd, 0] = -s_ * (kk - qq)
        for dl in range(4):
            abias[hd, 1 + dl] = -s_ * np.abs(128.0 * dl + kk - qq)
        for delta in range(-32, 32):
            if delta >= 4:
                v = -s_ * 128.0 * delta
            elif delta < 0:
                v = s_ * 128.0 * delta
            else:
                v = 0.0
            acst[:, hd * 64 + delta + 32] = v
    lam = np.stack([inp[k][l] for k in ("lambda_q1", "lambda_k1", "lambda_q2", "lambda_k2")])
    lam = np.ascontiguousarray(np.broadcast_to(lam[None], (128, 4, 64))).astype(np.float32)
    subg = np.ascontiguousarray(inp["subln_gain"][l].reshape(128, 1)).astype(np.float32)
    return dict(xb=np.ascontiguousarray(inp["x"][b]), wA=wA, abias=abias, acst=acst, lam=lam, subg=subg)


def kernel(**inp):
    inp = {k: np.asarray(v) for k, v in inp.items()}
    n = 8
    nc = build_fused()
    W = relayout_B(inp)
    in_maps = []
    for c in range(n):
        b, g = c // 4, c % 4
        im = dict(W)
        im.update(relayout_A(inp, c))
        im.update(relayout_rwkv(inp, c))
        im["xo"] = np.ascontiguousarray(inp["x"][b, 1024 * g:1024 * (g + 1)])
        sel = np.zeros((128, 4), np.float32)
        sel[:, g] = 1.0
        im["sel"] = sel
        in_maps.append(im)
    res = run_bass_kernel_spmd(nc, in_maps, core_ids=list(range(n)))
    out = np.zeros((2, S, D), np.float32)
    for c in range(n):
        b, g = c // 4, c % 4
        out[b, 1024 * g:1024 * (g + 1)] = res.results[c]["out"]
    return out
```
